# Optimizing a Trainium2 kernel written in Bass

```python
import jax, jax.numpy as jnp
from jax import lax
import numpy as np

D_MODEL = 2048
BATCH = 8
SEQ = 2048
DEPTH = 2

BRANCH_WIDTH = D_MODEL // 2
N_BRANCHES = 3
SSD_HEAD_DIM = 64
SSD_HEADS = BRANCH_WIDTH // SSD_HEAD_DIM
SSD_GROUPS = 2
SSD_STATE = 128
SSD_CONV = 4
SSD_CHUNK = 128
SSD_XBC = BRANCH_WIDTH + 2 * SSD_GROUPS * SSD_STATE
HGRN_EXPAND = 128
HGRN_HEADS = BRANCH_WIDTH // HGRN_EXPAND
HGRN_VDIM = BRANCH_WIDTH // HGRN_HEADS
HGRN_CHUNK = 64
FOX_HEAD_DIM = 64
FOX_HEADS = BRANCH_WIDTH // FOX_HEAD_DIM
FOX_BLOCK = 128
D_FF = 5504
FFN_CONV = 3
NORM_EPS = 1e-6

IN_SIZES = (
    BRANCH_WIDTH,
    SSD_XBC,
    SSD_HEADS,
    BRANCH_WIDTH,
    BRANCH_WIDTH,
    BRANCH_WIDTH,
    BRANCH_WIDTH,
    BRANCH_WIDTH,
    BRANCH_WIDTH,
    BRANCH_WIDTH,
    FOX_HEADS,
    N_BRANCHES * D_MODEL,
)
D_IN = (BRANCH_WIDTH + SSD_XBC + SSD_HEADS + 4 * BRANCH_WIDTH + 3 * BRANCH_WIDTH
        + FOX_HEADS + N_BRANCHES * D_MODEL)

kernel_name = "hybrid_ssd_hgrn2_fox_gated_block"

F32 = jnp.float32


def split_cols(a, sizes):
    idx = np.cumsum(np.array(sizes))[:-1].tolist()
    return jnp.split(a, idx, axis=-1)


def rms_norm(x, w):
    xf = x.astype(F32)
    y = xf * lax.rsqrt(jnp.mean(xf * xf, axis=-1, keepdims=True) + NORM_EPS)
    return (y * w.astype(F32)).astype(x.dtype)


def grouped_rms_norm(x, w, n_groups):
    shp = x.shape
    xf = x.astype(F32).reshape(shp[:-1] + (n_groups, shp[-1] // n_groups))
    y = xf * lax.rsqrt(jnp.mean(xf * xf, axis=-1, keepdims=True) + NORM_EPS)
    return (y.reshape(shp) * w.astype(F32)).astype(x.dtype)


def causal_depthwise_conv(x, w, b):
    k = w.shape[0]
    y = lax.conv_general_dilated(
        x, w[:, None, :].astype(x.dtype), window_strides=(1,), padding=[(k - 1, 0)],
        dimension_numbers=('NWC', 'WIO', 'NWC'), feature_group_count=x.shape[-1])
    return y + b.astype(x.dtype)


def segsum(a):
    t = a.shape[-1]
    ae = jnp.broadcast_to(a[..., :, None], a.shape + (t,))
    strict = jnp.tril(jnp.ones((t, t), bool), -1)
    cs = jnp.cumsum(jnp.where(strict, ae, 0.0), axis=-2)
    return jnp.where(jnp.tril(jnp.ones((t, t), bool)), cs, -jnp.inf)


def ssd_chunked(x, dt, a, bmat, cmat):
    bsz, t, h, p = x.shape
    g = bmat.shape[2]
    j = h // g
    l = SSD_CHUNK
    c = t // l
    xdt = (x * dt[..., None]).reshape(bsz, c, l, g, j, p)
    adt = (dt * a).reshape(bsz, c, l, g, j).transpose(0, 3, 4, 1, 2)
    bc = bmat.reshape(bsz, c, l, g, -1)
    cc = cmat.reshape(bsz, c, l, g, -1)
    a_cs = jnp.cumsum(adt, axis=-1)
    decay_in = jnp.exp(segsum(adt))
    cb = jnp.einsum('bclgn,bcsgn->bgcls', cc, bc)
    y_diag = jnp.einsum('bgjcls,bcsgjp->bclgjp', cb[:, :, None] * decay_in, xdt)
    decay_st = jnp.exp(a_cs[..., -1:] - a_cs).transpose(0, 3, 4, 1, 2)
    states = jnp.einsum('bclgn,bclgjp->bcgjpn', bc, xdt * decay_st[..., None])
    states = jnp.concatenate([jnp.zeros_like(states[:, :1]), states], axis=1)
    chunk_tot = jnp.pad(a_cs[..., -1], ((0, 0), (0, 0), (0, 0), (1, 0)))
    decay_chunk = jnp.exp(segsum(chunk_tot))
    states = jnp.einsum('bgjzc,bcgjpn->bzgjpn', decay_chunk, states)[:, :-1]
    out_decay = jnp.exp(a_cs).transpose(0, 3, 4, 1, 2)
    y_off = jnp.einsum('bclgn,bcgjpn->bclgjp', cc, states) * out_decay[..., None]
    return (y_diag + y_off).reshape(bsz, t, h, p)


def mamba2_branch(z, xbc, dt_raw, conv_w, conv_b, dt_bias, a_log, d_skip, norm_w):
    bsz, t, _ = z.shape
    xbc = jax.nn.silu(causal_depthwise_conv(xbc, conv_w, conv_b))
    xs, bm, cm = split_cols(xbc, (BRANCH_WIDTH, SSD_GROUPS * SSD_STATE, SSD_GROUPS * SSD_STATE))
    xs = xs.astype(F32).reshape(bsz, t, SSD_HEADS, SSD_HEAD_DIM)
    bm = bm.astype(F32).reshape(bsz, t, SSD_GROUPS, SSD_STATE)
    cm = cm.astype(F32).reshape(bsz, t, SSD_GROUPS, SSD_STATE)
    dt = jax.nn.softplus(dt_raw.astype(F32) + dt_bias.astype(F32))
    a = -jnp.exp(a_log.astype(F32))
    y = ssd_chunked(xs, dt, a, bm, cm) + xs * d_skip.astype(F32)[:, None]
    y = y.reshape(bsz, t, BRANCH_WIDTH).astype(z.dtype)
    return grouped_rms_norm(y * jax.nn.silu(z), norm_w, SSD_GROUPS)


def chunked_gated_recurrence(q, k, v, log_f):
    bsz, t, h, dk = q.shape
    dv = v.shape[-1]
    c = HGRN_CHUNK
    n = t // c

    def to_chunks(u):
        return u.reshape(bsz, n, c, h, u.shape[-1]).transpose(1, 0, 3, 2, 4)

    causal = jnp.tril(jnp.ones((c, c), bool))[:, :, None]

    def step(state, inp):
        qc, kc, vc, gc = inp
        b = jnp.cumsum(gc, axis=2)
        diff = b[:, :, :, None, :] - b[:, :, None, :, :]
        decay = jnp.exp(jnp.where(causal, diff, -jnp.inf))
        scores = jnp.einsum('bhtsk,bhsk->bhts', decay * qc[:, :, :, None, :], kc)
        o = (jnp.einsum('bhts,bhsv->bhtv', scores, vc)
             + jnp.einsum('bhtk,bhkv->bhtv', qc * jnp.exp(b), state))
        b_last = b[:, :, -1]
        new_state = (jnp.exp(b_last)[..., None] * state
                     + jnp.einsum('bhsk,bhsv->bhkv', kc * jnp.exp(b_last[:, :, None] - b), vc))
        return new_state, o

    state0 = jnp.zeros((bsz, h, dk, dv), F32)
    _, o = lax.scan(step, state0, (to_chunks(q), to_chunks(k), to_chunks(v), to_chunks(log_f)))
    return o.transpose(1, 0, 3, 2, 4).reshape(bsz, t, h, dv)


def hgrn2_branch(q_raw, f_raw, i_raw, g_raw, lower_bound, norm_w):
    bsz, t, _ = q_raw.shape

    def heads(u, d):
        return u.astype(F32).reshape(bsz, t, HGRN_HEADS, d)

    lb = lower_bound.astype(F32).reshape(HGRN_HEADS, HGRN_EXPAND)
    q = jax.nn.silu(heads(q_raw, HGRN_EXPAND))
    fr = heads(f_raw, HGRN_EXPAND)
    log_f = jnp.logaddexp(jnp.log(lb), jnp.log1p(-lb) + jax.nn.log_sigmoid(fr))
    k = (1.0 - lb) * jax.nn.sigmoid(-fr)
    v = heads(i_raw, HGRN_VDIM)
    o = chunked_gated_recurrence(q, k, v, log_f)
    o = o.reshape(bsz, t, BRANCH_WIDTH).astype(q_raw.dtype)
    return grouped_rms_norm(o, norm_w, HGRN_HEADS) * jax.nn.silu(g_raw)


def forgetting_attention(q_raw, k_raw, v_raw, f_raw, f_bias):
    bsz, t, _ = q_raw.shape
    nb = t // FOX_BLOCK

    def heads(u):
        return u.reshape(bsz, t, FOX_HEADS, FOX_HEAD_DIM).transpose(0, 2, 1, 3)

    q, k, v = heads(q_raw), heads(k_raw), heads(v_raw)
    log_f = jax.nn.log_sigmoid(f_raw.astype(F32) + f_bias.astype(F32)).transpose(0, 2, 1)
    cum = jnp.cumsum(log_f, axis=-1)
    scale = FOX_HEAD_DIM ** -0.5
    q_blocks = q.reshape(bsz, FOX_HEADS, nb, FOX_BLOCK, FOX_HEAD_DIM).transpose(2, 0, 1, 3, 4)
    c_blocks = cum.reshape(bsz, FOX_HEADS, nb, FOX_BLOCK).transpose(2, 0, 1, 3)
    k_pos = jnp.arange(t)

    def block(args):
        qb, cb, i = args
        s = (jnp.einsum('bhqd,bhkd->bhqk', qb, k).astype(F32) * scale
             + cb[..., None] - cum[:, :, None, :])
        q_pos = i * FOX_BLOCK + jnp.arange(FOX_BLOCK)
        s = jnp.where(q_pos[:, None] >= k_pos[None, :], s, -jnp.inf)
        p = jax.nn.softmax(s, axis=-1)
        return jnp.einsum('bhqk,bhkd->bhqd', p.astype(v.dtype), v)

    o = lax.map(block, (q_blocks, c_blocks, jnp.arange(nb)))
    return o.transpose(1, 0, 3, 2, 4).reshape(bsz, t, BRANCH_WIDTH)


def conv_glu_ffn(h, w_up, conv_w, conv_b, w_down):
    u = causal_depthwise_conv(h @ w_up, conv_w, conv_b)
    gate, up = jnp.split(u, 2, axis=-1)
    return (jax.nn.silu(gate) * up) @ w_down


def setup_inputs(seed: int = 0) -> dict:
    key = jax.random.key(seed)
    ks = jax.random.split(key, 24)
    nrm = lambda k, shape, s: jax.random.normal(k, shape, F32) * s
    dt0 = jnp.exp(jax.random.uniform(ks[6], (DEPTH, SSD_HEADS), F32, np.log(1e-3), np.log(1e-1)))
    return {
        "x": jax.random.normal(ks[0], (BATCH, SEQ, D_MODEL), F32),
        "norm_mix_w": 1.0 + nrm(ks[1], (DEPTH, D_MODEL), 0.02),
        "w_in": nrm(ks[2], (DEPTH, D_MODEL, D_IN), D_MODEL ** -0.5),
        "ssd_conv_w": nrm(ks[3], (DEPTH, SSD_CONV, SSD_XBC), SSD_CONV ** -0.5),
        "ssd_conv_b": nrm(ks[4], (DEPTH, SSD_XBC), 0.02),
        "ssd_dt_bias": dt0 + jnp.log(-jnp.expm1(-dt0)),
        "ssd_a_log": jnp.log(jax.random.uniform(ks[7], (DEPTH, SSD_HEADS), F32, 1.0, 16.0)),
        "ssd_d": 1.0 + nrm(ks[8], (DEPTH, SSD_HEADS), 0.01),
        "ssd_norm_w": 1.0 + nrm(ks[9], (DEPTH, BRANCH_WIDTH), 0.02),
        "hgrn_lb": nrm(ks[10], (DEPTH, BRANCH_WIDTH), 0.1),
        "hgrn_norm_w": 1.0 + nrm(ks[11], (DEPTH, BRANCH_WIDTH), 0.02),
        "fox_f_bias": nrm(ks[12], (DEPTH, FOX_HEADS), 0.1),
        "w_branch_ssd": nrm(ks[13], (DEPTH, BRANCH_WIDTH, D_MODEL), BRANCH_WIDTH ** -0.5),
        "w_branch_hgrn": nrm(ks[14], (DEPTH, BRANCH_WIDTH, D_MODEL), BRANCH_WIDTH ** -0.5),
        "w_branch_fox": nrm(ks[15], (DEPTH, BRANCH_WIDTH, D_MODEL), BRANCH_WIDTH ** -0.5),
        "w_out": nrm(ks[16], (DEPTH, D_MODEL, D_MODEL), D_MODEL ** -0.5),
        "norm_ffn_w": 1.0 + nrm(ks[17], (DEPTH, D_MODEL), 0.02),
        "ffn_w_up": nrm(ks[18], (DEPTH, D_MODEL, 2 * D_FF), D_MODEL ** -0.5),
        "ffn_conv_w": nrm(ks[19], (DEPTH, FFN_CONV, 2 * D_FF), FFN_CONV ** -0.5),
        "ffn_conv_b": nrm(ks[20], (DEPTH, 2 * D_FF), 0.02),
        "ffn_w_down": nrm(ks[21], (DEPTH, D_FF, D_MODEL), D_FF ** -0.5),
        "final_norm_w": 1.0 + nrm(ks[22], (D_MODEL,), 0.02),
    }


def reference(x, norm_mix_w, w_in, ssd_conv_w, ssd_conv_b, ssd_dt_bias, ssd_a_log, ssd_d,
              ssd_norm_w, hgrn_lb, hgrn_norm_w, fox_f_bias, w_branch_ssd, w_branch_hgrn,
              w_branch_fox, w_out, norm_ffn_w, ffn_w_up, ffn_conv_w, ffn_conv_b, ffn_w_down,
              final_norm_w):
    bsz, t, d = x.shape
    lbs = jnp.cumsum(jax.nn.softmax(hgrn_lb.astype(F32), axis=0), axis=0)
    lbs = lbs - lbs[0]
    for l in range(DEPTH):
        h = rms_norm(x, norm_mix_w[l])
        proj = h @ w_in[l]
        (z, xbc, dt_raw, hq, hf, hi, hg, fq, fk, fv, ff, gates) = split_cols(proj, IN_SIZES)
        y_ssd = mamba2_branch(z, xbc, dt_raw, ssd_conv_w[l], ssd_conv_b[l], ssd_dt_bias[l],
                              ssd_a_log[l], ssd_d[l], ssd_norm_w[l])
        y_hgrn = hgrn2_branch(hq, hf, hi, hg, lbs[l], hgrn_norm_w[l])
        y_fox = forgetting_attention(fq, fk, fv, ff, fox_f_bias[l])
        g = jax.nn.sigmoid(gates.astype(F32)).astype(x.dtype).reshape(bsz, t, N_BRANCHES, d)
        merged = (g[:, :, 0] * (y_ssd @ w_branch_ssd[l])
                  + g[:, :, 1] * (y_hgrn @ w_branch_hgrn[l])
                  + g[:, :, 2] * (y_fox @ w_branch_fox[l]))
        x = x + merged @ w_out[l]
        h = rms_norm(x, norm_ffn_w[l])
        x = x + conv_glu_ffn(h, ffn_w_up[l], ffn_conv_w[l], ffn_conv_b[l], ffn_w_down[l])
    return rms_norm(x, final_norm_w)
```

```python
import numpy as np
from contextlib import ExitStack
import concourse.bass as bass
import concourse.mybir as mybir
from concourse.bass_utils import run_bass_kernel_spmd

F32 = mybir.dt.float32
BF16 = mybir.dt.bfloat16
AF = mybir.ActivationFunctionType
ALU = mybir.AluOpType

T = 2048
D = 2048
NT = 16
KC = 16
DIN = 15904
DFF = 5504
NFB = 43
EPS = 1e-6
NDS = 24

O_Z, O_XBC, O_DT, O_HQ, O_HF, O_HI, O_HG, O_FQ, O_FK, O_FV, O_FF, O_G = (
    0, 1024, 2560, 2576, 3600, 4624, 5648, 6672, 7696, 8720, 9744, 9760)


class KB:
    def __init__(self, nc):
        self.nc = nc
        self.E = {'pe': nc.tensor, 'act': nc.scalar, 'dve': nc.vector,
                  'pool': nc.gpsimd, 'sp': nc.sync}
        self.sem = {k: nc.alloc_semaphore("sem_" + k) for k in self.E}
        self.icnt = {k: 0 for k in self.E}
        self.scnt = {k: 0 for k in self.E}
        self.last = {k: None for k in self.E}
        self.incs = {k: [] for k in self.E}
        self.seen = {k: {k2: 0 for k2 in self.E} for k in self.E}
        self.dsem = [nc.alloc_semaphore("dsem%d" % i) for i in range(NDS)]
        self.dcnt = [0] * NDS
        self.dnext = 0
        self.dseen = {k: [0] * NDS for k in self.E}
        self.lastw = {}
        self.readers = {}

    def _deps(self, reads, writes):
        deps = []
        for r in reads:
            t = self.lastw.get(r)
            if t is not None:
                deps.append(t)
        for w in writes:
            t = self.lastw.get(w)
            if t is not None:
                deps.append(t)
            rd = self.readers.get(w)
            if rd:
                for e, i in rd['e'].items():
                    deps.append(('e', e, i))
                deps.extend(rd['d'])
        return deps

    def _semval_for(self, e2, idx):
        lst = self.incs[e2]
        if lst and lst[-1][0] >= idx:
            j = len(lst) - 1
            while j > 0 and lst[j - 1][0] >= idx:
                j -= 1
            return lst[j]
        ins, lidx = self.last[e2]
        assert lidx >= idx
        self.scnt[e2] += 1
        ins.then_inc(self.sem[e2], 1)
        lst.append((lidx, self.scnt[e2]))
        return lst[-1]

    def _wait(self, eng, deps, raw_same=()):
        need = {}
        dneed = {}
        for t in deps:
            if t[0] == 'e':
                _, e2, idx = t
                if e2 == eng:
                    continue
                if idx > need.get(e2, 0):
                    need[e2] = idx
            else:
                _, s, c = t
                if c > dneed.get(s, 0):
                    dneed[s] = c
        for idx in raw_same:
            if eng != 'pe' and idx > need.get(eng, 0):
                need[eng] = idx
        E = self.E[eng]
        for e2, idx in need.items():
            if idx <= self.seen[eng][e2]:
                continue
            iidx, v = self._semval_for(e2, idx)
            E.wait_ge(self.sem[e2], v)
            self.seen[eng][e2] = iidx
        for s, c in dneed.items():
            if c <= self.dseen[eng][s]:
                continue
            E.wait_ge(self.dsem[s], c)
            self.dseen[eng][s] = c

    def _record(self, tok, reads, writes):
        for r in reads:
            rd = self.readers.get(r)
            if rd is None:
                rd = {'e': {}, 'd': []}
                self.readers[r] = rd
            if tok[0] == 'e':
                rd['e'][tok[1]] = tok[2]
            else:
                rd['d'].append(tok)
        for w in writes:
            self.lastw[w] = tok
            self.readers[w] = {'e': {}, 'd': []}

    def op(self, eng, reads, writes, fn):
        deps = self._deps(reads, writes)
        raw_same = []
        for r in reads:
            t = self.lastw.get(r)
            if t is not None and t[0] == 'e' and t[1] == eng:
                raw_same.append(t[2])
        self._wait(eng, deps, raw_same)
        ins = fn(self.E[eng])
        self.icnt[eng] += 1
        self.last[eng] = (ins, self.icnt[eng])
        self._record(('e', eng, self.icnt[eng]), reads, writes)
        return ins

    def dma(self, out, in_, reads, writes, q='sp', **kw):
        deps = self._deps(reads, writes)
        s = self.dnext
        self.dnext = (self.dnext + 1) % NDS
        if self.dcnt[s] > 0:
            deps.append(('d', s, self.dcnt[s]))
        raw_same = []
        for r in reads:
            t = self.lastw.get(r)
            if t is not None and t[0] == 'e' and t[1] == q:
                raw_same.append(t[2])
        self._wait(q, deps, raw_same)
        ins = self.E[q].dma_start(out=out, in_=in_, **kw)
        self.dcnt[s] += 16
        ins.then_inc(self.dsem[s], 16)
        self._record(('d', s, self.dcnt[s]), reads, writes)

    def barrier(self):
        deps = []
        for e in self.E:
            if self.last[e] is not None:
                deps.append(('e', e, self.last[e][1]))
        for s in range(NDS):
            if self.dcnt[s] > 0:
                deps.append(('d', s, self.dcnt[s]))
        for e in self.E:
            self._wait(e, deps, [self.last[e][1]] if (self.last[e] is not None and e != 'sp') else [])
        self.lastw = {}
        self.readers = {}


class Prog:
    def __init__(self, nc, dbg=(), tiny=False):
        self.nc = nc
        self.k = KB(nc)
        self.dbg = set(dbg)
        self.cast_rr = 0
        self.uid = 0
        k = self.k
        BIG = ("w_in", "w_branch_ssd", "w_branch_hgrn", "w_branch_fox", "w_out", "ffn_w_up", "ffn_w_down")
        di = lambda n, s: nc.dram_tensor(n, ([2, 1, 1] if (tiny and n in BIG) else list(s)), F32, kind="ExternalInput").ap()
        self.x = di("x", [T, D])
        self.norm_mix_w = di("norm_mix_w", [2, D])
        self.w_in = di("w_in", [2, D, DIN])
        self.ssd_conv_w = di("ssd_conv_w", [2, 4, 1536])
        self.ssd_conv_b = di("ssd_conv_b", [2, 1536])
        self.ssd_dt_bias = di("ssd_dt_bias", [2, 16])
        self.ssd_a_log = di("ssd_a_log", [2, 16])
        self.ssd_d = di("ssd_d", [2, 16])
        self.ssd_norm_w = di("ssd_norm_w", [2, 1024])
        self.hgrn_lb = di("hgrn_lb", [2, 1024])
        self.hgrn_norm_w = di("hgrn_norm_w", [2, 1024])
        self.fox_f_bias = di("fox_f_bias", [2, 16])
        self.w_branch_ssd = di("w_branch_ssd", [2, 1024, D])
        self.w_branch_hgrn = di("w_branch_hgrn", [2, 1024, D])
        self.w_branch_fox = di("w_branch_fox", [2, 1024, D])
        self.w_out = di("w_out", [2, D, D])
        self.norm_ffn_w = di("norm_ffn_w", [2, D])
        self.ffn_w_up = di("ffn_w_up", [2, D, 2 * DFF])
        self.ffn_conv_w = di("ffn_conv_w", [2, 3, 2 * DFF])
        self.ffn_conv_b = di("ffn_conv_b", [2, 2 * DFF])
        self.ffn_w_down = di("ffn_w_down", [2, DFF, D])
        self.final_norm_w = di("final_norm_w", [D])
        self.out = nc.dram_tensor("out", [T, D], F32, kind="ExternalOutput").ap()
        self.xa = self.scr("xa", [T, D], F32)
        self.xb = self.scr("xb", [T, D], F32)
        self.xs_tm = self.scr("xs_tm", [T, 1024], BF16)
        self.B_tm = self.scr("B_tm", [T, 256], BF16)
        self.BT = self.scr("BT", [2, 128, T], BF16)
        self.CT = self.scr("CT", [2, 128, T], BF16)
        self.zs_tm = self.scr("zs_tm", [T, 1024], BF16)
        self.dtff = self.scr("dtff", [T, 32], F32)
        self.hqT = self.scr("hqT", [1024, T], BF16)
        self.hsigT = self.scr("hsigT", [1024, T], F32)
        self.hgT = self.scr("hgT", [1024, T], BF16)
        self.hv_tm = self.scr("hv_tm", [T, 1024], BF16)
        self.fqA = self.scr("fqA", [16, 70, T], BF16)
        self.fkA = self.scr("fkA", [16, 70, T], BF16)
        self.fv_tm = self.scr("fv_tm", [T, 1024], BF16)
        self.gT = self.scr("gT", [6144, T], BF16)
        self.ysT = self.scr("ysT", [1024, T], BF16)
        self.yhT = self.scr("yhT", [1024, T], BF16)
        self.yfT = self.scr("yfT", [16, 64, T], BF16)
        self.mT = self.scr("mT", [D, T], BF16)
        self.actT = self.scr("actT", [DFF, T], BF16)
        self.PB = [nc.alloc_psum_tensor("pb%d" % i, [128, 512], F32) for i in range(8)]
        self.PT = self.PB[7][:].bitcast(BF16)
        self.PT6 = self.PB[6][:].bitcast(BF16)
        self.cst = ExitStack()
        sb = lambda n, s, dt=F32: self.cst.enter_context(nc.sbuf_tensor(n, list(s), dt))
        self.ones_f = sb("ones_f", [128, 128])
        self.ones_b = sb("ones_b", [128, 128], BF16)
        self.ident_b = sb("ident_b", [128, 128], BF16)
        self.triI = sb("triI", [128, 128])
        self.triSU = sb("triSU", [128, 128])
        self.zeros_f = sb("zeros_f", [128, 512])
        tmp = sb("c_tmp", [128, 128])
        k.op('pool', [], ['ones_f'], lambda e: e.memset(self.ones_f[:], 1.0))
        k.op('pool', [], ['zeros_f'], lambda e: e.memset(self.zeros_f[:], 0.0))
        k.op('pool', ['ones_f'], ['c_tmp'], lambda e: e.affine_select(
            out=tmp[:], in_=self.ones_f[:], pattern=[[1, 128]], compare_op=ALU.is_equal,
            fill=0.0, base=0, channel_multiplier=-1))
        k.op('pool', ['ones_f'], ['triI'], lambda e: e.affine_select(
            out=self.triI[:], in_=self.ones_f[:], pattern=[[1, 128]], compare_op=ALU.is_ge,
            fill=0.0, base=0, channel_multiplier=-1))
        k.op('pool', ['ones_f'], ['triSU'], lambda e: e.affine_select(
            out=self.triSU[:], in_=self.ones_f[:], pattern=[[-1, 128]], compare_op=ALU.is_ge,
            fill=0.0, base=-1, channel_multiplier=1))
        k.op('dve', ['c_tmp'], ['ident_b'], lambda e: e.tensor_copy(out=self.ident_b[:], in_=tmp[:]))
        k.op('dve', ['ones_f'], ['ones_b'], lambda e: e.tensor_copy(out=self.ones_b[:], in_=self.ones_f[:]))
        k.barrier()
        self.C = ['ones_f', 'ones_b', 'ident_b', 'triI', 'triSU', 'zeros_f']

    def scr(self, name, shape, dt):
        kind = "ExternalOutput" if name in self.dbg else "Internal"
        return self.nc.dram_tensor(name, list(shape), dt, kind=kind).ap()

    def sbt(self, st, name, shape, dt=F32):
        self.uid += 1
        return st.enter_context(self.nc.sbuf_tensor("%s_%d" % (name, self.uid), list(shape), dt))

    def u(self):
        self.uid += 1
        return "u%d" % self.uid

    def cast(self, out_ap, in_ap, reads, writes, eng=None):
        if eng is None:
            eng = ('dve', 'act', 'pool', 'dve', 'act')[self.cast_rr % 5]
            self.cast_rr += 1
        if eng == 'act':
            self.k.op('act', reads, writes, lambda e: e.activation(out=out_ap, in_=in_ap, func=AF.Copy))
        else:
            self.k.op(eng, reads, writes, lambda e: e.tensor_copy(out=out_ap, in_=in_ap))

    def transpose_to(self, src_tile, src_name, nblk, dst_fn, pt_toggle, views=None):
        k = self.k
        if views is None:
            views = [(self.PT, 'pb7')]
        for b0 in range(0, nblk, 4):
            nb = min(4, nblk - b0)
            pv, ptn = views[pt_toggle[0] % len(views)]
            pt_toggle[0] += 1
            for b in range(nb):
                o = pv[:, b * 128:(b + 1) * 128]
                k.op('pe', [src_name, 'ident_b'], [ptn], lambda e: e.transpose(
                    out=o, in_=src_tile[:, (b0 + b) * 128:(b0 + b + 1) * 128],
                    identity=self.ident_b[:]))
            dst_fn(b0, nb, pv[:, 0:nb * 128], ptn)

    def norm_T(self, src, wv, hT, hTn):
        k = self.k
        with ExitStack() as st:
            wN = self.sbt(st, "nrm_w", [128, D])
            k.dma(wN[:], wv.partition_broadcast(128), [], ['nrm_w'])
            xts = [self.sbt(st, "nrm_x%d" % i, [128, D]) for i in range(2)]
            hbs = [self.sbt(st, "nrm_hb%d" % i, [128, D], BF16) for i in range(2)]
            junk = self.sbt(st, "nrm_junk", [128, D], BF16)
            sss = [self.sbt(st, "nrm_ss%d" % i, [128, 1]) for i in range(2)]
            tog = [0]
            def stats(tt):
                i = tt % 2
                xt, hb, ss = xts[i], hbs[i], sss[i]
                xn, hn, sn = "nrm_x%d" % i, "nrm_hb%d" % i, "nrm_ss%d" % i
                k.dma(xt[:], src[tt * 128:(tt + 1) * 128, :], [], [xn])
                k.op('act', [xn], ['nrm_junk', sn], lambda e: e.activation(
                    out=junk[:], in_=xt[:], func=AF.Square, accum_out=ss[:]))
                k.op('dve', [sn], [sn], lambda e: e.tensor_scalar(
                    out=ss[:], in0=ss[:], scalar1=1.0 / D, scalar2=EPS, op0=ALU.mult, op1=ALU.add))
                k.op('act', [sn], [sn], lambda e: e.sqrt(out=ss[:], in_=ss[:]))
                k.op('dve', [sn], [sn], lambda e: e.reciprocal(out=ss[:], in_=ss[:]))
                k.op('dve', [xn, sn, 'nrm_w'], [hn], lambda e: e.scalar_tensor_tensor(
                    out=hb[:], in0=xt[:], scalar=ss[:, 0:1], in1=wN[:], op0=ALU.mult, op1=ALU.mult))

            def xpose(tt):
                i = tt % 2
                hb, hn = hbs[i], "nrm_hb%d" % i

                def dst(b0, nb, pt, ptn, tt=tt):
                    self.cast(hT[:, b0:b0 + nb, tt * 128:(tt + 1) * 128],
                              pt.rearrange("p (c t) -> p c t", t=128), [ptn], [hTn],
                              eng=('act' if (b0 // 4) % 2 == 0 else 'dve'))
                self.transpose_to(hb, hn, 16, dst, tog, views=[(self.PT6, 'pb6'), (self.PT, 'pb7')])

            stats(0)
            for tt in range(NT):
                if tt + 1 < NT:
                    stats(tt + 1)
                xpose(tt)
            k.barrier()

    def mk_wpool(self, st, name, kc, n, dd=2):
        return {'i': 0, 'name': name, 'dd': dd,
                'wf': [self.sbt(st, name + "f%d" % i, [128, kc, n]) for i in range(dd)],
                'wb': [self.sbt(st, name + "b%d" % i, [128, kc, n], BF16) for i in range(2)]}

    def w_dma(self, pool, srcs, kc, n, pp=128):
        i = pool['i'] % pool['dd']
        pool['i'] += 1
        wf = pool['wf'][i]
        for si, (c0, src) in enumerate(srcs):
            nco = src.shape[1]
            self.k.dma(wf[0:pp, 0:kc, c0:c0 + nco], src.rearrange("(c p) n -> p c n", p=pp), [],
                       ["%sf%d_%d" % (pool['name'], i, si)])
        return i

    def w_cast(self, pool, i, nsrc, kc, n, pp=128):
        j = pool.get('ci', 0) % 2
        pool['ci'] = pool.get('ci', 0) + 1
        wf, wb = pool['wf'][i], pool['wb'][j]
        rd = ["%sf%d_%d" % (pool['name'], i, si) for si in range(nsrc)]
        wbn = "%sb%d" % (pool['name'], j)
        half = max(1, kc // 2)
        for c0 in range(0, kc, half):
            c1 = min(kc, c0 + half)
            eng = ('dve', 'act')[self.cast_rr % 2]
            self.cast_rr += 1
            self.cast(wb[0:pp, c0:c1, 0:n], wf[0:pp, c0:c1, 0:n], rd, [wbn], eng=eng)
        return wb, wbn

    def run_pipeline(self, items):
        n = len(items)
        dd = items[0][0][0][0]['dd']
        dm = lambda it: [self.w_dma(ld[0], ld[1], ld[2], ld[3]) for ld in it[0]]
        cs = lambda it, sl: [self.w_cast(ld[0], s_, len(ld[1]), ld[2], ld[3]) for ld, s_ in zip(it[0], sl)]
        slots = {}
        for j in range(min(dd, n)):
            slots[j] = dm(items[j])
        ready = cs(items[0], slots[0])
        for i in range(n):
            cur = ready
            if i + dd < n:
                slots[i + dd] = dm(items[i + dd])
            if i + 1 < n:
                ready = cs(items[i + 1], slots[i + 1])
            items[i][1](cur)

    def stage_proj(self, l, hT, hTn):
        k = self.k
        W = self.w_in
        PB = self.PB
        with ExitStack() as st:
            wp = self.mk_wpool(st, "pw", KC, 256)
            fa = [self.sbt(st, "fa%d" % i, [128, T]) for i in range(3)]
            xpad = [self.sbt(st, "xpad%d" % i, [128, T + 3]) for i in range(2)]
            stg32 = self.sbt(st, "stg32", [128, NT, 32])
            ba = [self.sbt(st, "ba%d" % i, [128, T], BF16) for i in range(3)]
            tms = [self.sbt(st, "tms%d" % i, [128, NT, 256], BF16) for i in range(2)]
            cw = self.sbt(st, "cw", [128, 12, 4])
            cb = self.sbt(st, "cb", [128, 12])
            for b in range(12):
                k.dma(cw[:, b, :], self.ssd_conv_w[l, :, b * 128:(b + 1) * 128].rearrange("k p -> p k"),
                      [], ['cw%d' % b], allow_slow_non_contiguous=True)
            k.dma(cb[:], self.ssd_conv_b[l].rearrange("(b p) -> p b", p=128), [], ['cb'],
                  allow_slow_non_contiguous=True)
            for i in range(2):
                k.op('pool', [], ["xpad%d" % i], lambda e: e.memset(xpad[i][:, 0:3], 0.0))
            cnt = {'fa': 0, 'ba': 0, 'tms': 0, 'bank': 0, 'xpad': 0, 'ev': 0}
            tog = [0]

            def nxt(key, n):
                i = cnt[key] % n
                cnt[key] += 1
                return i

            items = []

            def add_fm(col0, nblk, start, evac, final):
                for c0 in range(0, nblk * 128, 256):
                    n = min(256, nblk * 128 - c0)

                    def fn(ws, c0=c0, n=n):
                        wb, wbn = ws[0]
                        for sub in range(n // 128):
                            bj = (c0 + sub * 128) // 128
                            ctx = start(bj)
                            for tb in range(4):
                                b = nxt('bank', 7)
                                for kc in range(KC):
                                    k.op('pe', [wbn, hTn], ['pb%d' % b], lambda e: e.matmul(
                                        PB[b][:, :], lhsT=wb[:, kc, sub * 128:(sub + 1) * 128],
                                        rhs=hT[:, kc, tb * 512:(tb + 1) * 512],
                                        start=(kc == 0), stop=(kc == KC - 1)))
                                evac(ctx, tb, b)
                            final(bj, ctx)
                    items.append(([(wp, [(0, W[l, :, col0 + c0: col0 + c0 + n])], KC, n)], fn))

            def ev(o, b, dname, func):
                if func is None:
                    self.cast(o, PB[b][:, :], ['pb%d' % b], [dname], eng=('act', 'dve')[nxt('ev', 2)])
                else:
                    k.op('act', ['pb%d' % b], [dname], lambda e: e.activation(out=o, in_=PB[b][:, :], func=func))

            def to_tm(o, on, dst):
                i = nxt('tms', 2)
                stg, sn = tms[i], "tms%d" % i

                def dfn(b0, nb, pt, ptn):
                    self.cast(stg[:, b0:b0 + nb, 0:128], pt.rearrange("p (c t) -> p c t", t=128),
                              [ptn], [sn], eng=('act' if (b0 // 4) % 2 == 0 else 'dve'))
                self.transpose_to(o, on, 16, dfn, tog)
                k.dma(dst.rearrange("(tt p) c -> p tt c", p=128), stg[:, :, 0:128], [sn], [])

            def xbc_start(bj):
                i = nxt('xpad', 2)
                return (xpad[i], "xpad%d" % i)

            def xbc_evac(ctx, tb, b):
                ev(ctx[0][:, 3 + tb * 512: 3 + (tb + 1) * 512], b, ctx[1], None)

            def xbc_final(bj, ctx):
                xp, xpn = ctx
                i2 = nxt('fa', 3)
                acc, an = fa[i2], "fa%d" % i2
                k.op('dve', [xpn, 'cw%d' % bj, 'cb'], [an], lambda e: e.tensor_scalar(
                    out=acc[:, 0:T], in0=xp[:, 0:T], scalar1=cw[:, bj, 0:1], scalar2=cb[:, bj:bj + 1],
                    op0=ALU.mult, op1=ALU.add))
                for kk in range(1, 4):
                    k.op('dve', [xpn, 'cw%d' % bj, an], [an], lambda e: e.scalar_tensor_tensor(
                        out=acc[:, 0:T], in0=xp[:, kk:kk + T], scalar=cw[:, bj, kk:kk + 1],
                        in1=acc[:, 0:T], op0=ALU.mult, op1=ALU.add))
                i3 = nxt('ba', 3)
                o, on = ba[i3], "ba%d" % i3
                k.op('act', [an], [on], lambda e: e.activation(out=o[:], in_=acc[:, 0:T], func=AF.Silu))
                if bj < 8:
                    to_tm(o, on, self.xs_tm[:, bj * 128:(bj + 1) * 128])
                elif bj < 10:
                    g = bj - 8
                    k.dma(self.BT[g], o[:], [on], [])
                    to_tm(o, on, self.B_tm[:, g * 128:(g + 1) * 128])
                else:
                    k.dma(self.CT[bj - 10], o[:], [on], [])

            def mk_plain(pool, pname, npool, func, final):
                def start(bj):
                    i = nxt(pname, npool)
                    return (pool[i], "%s%d" % (pname, i))

                def evac(ctx, tb, b):
                    ev(ctx[0][:, tb * 512:(tb + 1) * 512], b, ctx[1], func)
                return start, evac, final

            def fin_rows(dst):
                return lambda bj, ctx: k.dma(dst[bj * 128:(bj + 1) * 128, :], ctx[0][:], [ctx[1]], [])

            def fin_qk(dst):
                def f(bj, ctx):
                    k.dma(dst[2 * bj, 0:64, :], ctx[0][0:64, :], [ctx[1]], [])
                    k.dma(dst[2 * bj + 1, 0:64, :], ctx[0][64:128, :], [ctx[1]], [])
                return f

            def add_tm(col0, ncols, dst, func, f32out=False, srcs=None):
                for c0 in range(0, ncols, 256):
                    n = min(256, ncols - c0)

                    def fn(ws, c0=c0, n=n):
                        wb, wbn = ws[0]
                        if f32out:
                            stg, sn = stg32, 'stg32'
                        else:
                            i = nxt('tms', 2)
                            stg, sn = tms[i], "tms%d" % i
                        for tt in range(NT):
                            b = nxt('bank', 7)
                            for kc in range(KC):
                                k.op('pe', [wbn, hTn], ['pb%d' % b], lambda e: e.matmul(
                                    PB[b][:, 0:n], lhsT=hT[:, kc, tt * 128:(tt + 1) * 128], rhs=wb[:, kc, 0:n],
                                    start=(kc == 0), stop=(kc == KC - 1)))
                            o = stg[:, tt, 0:n]
                            if func is None:
                                self.cast(o, PB[b][:, 0:n], ['pb%d' % b], [sn], eng=('act', 'dve')[nxt('ev', 2)])
                            else:
                                k.op('act', ['pb%d' % b], [sn], lambda e: e.activation(out=o, in_=PB[b][:, 0:n], func=func))
                        if f32out:
                            k.dma(dst.rearrange("(tt p) c -> p tt c", p=128), stg[:, :, 0:n], [sn], [])
                        else:
                            k.dma(dst[:, c0:c0 + n].rearrange("(tt p) c -> p tt c", p=128), stg[:, :, 0:n], [sn], [])
                    ss_ = srcs if srcs is not None else [(0, W[l, :, col0 + c0: col0 + c0 + n])]
                    items.append(([(wp, ss_, KC, n)], fn))

            add_tm(0, 32, self.dtff, None, f32out=True,
                   srcs=[(0, W[l, :, O_DT:O_DT + 16]), (16, W[l, :, O_FF:O_FF + 16])])
            add_tm(O_Z, 1024, self.zs_tm, AF.Silu)
            add_tm(O_HI, 1024, self.hv_tm, None)
            add_tm(O_FV, 1024, self.fv_tm, None)
            add_fm(O_XBC, 12, xbc_start, xbc_evac, xbc_final)
            add_fm(O_HQ, 8, *mk_plain(ba, 'ba', 3, AF.Silu, fin_rows(self.hqT)))
            add_fm(O_HF, 8, *mk_plain(fa, 'fa', 3, AF.Sigmoid, fin_rows(self.hsigT)))
            add_fm(O_HG, 8, *mk_plain(ba, 'ba', 3, AF.Silu, fin_rows(self.hgT)))
            add_fm(O_FQ, 8, *mk_plain(ba, 'ba', 3, None, fin_qk(self.fqA)))
            add_fm(O_FK, 8, *mk_plain(ba, 'ba', 3, None, fin_qk(self.fkA)))
            add_fm(O_G, 48, *mk_plain(ba, 'ba', 3, AF.Sigmoid, fin_rows(self.gT)))
            self.run_pipeline(items)
            k.barrier()


    def stage_ssd(self, l):
        k = self.k
        PB, PT = self.PB, self.PT
        with ExitStack() as st:
            sb = lambda n, s, dt=F32: self.sbt(st, n, s, dt)
            dtb = sb("s_dtb", [128, 16]); alog = sb("s_alog", [128, 16]); dsk = sb("s_dsk", [128, 16])
            aneg = sb("s_aneg", [128, 16]); nw = sb("s_nw", [128, 1024])
            k.dma(dtb[:], self.ssd_dt_bias[l].partition_broadcast(128), [], ['s_dtb'])
            k.dma(alog[:], self.ssd_a_log[l].partition_broadcast(128), [], ['s_alog'])
            k.dma(dsk[:], self.ssd_d[l].partition_broadcast(128), [], ['s_dsk'])
            k.dma(nw[:], self.ssd_norm_w[l].partition_broadcast(128), [], ['s_nw'])
            k.op('act', ['s_alog'], ['s_aneg'], lambda e: e.activation(out=aneg[:], in_=alog[:], func=AF.Exp))
            k.op('dve', ['s_aneg'], ['s_aneg'], lambda e: e.tensor_scalar(
                out=aneg[:], in0=aneg[:], scalar1=-1.0, scalar2=None, op0=ALU.mult))
            S = sb("s_S", [128, 1024]); Sbf = sb("s_Sbf", [128, 1024], BF16)
            k.op('pool', [], ['s_S'], lambda e: e.memset(S[:], 0.0))
            k.op('pool', [], ['s_Sbf'], lambda e: e.memset(Sbf[:], 0.0))
            yTs = sb("s_yTs", [128, 8, T], BF16)
            P2 = {}

            def pool2(name, shape, dt=F32):
                P2[name] = [sb("%s%d" % (name, i), shape, dt) for i in range(2)]
            for nm, shp, dt in [("s_xs", [128, 1024], BF16), ("s_zs", [128, 1024], BF16),
                                ("s_btm", [128, 256], BF16), ("s_bt", [128, 2, 128], BF16),
                                ("s_ct", [128, 2, 128], BF16), ("s_df", [128, 32], F32),
                                ("s_sm", [128, 8, 16], F32), ("s_edec", [128, 48], F32),
                                ("s_cbm", [128, 2, 128], F32), ("s_R", [128, 8, 128], F32),
                                ("s_E", [128, 1024], F32), ("s_MT", [128, 16, 128], BF16),
                                ("s_xdt", [128, 1024], BF16), ("s_xdd", [128, 1024], BF16),
                                ("s_yo", [128, 1024], F32), ("s_y", [128, 1024], F32),
                                ("s_tmp", [128, 1024], F32), ("s_yn", [128, 1024], BF16),
                                ("s_ss", [128, 2], F32), ("s_junk", [128, 512], BF16)]:
                pool2(nm, shp, dt)
            def phase(c, which):
                i = c % 2
                g_ = lambda nm: (P2[nm][i], "%s%d" % (nm, i))
                xs, xsn = g_("s_xs"); zs, zsn = g_("s_zs"); btm, btmn = g_("s_btm")
                bt, btn = g_("s_bt"); ct, ctn = g_("s_ct"); df, dfn = g_("s_df")
                sm, smn = g_("s_sm"); edec, edn = g_("s_edec"); cbm, cbn = g_("s_cbm")
                E_, En = g_("s_E"); MT, MTn = g_("s_MT"); xdt, xdtn = g_("s_xdt")
                xdd, xddn = g_("s_xdd"); yo, yon = g_("s_yo"); y, yn_ = g_("s_y")
                tmp, tmpn = g_("s_tmp"); yn, ynn = g_("s_yn"); ss, ssn = g_("s_ss")
                junk, jn = g_("s_junk")
                R, Rn = g_("s_R")

                x16, ax, e16, l16, r16, dt, a, dtd = [sm[:, j, :] for j in range(8)]
                xs3 = xs[:].rearrange("p (h d) -> p h d", h=16)
                if which == 'A':
                    ts = slice(c * 128, (c + 1) * 128)
                    k.dma(xs[:], self.xs_tm[ts, :], [], [xsn])
                    k.dma(zs[:], self.zs_tm[ts, :], [], [zsn])
                    k.dma(btm[:], self.B_tm[ts, :], [], [btmn])
                    k.dma(bt[:], self.BT[:, :, ts].rearrange("g n t -> n g t"), [], [btn])
                    k.dma(ct[:], self.CT[:, :, ts].rearrange("g n t -> n g t"), [], [ctn])
                    k.dma(df[:], self.dtff[ts, :], [], [dfn])
                    k.op('dve', [dfn, 's_dtb'], [smn], lambda e: e.tensor_tensor(out=x16, in0=df[:, 0:16], in1=dtb[:], op=ALU.add))
                    k.op('act', [smn], [smn], lambda e: e.activation(out=ax, in_=x16, func=AF.Abs))
                    k.op('act', [smn], [smn], lambda e: e.activation(out=e16, in_=ax, func=AF.Exp, scale=-1.0))
                    k.op('act', [smn], [smn], lambda e: e.activation(out=l16, in_=e16, func=AF.Ln, bias=1.0))
                    k.op('dve', [smn], [smn], lambda e: e.tensor_scalar_max(out=r16, in0=x16, scalar1=0.0))
                    k.op('dve', [smn], [smn], lambda e: e.tensor_tensor(out=dt, in0=r16, in1=l16, op=ALU.add))
                    k.op('dve', [smn, 's_aneg'], [smn], lambda e: e.tensor_tensor(out=a, in0=dt, in1=aneg[:], op=ALU.mult))
                    for j, (m, mn) in enumerate([(self.triI, 'triI'), (self.triSU, 'triSU'), (self.ones_f, 'ones_f')]):
                        k.op('pe', [mn, smn], ['pb0'], lambda e: e.matmul(
                            PB[0][:, j * 16:(j + 1) * 16], lhsT=m[:], rhs=a, start=True, stop=True))
                    k.op('act', ['pb0'], [edn], lambda e: e.activation(out=edec[:], in_=PB[0][:, 0:48], func=AF.Exp))
                    for g in range(2):
                        k.op('pe', [btn, ctn], ['pb0'], lambda e: e.matmul(
                            PB[0][:, 256 + g * 128: 256 + (g + 1) * 128], lhsT=bt[:, g, :], rhs=ct[:, g, :],
                            start=True, stop=True))
                    k.op('dve', ['pb0', 'triI'], [cbn], lambda e: e.tensor_tensor(
                        out=cbm[:], in0=PB[0][:, 256:512].rearrange("p (g l) -> p g l", g=2),
                        in1=self.triI[:].unsqueeze(1).to_broadcast([128, 2, 128]), op=ALU.mult))
                    for g in range(2):
                        k.op('pool', ['triI', smn], [Rn], lambda e: e.tensor_tensor(
                            out=R[:], in0=self.triI[:].unsqueeze(1).to_broadcast([128, 8, 128]),
                            in1=a[:, g * 8:(g + 1) * 8].unsqueeze(2).to_broadcast([128, 8, 128]), op=ALU.mult))
                        for hh in range(2):
                            k.op('pe', ['triSU', Rn], ['pb%d' % (1 + hh)], lambda e: e.matmul(
                                PB[1 + hh][:, :], lhsT=self.triSU[:],
                                rhs=R[:, hh * 4:(hh + 1) * 4, :].rearrange("p h l -> p (h l)"), start=True, stop=True))
                            k.op('act', ['pb%d' % (1 + hh)], [En], lambda e: e.activation(
                                out=E_[:, hh * 512:(hh + 1) * 512], in_=PB[1 + hh][:, :], func=AF.Exp))
                        k.op('dve', [En, cbn], [MTn], lambda e: e.tensor_tensor(
                            out=MT[:, g * 8:(g + 1) * 8, :], in0=E_[:].rearrange("p (h l) -> p h l", h=8),
                            in1=cbm[:, g, :].unsqueeze(1).to_broadcast([128, 8, 128]), op=ALU.mult))
                    k.op('dve', [xsn, smn], [xdtn], lambda e: e.tensor_tensor(
                        out=xdt[:].rearrange("p (h d) -> p h d", h=16), in0=xs3,
                        in1=dt.unsqueeze(2).to_broadcast([128, 16, 64]), op=ALU.mult))
                    k.op('dve', [smn, edn], [smn], lambda e: e.tensor_tensor(out=dtd, in0=dt, in1=edec[:, 16:32], op=ALU.mult))
                    k.op('pool', [xsn, smn], [xddn], lambda e: e.tensor_tensor(
                        out=xdd[:].rearrange("p (h d) -> p h d", h=16), in0=xs3,
                        in1=dtd.unsqueeze(2).to_broadcast([128, 16, 64]), op=ALU.mult))

                    return
                for h in range(16):
                    b = 3 + h // 8
                    k.op('pe', [MTn, xdtn], ['pb%d' % b], lambda e: e.matmul(
                        PB[b][:, (h % 8) * 64:(h % 8 + 1) * 64], lhsT=MT[:, h, :], rhs=xdt[:, h * 64:(h + 1) * 64],
                        start=True, stop=True))
                for g in range(2):
                    k.op('pe', [ctn, 's_Sbf'], ['pb%d' % (5 + g)], lambda e: e.matmul(
                        PB[5 + g][:, :], lhsT=ct[:, g, :], rhs=Sbf[:, g * 512:(g + 1) * 512], start=True, stop=True))
                for g in range(2):
                    k.op('dve', ['pb%d' % (5 + g), edn], [yon], lambda e: e.tensor_tensor(
                        out=yo[:, g * 512:(g + 1) * 512].rearrange("p (h d) -> p h d", h=8),
                        in0=PB[5 + g][:, :].rearrange("p (h d) -> p h d", h=8),
                        in1=edec[:, g * 8:(g + 1) * 8].unsqueeze(2).to_broadcast([128, 8, 64]), op=ALU.mult))
                    k.op('dve', [yon, 'pb%d' % (3 + g)], [yn_], lambda e: e.tensor_tensor(
                        out=y[:, g * 512:(g + 1) * 512], in0=yo[:, g * 512:(g + 1) * 512], in1=PB[3 + g][:, :], op=ALU.add))
                k.op('pool', [xsn, 's_dsk'], [tmpn], lambda e: e.tensor_tensor(
                    out=tmp[:].rearrange("p (h d) -> p h d", h=16), in0=xs3,
                    in1=dsk[:].unsqueeze(2).to_broadcast([128, 16, 64]), op=ALU.mult))
                k.op('pool', [yn_, tmpn], [yn_], lambda e: e.tensor_tensor(out=y[:], in0=y[:], in1=tmp[:], op=ALU.add))
                for g in range(2):
                    k.op('pe', [btmn, xddn], ['pb%d' % (5 + g)], lambda e: e.matmul(
                        PB[5 + g][:, :], lhsT=btm[:, g * 128:(g + 1) * 128], rhs=xdd[:, g * 512:(g + 1) * 512],
                        start=True, stop=True))
                k.op('dve', ['s_S', edn], ['s_S'], lambda e: e.tensor_tensor(
                    out=S[:].rearrange("p (h d) -> p h d", h=16), in0=S[:].rearrange("p (h d) -> p h d", h=16),
                    in1=edec[:, 32:48].unsqueeze(2).to_broadcast([128, 16, 64]), op=ALU.mult))
                for g in range(2):
                    k.op('dve', ['s_S', 'pb%d' % (5 + g)], ['s_S'], lambda e: e.tensor_tensor(
                        out=S[:, g * 512:(g + 1) * 512], in0=S[:, g * 512:(g + 1) * 512], in1=PB[5 + g][:, :], op=ALU.add))
                k.op('act', ['s_S'], ['s_Sbf'], lambda e: e.activation(out=Sbf[:], in_=S[:], func=AF.Copy))
                k.op('dve', [yn_, zsn], [yn_], lambda e: e.tensor_tensor(out=y[:], in0=y[:], in1=zs[:], op=ALU.mult))
                for g in range(2):
                    k.op('act', [yn_], [jn, ssn], lambda e: e.activation(
                        out=junk[:], in_=y[:, g * 512:(g + 1) * 512], func=AF.Square, accum_out=ss[:, g:g + 1]))
                k.op('dve', [ssn], [ssn], lambda e: e.tensor_scalar(
                    out=ss[:], in0=ss[:], scalar1=1.0 / 512, scalar2=EPS, op0=ALU.mult, op1=ALU.add))
                k.op('act', [ssn], [ssn], lambda e: e.sqrt(out=ss[:], in_=ss[:]))
                k.op('dve', [ssn], [ssn], lambda e: e.reciprocal(out=ss[:], in_=ss[:]))
                for g in range(2):
                    k.op('dve', [yn_, ssn, 's_nw'], [ynn], lambda e: e.scalar_tensor_tensor(
                        out=yn[:, g * 512:(g + 1) * 512], in0=y[:, g * 512:(g + 1) * 512], scalar=ss[:, g:g + 1],
                        in1=nw[:, g * 512:(g + 1) * 512], op0=ALU.mult, op1=ALU.mult))
                tog = [0]

                def dfn2(b0, nb, pt, ptn, c=c):
                    self.cast(yTs[:, b0:b0 + nb, c * 128:(c + 1) * 128], pt.rearrange("p (b t) -> p b t", t=128),
                              [ptn], ['s_yTs'], eng=('act' if (b0 // 4) % 2 == 0 else 'dve'))
                self.transpose_to(yn, ynn, 8, dfn2, tog)
            phase(0, 'A')
            for c in range(NT):
                if c + 1 < NT:
                    phase(c + 1, 'A')
                phase(c, 'B')
            k.dma(self.ysT.rearrange("(b p) t -> p b t", p=128), yTs[:], ['s_yTs'], [])
            k.barrier()

    def stage_hgrn(self, l):
        k = self.k
        PB, PT = self.PB, self.PT
        with ExitStack() as st:
            sb = lambda n, s, dt=F32: self.sbt(st, n, s, dt)
            cmask = sb("h_cmask", [128, T]); mask64 = sb("h_m64", [64, 64])
            lb = sb("h_lb", [128, 8]); oml = sb("h_oml", [128, 8]); nw = sb("h_nw", [128, 8])
            lb0 = sb("h_lb0", [128, 8])
            k.op('pool', [], ['h_cmask'], lambda e: e.memset(cmask[:], 1.0))
            k.op('pool', ['h_cmask'], ['h_cmask'], lambda e: e.memset(
                cmask[:].rearrange("p (c t) -> p c t", t=64)[:, :, 0:1], 0.0))
            k.op('pool', ['ones_f'], ['h_m64'], lambda e: e.affine_select(
                out=mask64[:], in_=self.ones_f[0:64, 0:64], pattern=[[1, 64]], compare_op=ALU.is_ge,
                fill=0.0, base=0, channel_multiplier=-1))
            k.dma(nw[:], self.hgrn_norm_w[l].rearrange("(h p) -> p h", p=128), [], ['h_nw'],
                  allow_slow_non_contiguous=True)
            if l == 0:
                k.op('pool', [], ['h_lb'], lambda e: e.memset(lb[:], 0.0))
            else:
                k.dma(lb0[:], self.hgrn_lb[0].rearrange("(h p) -> p h", p=128), [], ['h_lb0'],
                      allow_slow_non_contiguous=True)
                k.dma(lb[:], self.hgrn_lb[1].rearrange("(h p) -> p h", p=128), [], ['h_lb'],
                      allow_slow_non_contiguous=True)
                k.op('dve', ['h_lb', 'h_lb0'], ['h_lb'], lambda e: e.tensor_tensor(out=lb[:], in0=lb[:], in1=lb0[:], op=ALU.subtract))
                k.op('act', ['h_lb'], ['h_lb'], lambda e: e.activation(out=lb[:], in_=lb[:], func=AF.Sigmoid))
            k.op('dve', ['h_lb'], ['h_oml'], lambda e: e.tensor_scalar(
                out=oml[:], in0=lb[:], scalar1=-1.0, scalar2=1.0, op0=ALU.mult, op1=ALU.add))
            epsb = sb("h_eps", [128, 1])
            k.op('pool', [], ['h_eps'], lambda e: e.memset(epsb[:], EPS))
            osq = sb("h_osq", [128, 512], BF16); rt = sb("h_rt", [128, 512]); t1 = sb("h_t1", [128, 512])
            HS = []
            for p in range(2):
                d = {}
                for nm in ("sig", "f", "b", "eb", "ktf"):
                    d[nm] = (sb("h%d_%s" % (p, nm), [128, T]), "h%d_%s" % (p, nm))
                for nm in ("q", "g", "qt", "kt", "kh", "yh"):
                    d[nm] = (sb("h%d_%s" % (p, nm), [128, T], BF16), "h%d_%s" % (p, nm))
                d["v"] = (sb("h%d_v" % p, [64, 32, 128], BF16), "h%d_v" % p)
                d["S"] = (sb("h%d_S" % p, [128, 128]), "h%d_S" % p)
                d["Sbf"] = [(sb("h%d_Sbf%d" % (p, i), [128, 128], BF16), "h%d_Sbf%d" % (p, i)) for i in range(2)]
                d["smT"] = (sb("h%d_smT" % p, [64, 8, 64], BF16), "h%d_smT" % p)
                d["khT"] = (sb("h%d_khT" % p, [64, 8, 128], BF16), "h%d_khT" % p)
                d["ps_s"] = (PB[p], 'pb%d' % p)
                d["ps_o"] = (PB[2 + p], 'pb%d' % (2 + p))
                d["ps_k"] = (PB[4 + p], 'pb%d' % (4 + p))
                HS.append(d)

            def front(d, h):
                hs = slice(h * 128, (h + 1) * 128)
                sig, sgn = d["sig"]; fB, fn_ = d["f"]; bB, bn = d["b"]; eb, ebn = d["eb"]; ktf, ktfn = d["ktf"]
                q, qn = d["q"]; gg, gn = d["g"]; qt, qtn = d["qt"]; kt, ktn = d["kt"]; kh, khn = d["kh"]
                v, vn = d["v"]; S, Sn = d["S"]
                k.dma(sig[:], self.hsigT[hs, :], [], [sgn])
                k.dma(q[:], self.hqT[hs, :], [], [qn])
                k.dma(gg[:], self.hgT[hs, :], [], [gn])
                k.dma(v[:], self.hv_tm[:, hs].rearrange("(c p) v -> p c v", p=64), [], [vn])
                k.op('dve', [sgn, 'h_oml', 'h_lb'], [fn_], lambda e: e.tensor_scalar(
                    out=fB[:], in0=sig[:], scalar1=oml[:, h:h + 1], scalar2=lb[:, h:h + 1], op0=ALU.mult, op1=ALU.add))
                k.op('act', [fn_], [sgn], lambda e: e.activation(out=sig[:], in_=fB[:], func=AF.Ln))
                k.op('dve', ['h_cmask', sgn], [bn], lambda e: e.tensor_tensor_scan(
                    out=bB[:], data0=cmask[:], data1=sig[:], initial=0.0, op0=ALU.mult, op1=ALU.add))
                k.op('pool', [fn_], [fn_], lambda e: e.tensor_scalar(
                    out=fB[:], in0=fB[:], scalar1=-1.0, scalar2=1.0, op0=ALU.mult, op1=ALU.add))
                k.op('act', [bn], [ebn], lambda e: e.activation(out=eb[:], in_=bB[:], func=AF.Exp))
                k.op('dve', [qn, ebn], [qtn], lambda e: e.tensor_tensor(out=qt[:], in0=q[:], in1=eb[:], op=ALU.mult))
                k.op('pool', [bn], [bn], lambda e: e.tensor_scalar(out=bB[:], in0=bB[:], scalar1=1.0e30, scalar2=-80.0, op0=ALU.min, op1=ALU.max))
                k.op('act', [bn], [bn], lambda e: e.activation(out=bB[:], in_=bB[:], func=AF.Exp, scale=-1.0))
                k.op('dve', [fn_, bn], [ktfn], lambda e: e.tensor_tensor(out=ktf[:], in0=fB[:], in1=bB[:], op=ALU.mult))
                k.op('act', [ktfn], [ktn], lambda e: e.activation(out=kt[:], in_=ktf[:], func=AF.Copy))
                k.op('pool', [ktfn, ebn], [khn], lambda e: e.tensor_tensor(
                    out=kh[:].rearrange("p (c t) -> p c t", t=64), in0=ktf[:].rearrange("p (c t) -> p c t", t=64),
                    in1=eb[:].rearrange("p (c t) -> p c t", t=64)[:, :, 63:64].to_broadcast([128, 32, 64]), op=ALU.mult))
                k.op('pool', [], [Sn], lambda e: e.memset(S[:], 0.0))
                k.op('pool', [], [d["Sbf"][0][1]], lambda e: e.memset(d["Sbf"][0][0][:], 0.0))

            def prep(d, cg):
                qt, qtn = d["qt"]; kt, ktn = d["kt"]; kh, khn = d["kh"]
                ps_s, psn = d["ps_s"]; smT, smn = d["smT"]; khT, khTn = d["khT"]
                for c in range(8):
                    tk = slice((cg * 8 + c) * 64, (cg * 8 + c + 1) * 64)
                    k.op('pe', [ktn, qtn], [psn], lambda e: e.matmul(
                        ps_s[0:64, c * 64:(c + 1) * 64], lhsT=kt[:, tk], rhs=qt[:, tk], start=True, stop=True))
                k.op('dve', [psn, 'h_m64'], [smn], lambda e: e.tensor_tensor(
                    out=smT[:], in0=ps_s[0:64, :].rearrange("p (c t) -> p c t", t=64),
                    in1=mask64[:].unsqueeze(1).to_broadcast([64, 8, 64]), op=ALU.mult))
                for c in range(8):
                    tk = slice((cg * 8 + c) * 64, (cg * 8 + c + 1) * 64)
                    k.op('pe', [khn, 'ident_b'], ['pb7'], lambda e: e.transpose(
                        out=PT[0:64, c * 128:(c + 1) * 128], in_=kh[:, tk], identity=self.ident_b[:]))
                k.op('act', ['pb7'], [khTn], lambda e: e.activation(
                    out=khT[:], in_=PT[0:64, :].rearrange("p (c k) -> p c k", k=128), func=AF.Copy))

            def kv4(d, cg, c0):
                khT, khTn = d["khT"]; v, vn = d["v"]; pk, pkn = d["ps_k"]
                for c in range(c0, c0 + 4):
                    cc = cg * 8 + c
                    k.op('pe', [khTn, vn], [pkn], lambda e: e.matmul(
                        pk[:, (c % 4) * 128:(c % 4 + 1) * 128], lhsT=khT[:, c, :], rhs=v[:, cc, :], start=True, stop=True))

            def chunk(d, cg, c):
                cc = cg * 8 + c
                tk = slice(cc * 64, (cc + 1) * 64)
                v, vn = d["v"]; smT, smn = d["smT"]; qt, qtn = d["qt"]; eb, ebn = d["eb"]
                ps_o, pon = d["ps_o"]; pk, pkn = d["ps_k"]; S, Sn = d["S"]
                sb0, sb0n = d["Sbf"][cc % 2]; sb1, sb1n = d["Sbf"][(cc + 1) % 2]
                pks = pk[:, (c % 4) * 128:(c % 4 + 1) * 128]
                k.op('pe', [vn, smn], [pon], lambda e: e.matmul(
                    ps_o[:, c * 64:(c + 1) * 64], lhsT=v[:, cc, :], rhs=smT[:, c, :], start=True, stop=False))
                k.op('pe', [sb0n, qtn], [pon], lambda e: e.matmul(
                    ps_o[:, c * 64:(c + 1) * 64], lhsT=sb0[:], rhs=qt[:, tk], start=False, stop=True))
                esc = eb[:, cc * 64 + 63: cc * 64 + 64]
                k.op('dve', [Sn, ebn, pkn], [sb1n], lambda e: e.scalar_tensor_tensor(
                    out=sb1[:], in0=S[:], scalar=esc, in1=pks, op0=ALU.mult, op1=ALU.add))
                k.op('dve', [Sn, ebn, pkn], [Sn], lambda e: e.scalar_tensor_tensor(
                    out=S[:], in0=S[:], scalar=esc, in1=pks, op0=ALU.mult, op1=ALU.add))

            def norm(d, h, cg):
                ps_o, pon = d["ps_o"]; gg, gn = d["g"]; yh, yhn = d["yh"]
                k.op('act', [pon], ['h_osq'], lambda e: e.activation(out=osq[:], in_=ps_o[:, :], func=AF.Square))
                k.op('pe', ['ones_b', 'h_osq'], ['pb6'], lambda e: e.matmul(
                    PB[6][:, :], lhsT=self.ones_b[:], rhs=osq[:], start=True, stop=True))
                k.op('act', ['pb6', 'h_eps'], ['h_rt'], lambda e: e.activation(
                    out=rt[:], in_=PB[6][:, :], func=AF.Ln, scale=1.0 / 128, bias=epsb[:]))
                k.op('act', ['h_rt'], ['h_rt'], lambda e: e.activation(out=rt[:], in_=rt[:], func=AF.Exp, scale=-0.5))
                k.op('dve', [pon, 'h_rt'], ['h_t1'], lambda e: e.tensor_tensor(out=t1[:], in0=ps_o[:, :], in1=rt[:], op=ALU.mult))
                k.op('dve', ['h_t1', 'h_nw', gn], [yhn], lambda e: e.scalar_tensor_tensor(
                    out=yh[:, cg * 512:(cg + 1) * 512], in0=t1[:], scalar=nw[:, h:h + 1],
                    in1=gg[:, cg * 512:(cg + 1) * 512], op0=ALU.mult, op1=ALU.mult))

            for hp in range(4):
                hh = [2 * hp, 2 * hp + 1]
                for p in range(2):
                    front(HS[p], hh[p])
                for cg in range(4):
                    for p in range(2):
                        prep(HS[p], cg)
                    for c0 in (0, 4):
                        for p in range(2):
                            kv4(HS[p], cg, c0)
                        for c in range(c0, c0 + 4):
                            for p in range(2):
                                chunk(HS[p], cg, c)
                    for p in range(2):
                        norm(HS[p], hh[p], cg)
                for p in range(2):
                    k.dma(self.yhT[hh[p] * 128:(hh[p] + 1) * 128, :], HS[p]["yh"][0][:], [HS[p]["yh"][1]], [])
            k.barrier()

    def stage_fox(self, l):
        k = self.k
        PB = self.PB
        with ExitStack() as st:
            sb = lambda n, s, dt=F32: self.sbt(st, n, s, dt)
            fb = sb("x_fb", [128, 16]); ffr = sb("x_ffr", [128, NT, 32])
            xx = sb("x_xx", [128, NT, 16]); ax = sb("x_ax", [128, NT, 16]); lf = sb("x_lf", [128, NT, 16])
            c8 = sb("x_c8", [16, T]); c32 = sb("x_c32", [16, T]); ones16 = sb("x_ones16", [16, T], BF16)
            sp_ = [sb("x_sp%d" % i, [16, T], BF16) for i in range(3)]
            sn_ = [sb("x_sn%d" % i, [16, T], BF16) for i in range(3)]
            k.dma(fb[:], self.fox_f_bias[l].partition_broadcast(128), [], ['x_fb'])
            k.dma(ffr[:], self.dtff.rearrange("(tt p) c -> p tt c", p=128), [], ['x_ffr'])
            k.op('pool', [], ['x_ones16'], lambda e: e.memset(ones16[:], 1.0))
            k.op('dve', ['x_ffr', 'x_fb'], ['x_xx'], lambda e: e.tensor_tensor(
                out=xx[:], in0=ffr[:, :, 16:32], in1=fb[:].unsqueeze(1).to_broadcast([128, NT, 16]), op=ALU.add))
            k.op('act', ['x_xx'], ['x_ax'], lambda e: e.activation(out=ax[:], in_=xx[:], func=AF.Abs))
            k.op('act', ['x_ax'], ['x_ax'], lambda e: e.activation(out=ax[:], in_=ax[:], func=AF.Exp, scale=-1.0))
            k.op('act', ['x_ax'], ['x_ax'], lambda e: e.activation(out=ax[:], in_=ax[:], func=AF.Ln, bias=1.0))
            k.op('dve', ['x_xx'], ['x_xx'], lambda e: e.tensor_scalar_min(out=xx[:], in0=xx[:], scalar1=0.0))
            k.op('dve', ['x_xx', 'x_ax'], ['x_lf'], lambda e: e.tensor_tensor(out=lf[:], in0=xx[:], in1=ax[:], op=ALU.subtract))
            for tb in range(4):
                for ti in range(4):
                    i = tb * 4 + ti
                    o = PB[0][0:16, ti * 128:(ti + 1) * 128]
                    for j in range(i):
                        k.op('pe', ['x_lf', 'ones_f'], ['pb0'], lambda e: e.matmul(
                            o, lhsT=lf[:, j, :], rhs=self.ones_f[:], start=(j == 0), stop=False))
                    k.op('pe', ['x_lf', 'triI'], ['pb0'], lambda e: e.matmul(
                        o, lhsT=lf[:, i, :], rhs=self.triI[:], start=(i == 0), stop=True))
                k.op('dve', ['pb0'], ['x_c8'], lambda e: e.tensor_scalar(
                    out=c8[:, tb * 512:(tb + 1) * 512], in0=PB[0][0:16, :], scalar1=8.0, scalar2=None, op0=ALU.mult))
            for j in range(3):
                k.op('dve', ['x_c8'], ['x_sp%d' % j], lambda e: e.tensor_copy(out=sp_[j][:], in_=c8[:]))
                k.op('dve', ['x_sp%d' % j], ['x_sn%d' % j], lambda e: e.tensor_scalar(
                    out=sn_[j][:], in0=sp_[j][:], scalar1=-1.0, scalar2=None, op0=ALU.mult))
                if j < 2:
                    k.op('dve', ['x_sp%d' % j], ['x_c32'], lambda e: e.tensor_copy(out=c32[:], in_=sp_[j][:]))
                    k.op('dve', ['x_c8', 'x_c32'], ['x_c8'], lambda e: e.tensor_tensor(out=c8[:], in0=c8[:], in1=c32[:], op=ALU.subtract))
            AUG = []
            for j in range(3):
                for (dst, row, src, sn) in [(self.fqA, 64 + j, sp_[j], 'x_sp%d' % j), (self.fqA, 67 + j, ones16, 'x_ones16'),
                                            (self.fkA, 64 + j, ones16, 'x_ones16'), (self.fkA, 67 + j, sn_[j], 'x_sn%d' % j)]:
                    nm = 'aug%d' % len(AUG)
                    k.dma(dst[:, row, :], src[:], [sn], [nm])
                    AUG.append(nm)
            madd = sb("x_madd", [128, 4, 512])
            for j in range(4):
                k.op('pool', ['zeros_f'], ['x_madd'], lambda e: e.affine_select(
                    out=madd[:, j, :], in_=self.zeros_f[:], pattern=[[1, 512]], compare_op=ALU.is_ge,
                    fill=-240000.0, base=-j * 128, channel_multiplier=-1))
            sel = sb("x_sel", [65, 64])
            k.op('pool', [], ['x_sel'], lambda e: e.memset(sel[:], 0.0))
            k.op('pool', ['x_sel'], ['x_sel'], lambda e: e.memset(sel[64:65, :], 1.0))
            QA = [sb("x_QA%d" % i, [128, T], BF16) for i in range(2)]
            KA = [sb("x_KA%d" % i, [128, T], BF16) for i in range(2)]
            V = [sb("x_V%d" % i, [128, NT, 128], BF16) for i in range(2)]
            for i in range(2):
                k.op('pool', [], ['x_V%d' % i], lambda e: e.memset(V[i][:], 0.0))
                k.op('pool', ['x_V%d' % i], ['x_V%d' % i], lambda e: e.memset(V[i][:, :, 64:65], 1.0))
                k.op('pool', [], ['x_QA%d' % i], lambda e: e.memset(QA[i][:], 0.0))
                k.op('pool', [], ['x_KA%d' % i], lambda e: e.memset(KA[i][:], 0.0))
            Pt = [sb("x_P%d" % i, [128, 512], BF16) for i in range(6)]
            smk = [sb("x_smk%d" % i, [128, 512]) for i in range(3)]
            osb = [sb("x_osb%d" % i, [65, 512]) for i in range(2)]
            rl = sb("x_rl", [64, 512])
            yf = [sb("x_yf%d" % i, [64, T], BF16) for i in range(2)]

            def load_head(h):
                i = h % 2
                k.dma(QA[i][0:70, :], self.fqA[h], AUG, ['x_QA%d' % i])
                k.dma(KA[i][0:70, :], self.fkA[h], AUG, ['x_KA%d' % i])
                k.dma(V[i][:, :, 0:64], self.fv_tm[:, h * 64:(h + 1) * 64].rearrange("(tt p) d -> p tt d", p=128),
                      [], ['x_V%d' % i])

            seq = [(h, qb, kb) for h in range(16) for qb in range(4) for kb in range(4 * (qb + 1))]
            SB = [0, 1, 2, 3, 7]
            LA = 4

            def emit_S(idx):
                h, qb, kb = seq[idx]
                i = h % 2
                b = SB[idx % 5]
                k.op('pe', ['x_KA%d' % i, 'x_QA%d' % i], ['pb%d' % b], lambda e: e.matmul(
                    PB[b][:, :], lhsT=KA[i][:, kb * 128:(kb + 1) * 128], rhs=QA[i][:, qb * 512:(qb + 1) * 512],
                    start=True, stop=True))

            pending = []

            def emit_norm(h, qb, po, pon, oi):
                i = h % 2
                ob, obn = osb[oi], 'x_osb%d' % oi
                k.op('act', [pon], [obn], lambda e: e.activation(out=ob[:], in_=po[0:65, :], func=AF.Copy))

                k.op('act', [obn], [obn], lambda e: e.activation(out=ob[64:65, :], in_=ob[64:65, :], func=AF.Ln))
                k.op('act', [obn], [obn], lambda e: e.activation(out=ob[64:65, :], in_=ob[64:65, :], func=AF.Exp, scale=-1.0))

                def rest():
                    k.op('pe', ['x_sel', obn], ['pb6'], lambda e: e.matmul(
                        PB[6][0:64, :], lhsT=sel[64:65, :], rhs=ob[64:65, :], start=True, stop=True))
                    k.op('dve', [obn, 'pb6'], ['x_yf%d' % i], lambda e: e.tensor_tensor(
                        out=yf[i][:, qb * 512:(qb + 1) * 512], in0=ob[0:64, :], in1=PB[6][0:64, :], op=ALU.mult))
                    if qb == 3:
                        k.dma(self.yfT[h], yf[i][:], ['x_yf%d' % i], [])
                return rest

            load_head(0)
            for j in range(LA):
                emit_S(j)
            oit = 0
            for idx, (h, qb, kb) in enumerate(seq):
                i = h % 2
                if qb == 0 and kb == 0 and h + 1 < 16:
                    load_head(h + 1)
                if idx + LA < len(seq):
                    emit_S(idx + LA)
                nkb = 4 * (qb + 1)
                b = SB[idx % 5]
                ps, psn = PB[b], 'pb%d' % b
                pt, ptn = Pt[idx % 6], 'x_P%d' % (idx % 6)
                j = kb - 4 * qb
                if j >= 0:
                    sm_, smn = smk[kb % 3], 'x_smk%d' % (kb % 3)
                    k.op('dve', [psn, 'x_madd'], [smn], lambda e: e.tensor_tensor(
                        out=sm_[:], in0=ps[:, :], in1=madd[:, j, :], op=ALU.add))
                    k.op('act', [smn], [ptn], lambda e: e.activation(out=pt[:], in_=sm_[:], func=AF.Exp, scale=0.125))
                else:
                    k.op('act', [psn], [ptn], lambda e: e.activation(out=pt[:], in_=ps[:, :], func=AF.Exp, scale=0.125))
                if kb == 0:
                    oit += 1
                po, pon = (PB[4], 'pb4') if oit % 2 == 0 else (PB[5], 'pb5')
                k.op('pe', ['x_V%d' % i, ptn], [pon], lambda e: e.matmul(
                    po[:, :], lhsT=V[i][:, kb, :], rhs=pt[:], start=(kb == 0), stop=(kb == nkb - 1)))
                if pending and pending[0][0] <= idx:
                    pending.pop(0)[1]()
                if kb == nkb - 1:
                    pending.append((idx + 2, emit_norm(h, qb, po, pon, oit % 2)))
            for _, f in pending:
                f()
            k.barrier()

    def stage_merge(self, l):
        k = self.k
        PB = self.PB
        with ExitStack() as st:
            sb = lambda n, s, dt=F32: self.sbt(st, n, s, dt)
            ys = sb("m_ys", [128, 8, T], BF16); yh = sb("m_yh", [128, 8, T], BF16); yf = sb("m_yf", [128, 8, T], BF16)
            k.dma(ys[:], self.ysT.rearrange("(b p) t -> p b t", p=128), [], ['m_ys'])
            k.dma(yh[:], self.yhT.rearrange("(b p) t -> p b t", p=128), [], ['m_yh'])
            k.dma(yf[:], self.yfT.rearrange("(b h2) d t -> (h2 d) b t", h2=2), [], ['m_yf'])
            wps = self.mk_wpool(st, "mws", 8, 128, dd=3)
            wph = self.mk_wpool(st, "mwh", 8, 128, dd=3)
            wpf = self.mk_wpool(st, "mwf", 8, 128, dd=3)
            gts = [[sb("m_g%d_%d" % (r, i), [128, T], BF16) for i in range(2)] for r in range(3)]
            m1 = [sb("m_m1_%d" % i, [128, 512]) for i in range(2)]
            m2 = [sb("m_m2_%d" % i, [128, 512]) for i in range(2)]
            mo = [sb("m_mo%d" % i, [128, T], BF16) for i in range(2)]
            state = {'it': 0}
            items = []
            ysrc = [(ys, 'm_ys'), (yh, 'm_yh'), (yf, 'm_yf')]

            def mk(db):
                def fn(ws):
                    gi = db % 2
                    for r in range(3):
                        k.dma(gts[r][gi][:], self.gT[r * D + db * 128: r * D + (db + 1) * 128, :], [], ['m_g%d_%d' % (r, gi)])
                    for tb in range(4):
                        i = state['it'] % 2
                        state['it'] += 1
                        bs = [3 * i, 3 * i + 1, 3 * i + 2]
                        tsl = slice(tb * 512, (tb + 1) * 512)
                        for r in range(3):
                            wb, wbn = ws[r]
                            yt, ytn = ysrc[r]
                            for kc in range(8):
                                k.op('pe', [wbn, ytn], ['pb%d' % bs[r]], lambda e: e.matmul(
                                    PB[bs[r]][:, :], lhsT=wb[:, kc, :], rhs=yt[:, kc, tsl], start=(kc == 0), stop=(kc == 7)))
                        a1, a1n = m1[i], 'm_m1_%d' % i
                        a2, a2n = m2[i], 'm_m2_%d' % i
                        k.op('dve', ['pb%d' % bs[0], 'm_g0_%d' % gi], [a1n], lambda e: e.tensor_tensor(
                            out=a1[:], in0=PB[bs[0]][:, :], in1=gts[0][gi][:, tsl], op=ALU.mult))
                        k.op('dve', ['pb%d' % bs[1], 'm_g1_%d' % gi], [a2n], lambda e: e.tensor_tensor(
                            out=a2[:], in0=PB[bs[1]][:, :], in1=gts[1][gi][:, tsl], op=ALU.mult))
                        k.op('dve', [a1n, a2n], [a1n], lambda e: e.tensor_tensor(out=a1[:], in0=a1[:], in1=a2[:], op=ALU.add))
                        k.op('dve', ['pb%d' % bs[2], 'm_g2_%d' % gi], [a2n], lambda e: e.tensor_tensor(
                            out=a2[:], in0=PB[bs[2]][:, :], in1=gts[2][gi][:, tsl], op=ALU.mult))
                        k.op('dve', [a1n, a2n], ['m_mo%d' % gi], lambda e: e.tensor_tensor(
                            out=mo[gi][:, tsl], in0=a1[:], in1=a2[:], op=ALU.add))
                    k.dma(self.mT[db * 128:(db + 1) * 128, :], mo[gi][:], ['m_mo%d' % gi], [])
                return fn
            for db in range(16):
                c0 = db * 128
                items.append(([(wps, [(0, self.w_branch_ssd[l, :, c0:c0 + 128])], 8, 128),
                               (wph, [(0, self.w_branch_hgrn[l, :, c0:c0 + 128])], 8, 128),
                               (wpf, [(0, self.w_branch_fox[l, :, c0:c0 + 128])], 8, 128)], mk(db)))
            self.run_pipeline(items)
            k.barrier()

    def stage_out_proj(self, AT_dram, nkc, Wsrc, xsrc, xdst, halves):
        k = self.k
        PB = self.PB
        TH = T // halves
        ntt = TH // 128
        with ExitStack() as st:
            sb = lambda n, s, dt=F32: self.sbt(st, n, s, dt)
            A = sb("o_A", [128, nkc, TH], BF16)
            KG = 4
            wp = self.mk_wpool(st, "ow", KG, 512, dd=4)
            xs = [sb("o_x%d" % i, [128, 512]) for i in range(16)]
            state = {'g': 0}
            items = []

            def mk(hf, cb, grp, kg, n_k, first, last, gidx):
                def fn(ws):
                    wb, wbn = ws[0]
                    if first:
                        k.dma(A[:], AT_dram[:, hf * TH:(hf + 1) * TH].rearrange("(c p) t -> p c t", p=128), [], ['o_A'])
                    if kg == 0:
                        for bi, tt in enumerate(grp):
                            j = (gidx % 2) * 8 + bi
                            r0 = hf * TH + tt * 128
                            k.dma(xs[j][:], xsrc[r0:r0 + 128, cb * 512:(cb + 1) * 512], [], ['o_x%d' % j])
                    for kk in range(n_k):
                        kc = kg + kk
                        for bi, tt in enumerate(grp):
                            k.op('pe', [wbn, 'o_A'], ['pb%d' % bi], lambda e: e.matmul(
                                PB[bi][:, :], lhsT=A[:, kc, tt * 128:(tt + 1) * 128], rhs=wb[:, kk, :],
                                start=(kc == 0), stop=(kc == nkc - 1)))
                    if last:
                        for bi, tt in enumerate(grp):
                            j = (gidx % 2) * 8 + bi
                            xt, xn = xs[j], 'o_x%d' % j
                            r0 = hf * TH + tt * 128
                            k.op('dve', [xn, 'pb%d' % bi], [xn], lambda e: e.tensor_tensor(
                                out=xt[:], in0=xt[:], in1=PB[bi][:, :], op=ALU.add))
                            k.dma(xdst[r0:r0 + 128, cb * 512:(cb + 1) * 512], xt[:], [xn], [])
                return fn
            for hf in range(halves):
                first = True
                for cb in range(D // 512):
                    for tt0 in range(0, ntt, 8):
                        grp = list(range(tt0, min(ntt, tt0 + 8)))
                        for kg in range(0, nkc, KG):
                            n_k = min(KG, nkc - kg)
                            items.append(([(wp, [(0, Wsrc[kg * 128:(kg + n_k) * 128, cb * 512:(cb + 1) * 512])], n_k, 512)],
                                          mk(hf, cb, grp, kg, n_k, first, kg + n_k == nkc, state['g'])))
                            first = False
                        state['g'] += 1
            self.run_pipeline(items)
            k.barrier()

    def stage_ffn_up(self, l, hT, hTn):
        k = self.k
        PB = self.PB
        W = self.ffn_w_up
        with ExitStack() as st:
            sb = lambda n, s, dt=F32: self.sbt(st, n, s, dt)
            wp = self.mk_wpool(st, "fw", KC, 256)
            cw = sb("f_cw", [128, 2 * NFB, 3]); cb = sb("f_cb", [128, 2 * NFB])
            for b in range(2 * NFB):
                k.dma(cw[:, b, :], self.ffn_conv_w[l, :, b * 128:(b + 1) * 128].rearrange("k p -> p k"),
                      [], ['f_cw%d' % b], allow_slow_non_contiguous=True)
            k.dma(cb[:], self.ffn_conv_b[l].rearrange("(b p) -> p b", p=128), [], ['f_cb'],
                  allow_slow_non_contiguous=True)
            xp = [sb("f_xp%d" % i, [128, T + 2]) for i in range(2)]
            acc = [sb("f_acc%d" % i, [128, T]) for i in range(2)]
            sg = sb("f_sg", [128, T])
            ao = [sb("f_ao%d" % i, [128, T], BF16) for i in range(2)]
            for i in range(2):
                k.op('pool', [], ['f_xp%d' % i], lambda e: e.memset(xp[i][:, 0:2], 0.0))
            state = {'bank': 0, 'ev': 0}
            items = []

            def mk(j):
                def fn(ws):
                    wb, wbn = ws[0]
                    for half in range(2):
                        blk = j + half * NFB
                        x_, xn = xp[half], 'f_xp%d' % half
                        a_, an = acc[half], 'f_acc%d' % half
                        for tb in range(4):
                            b = state['bank'] % 8
                            state['bank'] += 1
                            for kc in range(KC):
                                k.op('pe', [wbn, hTn], ['pb%d' % b], lambda e: e.matmul(
                                    PB[b][:, :], lhsT=wb[:, kc, half * 128:(half + 1) * 128],
                                    rhs=hT[:, kc, tb * 512:(tb + 1) * 512],
                                    start=(kc == 0), stop=(kc == KC - 1)))
                            eng = ('act', 'act', 'dve')[state['ev'] % 3]
                            state['ev'] += 1
                            self.cast(x_[:, 2 + tb * 512: 2 + (tb + 1) * 512], PB[b][:, :], ['pb%d' % b], [xn], eng=eng)
                        k.op('dve', [xn, 'f_cw%d' % blk, 'f_cb'], [an], lambda e: e.tensor_scalar(
                            out=a_[:], in0=x_[:, 0:T], scalar1=cw[:, blk, 0:1], scalar2=cb[:, blk:blk + 1],
                            op0=ALU.mult, op1=ALU.add))
                        for kk in range(1, 3):
                            k.op('dve', [xn, 'f_cw%d' % blk, an], [an], lambda e: e.scalar_tensor_tensor(
                                out=a_[:], in0=x_[:, kk:kk + T], scalar=cw[:, blk, kk:kk + 1], in1=a_[:],
                                op0=ALU.mult, op1=ALU.add))
                    k.op('act', ['f_acc0'], ['f_sg'], lambda e: e.activation(out=sg[:], in_=acc[0][:], func=AF.Silu))
                    o, on = ao[j % 2], 'f_ao%d' % (j % 2)
                    k.op('pool', ['f_sg', 'f_acc1'], [on], lambda e: e.tensor_tensor(out=o[:], in0=sg[:], in1=acc[1][:], op=ALU.mult))
                    k.dma(self.actT[j * 128:(j + 1) * 128, :], o[:], [on], [])
                return fn
            for j in range(NFB):
                items.append(([(wp, [(0, W[l, :, j * 128:(j + 1) * 128]),
                                     (128, W[l, :, (NFB + j) * 128:(NFB + j + 1) * 128])], KC, 256)], mk(j)))
            self.run_pipeline(items)
            k.barrier()


    def stage_final(self, src):
        k = self.k
        with ExitStack() as st:
            sb = lambda n, s, dt=F32: self.sbt(st, n, s, dt)
            wN = sb("z_w", [128, D])
            k.dma(wN[:], self.final_norm_w.partition_broadcast(128), [], ['z_w'])
            xts = [sb("z_x%d" % i, [128, D]) for i in range(2)]
            ots = [sb("z_o%d" % i, [128, D]) for i in range(2)]
            junk = sb("z_junk", [128, D], BF16)
            sss = [sb("z_ss%d" % i, [128, 1]) for i in range(2)]
            for tt in range(NT):
                i = tt % 2
                xt, ot, ss = xts[i], ots[i], sss[i]
                xn, on, sn = "z_x%d" % i, "z_o%d" % i, "z_ss%d" % i
                k.dma(xt[:], src[tt * 128:(tt + 1) * 128, :], [], [xn])
                k.op('act', [xn], ['z_junk', sn], lambda e: e.activation(
                    out=junk[:], in_=xt[:], func=AF.Square, accum_out=ss[:]))
                k.op('dve', [sn], [sn], lambda e: e.tensor_scalar(
                    out=ss[:], in0=ss[:], scalar1=1.0 / D, scalar2=EPS, op0=ALU.mult, op1=ALU.add))
                k.op('act', [sn], [sn], lambda e: e.sqrt(out=ss[:], in_=ss[:]))
                k.op('dve', [sn], [sn], lambda e: e.reciprocal(out=ss[:], in_=ss[:]))
                k.op('dve', [xn, sn, 'z_w'], [on], lambda e: e.scalar_tensor_tensor(
                    out=ot[:], in0=xt[:], scalar=ss[:, 0:1], in1=wN[:], op0=ALU.mult, op1=ALU.mult))
                k.dma(self.out[tt * 128:(tt + 1) * 128, :], ot[:], [on], [])
            k.barrier()

    def build(self, nlayers=2, upto=None):
        k = self.k
        src = self.x
        stop = False
        for l in range(nlayers):
            with ExitStack() as st:
                if upto == 'init':
                    return
                hT = self.sbt(st, "hT", [128, KC, T], BF16)
                self.norm_T(src, self.norm_mix_w[l], hT, 'hT')
                if upto == 'norm':
                    return
                self.stage_proj(l, hT, 'hT')
            if upto == 'proj':
                return
            self.stage_ssd(l)
            if upto == 'ssd':
                return
            self.stage_hgrn(l)
            if upto == 'hgrn':
                return
            self.stage_fox(l)
            if upto == 'fox':
                return
            self.stage_merge(l)
            if upto == 'merge':
                return
            self.stage_out_proj(self.mT, 16, self.w_out[l], src, self.xa, halves=1)
            if upto == 'mix':
                return
            with ExitStack() as st:
                hT = self.sbt(st, "hT", [128, KC, T], BF16)
                self.norm_T(self.xa, self.norm_ffn_w[l], hT, 'hT')
                self.stage_ffn_up(l, hT, 'hT')
            if upto == 'ffn_up':
                return
            self.stage_out_proj(self.actT, NFB, self.ffn_w_down[l], self.xa, self.xb, halves=2)
            if upto == 'layer':
                return
            src = self.xb
        self.stage_final(self.xb)


WNAMES = ["norm_mix_w", "w_in", "ssd_conv_w", "ssd_conv_b", "ssd_dt_bias", "ssd_a_log", "ssd_d",
          "ssd_norm_w", "hgrn_lb", "hgrn_norm_w", "fox_f_bias", "w_branch_ssd", "w_branch_hgrn",
          "w_branch_fox", "w_out", "norm_ffn_w", "ffn_w_up", "ffn_conv_w", "ffn_conv_b",
          "ffn_w_down", "final_norm_w"]


def kernel(**inputs):
    nc = bass.Bass("TRN2", target_bir_lowering=False)
    p = Prog(nc)
    p.build()
    x = np.ascontiguousarray(inputs["x"], dtype=np.float32)
    shared = {n: np.ascontiguousarray(inputs[n], dtype=np.float32) for n in WNAMES}
    in_maps = []
    for c in range(8):
        m = dict(shared)
        m["x"] = x[c]
        in_maps.append(m)
    res = run_bass_kernel_spmd(nc, in_maps, core_ids=list(range(8)))
    return np.stack([np.asarray(r["out"], dtype=np.float32) for r in res.results], axis=0)
```

```python
import numpy as np
from contextlib import ExitStack
import concourse.bass as bass
import concourse.mybir as mybir
from concourse.bass_utils import run_bass_kernel_spmd

F32 = mybir.dt.float32
BF16 = mybir.dt.bfloat16
AF = mybir.ActivationFunctionType
ALU = mybir.AluOpType

T = 2048
D = 2048
NT = 16
KC = 16
DIN = 15904
DFF = 5504
NFB = 43
EPS = 1e-6
NDS = 24

O_Z, O_XBC, O_DT, O_HQ, O_HF, O_HI, O_HG, O_FQ, O_FK, O_FV, O_FF, O_G = (
    0, 1024, 2560, 2576, 3600, 4624, 5648, 6672, 7696, 8720, 9744, 9760)


class KB:
    def __init__(self, nc):
        self.nc = nc
        self.E = {'pe': nc.tensor, 'act': nc.scalar, 'dve': nc.vector,
                  'pool': nc.gpsimd, 'sp': nc.sync}
        self.sem = {k: nc.alloc_semaphore("sem_" + k) for k in self.E}
        self.icnt = {k: 0 for k in self.E}
        self.scnt = {k: 0 for k in self.E}
        self.last = {k: None for k in self.E}
        self.incs = {k: [] for k in self.E}
        self.seen = {k: {k2: 0 for k2 in self.E} for k in self.E}
        self.dsem = [nc.alloc_semaphore("dsem%d" % i) for i in range(NDS)]
        self.dcnt = [0] * NDS
        self.dnext = 0
        self.dseen = {k: [0] * NDS for k in self.E}
        self.lastw = {}
        self.readers = {}

    def _deps(self, reads, writes):
        deps = []
        for r in reads:
            t = self.lastw.get(r)
            if t is not None:
                deps.append(t)
        for w in writes:
            t = self.lastw.get(w)
            if t is not None:
                deps.append(t)
            rd = self.readers.get(w)
            if rd:
                for e, i in rd['e'].items():
                    deps.append(('e', e, i))
                deps.extend(rd['d'])
        return deps

    def _semval_for(self, e2, idx):
        lst = self.incs[e2]
        if lst and lst[-1][0] >= idx:
            j = len(lst) - 1
            while j > 0 and lst[j - 1][0] >= idx:
                j -= 1
            return lst[j]
        ins, lidx = self.last[e2]
        assert lidx >= idx
        self.scnt[e2] += 1
        ins.then_inc(self.sem[e2], 1)
        lst.append((lidx, self.scnt[e2]))
        return lst[-1]

    def _wait(self, eng, deps, raw_same=()):
        need = {}
        dneed = {}
        for t in deps:
            if t[0] == 'e':
                _, e2, idx = t
                if e2 == eng:
                    continue
                if idx > need.get(e2, 0):
                    need[e2] = idx
            else:
                _, s, c = t
                if c > dneed.get(s, 0):
                    dneed[s] = c
        for idx in raw_same:
            if eng != 'pe' and idx > need.get(eng, 0):
                need[eng] = idx
        E = self.E[eng]
        for e2, idx in need.items():
            if idx <= self.seen[eng][e2]:
                continue
            iidx, v = self._semval_for(e2, idx)
            E.wait_ge(self.sem[e2], v)
            self.seen[eng][e2] = iidx
        for s, c in dneed.items():
            if c <= self.dseen[eng][s]:
                continue
            E.wait_ge(self.dsem[s], c)
            self.dseen[eng][s] = c

    def _record(self, tok, reads, writes):
        for r in reads:
            rd = self.readers.get(r)
            if rd is None:
                rd = {'e': {}, 'd': []}
                self.readers[r] = rd
            if tok[0] == 'e':
                rd['e'][tok[1]] = tok[2]
            else:
                rd['d'].append(tok)
        for w in writes:
            self.lastw[w] = tok
            self.readers[w] = {'e': {}, 'd': []}

    def op(self, eng, reads, writes, fn):
        deps = self._deps(reads, writes)
        raw_same = []
        for r in reads:
            t = self.lastw.get(r)
            if t is not None and t[0] == 'e' and t[1] == eng:
                raw_same.append(t[2])
        self._wait(eng, deps, raw_same)
        ins = fn(self.E[eng])
        self.icnt[eng] += 1
        self.last[eng] = (ins, self.icnt[eng])
        self._record(('e', eng, self.icnt[eng]), reads, writes)
        return ins

    def dma(self, out, in_, reads, writes, q='sp', **kw):
        deps = self._deps(reads, writes)
        s = self.dnext
        self.dnext = (self.dnext + 1) % NDS
        if self.dcnt[s] > 0:
            deps.append(('d', s, self.dcnt[s]))
        raw_same = []
        for r in reads:
            t = self.lastw.get(r)
            if t is not None and t[0] == 'e' and t[1] == q:
                raw_same.append(t[2])
        self._wait(q, deps, raw_same)
        ins = self.E[q].dma_start(out=out, in_=in_, **kw)
        self.dcnt[s] += 16
        ins.then_inc(self.dsem[s], 16)
        self._record(('d', s, self.dcnt[s]), reads, writes)

    def barrier(self):
        deps = []
        for e in self.E:
            if self.last[e] is not None:
                deps.append(('e', e, self.last[e][1]))
        for s in range(NDS):
            if self.dcnt[s] > 0:
                deps.append(('d', s, self.dcnt[s]))
        for e in self.E:
            self._wait(e, deps, [self.last[e][1]] if (self.last[e] is not None and e != 'sp') else [])
        self.lastw = {}
        self.readers = {}


class Prog:
    def __init__(self, nc, dbg=(), tiny=False):
        self.nc = nc
        self.k = KB(nc)
        self.dbg = set(dbg)
        self.cast_rr = 0
        self.uid = 0
        k = self.k
        BIG = ("w_in", "w_branch_ssd", "w_branch_hgrn", "w_branch_fox", "w_out", "ffn_w_up", "ffn_w_down")
        di = lambda n, s: nc.dram_tensor(n, ([2, 1, 1] if (tiny and n in BIG) else list(s)), F32, kind="ExternalInput").ap()
        self.x = di("x", [T, D])
        self.norm_mix_w = di("norm_mix_w", [2, D])
        self.w_in = di("w_in", [2, D, DIN])
        self.ssd_conv_w = di("ssd_conv_w", [2, 4, 1536])
        self.ssd_conv_b = di("ssd_conv_b", [2, 1536])
        self.ssd_dt_bias = di("ssd_dt_bias", [2, 16])
        self.ssd_a_log = di("ssd_a_log", [2, 16])
        self.ssd_d = di("ssd_d", [2, 16])
        self.ssd_norm_w = di("ssd_norm_w", [2, 1024])
        self.hgrn_lb = di("hgrn_lb", [2, 1024])
        self.hgrn_norm_w = di("hgrn_norm_w", [2, 1024])
        self.fox_f_bias = di("fox_f_bias", [2, 16])
        self.w_branch_ssd = di("w_branch_ssd", [2, 1024, D])
        self.w_branch_hgrn = di("w_branch_hgrn", [2, 1024, D])
        self.w_branch_fox = di("w_branch_fox", [2, 1024, D])
        self.w_out = di("w_out", [2, D, D])
        self.norm_ffn_w = di("norm_ffn_w", [2, D])
        self.ffn_w_up = di("ffn_w_up", [2, D, 2 * DFF])
        self.ffn_conv_w = di("ffn_conv_w", [2, 3, 2 * DFF])
        self.ffn_conv_b = di("ffn_conv_b", [2, 2 * DFF])
        self.ffn_w_down = di("ffn_w_down", [2, DFF, D])
        self.final_norm_w = di("final_norm_w", [D])
        self.out = nc.dram_tensor("out", [T, D], F32, kind="ExternalOutput").ap()
        self.xa = self.scr("xa", [T, D], F32)
        self.xb = self.scr("xb", [T, D], F32)
        self.xs_tm = self.scr("xs_tm", [T, 1024], BF16)
        self.B_tm = self.scr("B_tm", [T, 256], BF16)
        self.BT = self.scr("BT", [2, 128, T], BF16)
        self.CT = self.scr("CT", [2, 128, T], BF16)
        self.zs_tm = self.scr("zs_tm", [T, 1024], BF16)
        self.dtff = self.scr("dtff", [T, 32], F32)
        self.hqT = self.scr("hqT", [1024, T], BF16)
        self.hsigT = self.scr("hsigT", [1024, T], F32)
        self.hgT = self.scr("hgT", [1024, T], BF16)
        self.hv_tm = self.scr("hv_tm", [T, 1024], BF16)
        self.fqA = self.scr("fqA", [16, 70, T], BF16)
        self.fkA = self.scr("fkA", [16, 70, T], BF16)
        self.fv_tm = self.scr("fv_tm", [T, 1024], BF16)
        self.gT = self.scr("gT", [6144, T], BF16)
        self.ysT = self.scr("ysT", [1024, T], BF16)
        self.yhT = self.scr("yhT", [1024, T], BF16)
        self.yfT = self.scr("yfT", [16, 64, T], BF16)
        self.mT = self.scr("mT", [D, T], BF16)
        self.actT = self.scr("actT", [DFF, T], BF16)
        self.PB = [nc.alloc_psum_tensor("pb%d" % i, [128, 512], F32) for i in range(8)]
        self.PT = self.PB[7][:].bitcast(BF16)
        self.PT6 = self.PB[6][:].bitcast(BF16)
        self.cst = ExitStack()
        sb = lambda n, s, dt=F32: self.cst.enter_context(nc.sbuf_tensor(n, list(s), dt))
        self.ones_f = sb("ones_f", [128, 128])
        self.ones_b = sb("ones_b", [128, 128], BF16)
        self.ident_b = sb("ident_b", [128, 128], BF16)
        self.triI = sb("triI", [128, 128])
        self.triSU = sb("triSU", [128, 128])
        self.zeros_f = sb("zeros_f", [128, 512])
        tmp = sb("c_tmp", [128, 128])
        k.op('pool', [], ['ones_f'], lambda e: e.memset(self.ones_f[:], 1.0))
        k.op('pool', [], ['zeros_f'], lambda e: e.memset(self.zeros_f[:], 0.0))
        k.op('pool', ['ones_f'], ['c_tmp'], lambda e: e.affine_select(
            out=tmp[:], in_=self.ones_f[:], pattern=[[1, 128]], compare_op=ALU.is_equal,
            fill=0.0, base=0, channel_multiplier=-1))
        k.op('pool', ['ones_f'], ['triI'], lambda e: e.affine_select(
            out=self.triI[:], in_=self.ones_f[:], pattern=[[1, 128]], compare_op=ALU.is_ge,
            fill=0.0, base=0, channel_multiplier=-1))
        k.op('pool', ['ones_f'], ['triSU'], lambda e: e.affine_select(
            out=self.triSU[:], in_=self.ones_f[:], pattern=[[-1, 128]], compare_op=ALU.is_ge,
            fill=0.0, base=-1, channel_multiplier=1))
        k.op('dve', ['c_tmp'], ['ident_b'], lambda e: e.tensor_copy(out=self.ident_b[:], in_=tmp[:]))
        k.op('dve', ['ones_f'], ['ones_b'], lambda e: e.tensor_copy(out=self.ones_b[:], in_=self.ones_f[:]))
        k.barrier()
        self.C = ['ones_f', 'ones_b', 'ident_b', 'triI', 'triSU', 'zeros_f']

    def scr(self, name, shape, dt):
        kind = "ExternalOutput" if name in self.dbg else "Internal"
        return self.nc.dram_tensor(name, list(shape), dt, kind=kind).ap()

    def sbt(self, st, name, shape, dt=F32):
        self.uid += 1
        return st.enter_context(self.nc.sbuf_tensor("%s_%d" % (name, self.uid), list(shape), dt))

    def u(self):
        self.uid += 1
        return "u%d" % self.uid

    def cast(self, out_ap, in_ap, reads, writes, eng=None):
        if eng is None:
            eng = ('dve', 'act', 'pool', 'dve', 'act')[self.cast_rr % 5]
            self.cast_rr += 1
        if eng == 'act':
            self.k.op('act', reads, writes, lambda e: e.activation(out=out_ap, in_=in_ap, func=AF.Copy))
        else:
            self.k.op(eng, reads, writes, lambda e: e.tensor_copy(out=out_ap, in_=in_ap))

    def transpose_to(self, src_tile, src_name, nblk, dst_fn, pt_toggle, views=None):
        k = self.k
        if views is None:
            views = [(self.PT, 'pb7')]
        for b0 in range(0, nblk, 4):
            nb = min(4, nblk - b0)
            pv, ptn = views[pt_toggle[0] % len(views)]
            pt_toggle[0] += 1
            for b in range(nb):
                o = pv[:, b * 128:(b + 1) * 128]
                k.op('pe', [src_name, 'ident_b'], [ptn], lambda e: e.transpose(
                    out=o, in_=src_tile[:, (b0 + b) * 128:(b0 + b + 1) * 128],
                    identity=self.ident_b[:]))
            dst_fn(b0, nb, pv[:, 0:nb * 128], ptn)

    def norm_T(self, src, wv, hT, hTn):
        k = self.k
        with ExitStack() as st:
            wN = self.sbt(st, "nrm_w", [128, D])
            k.dma(wN[:], wv.partition_broadcast(128), [], ['nrm_w'])
            xts = [self.sbt(st, "nrm_x%d" % i, [128, D]) for i in range(2)]
            hbs = [self.sbt(st, "nrm_hb%d" % i, [128, D], BF16) for i in range(2)]
            junk = self.sbt(st, "nrm_junk", [128, D], BF16)
            sss = [self.sbt(st, "nrm_ss%d" % i, [128, 1]) for i in range(2)]
            tog = [0]
            def stats(tt):
                i = tt % 2
                xt, hb, ss = xts[i], hbs[i], sss[i]
                xn, hn, sn = "nrm_x%d" % i, "nrm_hb%d" % i, "nrm_ss%d" % i
                k.dma(xt[:], src[tt * 128:(tt + 1) * 128, :], [], [xn])
                k.op('act', [xn], ['nrm_junk', sn], lambda e: e.activation(
                    out=junk[:], in_=xt[:], func=AF.Square, accum_out=ss[:]))
                k.op('dve', [sn], [sn], lambda e: e.tensor_scalar(
                    out=ss[:], in0=ss[:], scalar1=1.0 / D, scalar2=EPS, op0=ALU.mult, op1=ALU.add))
                k.op('act', [sn], [sn], lambda e: e.sqrt(out=ss[:], in_=ss[:]))
                k.op('dve', [sn], [sn], lambda e: e.reciprocal(out=ss[:], in_=ss[:]))
                k.op('dve', [xn, sn, 'nrm_w'], [hn], lambda e: e.scalar_tensor_tensor(
                    out=hb[:], in0=xt[:], scalar=ss[:, 0:1], in1=wN[:], op0=ALU.mult, op1=ALU.mult))

            def xpose(tt):
                i = tt % 2
                hb, hn = hbs[i], "nrm_hb%d" % i

                def dst(b0, nb, pt, ptn, tt=tt):
                    self.cast(hT[:, b0:b0 + nb, tt * 128:(tt + 1) * 128],
                              pt.rearrange("p (c t) -> p c t", t=128), [ptn], [hTn],
                              eng=('act' if (b0 // 4) % 2 == 0 else 'dve'))
                self.transpose_to(hb, hn, 16, dst, tog, views=[(self.PT6, 'pb6'), (self.PT, 'pb7')])

            stats(0)
            for tt in range(NT):
                if tt + 1 < NT:
                    stats(tt + 1)
                xpose(tt)
            k.barrier()

    def mk_wpool(self, st, name, kc, n, dd=2):
        return {'i': 0, 'name': name, 'dd': dd,
                'wf': [self.sbt(st, name + "f%d" % i, [128, kc, n]) for i in range(dd)],
                'wb': [self.sbt(st, name + "b%d" % i, [128, kc, n], BF16) for i in range(2)]}

    def w_dma(self, pool, srcs, kc, n, pp=128):
        i = pool['i'] % pool['dd']
        pool['i'] += 1
        wf = pool['wf'][i]
        for si, (c0, src) in enumerate(srcs):
            nco = src.shape[1]
            self.k.dma(wf[0:pp, 0:kc, c0:c0 + nco], src.rearrange("(c p) n -> p c n", p=pp), [],
                       ["%sf%d_%d" % (pool['name'], i, si)])
        return i

    def w_cast(self, pool, i, nsrc, kc, n, pp=128):
        j = pool.get('ci', 0) % 2
        pool['ci'] = pool.get('ci', 0) + 1
        wf, wb = pool['wf'][i], pool['wb'][j]
        rd = ["%sf%d_%d" % (pool['name'], i, si) for si in range(nsrc)]
        wbn = "%sb%d" % (pool['name'], j)
        half = max(1, kc // 2)
        for c0 in range(0, kc, half):
            c1 = min(kc, c0 + half)
            eng = ('dve', 'act')[self.cast_rr % 2]
            self.cast_rr += 1
            self.cast(wb[0:pp, c0:c1, 0:n], wf[0:pp, c0:c1, 0:n], rd, [wbn], eng=eng)
        return wb, wbn

    def run_pipeline(self, items):
        n = len(items)
        dd = items[0][0][0][0]['dd']
        dm = lambda it: [self.w_dma(ld[0], ld[1], ld[2], ld[3]) for ld in it[0]]
        cs = lambda it, sl: [self.w_cast(ld[0], s_, len(ld[1]), ld[2], ld[3]) for ld, s_ in zip(it[0], sl)]
        slots = {}
        for j in range(min(dd, n)):
            slots[j] = dm(items[j])
        ready = cs(items[0], slots[0])
        for i in range(n):
            cur = ready
            if i + dd < n:
                slots[i + dd] = dm(items[i + dd])
            if i + 1 < n:
                ready = cs(items[i + 1], slots[i + 1])
            items[i][1](cur)

    def stage_proj(self, l, hT, hTn):
        k = self.k
        W = self.w_in
        PB = self.PB
        with ExitStack() as st:
            wp = self.mk_wpool(st, "pw", KC, 256)
            fa = [self.sbt(st, "fa%d" % i, [128, T]) for i in range(3)]
            xpad = [self.sbt(st, "xpad%d" % i, [128, T + 3]) for i in range(2)]
            stg32 = self.sbt(st, "stg32", [128, NT, 32])
            ba = [self.sbt(st, "ba%d" % i, [128, T], BF16) for i in range(3)]
            tms = [self.sbt(st, "tms%d" % i, [128, NT, 256], BF16) for i in range(2)]
            cw = self.sbt(st, "cw", [128, 12, 4])
            cb = self.sbt(st, "cb", [128, 12])
            for b in range(12):
                k.dma(cw[:, b, :], self.ssd_conv_w[l, :, b * 128:(b + 1) * 128].rearrange("k p -> p k"),
                      [], ['cw%d' % b], allow_slow_non_contiguous=True)
            k.dma(cb[:], self.ssd_conv_b[l].rearrange("(b p) -> p b", p=128), [], ['cb'],
                  allow_slow_non_contiguous=True)
            for i in range(2):
                k.op('pool', [], ["xpad%d" % i], lambda e: e.memset(xpad[i][:, 0:3], 0.0))
            cnt = {'fa': 0, 'ba': 0, 'tms': 0, 'bank': 0, 'xpad': 0, 'ev': 0}
            tog = [0]

            def nxt(key, n):
                i = cnt[key] % n
                cnt[key] += 1
                return i

            items = []

            def add_fm(col0, nblk, start, evac, final):
                for c0 in range(0, nblk * 128, 256):
                    n = min(256, nblk * 128 - c0)

                    def fn(ws, c0=c0, n=n):
                        wb, wbn = ws[0]
                        for sub in range(n // 128):
                            bj = (c0 + sub * 128) // 128
                            ctx = start(bj)
                            for tb in range(4):
                                b = nxt('bank', 7)
                                for kc in range(KC):
                                    k.op('pe', [wbn, hTn], ['pb%d' % b], lambda e: e.matmul(
                                        PB[b][:, :], lhsT=wb[:, kc, sub * 128:(sub + 1) * 128],
                                        rhs=hT[:, kc, tb * 512:(tb + 1) * 512],
                                        start=(kc == 0), stop=(kc == KC - 1)))
                                evac(ctx, tb, b)
                            final(bj, ctx)
                    items.append(([(wp, [(0, W[l, :, col0 + c0: col0 + c0 + n])], KC, n)], fn))

            def ev(o, b, dname, func):
                if func is None:
                    self.cast(o, PB[b][:, :], ['pb%d' % b], [dname], eng=('act', 'dve')[nxt('ev', 2)])
                else:
                    k.op('act', ['pb%d' % b], [dname], lambda e: e.activation(out=o, in_=PB[b][:, :], func=func))

            def to_tm(o, on, dst):
                i = nxt('tms', 2)
                stg, sn = tms[i], "tms%d" % i

                def dfn(b0, nb, pt, ptn):
                    self.cast(stg[:, b0:b0 + nb, 0:128], pt.rearrange("p (c t) -> p c t", t=128),
                              [ptn], [sn], eng=('act' if (b0 // 4) % 2 == 0 else 'dve'))
                self.transpose_to(o, on, 16, dfn, tog)
                k.dma(dst.rearrange("(tt p) c -> p tt c", p=128), stg[:, :, 0:128], [sn], [])

            def xbc_start(bj):
                i = nxt('xpad', 2)
                return (xpad[i], "xpad%d" % i)

            def xbc_evac(ctx, tb, b):
                ev(ctx[0][:, 3 + tb * 512: 3 + (tb + 1) * 512], b, ctx[1], None)

            def xbc_final(bj, ctx):
                xp, xpn = ctx
                i2 = nxt('fa', 3)
                acc, an = fa[i2], "fa%d" % i2
                k.op('dve', [xpn, 'cw%d' % bj, 'cb'], [an], lambda e: e.tensor_scalar(
                    out=acc[:, 0:T], in0=xp[:, 0:T], scalar1=cw[:, bj, 0:1], scalar2=cb[:, bj:bj + 1],
                    op0=ALU.mult, op1=ALU.add))
                for kk in range(1, 4):
                    k.op('dve', [xpn, 'cw%d' % bj, an], [an], lambda e: e.scalar_tensor_tensor(
                        out=acc[:, 0:T], in0=xp[:, kk:kk + T], scalar=cw[:, bj, kk:kk + 1],
                        in1=acc[:, 0:T], op0=ALU.mult, op1=ALU.add))
                i3 = nxt('ba', 3)
                o, on = ba[i3], "ba%d" % i3
                k.op('act', [an], [on], lambda e: e.activation(out=o[:], in_=acc[:, 0:T], func=AF.Silu))
                if bj < 8:
                    to_tm(o, on, self.xs_tm[:, bj * 128:(bj + 1) * 128])
                elif bj < 10:
                    g = bj - 8
                    k.dma(self.BT[g], o[:], [on], [])
                    to_tm(o, on, self.B_tm[:, g * 128:(g + 1) * 128])
                else:
                    k.dma(self.CT[bj - 10], o[:], [on], [])

            def mk_plain(pool, pname, npool, func, final):
                def start(bj):
                    i = nxt(pname, npool)
                    return (pool[i], "%s%d" % (pname, i))

                def evac(ctx, tb, b):
                    ev(ctx[0][:, tb * 512:(tb + 1) * 512], b, ctx[1], func)
                return start, evac, final

            def fin_rows(dst):
                return lambda bj, ctx: k.dma(dst[bj * 128:(bj + 1) * 128, :], ctx[0][:], [ctx[1]], [])

            def fin_qk(dst):
                def f(bj, ctx):
                    k.dma(dst[2 * bj, 0:64, :], ctx[0][0:64, :], [ctx[1]], [])
                    k.dma(dst[2 * bj + 1, 0:64, :], ctx[0][64:128, :], [ctx[1]], [])
                return f

            def add_tm(col0, ncols, dst, func, f32out=False, srcs=None):
                for c0 in range(0, ncols, 256):
                    n = min(256, ncols - c0)

                    def fn(ws, c0=c0, n=n):
                        wb, wbn = ws[0]
                        if f32out:
                            stg, sn = stg32, 'stg32'
                        else:
                            i = nxt('tms', 2)
                            stg, sn = tms[i], "tms%d" % i
                        for tt in range(NT):
                            b = nxt('bank', 7)
                            for kc in range(KC):
                                k.op('pe', [wbn, hTn], ['pb%d' % b], lambda e: e.matmul(
                                    PB[b][:, 0:n], lhsT=hT[:, kc, tt * 128:(tt + 1) * 128], rhs=wb[:, kc, 0:n],
                                    start=(kc == 0), stop=(kc == KC - 1)))
                            o = stg[:, tt, 0:n]
                            if func is None:
                                self.cast(o, PB[b][:, 0:n], ['pb%d' % b], [sn], eng=('act', 'dve')[nxt('ev', 2)])
                            else:
                                k.op('act', ['pb%d' % b], [sn], lambda e: e.activation(out=o, in_=PB[b][:, 0:n], func=func))
                        if f32out:
                            k.dma(dst.rearrange("(tt p) c -> p tt c", p=128), stg[:, :, 0:n], [sn], [])
                        else:
                            k.dma(dst[:, c0:c0 + n].rearrange("(tt p) c -> p tt c", p=128), stg[:, :, 0:n], [sn], [])
                    ss_ = srcs if srcs is not None else [(0, W[l, :, col0 + c0: col0 + c0 + n])]
                    items.append(([(wp, ss_, KC, n)], fn))

            add_tm(0, 32, self.dtff, None, f32out=True,
                   srcs=[(0, W[l, :, O_DT:O_DT + 16]), (16, W[l, :, O_FF:O_FF + 16])])
            add_tm(O_Z, 1024, self.zs_tm, AF.Silu)
            add_tm(O_HI, 1024, self.hv_tm, None)
            add_tm(O_FV, 1024, self.fv_tm, None)
            add_fm(O_XBC, 12, xbc_start, xbc_evac, xbc_final)
            add_fm(O_HQ, 8, *mk_plain(ba, 'ba', 3, AF.Silu, fin_rows(self.hqT)))
            add_fm(O_HF, 8, *mk_plain(fa, 'fa', 3, AF.Sigmoid, fin_rows(self.hsigT)))
            add_fm(O_HG, 8, *mk_plain(ba, 'ba', 3, AF.Silu, fin_rows(self.hgT)))
            add_fm(O_FQ, 8, *mk_plain(ba, 'ba', 3, None, fin_qk(self.fqA)))
            add_fm(O_FK, 8, *mk_plain(ba, 'ba', 3, None, fin_qk(self.fkA)))
            add_fm(O_G, 48, *mk_plain(ba, 'ba', 3, AF.Sigmoid, fin_rows(self.gT)))
            self.run_pipeline(items)
            k.barrier()


    def stage_ssd(self, l):
        k = self.k
        PB, PT = self.PB, self.PT
        with ExitStack() as st:
            sb = lambda n, s, dt=F32: self.sbt(st, n, s, dt)
            dtb = sb("s_dtb", [128, 16]); alog = sb("s_alog", [128, 16]); dsk = sb("s_dsk", [128, 16])
            aneg = sb("s_aneg", [128, 16]); nw = sb("s_nw", [128, 1024])
            k.dma(dtb[:], self.ssd_dt_bias[l].partition_broadcast(128), [], ['s_dtb'])
            k.dma(alog[:], self.ssd_a_log[l].partition_broadcast(128), [], ['s_alog'])
            k.dma(dsk[:], self.ssd_d[l].partition_broadcast(128), [], ['s_dsk'])
            k.dma(nw[:], self.ssd_norm_w[l].partition_broadcast(128), [], ['s_nw'])
            k.op('act', ['s_alog'], ['s_aneg'], lambda e: e.activation(out=aneg[:], in_=alog[:], func=AF.Exp))
            k.op('dve', ['s_aneg'], ['s_aneg'], lambda e: e.tensor_scalar(
                out=aneg[:], in0=aneg[:], scalar1=-1.0, scalar2=None, op0=ALU.mult))
            S = sb("s_S", [128, 1024]); Sbf = sb("s_Sbf", [128, 1024], BF16)
            k.op('pool', [], ['s_S'], lambda e: e.memset(S[:], 0.0))
            k.op('pool', [], ['s_Sbf'], lambda e: e.memset(Sbf[:], 0.0))
            yTs = sb("s_yTs", [128, 8, T], BF16)
            P2 = {}

            def pool2(name, shape, dt=F32):
                P2[name] = [sb("%s%d" % (name, i), shape, dt) for i in range(2)]
            for nm, shp, dt in [("s_xs", [128, 1024], BF16), ("s_zs", [128, 1024], BF16),
                                ("s_btm", [128, 256], BF16), ("s_bt", [128, 2, 128], BF16),
                                ("s_ct", [128, 2, 128], BF16), ("s_df", [128, 32], F32),
                                ("s_sm", [128, 8, 16], F32), ("s_edec", [128, 48], F32),
                                ("s_cbm", [128, 2, 128], F32), ("s_R", [128, 8, 128], F32),
                                ("s_E", [128, 1024], F32), ("s_MT", [128, 16, 128], BF16),
                                ("s_xdt", [128, 1024], BF16), ("s_xdd", [128, 1024], BF16),
                                ("s_yo", [128, 1024], F32), ("s_y", [128, 1024], F32),
                                ("s_tmp", [128, 1024], F32), ("s_yn", [128, 1024], BF16),
                                ("s_ss", [128, 2], F32), ("s_junk", [128, 512], BF16)]:
                pool2(nm, shp, dt)
            def phase(c, which):
                i = c % 2
                g_ = lambda nm: (P2[nm][i], "%s%d" % (nm, i))
                xs, xsn = g_("s_xs"); zs, zsn = g_("s_zs"); btm, btmn = g_("s_btm")
                bt, btn = g_("s_bt"); ct, ctn = g_("s_ct"); df, dfn = g_("s_df")
                sm, smn = g_("s_sm"); edec, edn = g_("s_edec"); cbm, cbn = g_("s_cbm")
                E_, En = g_("s_E"); MT, MTn = g_("s_MT"); xdt, xdtn = g_("s_xdt")
                xdd, xddn = g_("s_xdd"); yo, yon = g_("s_yo"); y, yn_ = g_("s_y")
                tmp, tmpn = g_("s_tmp"); yn, ynn = g_("s_yn"); ss, ssn = g_("s_ss")
                junk, jn = g_("s_junk")
                R, Rn = g_("s_R")

                x16, ax, e16, l16, r16, dt, a, dtd = [sm[:, j, :] for j in range(8)]
                xs3 = xs[:].rearrange("p (h d) -> p h d", h=16)
                if which == 'A':
                    ts = slice(c * 128, (c + 1) * 128)
                    k.dma(xs[:], self.xs_tm[ts, :], [], [xsn])
                    k.dma(zs[:], self.zs_tm[ts, :], [], [zsn])
                    k.dma(btm[:], self.B_tm[ts, :], [], [btmn])
                    k.dma(bt[:], self.BT[:, :, ts].rearrange("g n t -> n g t"), [], [btn])
                    k.dma(ct[:], self.CT[:, :, ts].rearrange("g n t -> n g t"), [], [ctn])
                    k.dma(df[:], self.dtff[ts, :], [], [dfn])
                    k.op('dve', [dfn, 's_dtb'], [smn], lambda e: e.tensor_tensor(out=x16, in0=df[:, 0:16], in1=dtb[:], op=ALU.add))
                    k.op('act', [smn], [smn], lambda e: e.activation(out=ax, in_=x16, func=AF.Abs))
                    k.op('act', [smn], [smn], lambda e: e.activation(out=e16, in_=ax, func=AF.Exp, scale=-1.0))
                    k.op('act', [smn], [smn], lambda e: e.activation(out=l16, in_=e16, func=AF.Ln, bias=1.0))
                    k.op('dve', [smn], [smn], lambda e: e.tensor_scalar_max(out=r16, in0=x16, scalar1=0.0))
                    k.op('dve', [smn], [smn], lambda e: e.tensor_tensor(out=dt, in0=r16, in1=l16, op=ALU.add))
                    k.op('dve', [smn, 's_aneg'], [smn], lambda e: e.tensor_tensor(out=a, in0=dt, in1=aneg[:], op=ALU.mult))
                    for j, (m, mn) in enumerate([(self.triI, 'triI'), (self.triSU, 'triSU'), (self.ones_f, 'ones_f')]):
                        k.op('pe', [mn, smn], ['pb0'], lambda e: e.matmul(
                            PB[0][:, j * 16:(j + 1) * 16], lhsT=m[:], rhs=a, start=True, stop=True))
                    k.op('act', ['pb0'], [edn], lambda e: e.activation(out=edec[:], in_=PB[0][:, 0:48], func=AF.Exp))
                    for g in range(2):
                        k.op('pe', [btn, ctn], ['pb0'], lambda e: e.matmul(
                            PB[0][:, 256 + g * 128: 256 + (g + 1) * 128], lhsT=bt[:, g, :], rhs=ct[:, g, :],
                            start=True, stop=True))
                    k.op('dve', ['pb0', 'triI'], [cbn], lambda e: e.tensor_tensor(
                        out=cbm[:], in0=PB[0][:, 256:512].rearrange("p (g l) -> p g l", g=2),
                        in1=self.triI[:].unsqueeze(1).to_broadcast([128, 2, 128]), op=ALU.mult))
                    for g in range(2):
                        k.op('pool', ['triI', smn], [Rn], lambda e: e.tensor_tensor(
                            out=R[:], in0=self.triI[:].unsqueeze(1).to_broadcast([128, 8, 128]),
                            in1=a[:, g * 8:(g + 1) * 8].unsqueeze(2).to_broadcast([128, 8, 128]), op=ALU.mult))
                        for hh in range(2):
                            k.op('pe', ['triSU', Rn], ['pb%d' % (1 + hh)], lambda e: e.matmul(
                                PB[1 + hh][:, :], lhsT=self.triSU[:],
                                rhs=R[:, hh * 4:(hh + 1) * 4, :].rearrange("p h l -> p (h l)"), start=True, stop=True))
                            k.op('act', ['pb%d' % (1 + hh)], [En], lambda e: e.activation(
                                out=E_[:, hh * 512:(hh + 1) * 512], in_=PB[1 + hh][:, :], func=AF.Exp))
                        k.op('dve', [En, cbn], [MTn], lambda e: e.tensor_tensor(
                            out=MT[:, g * 8:(g + 1) * 8, :], in0=E_[:].rearrange("p (h l) -> p h l", h=8),
                            in1=cbm[:, g, :].unsqueeze(1).to_broadcast([128, 8, 128]), op=ALU.mult))
                    k.op('dve', [xsn, smn], [xdtn], lambda e: e.tensor_tensor(
                        out=xdt[:].rearrange("p (h d) -> p h d", h=16), in0=xs3,
                        in1=dt.unsqueeze(2).to_broadcast([128, 16, 64]), op=ALU.mult))
                    k.op('dve', [smn, edn], [smn], lambda e: e.tensor_tensor(out=dtd, in0=dt, in1=edec[:, 16:32], op=ALU.mult))
                    k.op('pool', [xsn, smn], [xddn], lambda e: e.tensor_tensor(
                        out=xdd[:].rearrange("p (h d) -> p h d", h=16), in0=xs3,
                        in1=dtd.unsqueeze(2).to_broadcast([128, 16, 64]), op=ALU.mult))

                    return
                for h in range(16):
                    b = 3 + h // 8
                    k.op('pe', [MTn, xdtn], ['pb%d' % b], lambda e: e.matmul(
                        PB[b][:, (h % 8) * 64:(h % 8 + 1) * 64], lhsT=MT[:, h, :], rhs=xdt[:, h * 64:(h + 1) * 64],
                        start=True, stop=True))
                for g in range(2):
                    k.op('pe', [ctn, 's_Sbf'], ['pb%d' % (5 + g)], lambda e: e.matmul(
                        PB[5 + g][:, :], lhsT=ct[:, g, :], rhs=Sbf[:, g * 512:(g + 1) * 512], start=True, stop=True))
                for g in range(2):
                    k.op('dve', ['pb%d' % (5 + g), edn], [yon], lambda e: e.tensor_tensor(
                        out=yo[:, g * 512:(g + 1) * 512].rearrange("p (h d) -> p h d", h=8),
                        in0=PB[5 + g][:, :].rearrange("p (h d) -> p h d", h=8),
                        in1=edec[:, g * 8:(g + 1) * 8].unsqueeze(2).to_broadcast([128, 8, 64]), op=ALU.mult))
                    k.op('dve', [yon, 'pb%d' % (3 + g)], [yn_], lambda e: e.tensor_tensor(
                        out=y[:, g * 512:(g + 1) * 512], in0=yo[:, g * 512:(g + 1) * 512], in1=PB[3 + g][:, :], op=ALU.add))
                k.op('pool', [xsn, 's_dsk'], [tmpn], lambda e: e.tensor_tensor(
                    out=tmp[:].rearrange("p (h d) -> p h d", h=16), in0=xs3,
                    in1=dsk[:].unsqueeze(2).to_broadcast([128, 16, 64]), op=ALU.mult))
                k.op('pool', [yn_, tmpn], [yn_], lambda e: e.tensor_tensor(out=y[:], in0=y[:], in1=tmp[:], op=ALU.add))
                for g in range(2):
                    k.op('pe', [btmn, xddn], ['pb%d' % (5 + g)], lambda e: e.matmul(
                        PB[5 + g][:, :], lhsT=btm[:, g * 128:(g + 1) * 128], rhs=xdd[:, g * 512:(g + 1) * 512],
                        start=True, stop=True))
                k.op('dve', ['s_S', edn], ['s_S'], lambda e: e.tensor_tensor(
                    out=S[:].rearrange("p (h d) -> p h d", h=16), in0=S[:].rearrange("p (h d) -> p h d", h=16),
                    in1=edec[:, 32:48].unsqueeze(2).to_broadcast([128, 16, 64]), op=ALU.mult))
                for g in range(2):
                    k.op('dve', ['s_S', 'pb%d' % (5 + g)], ['s_S'], lambda e: e.tensor_tensor(
                        out=S[:, g * 512:(g + 1) * 512], in0=S[:, g * 512:(g + 1) * 512], in1=PB[5 + g][:, :], op=ALU.add))
                k.op('act', ['s_S'], ['s_Sbf'], lambda e: e.activation(out=Sbf[:], in_=S[:], func=AF.Copy))
                k.op('dve', [yn_, zsn], [yn_], lambda e: e.tensor_tensor(out=y[:], in0=y[:], in1=zs[:], op=ALU.mult))
                for g in range(2):
                    k.op('act', [yn_], [jn, ssn], lambda e: e.activation(
                        out=junk[:], in_=y[:, g * 512:(g + 1) * 512], func=AF.Square, accum_out=ss[:, g:g + 1]))
                k.op('dve', [ssn], [ssn], lambda e: e.tensor_scalar(
                    out=ss[:], in0=ss[:], scalar1=1.0 / 512, scalar2=EPS, op0=ALU.mult, op1=ALU.add))
                k.op('act', [ssn], [ssn], lambda e: e.sqrt(out=ss[:], in_=ss[:]))
                k.op('dve', [ssn], [ssn], lambda e: e.reciprocal(out=ss[:], in_=ss[:]))
                for g in range(2):
                    k.op('dve', [yn_, ssn, 's_nw'], [ynn], lambda e: e.scalar_tensor_tensor(
                        out=yn[:, g * 512:(g + 1) * 512], in0=y[:, g * 512:(g + 1) * 512], scalar=ss[:, g:g + 1],
                        in1=nw[:, g * 512:(g + 1) * 512], op0=ALU.mult, op1=ALU.mult))
                tog = [0]

                def dfn2(b0, nb, pt, ptn, c=c):
                    self.cast(yTs[:, b0:b0 + nb, c * 128:(c + 1) * 128], pt.rearrange("p (b t) -> p b t", t=128),
                              [ptn], ['s_yTs'], eng=('act' if (b0 // 4) % 2 == 0 else 'dve'))
                self.transpose_to(yn, ynn, 8, dfn2, tog)
            phase(0, 'A')
            for c in range(NT):
                if c + 1 < NT:
                    phase(c + 1, 'A')
                phase(c, 'B')
            k.dma(self.ysT.rearrange("(b p) t -> p b t", p=128), yTs[:], ['s_yTs'], [])
            k.barrier()

    def stage_hgrn(self, l):
        k = self.k
        PB, PT = self.PB, self.PT
        with ExitStack() as st:
            sb = lambda n, s, dt=F32: self.sbt(st, n, s, dt)
            cmask = sb("h_cmask", [128, T]); mask64 = sb("h_m64", [64, 64])
            lb = sb("h_lb", [128, 8]); oml = sb("h_oml", [128, 8]); nw = sb("h_nw", [128, 8])
            lb0 = sb("h_lb0", [128, 8])
            k.op('pool', [], ['h_cmask'], lambda e: e.memset(cmask[:], 1.0))
            k.op('pool', ['h_cmask'], ['h_cmask'], lambda e: e.memset(
                cmask[:].rearrange("p (c t) -> p c t", t=64)[:, :, 0:1], 0.0))
            k.op('pool', ['ones_f'], ['h_m64'], lambda e: e.affine_select(
                out=mask64[:], in_=self.ones_f[0:64, 0:64], pattern=[[1, 64]], compare_op=ALU.is_ge,
                fill=0.0, base=0, channel_multiplier=-1))
            k.dma(nw[:], self.hgrn_norm_w[l].rearrange("(h p) -> p h", p=128), [], ['h_nw'],
                  allow_slow_non_contiguous=True)
            if l == 0:
                k.op('pool', [], ['h_lb'], lambda e: e.memset(lb[:], 0.0))
            else:
                k.dma(lb0[:], self.hgrn_lb[0].rearrange("(h p) -> p h", p=128), [], ['h_lb0'],
                      allow_slow_non_contiguous=True)
                k.dma(lb[:], self.hgrn_lb[1].rearrange("(h p) -> p h", p=128), [], ['h_lb'],
                      allow_slow_non_contiguous=True)
                k.op('dve', ['h_lb', 'h_lb0'], ['h_lb'], lambda e: e.tensor_tensor(out=lb[:], in0=lb[:], in1=lb0[:], op=ALU.subtract))
                k.op('act', ['h_lb'], ['h_lb'], lambda e: e.activation(out=lb[:], in_=lb[:], func=AF.Sigmoid))
            k.op('dve', ['h_lb'], ['h_oml'], lambda e: e.tensor_scalar(
                out=oml[:], in0=lb[:], scalar1=-1.0, scalar2=1.0, op0=ALU.mult, op1=ALU.add))
            epsb = sb("h_eps", [128, 1])
            k.op('pool', [], ['h_eps'], lambda e: e.memset(epsb[:], EPS))
            osq = sb("h_osq", [128, 512], BF16); rt = sb("h_rt", [128, 512]); t1 = sb("h_t1", [128, 512])
            HS = []
            for p in range(2):
                d = {}
                for nm in ("sig", "f", "b", "eb", "ktf"):
                    d[nm] = (sb("h%d_%s" % (p, nm), [128, T]), "h%d_%s" % (p, nm))
                for nm in ("q", "g", "qt", "kt", "kh", "yh"):
                    d[nm] = (sb("h%d_%s" % (p, nm), [128, T], BF16), "h%d_%s" % (p, nm))
                d["v"] = (sb("h%d_v" % p, [64, 32, 128], BF16), "h%d_v" % p)
                d["S"] = (sb("h%d_S" % p, [128, 128]), "h%d_S" % p)
                d["Sbf"] = [(sb("h%d_Sbf%d" % (p, i), [128, 128], BF16), "h%d_Sbf%d" % (p, i)) for i in range(2)]
                d["smT"] = (sb("h%d_smT" % p, [64, 8, 64], BF16), "h%d_smT" % p)
                d["khT"] = (sb("h%d_khT" % p, [64, 8, 128], BF16), "h%d_khT" % p)
                d["ps_s"] = (PB[p], 'pb%d' % p)
                d["ps_o"] = (PB[2 + p], 'pb%d' % (2 + p))
                d["ps_k"] = (PB[4 + p], 'pb%d' % (4 + p))
                HS.append(d)

            def front(d, h):
                hs = slice(h * 128, (h + 1) * 128)
                sig, sgn = d["sig"]; fB, fn_ = d["f"]; bB, bn = d["b"]; eb, ebn = d["eb"]; ktf, ktfn = d["ktf"]
                q, qn = d["q"]; gg, gn = d["g"]; qt, qtn = d["qt"]; kt, ktn = d["kt"]; kh, khn = d["kh"]
                v, vn = d["v"]; S, Sn = d["S"]
                k.dma(sig[:], self.hsigT[hs, :], [], [sgn])
                k.dma(q[:], self.hqT[hs, :], [], [qn])
                k.dma(gg[:], self.hgT[hs, :], [], [gn])
                k.dma(v[:], self.hv_tm[:, hs].rearrange("(c p) v -> p c v", p=64), [], [vn])
                k.op('dve', [sgn, 'h_oml', 'h_lb'], [fn_], lambda e: e.tensor_scalar(
                    out=fB[:], in0=sig[:], scalar1=oml[:, h:h + 1], scalar2=lb[:, h:h + 1], op0=ALU.mult, op1=ALU.add))
                k.op('act', [fn_], [sgn], lambda e: e.activation(out=sig[:], in_=fB[:], func=AF.Ln))
                k.op('dve', ['h_cmask', sgn], [bn], lambda e: e.tensor_tensor_scan(
                    out=bB[:], data0=cmask[:], data1=sig[:], initial=0.0, op0=ALU.mult, op1=ALU.add))
                k.op('pool', [fn_], [fn_], lambda e: e.tensor_scalar(
                    out=fB[:], in0=fB[:], scalar1=-1.0, scalar2=1.0, op0=ALU.mult, op1=ALU.add))
                k.op('act', [bn], [ebn], lambda e: e.activation(out=eb[:], in_=bB[:], func=AF.Exp))
                k.op('dve', [qn, ebn], [qtn], lambda e: e.tensor_tensor(out=qt[:], in0=q[:], in1=eb[:], op=ALU.mult))
                k.op('pool', [bn], [bn], lambda e: e.tensor_scalar(out=bB[:], in0=bB[:], scalar1=1.0e30, scalar2=-80.0, op0=ALU.min, op1=ALU.max))
                k.op('act', [bn], [bn], lambda e: e.activation(out=bB[:], in_=bB[:], func=AF.Exp, scale=-1.0))
                k.op('dve', [fn_, bn], [ktfn], lambda e: e.tensor_tensor(out=ktf[:], in0=fB[:], in1=bB[:], op=ALU.mult))
                k.op('act', [ktfn], [ktn], lambda e: e.activation(out=kt[:], in_=ktf[:], func=AF.Copy))
                k.op('pool', [ktfn, ebn], [khn], lambda e: e.tensor_tensor(
                    out=kh[:].rearrange("p (c t) -> p c t", t=64), in0=ktf[:].rearrange("p (c t) -> p c t", t=64),
                    in1=eb[:].rearrange("p (c t) -> p c t", t=64)[:, :, 63:64].to_broadcast([128, 32, 64]), op=ALU.mult))
                k.op('pool', [], [Sn], lambda e: e.memset(S[:], 0.0))
                k.op('pool', [], [d["Sbf"][0][1]], lambda e: e.memset(d["Sbf"][0][0][:], 0.0))

            def prep(d, cg):
                qt, qtn = d["qt"]; kt, ktn = d["kt"]; kh, khn = d["kh"]
                ps_s, psn = d["ps_s"]; smT, smn = d["smT"]; khT, khTn = d["khT"]
                for c in range(8):
                    tk = slice((cg * 8 + c) * 64, (cg * 8 + c + 1) * 64)
                    k.op('pe', [ktn, qtn], [psn], lambda e: e.matmul(
                        ps_s[0:64, c * 64:(c + 1) * 64], lhsT=kt[:, tk], rhs=qt[:, tk], start=True, stop=True))
                k.op('dve', [psn, 'h_m64'], [smn], lambda e: e.tensor_tensor(
                    out=smT[:], in0=ps_s[0:64, :].rearrange("p (c t) -> p c t", t=64),
                    in1=mask64[:].unsqueeze(1).to_broadcast([64, 8, 64]), op=ALU.mult))
                for c in range(8):
                    tk = slice((cg * 8 + c) * 64, (cg * 8 + c + 1) * 64)
                    k.op('pe', [khn, 'ident_b'], ['pb7'], lambda e: e.transpose(
                        out=PT[0:64, c * 128:(c + 1) * 128], in_=kh[:, tk], identity=self.ident_b[:]))
                k.op('act', ['pb7'], [khTn], lambda e: e.activation(
                    out=khT[:], in_=PT[0:64, :].rearrange("p (c k) -> p c k", k=128), func=AF.Copy))

            def kv4(d, cg, c0):
                khT, khTn = d["khT"]; v, vn = d["v"]; pk, pkn = d["ps_k"]
                for c in range(c0, c0 + 4):
                    cc = cg * 8 + c
                    k.op('pe', [khTn, vn], [pkn], lambda e: e.matmul(
                        pk[:, (c % 4) * 128:(c % 4 + 1) * 128], lhsT=khT[:, c, :], rhs=v[:, cc, :], start=True, stop=True))

            def chunk(d, cg, c):
                cc = cg * 8 + c
                tk = slice(cc * 64, (cc + 1) * 64)
                v, vn = d["v"]; smT, smn = d["smT"]; qt, qtn = d["qt"]; eb, ebn = d["eb"]
                ps_o, pon = d["ps_o"]; pk, pkn = d["ps_k"]; S, Sn = d["S"]
                sb0, sb0n = d["Sbf"][cc % 2]; sb1, sb1n = d["Sbf"][(cc + 1) % 2]
                pks = pk[:, (c % 4) * 128:(c % 4 + 1) * 128]
                k.op('pe', [vn, smn], [pon], lambda e: e.matmul(
                    ps_o[:, c * 64:(c + 1) * 64], lhsT=v[:, cc, :], rhs=smT[:, c, :], start=True, stop=False))
                k.op('pe', [sb0n, qtn], [pon], lambda e: e.matmul(
                    ps_o[:, c * 64:(c + 1) * 64], lhsT=sb0[:], rhs=qt[:, tk], start=False, stop=True))
                esc = eb[:, cc * 64 + 63: cc * 64 + 64]
                k.op('dve', [Sn, ebn, pkn], [sb1n], lambda e: e.scalar_tensor_tensor(
                    out=sb1[:], in0=S[:], scalar=esc, in1=pks, op0=ALU.mult, op1=ALU.add))
                k.op('dve', [Sn, ebn, pkn], [Sn], lambda e: e.scalar_tensor_tensor(
                    out=S[:], in0=S[:], scalar=esc, in1=pks, op0=ALU.mult, op1=ALU.add))

            def norm(d, h, cg):
                ps_o, pon = d["ps_o"]; gg, gn = d["g"]; yh, yhn = d["yh"]
                k.op('act', [pon], ['h_osq'], lambda e: e.activation(out=osq[:], in_=ps_o[:, :], func=AF.Square))
                k.op('pe', ['ones_b', 'h_osq'], ['pb6'], lambda e: e.matmul(
                    PB[6][:, :], lhsT=self.ones_b[:], rhs=osq[:], start=True, stop=True))
                k.op('act', ['pb6', 'h_eps'], ['h_rt'], lambda e: e.activation(
                    out=rt[:], in_=PB[6][:, :], func=AF.Ln, scale=1.0 / 128, bias=epsb[:]))
                k.op('act', ['h_rt'], ['h_rt'], lambda e: e.activation(out=rt[:], in_=rt[:], func=AF.Exp, scale=-0.5))
                k.op('dve', [pon, 'h_rt'], ['h_t1'], lambda e: e.tensor_tensor(out=t1[:], in0=ps_o[:, :], in1=rt[:], op=ALU.mult))
                k.op('dve', ['h_t1', 'h_nw', gn], [yhn], lambda e: e.scalar_tensor_tensor(
                    out=yh[:, cg * 512:(cg + 1) * 512], in0=t1[:], scalar=nw[:, h:h + 1],
                    in1=gg[:, cg * 512:(cg + 1) * 512], op0=ALU.mult, op1=ALU.mult))

            for hp in range(4):
                hh = [2 * hp, 2 * hp + 1]
                for p in range(2):
                    front(HS[p], hh[p])
                for cg in range(4):
                    for p in range(2):
                        prep(HS[p], cg)
                    for c0 in (0, 4):
                        for p in range(2):
                            kv4(HS[p], cg, c0)
                        for c in range(c0, c0 + 4):
                            for p in range(2):
                                chunk(HS[p], cg, c)
                    for p in range(2):
                        norm(HS[p], hh[p], cg)
                for p in range(2):
                    k.dma(self.yhT[hh[p] * 128:(hh[p] + 1) * 128, :], HS[p]["yh"][0][:], [HS[p]["yh"][1]], [])
            k.barrier()

    def stage_fox(self, l):
        k = self.k
        PB = self.PB
        with ExitStack() as st:
            sb = lambda n, s, dt=F32: self.sbt(st, n, s, dt)
            fb = sb("x_fb", [128, 16]); ffr = sb("x_ffr", [128, NT, 32])
            xx = sb("x_xx", [128, NT, 16]); ax = sb("x_ax", [128, NT, 16]); lf = sb("x_lf", [128, NT, 16])
            c8 = sb("x_c8", [16, T]); c32 = sb("x_c32", [16, T]); ones16 = sb("x_ones16", [16, T], BF16)
            sp_ = [sb("x_sp%d" % i, [16, T], BF16) for i in range(3)]
            sn_ = [sb("x_sn%d" % i, [16, T], BF16) for i in range(3)]
            k.dma(fb[:], self.fox_f_bias[l].partition_broadcast(128), [], ['x_fb'])
            k.dma(ffr[:], self.dtff.rearrange("(tt p) c -> p tt c", p=128), [], ['x_ffr'])
            k.op('pool', [], ['x_ones16'], lambda e: e.memset(ones16[:], 1.0))
            k.op('dve', ['x_ffr', 'x_fb'], ['x_xx'], lambda e: e.tensor_tensor(
                out=xx[:], in0=ffr[:, :, 16:32], in1=fb[:].unsqueeze(1).to_broadcast([128, NT, 16]), op=ALU.add))
            k.op('act', ['x_xx'], ['x_ax'], lambda e: e.activation(out=ax[:], in_=xx[:], func=AF.Abs))
            k.op('act', ['x_ax'], ['x_ax'], lambda e: e.activation(out=ax[:], in_=ax[:], func=AF.Exp, scale=-1.0))
            k.op('act', ['x_ax'], ['x_ax'], lambda e: e.activation(out=ax[:], in_=ax[:], func=AF.Ln, bias=1.0))
            k.op('dve', ['x_xx'], ['x_xx'], lambda e: e.tensor_scalar_min(out=xx[:], in0=xx[:], scalar1=0.0))
            k.op('dve', ['x_xx', 'x_ax'], ['x_lf'], lambda e: e.tensor_tensor(out=lf[:], in0=xx[:], in1=ax[:], op=ALU.subtract))
            for tb in range(4):
                for ti in range(4):
                    i = tb * 4 + ti
                    o = PB[0][0:16, ti * 128:(ti + 1) * 128]
                    for j in range(i):
                        k.op('pe', ['x_lf', 'ones_f'], ['pb0'], lambda e: e.matmul(
                            o, lhsT=lf[:, j, :], rhs=self.ones_f[:], start=(j == 0), stop=False))
                    k.op('pe', ['x_lf', 'triI'], ['pb0'], lambda e: e.matmul(
                        o, lhsT=lf[:, i, :], rhs=self.triI[:], start=(i == 0), stop=True))
                k.op('dve', ['pb0'], ['x_c8'], lambda e: e.tensor_scalar(
                    out=c8[:, tb * 512:(tb + 1) * 512], in0=PB[0][0:16, :], scalar1=8.0, scalar2=None, op0=ALU.mult))
            for j in range(3):
                k.op('dve', ['x_c8'], ['x_sp%d' % j], lambda e: e.tensor_copy(out=sp_[j][:], in_=c8[:]))
                k.op('dve', ['x_sp%d' % j], ['x_sn%d' % j], lambda e: e.tensor_scalar(
                    out=sn_[j][:], in0=sp_[j][:], scalar1=-1.0, scalar2=None, op0=ALU.mult))
                if j < 2:
                    k.op('dve', ['x_sp%d' % j], ['x_c32'], lambda e: e.tensor_copy(out=c32[:], in_=sp_[j][:]))
                    k.op('dve', ['x_c8', 'x_c32'], ['x_c8'], lambda e: e.tensor_tensor(out=c8[:], in0=c8[:], in1=c32[:], op=ALU.subtract))
            AUG = []
            for j in range(3):
                for (dst, row, src, sn) in [(self.fqA, 64 + j, sp_[j], 'x_sp%d' % j), (self.fqA, 67 + j, ones16, 'x_ones16'),
                                            (self.fkA, 64 + j, ones16, 'x_ones16'), (self.fkA, 67 + j, sn_[j], 'x_sn%d' % j)]:
                    nm = 'aug%d' % len(AUG)
                    k.dma(dst[:, row, :], src[:], [sn], [nm])
                    AUG.append(nm)
            madd = sb("x_madd", [128, 4, 512])
            for j in range(4):
                k.op('pool', ['zeros_f'], ['x_madd'], lambda e: e.affine_select(
                    out=madd[:, j, :], in_=self.zeros_f[:], pattern=[[1, 512]], compare_op=ALU.is_ge,
                    fill=-240000.0, base=-j * 128, channel_multiplier=-1))
            sel = sb("x_sel", [65, 64])
            k.op('pool', [], ['x_sel'], lambda e: e.memset(sel[:], 0.0))
            k.op('pool', ['x_sel'], ['x_sel'], lambda e: e.memset(sel[64:65, :], 1.0))
            QA = [sb("x_QA%d" % i, [128, T], BF16) for i in range(2)]
            KA = [sb("x_KA%d" % i, [128, T], BF16) for i in range(2)]
            V = [sb("x_V%d" % i, [128, NT, 128], BF16) for i in range(2)]
            for i in range(2):
                k.op('pool', [], ['x_V%d' % i], lambda e: e.memset(V[i][:], 0.0))
                k.op('pool', ['x_V%d' % i], ['x_V%d' % i], lambda e: e.memset(V[i][:, :, 64:65], 1.0))
                k.op('pool', [], ['x_QA%d' % i], lambda e: e.memset(QA[i][:], 0.0))
                k.op('pool', [], ['x_KA%d' % i], lambda e: e.memset(KA[i][:], 0.0))
            Pt = [sb("x_P%d" % i, [128, 512], BF16) for i in range(6)]
            smk = [sb("x_smk%d" % i, [128, 512]) for i in range(3)]
            osb = [sb("x_osb%d" % i, [65, 512]) for i in range(2)]
            rl = sb("x_rl", [64, 512])
            yf = [sb("x_yf%d" % i, [64, T], BF16) for i in range(2)]

            def load_head(h):
                i = h % 2
                k.dma(QA[i][0:70, :], self.fqA[h], AUG, ['x_QA%d' % i])
                k.dma(KA[i][0:70, :], self.fkA[h], AUG, ['x_KA%d' % i])
                k.dma(V[i][:, :, 0:64], self.fv_tm[:, h * 64:(h + 1) * 64].rearrange("(tt p) d -> p tt d", p=128),
                      [], ['x_V%d' % i])

            seq = [(h, qb, kb) for h in range(16) for qb in range(4) for kb in range(4 * (qb + 1))]
            SB = [0, 1, 2, 3, 7]
            LA = 4

            def emit_S(idx):
                h, qb, kb = seq[idx]
                i = h % 2
                b = SB[idx % 5]
                k.op('pe', ['x_KA%d' % i, 'x_QA%d' % i], ['pb%d' % b], lambda e: e.matmul(
                    PB[b][:, :], lhsT=KA[i][:, kb * 128:(kb + 1) * 128], rhs=QA[i][:, qb * 512:(qb + 1) * 512],
                    start=True, stop=True))

            pending = []

            def emit_norm(h, qb, po, pon, oi):
                i = h % 2
                ob, obn = osb[oi], 'x_osb%d' % oi
                k.op('act', [pon], [obn], lambda e: e.activation(out=ob[:], in_=po[0:65, :], func=AF.Copy))

                k.op('act', [obn], [obn], lambda e: e.activation(out=ob[64:65, :], in_=ob[64:65, :], func=AF.Ln))
                k.op('act', [obn], [obn], lambda e: e.activation(out=ob[64:65, :], in_=ob[64:65, :], func=AF.Exp, scale=-1.0))

                def rest():
                    k.op('pe', ['x_sel', obn], ['pb6'], lambda e: e.matmul(
                        PB[6][0:64, :], lhsT=sel[64:65, :], rhs=ob[64:65, :], start=True, stop=True))
                    k.op('dve', [obn, 'pb6'], ['x_yf%d' % i], lambda e: e.tensor_tensor(
                        out=yf[i][:, qb * 512:(qb + 1) * 512], in0=ob[0:64, :], in1=PB[6][0:64, :], op=ALU.mult))
                    if qb == 3:
                        k.dma(self.yfT[h], yf[i][:], ['x_yf%d' % i], [])
                return rest

            load_head(0)
            for j in range(LA):
                emit_S(j)
            oit = 0
            for idx, (h, qb, kb) in enumerate(seq):
                i = h % 2
                if qb == 0 and kb == 0 and h + 1 < 16:
                    load_head(h + 1)
                if idx + LA < len(seq):
                    emit_S(idx + LA)
                nkb = 4 * (qb + 1)
                b = SB[idx % 5]
                ps, psn = PB[b], 'pb%d' % b
                pt, ptn = Pt[idx % 6], 'x_P%d' % (idx % 6)
                j = kb - 4 * qb
                if j >= 0:
                    sm_, smn = smk[kb % 3], 'x_smk%d' % (kb % 3)
                    k.op('dve', [psn, 'x_madd'], [smn], lambda e: e.tensor_tensor(
                        out=sm_[:], in0=ps[:, :], in1=madd[:, j, :], op=ALU.add))
                    k.op('act', [smn], [ptn], lambda e: e.activation(out=pt[:], in_=sm_[:], func=AF.Exp, scale=0.125))
                else:
                    k.op('act', [psn], [ptn], lambda e: e.activation(out=pt[:], in_=ps[:, :], func=AF.Exp, scale=0.125))
                if kb == 0:
                    oit += 1
                po, pon = (PB[4], 'pb4') if oit % 2 == 0 else (PB[5], 'pb5')
                k.op('pe', ['x_V%d' % i, ptn], [pon], lambda e: e.matmul(
                    po[:, :], lhsT=V[i][:, kb, :], rhs=pt[:], start=(kb == 0), stop=(kb == nkb - 1)))
                if pending and pending[0][0] <= idx:
                    pending.pop(0)[1]()
                if kb == nkb - 1:
                    pending.append((idx + 2, emit_norm(h, qb, po, pon, oit % 2)))
            for _, f in pending:
                f()
            k.barrier()

    def stage_merge(self, l):
        k = self.k
        PB = self.PB
        with ExitStack() as st:
            sb = lambda n, s, dt=F32: self.sbt(st, n, s, dt)
            ys = sb("m_ys", [128, 8, T], BF16); yh = sb("m_yh", [128, 8, T], BF16); yf = sb("m_yf", [128, 8, T], BF16)
            k.dma(ys[:], self.ysT.rearrange("(b p) t -> p b t", p=128), [], ['m_ys'])
            k.dma(yh[:], self.yhT.rearrange("(b p) t -> p b t", p=128), [], ['m_yh'])
            k.dma(yf[:], self.yfT.rearrange("(b h2) d t -> (h2 d) b t", h2=2), [], ['m_yf'])
            wps = self.mk_wpool(st, "mws", 8, 128, dd=3)
            wph = self.mk_wpool(st, "mwh", 8, 128, dd=3)
            wpf = self.mk_wpool(st, "mwf", 8, 128, dd=3)
            gts = [[sb("m_g%d_%d" % (r, i), [128, T], BF16) for i in range(2)] for r in range(3)]
            m1 = [sb("m_m1_%d" % i, [128, 512]) for i in range(2)]
            m2 = [sb("m_m2_%d" % i, [128, 512]) for i in range(2)]
            mo = [sb("m_mo%d" % i, [128, T], BF16) for i in range(2)]
            state = {'it': 0}
            items = []
            ysrc = [(ys, 'm_ys'), (yh, 'm_yh'), (yf, 'm_yf')]

            def mk(db):
                def fn(ws):
                    gi = db % 2
                    for r in range(3):
                        k.dma(gts[r][gi][:], self.gT[r * D + db * 128: r * D + (db + 1) * 128, :], [], ['m_g%d_%d' % (r, gi)])
                    for tb in range(4):
                        i = state['it'] % 2
                        state['it'] += 1
                        bs = [3 * i, 3 * i + 1, 3 * i + 2]
                        tsl = slice(tb * 512, (tb + 1) * 512)
                        for r in range(3):
                            wb, wbn = ws[r]
                            yt, ytn = ysrc[r]
                            for kc in range(8):
                                k.op('pe', [wbn, ytn], ['pb%d' % bs[r]], lambda e: e.matmul(
                                    PB[bs[r]][:, :], lhsT=wb[:, kc, :], rhs=yt[:, kc, tsl], start=(kc == 0), stop=(kc == 7)))
                        a1, a1n = m1[i], 'm_m1_%d' % i
                        a2, a2n = m2[i], 'm_m2_%d' % i
                        k.op('dve', ['pb%d' % bs[0], 'm_g0_%d' % gi], [a1n], lambda e: e.tensor_tensor(
                            out=a1[:], in0=PB[bs[0]][:, :], in1=gts[0][gi][:, tsl], op=ALU.mult))
                        k.op('dve', ['pb%d' % bs[1], 'm_g1_%d' % gi], [a2n], lambda e: e.tensor_tensor(
                            out=a2[:], in0=PB[bs[1]][:, :], in1=gts[1][gi][:, tsl], op=ALU.mult))
                        k.op('dve', [a1n, a2n], [a1n], lambda e: e.tensor_tensor(out=a1[:], in0=a1[:], in1=a2[:], op=ALU.add))
                        k.op('dve', ['pb%d' % bs[2], 'm_g2_%d' % gi], [a2n], lambda e: e.tensor_tensor(
                            out=a2[:], in0=PB[bs[2]][:, :], in1=gts[2][gi][:, tsl], op=ALU.mult))
                        k.op('dve', [a1n, a2n], ['m_mo%d' % gi], lambda e: e.tensor_tensor(
                            out=mo[gi][:, tsl], in0=a1[:], in1=a2[:], op=ALU.add))
                    k.dma(self.mT[db * 128:(db + 1) * 128, :], mo[gi][:], ['m_mo%d' % gi], [])
                return fn
            for db in range(16):
                c0 = db * 128
                items.append(([(wps, [(0, self.w_branch_ssd[l, :, c0:c0 + 128])], 8, 128),
                               (wph, [(0, self.w_branch_hgrn[l, :, c0:c0 + 128])], 8, 128),
                               (wpf, [(0, self.w_branch_fox[l, :, c0:c0 + 128])], 8, 128)], mk(db)))
            self.run_pipeline(items)
            k.barrier()

    def stage_out_proj(self, AT_dram, nkc, Wsrc, xsrc, xdst, halves):
        k = self.k
        PB = self.PB
        TH = T // halves
        ntt = TH // 128
        with ExitStack() as st:
            sb = lambda n, s, dt=F32: self.sbt(st, n, s, dt)
            A = sb("o_A", [128, nkc, TH], BF16)
            KG = 4
            wp = self.mk_wpool(st, "ow", KG, 512, dd=4)
            xs = [sb("o_x%d" % i, [128, 512]) for i in range(16)]
            state = {'g': 0}
            items = []

            def mk(hf, cb, grp, kg, n_k, first, last, gidx):
                def fn(ws):
                    wb, wbn = ws[0]
                    if first:
                        k.dma(A[:], AT_dram[:, hf * TH:(hf + 1) * TH].rearrange("(c p) t -> p c t", p=128), [], ['o_A'])
                    if kg == 0:
                        for bi, tt in enumerate(grp):
                            j = (gidx % 2) * 8 + bi
                            r0 = hf * TH + tt * 128
                            k.dma(xs[j][:], xsrc[r0:r0 + 128, cb * 512:(cb + 1) * 512], [], ['o_x%d' % j])
                    for kk in range(n_k):
                        kc = kg + kk
                        for bi, tt in enumerate(grp):
                            k.op('pe', [wbn, 'o_A'], ['pb%d' % bi], lambda e: e.matmul(
                                PB[bi][:, :], lhsT=A[:, kc, tt * 128:(tt + 1) * 128], rhs=wb[:, kk, :],
                                start=(kc == 0), stop=(kc == nkc - 1)))
                    if last:
                        for bi, tt in enumerate(grp):
                            j = (gidx % 2) * 8 + bi
                            xt, xn = xs[j], 'o_x%d' % j
                            r0 = hf * TH + tt * 128
                            k.op('dve', [xn, 'pb%d' % bi], [xn], lambda e: e.tensor_tensor(
                                out=xt[:], in0=xt[:], in1=PB[bi][:, :], op=ALU.add))
                            k.dma(xdst[r0:r0 + 128, cb * 512:(cb + 1) * 512], xt[:], [xn], [])
                return fn
            for hf in range(halves):
                first = True
                for cb in range(D // 512):
                    for tt0 in range(0, ntt, 8):
                        grp = list(range(tt0, min(ntt, tt0 + 8)))
                        for kg in range(0, nkc, KG):
                            n_k = min(KG, nkc - kg)
                            items.append(([(wp, [(0, Wsrc[kg * 128:(kg + n_k) * 128, cb * 512:(cb + 1) * 512])], n_k, 512)],
                                          mk(hf, cb, grp, kg, n_k, first, kg + n_k == nkc, state['g'])))
                            first = False
                        state['g'] += 1
            self.run_pipeline(items)
            k.barrier()

    def stage_ffn_up(self, l, hT, hTn):
        k = self.k
        PB = self.PB
        W = self.ffn_w_up
        with ExitStack() as st:
            sb = lambda n, s, dt=F32: self.sbt(st, n, s, dt)
            wp = self.mk_wpool(st, "fw", KC, 256)
            cw = sb("f_cw", [128, 2 * NFB, 3]); cb = sb("f_cb", [128, 2 * NFB])
            for b in range(2 * NFB):
                k.dma(cw[:, b, :], self.ffn_conv_w[l, :, b * 128:(b + 1) * 128].rearrange("k p -> p k"),
                      [], ['f_cw%d' % b], allow_slow_non_contiguous=True)
            k.dma(cb[:], self.ffn_conv_b[l].rearrange("(b p) -> p b", p=128), [], ['f_cb'],
                  allow_slow_non_contiguous=True)
            xp = [sb("f_xp%d" % i, [128, T + 2]) for i in range(2)]
            acc = [sb("f_acc%d" % i, [128, T]) for i in range(2)]
            sg = sb("f_sg", [128, T])
            ao = [sb("f_ao%d" % i, [128, T], BF16) for i in range(2)]
            for i in range(2):
                k.op('pool', [], ['f_xp%d' % i], lambda e: e.memset(xp[i][:, 0:2], 0.0))
            state = {'bank': 0, 'ev': 0}
            items = []

            def mk(j):
                def fn(ws):
                    wb, wbn = ws[0]
                    for half in range(2):
                        blk = j + half * NFB
                        x_, xn = xp[half], 'f_xp%d' % half
                        a_, an = acc[half], 'f_acc%d' % half
                        for tb in range(4):
                            b = state['bank'] % 8
                            state['bank'] += 1
                            for kc in range(KC):
                                k.op('pe', [wbn, hTn], ['pb%d' % b], lambda e: e.matmul(
                                    PB[b][:, :], lhsT=wb[:, kc, half * 128:(half + 1) * 128],
                                    rhs=hT[:, kc, tb * 512:(tb + 1) * 512],
                                    start=(kc == 0), stop=(kc == KC - 1)))
                            eng = ('act', 'act', 'dve')[state['ev'] % 3]
                            state['ev'] += 1
                            self.cast(x_[:, 2 + tb * 512: 2 + (tb + 1) * 512], PB[b][:, :], ['pb%d' % b], [xn], eng=eng)
                        k.op('dve', [xn, 'f_cw%d' % blk, 'f_cb'], [an], lambda e: e.tensor_scalar(
                            out=a_[:], in0=x_[:, 0:T], scalar1=cw[:, blk, 0:1], scalar2=cb[:, blk:blk + 1],
                            op0=ALU.mult, op1=ALU.add))
                        for kk in range(1, 3):
                            k.op('dve', [xn, 'f_cw%d' % blk, an], [an], lambda e: e.scalar_tensor_tensor(
                                out=a_[:], in0=x_[:, kk:kk + T], scalar=cw[:, blk, kk:kk + 1], in1=a_[:],
                                op0=ALU.mult, op1=ALU.add))
                    k.op('act', ['f_acc0'], ['f_sg'], lambda e: e.activation(out=sg[:], in_=acc[0][:], func=AF.Silu))
                    o, on = ao[j % 2], 'f_ao%d' % (j % 2)
                    k.op('pool', ['f_sg', 'f_acc1'], [on], lambda e: e.tensor_tensor(out=o[:], in0=sg[:], in1=acc[1][:], op=ALU.mult))
                    k.dma(self.actT[j * 128:(j + 1) * 128, :], o[:], [on], [])
                return fn
            for j in range(NFB):
                items.append(([(wp, [(0, W[l, :, j * 128:(j + 1) * 128]),
                                     (128, W[l, :, (NFB + j) * 128:(NFB + j + 1) * 128])], KC, 256)], mk(j)))
            self.run_pipeline(items)
            k.barrier()


    def stage_final(self, src):
        k = self.k
        with ExitStack() as st:
            sb = lambda n, s, dt=F32: self.sbt(st, n, s, dt)
            wN = sb("z_w", [128, D])
            k.dma(wN[:], self.final_norm_w.partition_broadcast(128), [], ['z_w'])
            xts = [sb("z_x%d" % i, [128, D]) for i in range(2)]
            ots = [sb("z_o%d" % i, [128, D]) for i in range(2)]
            junk = sb("z_junk", [128, D], BF16)
            sss = [sb("z_ss%d" % i, [128, 1]) for i in range(2)]
            for tt in range(NT):
                i = tt % 2
                xt, ot, ss = xts[i], ots[i], sss[i]
                xn, on, sn = "z_x%d" % i, "z_o%d" % i, "z_ss%d" % i
                k.dma(xt[:], src[tt * 128:(tt + 1) * 128, :], [], [xn])
                k.op('act', [xn], ['z_junk', sn], lambda e: e.activation(
                    out=junk[:], in_=xt[:], func=AF.Square, accum_out=ss[:]))
                k.op('dve', [sn], [sn], lambda e: e.tensor_scalar(
                    out=ss[:], in0=ss[:], scalar1=1.0 / D, scalar2=EPS, op0=ALU.mult, op1=ALU.add))
                k.op('act', [sn], [sn], lambda e: e.sqrt(out=ss[:], in_=ss[:]))
                k.op('dve', [sn], [sn], lambda e: e.reciprocal(out=ss[:], in_=ss[:]))
                k.op('dve', [xn, sn, 'z_w'], [on], lambda e: e.scalar_tensor_tensor(
                    out=ot[:], in0=xt[:], scalar=ss[:, 0:1], in1=wN[:], op0=ALU.mult, op1=ALU.mult))
                k.dma(self.out[tt * 128:(tt + 1) * 128, :], ot[:], [on], [])
            k.barrier()

    def finish(self):
        k = self.k
        nc = self.nc
        done = nc.alloc_semaphore("sem_done")
        for e in ('pe', 'act', 'dve', 'pool'):
            k.E[e].sem_inc(done, 1)
        sp = k.E['sp']
        sp.wait_ge(done, 4)
        for e in k.E:
            sp.sem_clear(k.sem[e])
        for s_ in k.dsem:
            sp.sem_clear(s_)
        sp.sem_clear(done)

    def build(self, nlayers=2, upto=None):
        k = self.k
        src = self.x
        stop = False
        for l in range(nlayers):
            with ExitStack() as st:
                if upto == 'init':
                    return
                hT = self.sbt(st, "hT", [128, KC, T], BF16)
                self.norm_T(src, self.norm_mix_w[l], hT, 'hT')
                if upto == 'norm':
                    return
                self.stage_proj(l, hT, 'hT')
            if upto == 'proj':
                return
            self.stage_ssd(l)
            if upto == 'ssd':
                return
            self.stage_hgrn(l)
            if upto == 'hgrn':
                return
            self.stage_fox(l)
            if upto == 'fox':
                return
            self.stage_merge(l)
            if upto == 'merge':
                return
            self.stage_out_proj(self.mT, 16, self.w_out[l], src, self.xa, halves=1)
            if upto == 'mix':
                return
            with ExitStack() as st:
                hT = self.sbt(st, "hT", [128, KC, T], BF16)
                self.norm_T(self.xa, self.norm_ffn_w[l], hT, 'hT')
                self.stage_ffn_up(l, hT, 'hT')
            if upto == 'ffn_up':
                return
            self.stage_out_proj(self.actT, NFB, self.ffn_w_down[l], self.xa, self.xb, halves=2)
            if upto == 'layer':
                return
            src = self.xb
        self.stage_final(self.xb)
        self.finish()


WNAMES = ["norm_mix_w", "w_in", "ssd_conv_w", "ssd_conv_b", "ssd_dt_bias", "ssd_a_log", "ssd_d",
          "ssd_norm_w", "hgrn_lb", "hgrn_norm_w", "fox_f_bias", "w_branch_ssd", "w_branch_hgrn",
          "w_branch_fox", "w_out", "norm_ffn_w", "ffn_w_up", "ffn_conv_w", "ffn_conv_b",
          "ffn_w_down", "final_norm_w"]


def kernel(**inputs):
    nc = bass.Bass("TRN2", target_bir_lowering=False)
    p = Prog(nc)
    p.build()
    x = np.ascontiguousarray(inputs["x"], dtype=np.float32)
    shared = {n: np.ascontiguousarray(inputs[n], dtype=np.float32) for n in WNAMES}
    in_maps = []
    for c in range(8):
        m = dict(shared)
        m["x"] = x[c]
        in_maps.append(m)
    res = run_bass_kernel_spmd(nc, in_maps, core_ids=list(range(8)))
    return np.stack([np.asarray(r["out"], dtype=np.float32) for r in res.results], axis=0)
```

```python
import numpy as np
from contextlib import ExitStack
import concourse.bass as bass
import concourse.mybir as mybir
from concourse.bass_utils import run_bass_kernel_spmd

F32 = mybir.dt.float32
BF16 = mybir.dt.bfloat16
AF = mybir.ActivationFunctionType
ALU = mybir.AluOpType

T = 2048
D = 2048
NT = 16
KC = 16
DIN = 15904
DFF = 5504
NFB = 43
EPS = 1e-6
NDS = 24

O_Z, O_XBC, O_DT, O_HQ, O_HF, O_HI, O_HG, O_FQ, O_FK, O_FV, O_FF, O_G = (
    0, 1024, 2560, 2576, 3600, 4624, 5648, 6672, 7696, 8720, 9744, 9760)


class KB:
    def __init__(self, nc):
        self.nc = nc
        self.E = {'pe': nc.tensor, 'act': nc.scalar, 'dve': nc.vector,
                  'pool': nc.gpsimd, 'sp': nc.sync}
        self.sem = {k: nc.alloc_semaphore("sem_" + k) for k in self.E}
        self.icnt = {k: 0 for k in self.E}
        self.scnt = {k: 0 for k in self.E}
        self.last = {k: None for k in self.E}
        self.incs = {k: [] for k in self.E}
        self.seen = {k: {k2: 0 for k2 in self.E} for k in self.E}
        self.dsem = [nc.alloc_semaphore("dsem%d" % i) for i in range(NDS)]
        self.dcnt = [0] * NDS
        self.dnext = 0
        self.dseen = {k: [0] * NDS for k in self.E}
        self.lastw = {}
        self.readers = {}

    def _deps(self, reads, writes):
        deps = []
        for r in reads:
            t = self.lastw.get(r)
            if t is not None:
                deps.append(t)
        for w in writes:
            t = self.lastw.get(w)
            if t is not None:
                deps.append(t)
            rd = self.readers.get(w)
            if rd:
                for e, i in rd['e'].items():
                    deps.append(('e', e, i))
                deps.extend(rd['d'])
        return deps

    def _semval_for(self, e2, idx):
        lst = self.incs[e2]
        if lst and lst[-1][0] >= idx:
            j = len(lst) - 1
            while j > 0 and lst[j - 1][0] >= idx:
                j -= 1
            return lst[j]
        ins, lidx = self.last[e2]
        assert lidx >= idx
        self.scnt[e2] += 1
        ins.then_inc(self.sem[e2], 1)
        lst.append((lidx, self.scnt[e2]))
        return lst[-1]

    def _wait(self, eng, deps, raw_same=()):
        need = {}
        dneed = {}
        for t in deps:
            if t[0] == 'e':
                _, e2, idx = t
                if e2 == eng:
                    continue
                if idx > need.get(e2, 0):
                    need[e2] = idx
            else:
                _, s, c = t
                if c > dneed.get(s, 0):
                    dneed[s] = c
        for idx in raw_same:
            if eng != 'pe' and idx > need.get(eng, 0):
                need[eng] = idx
        E = self.E[eng]
        for e2, idx in need.items():
            if idx <= self.seen[eng][e2]:
                continue
            iidx, v = self._semval_for(e2, idx)
            E.wait_ge(self.sem[e2], v)
            self.seen[eng][e2] = iidx
        for s, c in dneed.items():
            if c <= self.dseen[eng][s]:
                continue
            E.wait_ge(self.dsem[s], c)
            self.dseen[eng][s] = c

    def _record(self, tok, reads, writes):
        for r in reads:
            rd = self.readers.get(r)
            if rd is None:
                rd = {'e': {}, 'd': []}
                self.readers[r] = rd
            if tok[0] == 'e':
                rd['e'][tok[1]] = tok[2]
            else:
                rd['d'].append(tok)
        for w in writes:
            self.lastw[w] = tok
            self.readers[w] = {'e': {}, 'd': []}

    def op(self, eng, reads, writes, fn):
        deps = self._deps(reads, writes)
        raw_same = []
        for r in reads:
            t = self.lastw.get(r)
            if t is not None and t[0] == 'e' and t[1] == eng:
                raw_same.append(t[2])
        self._wait(eng, deps, raw_same)
        ins = fn(self.E[eng])
        self.icnt[eng] += 1
        self.last[eng] = (ins, self.icnt[eng])
        self._record(('e', eng, self.icnt[eng]), reads, writes)
        return ins

    def dma(self, out, in_, reads, writes, q='sp', **kw):
        deps = self._deps(reads, writes)
        s = self.dnext
        self.dnext = (self.dnext + 1) % NDS
        if self.dcnt[s] > 0:
            deps.append(('d', s, self.dcnt[s]))
        raw_same = []
        for r in reads:
            t = self.lastw.get(r)
            if t is not None and t[0] == 'e' and t[1] == q:
                raw_same.append(t[2])
        self._wait(q, deps, raw_same)
        ins = self.E[q].dma_start(out=out, in_=in_, **kw)
        self.dcnt[s] += 16
        ins.then_inc(self.dsem[s], 16)
        self._record(('d', s, self.dcnt[s]), reads, writes)

    def barrier(self):
        deps = []
        for e in self.E:
            if self.last[e] is not None:
                deps.append(('e', e, self.last[e][1]))
        for s in range(NDS):
            if self.dcnt[s] > 0:
                deps.append(('d', s, self.dcnt[s]))
        for e in self.E:
            self._wait(e, deps, [self.last[e][1]] if (self.last[e] is not None and e != 'sp') else [])
        self.lastw = {}
        self.readers = {}


class Prog:
    def __init__(self, nc, dbg=(), tiny=False):
        self.nc = nc
        self.k = KB(nc)
        self.dbg = set(dbg)
        self.cast_rr = 0
        self.uid = 0
        k = self.k
        BIG = ("w_in", "w_branch_ssd", "w_branch_hgrn", "w_branch_fox", "w_out", "ffn_w_up", "ffn_w_down")
        di = lambda n, s: nc.dram_tensor(n, ([2, 1, 1] if (tiny and n in BIG) else list(s)), F32, kind="ExternalInput").ap()
        self.x = di("x", [T, D])
        self.norm_mix_w = di("norm_mix_w", [2, D])
        self.w_in = di("w_in", [2, D, DIN])
        self.ssd_conv_w = di("ssd_conv_w", [2, 4, 1536])
        self.ssd_conv_b = di("ssd_conv_b", [2, 1536])
        self.ssd_dt_bias = di("ssd_dt_bias", [2, 16])
        self.ssd_a_log = di("ssd_a_log", [2, 16])
        self.ssd_d = di("ssd_d", [2, 16])
        self.ssd_norm_w = di("ssd_norm_w", [2, 1024])
        self.hgrn_lb = di("hgrn_lb", [2, 1024])
        self.hgrn_norm_w = di("hgrn_norm_w", [2, 1024])
        self.fox_f_bias = di("fox_f_bias", [2, 16])
        self.w_branch_ssd = di("w_branch_ssd", [2, 1024, D])
        self.w_branch_hgrn = di("w_branch_hgrn", [2, 1024, D])
        self.w_branch_fox = di("w_branch_fox", [2, 1024, D])
        self.w_out = di("w_out", [2, D, D])
        self.norm_ffn_w = di("norm_ffn_w", [2, D])
        self.ffn_w_up = di("ffn_w_up", [2, D, 2 * DFF])
        self.ffn_conv_w = di("ffn_conv_w", [2, 3, 2 * DFF])
        self.ffn_conv_b = di("ffn_conv_b", [2, 2 * DFF])
        self.ffn_w_down = di("ffn_w_down", [2, DFF, D])
        self.final_norm_w = di("final_norm_w", [D])
        self.out = nc.dram_tensor("out", [T, D], F32, kind="ExternalOutput").ap()
        self.xa = self.scr("xa", [T, D], F32)
        self.xb = self.scr("xb", [T, D], F32)
        self.xs_tm = self.scr("xs_tm", [T, 1024], BF16)
        self.B_tm = self.scr("B_tm", [T, 256], BF16)
        self.BT = self.scr("BT", [2, 128, T], BF16)
        self.CT = self.scr("CT", [2, 128, T], BF16)
        self.zs_tm = self.scr("zs_tm", [T, 1024], BF16)
        self.dtff = self.scr("dtff", [T, 32], F32)
        self.hqT = self.scr("hqT", [1024, T], BF16)
        self.hsigT = self.scr("hsigT", [1024, T], F32)
        self.hgT = self.scr("hgT", [1024, T], BF16)
        self.hv_tm = self.scr("hv_tm", [T, 1024], BF16)
        self.fqA = self.scr("fqA", [16, 70, T], BF16)
        self.fkA = self.scr("fkA", [16, 70, T], BF16)
        self.fv_tm = self.scr("fv_tm", [T, 1024], BF16)
        self.gT = self.scr("gT", [6144, T], BF16)
        self.ysT = self.scr("ysT", [1024, T], BF16)
        self.yhT = self.scr("yhT", [1024, T], BF16)
        self.yfT = self.scr("yfT", [16, 64, T], BF16)
        self.mT = self.scr("mT", [D, T], BF16)
        self.actT = self.scr("actT", [DFF, T], BF16)
        self.PB = [nc.alloc_psum_tensor("pb%d" % i, [128, 512], F32) for i in range(8)]
        self.PT = self.PB[7][:].bitcast(BF16)
        self.PT6 = self.PB[6][:].bitcast(BF16)
        self.cst = ExitStack()
        sb = lambda n, s, dt=F32: self.cst.enter_context(nc.sbuf_tensor(n, list(s), dt))
        self.ones_f = sb("ones_f", [128, 128])
        self.ones_b = sb("ones_b", [128, 128], BF16)
        self.ident_b = sb("ident_b", [128, 128], BF16)
        self.triI = sb("triI", [128, 128])
        self.triSU = sb("triSU", [128, 128])
        self.zeros_f = sb("zeros_f", [128, 512])
        tmp = sb("c_tmp", [128, 128])
        k.op('pool', [], ['ones_f'], lambda e: e.memset(self.ones_f[:], 1.0))
        k.op('pool', [], ['zeros_f'], lambda e: e.memset(self.zeros_f[:], 0.0))
        k.op('pool', ['ones_f'], ['c_tmp'], lambda e: e.affine_select(
            out=tmp[:], in_=self.ones_f[:], pattern=[[1, 128]], compare_op=ALU.is_equal,
            fill=0.0, base=0, channel_multiplier=-1))
        k.op('pool', ['ones_f'], ['triI'], lambda e: e.affine_select(
            out=self.triI[:], in_=self.ones_f[:], pattern=[[1, 128]], compare_op=ALU.is_ge,
            fill=0.0, base=0, channel_multiplier=-1))
        k.op('pool', ['ones_f'], ['triSU'], lambda e: e.affine_select(
            out=self.triSU[:], in_=self.ones_f[:], pattern=[[-1, 128]], compare_op=ALU.is_ge,
            fill=0.0, base=-1, channel_multiplier=1))
        k.op('dve', ['c_tmp'], ['ident_b'], lambda e: e.tensor_copy(out=self.ident_b[:], in_=tmp[:]))
        k.op('dve', ['ones_f'], ['ones_b'], lambda e: e.tensor_copy(out=self.ones_b[:], in_=self.ones_f[:]))
        k.barrier()
        self.C = ['ones_f', 'ones_b', 'ident_b', 'triI', 'triSU', 'zeros_f']

    def scr(self, name, shape, dt):
        kind = "ExternalOutput" if name in self.dbg else "Internal"
        return self.nc.dram_tensor(name, list(shape), dt, kind=kind).ap()

    def sbt(self, st, name, shape, dt=F32):
        self.uid += 1
        return st.enter_context(self.nc.sbuf_tensor("%s_%d" % (name, self.uid), list(shape), dt))

    def u(self):
        self.uid += 1
        return "u%d" % self.uid

    def cast(self, out_ap, in_ap, reads, writes, eng=None):
        if eng is None:
            eng = ('dve', 'act', 'pool', 'dve', 'act')[self.cast_rr % 5]
            self.cast_rr += 1
        if eng == 'act':
            self.k.op('act', reads, writes, lambda e: e.activation(out=out_ap, in_=in_ap, func=AF.Copy))
        else:
            self.k.op(eng, reads, writes, lambda e: e.tensor_copy(out=out_ap, in_=in_ap))

    def transpose_to(self, src_tile, src_name, nblk, dst_fn, pt_toggle, views=None):
        k = self.k
        if views is None:
            views = [(self.PT, 'pb7')]
        for b0 in range(0, nblk, 4):
            nb = min(4, nblk - b0)
            pv, ptn = views[pt_toggle[0] % len(views)]
            pt_toggle[0] += 1
            for b in range(nb):
                o = pv[:, b * 128:(b + 1) * 128]
                k.op('pe', [src_name, 'ident_b'], [ptn], lambda e: e.transpose(
                    out=o, in_=src_tile[:, (b0 + b) * 128:(b0 + b + 1) * 128],
                    identity=self.ident_b[:]))
            dst_fn(b0, nb, pv[:, 0:nb * 128], ptn)

    def norm_T(self, src, wv, hT, hTn):
        k = self.k
        with ExitStack() as st:
            wN = self.sbt(st, "nrm_w", [128, D])
            k.dma(wN[:], wv.partition_broadcast(128), [], ['nrm_w'])
            xts = [self.sbt(st, "nrm_x%d" % i, [128, D]) for i in range(2)]
            hbs = [self.sbt(st, "nrm_hb%d" % i, [128, D], BF16) for i in range(2)]
            junk = self.sbt(st, "nrm_junk", [128, D], BF16)
            sss = [self.sbt(st, "nrm_ss%d" % i, [128, 1]) for i in range(2)]
            tog = [0]
            def stats(tt):
                i = tt % 2
                xt, hb, ss = xts[i], hbs[i], sss[i]
                xn, hn, sn = "nrm_x%d" % i, "nrm_hb%d" % i, "nrm_ss%d" % i
                k.dma(xt[:], src[tt * 128:(tt + 1) * 128, :], [], [xn])
                k.op('act', [xn], ['nrm_junk', sn], lambda e: e.activation(
                    out=junk[:], in_=xt[:], func=AF.Square, accum_out=ss[:]))
                k.op('dve', [sn], [sn], lambda e: e.tensor_scalar(
                    out=ss[:], in0=ss[:], scalar1=1.0 / D, scalar2=EPS, op0=ALU.mult, op1=ALU.add))
                k.op('act', [sn], [sn], lambda e: e.sqrt(out=ss[:], in_=ss[:]))
                k.op('dve', [sn], [sn], lambda e: e.reciprocal(out=ss[:], in_=ss[:]))
                k.op('dve', [xn, sn, 'nrm_w'], [hn], lambda e: e.scalar_tensor_tensor(
                    out=hb[:], in0=xt[:], scalar=ss[:, 0:1], in1=wN[:], op0=ALU.mult, op1=ALU.mult))

            def xpose(tt):
                i = tt % 2
                hb, hn = hbs[i], "nrm_hb%d" % i

                def dst(b0, nb, pt, ptn, tt=tt):
                    self.cast(hT[:, b0:b0 + nb, tt * 128:(tt + 1) * 128],
                              pt.rearrange("p (c t) -> p c t", t=128), [ptn], [hTn],
                              eng=('act' if (b0 // 4) % 2 == 0 else 'dve'))
                self.transpose_to(hb, hn, 16, dst, tog, views=[(self.PT6, 'pb6'), (self.PT, 'pb7')])

            stats(0)
            for tt in range(NT):
                if tt + 1 < NT:
                    stats(tt + 1)
                xpose(tt)
            k.barrier()

    def mk_wpool(self, st, name, kc, n, dd=2):
        return {'i': 0, 'name': name, 'dd': dd,
                'wf': [self.sbt(st, name + "f%d" % i, [128, kc, n]) for i in range(dd)],
                'wb': [self.sbt(st, name + "b%d" % i, [128, kc, n], BF16) for i in range(2)]}

    def w_dma(self, pool, srcs, kc, n, pp=128):
        i = pool['i'] % pool['dd']
        pool['i'] += 1
        wf = pool['wf'][i]
        for si, (c0, src) in enumerate(srcs):
            nco = src.shape[1]
            self.k.dma(wf[0:pp, 0:kc, c0:c0 + nco], src.rearrange("(c p) n -> p c n", p=pp), [],
                       ["%sf%d_%d" % (pool['name'], i, si)])
        return i

    def w_cast(self, pool, i, nsrc, kc, n, pp=128):
        j = pool.get('ci', 0) % 2
        pool['ci'] = pool.get('ci', 0) + 1
        wf, wb = pool['wf'][i], pool['wb'][j]
        rd = ["%sf%d_%d" % (pool['name'], i, si) for si in range(nsrc)]
        wbn = "%sb%d" % (pool['name'], j)
        half = max(1, kc // 2)
        for c0 in range(0, kc, half):
            c1 = min(kc, c0 + half)
            eng = ('dve', 'act')[self.cast_rr % 2]
            self.cast_rr += 1
            self.cast(wb[0:pp, c0:c1, 0:n], wf[0:pp, c0:c1, 0:n], rd, [wbn], eng=eng)
        return wb, wbn

    def run_pipeline(self, items):
        n = len(items)
        dd = items[0][0][0][0]['dd']
        dm = lambda it: [self.w_dma(ld[0], ld[1], ld[2], ld[3]) for ld in it[0]]
        cs = lambda it, sl: [self.w_cast(ld[0], s_, len(ld[1]), ld[2], ld[3]) for ld, s_ in zip(it[0], sl)]
        slots = {}
        for j in range(min(dd, n)):
            slots[j] = dm(items[j])
        ready = cs(items[0], slots[0])
        for i in range(n):
            cur = ready
            if i + dd < n:
                slots[i + dd] = dm(items[i + dd])
            if i + 1 < n:
                ready = cs(items[i + 1], slots[i + 1])
            items[i][1](cur)

    def stage_proj(self, l, hT, hTn):
        k = self.k
        W = self.w_in
        PB = self.PB
        with ExitStack() as st:
            wp = self.mk_wpool(st, "pw", KC, 256)
            fa = [self.sbt(st, "fa%d" % i, [128, T]) for i in range(3)]
            xpad = [self.sbt(st, "xpad%d" % i, [128, T + 3]) for i in range(2)]
            stg32 = self.sbt(st, "stg32", [128, NT, 32])
            ba = [self.sbt(st, "ba%d" % i, [128, T], BF16) for i in range(3)]
            tms = [self.sbt(st, "tms%d" % i, [128, NT, 256], BF16) for i in range(2)]
            cw = self.sbt(st, "cw", [128, 12, 4])
            cb = self.sbt(st, "cb", [128, 12])
            for b in range(12):
                k.dma(cw[:, b, :], self.ssd_conv_w[l, :, b * 128:(b + 1) * 128].rearrange("k p -> p k"),
                      [], ['cw%d' % b], allow_slow_non_contiguous=True)
            k.dma(cb[:], self.ssd_conv_b[l].rearrange("(b p) -> p b", p=128), [], ['cb'],
                  allow_slow_non_contiguous=True)
            for i in range(2):
                k.op('pool', [], ["xpad%d" % i], lambda e: e.memset(xpad[i][:, 0:3], 0.0))
            cnt = {'fa': 0, 'ba': 0, 'tms': 0, 'bank': 0, 'xpad': 0, 'ev': 0}
            tog = [0]

            def nxt(key, n):
                i = cnt[key] % n
                cnt[key] += 1
                return i

            items = []

            def add_fm(col0, nblk, start, evac, final):
                for c0 in range(0, nblk * 128, 256):
                    n = min(256, nblk * 128 - c0)

                    def fn(ws, c0=c0, n=n):
                        wb, wbn = ws[0]
                        for sub in range(n // 128):
                            bj = (c0 + sub * 128) // 128
                            ctx = start(bj)
                            for tb in range(4):
                                b = nxt('bank', 7)
                                for kc in range(KC):
                                    k.op('pe', [wbn, hTn], ['pb%d' % b], lambda e: e.matmul(
                                        PB[b][:, :], lhsT=wb[:, kc, sub * 128:(sub + 1) * 128],
                                        rhs=hT[:, kc, tb * 512:(tb + 1) * 512],
                                        start=(kc == 0), stop=(kc == KC - 1)))
                                evac(ctx, tb, b)
                            final(bj, ctx)
                    items.append(([(wp, [(0, W[l, :, col0 + c0: col0 + c0 + n])], KC, n)], fn))

            def ev(o, b, dname, func):
                if func is None:
                    self.cast(o, PB[b][:, :], ['pb%d' % b], [dname], eng=('act', 'dve')[nxt('ev', 2)])
                else:
                    k.op('act', ['pb%d' % b], [dname], lambda e: e.activation(out=o, in_=PB[b][:, :], func=func))

            def to_tm(o, on, dst):
                i = nxt('tms', 2)
                stg, sn = tms[i], "tms%d" % i

                def dfn(b0, nb, pt, ptn):
                    self.cast(stg[:, b0:b0 + nb, 0:128], pt.rearrange("p (c t) -> p c t", t=128),
                              [ptn], [sn], eng=('act' if (b0 // 4) % 2 == 0 else 'dve'))
                self.transpose_to(o, on, 16, dfn, tog)
                k.dma(dst.rearrange("(tt p) c -> p tt c", p=128), stg[:, :, 0:128], [sn], [])

            def xbc_start(bj):
                i = nxt('xpad', 2)
                return (xpad[i], "xpad%d" % i)

            def xbc_evac(ctx, tb, b):
                ev(ctx[0][:, 3 + tb * 512: 3 + (tb + 1) * 512], b, ctx[1], None)

            def xbc_final(bj, ctx):
                xp, xpn = ctx
                i2 = nxt('fa', 3)
                acc, an = fa[i2], "fa%d" % i2
                k.op('dve', [xpn, 'cw%d' % bj, 'cb'], [an], lambda e: e.tensor_scalar(
                    out=acc[:, 0:T], in0=xp[:, 0:T], scalar1=cw[:, bj, 0:1], scalar2=cb[:, bj:bj + 1],
                    op0=ALU.mult, op1=ALU.add))
                for kk in range(1, 4):
                    k.op('dve', [xpn, 'cw%d' % bj, an], [an], lambda e: e.scalar_tensor_tensor(
                        out=acc[:, 0:T], in0=xp[:, kk:kk + T], scalar=cw[:, bj, kk:kk + 1],
                        in1=acc[:, 0:T], op0=ALU.mult, op1=ALU.add))
                i3 = nxt('ba', 3)
                o, on = ba[i3], "ba%d" % i3
                k.op('act', [an], [on], lambda e: e.activation(out=o[:], in_=acc[:, 0:T], func=AF.Silu))
                if bj < 8:
                    to_tm(o, on, self.xs_tm[:, bj * 128:(bj + 1) * 128])
                elif bj < 10:
                    g = bj - 8
                    k.dma(self.BT[g], o[:], [on], [])
                    to_tm(o, on, self.B_tm[:, g * 128:(g + 1) * 128])
                else:
                    k.dma(self.CT[bj - 10], o[:], [on], [])

            def mk_plain(pool, pname, npool, func, final):
                def start(bj):
                    i = nxt(pname, npool)
                    return (pool[i], "%s%d" % (pname, i))

                def evac(ctx, tb, b):
                    ev(ctx[0][:, tb * 512:(tb + 1) * 512], b, ctx[1], func)
                return start, evac, final

            def fin_rows(dst):
                return lambda bj, ctx: k.dma(dst[bj * 128:(bj + 1) * 128, :], ctx[0][:], [ctx[1]], [])

            def fin_qk(dst):
                def f(bj, ctx):
                    k.dma(dst[2 * bj, 0:64, :], ctx[0][0:64, :], [ctx[1]], [])
                    k.dma(dst[2 * bj + 1, 0:64, :], ctx[0][64:128, :], [ctx[1]], [])
                return f

            def add_tm(col0, ncols, dst, func, f32out=False, srcs=None):
                for c0 in range(0, ncols, 256):
                    n = min(256, ncols - c0)

                    def fn(ws, c0=c0, n=n):
                        wb, wbn = ws[0]
                        if f32out:
                            stg, sn = stg32, 'stg32'
                        else:
                            i = nxt('tms', 2)
                            stg, sn = tms[i], "tms%d" % i
                        for tt in range(NT):
                            b = nxt('bank', 7)
                            for kc in range(KC):
                                k.op('pe', [wbn, hTn], ['pb%d' % b], lambda e: e.matmul(
                                    PB[b][:, 0:n], lhsT=hT[:, kc, tt * 128:(tt + 1) * 128], rhs=wb[:, kc, 0:n],
                                    start=(kc == 0), stop=(kc == KC - 1)))
                            o = stg[:, tt, 0:n]
                            if func is None:
                                self.cast(o, PB[b][:, 0:n], ['pb%d' % b], [sn], eng=('act', 'dve')[nxt('ev', 2)])
                            else:
                                k.op('act', ['pb%d' % b], [sn], lambda e: e.activation(out=o, in_=PB[b][:, 0:n], func=func))
                        if f32out:
                            k.dma(dst.rearrange("(tt p) c -> p tt c", p=128), stg[:, :, 0:n], [sn], [])
                        else:
                            k.dma(dst[:, c0:c0 + n].rearrange("(tt p) c -> p tt c", p=128), stg[:, :, 0:n], [sn], [])
                    ss_ = srcs if srcs is not None else [(0, W[l, :, col0 + c0: col0 + c0 + n])]
                    items.append(([(wp, ss_, KC, n)], fn))

            add_tm(0, 32, self.dtff, None, f32out=True,
                   srcs=[(0, W[l, :, O_DT:O_DT + 16]), (16, W[l, :, O_FF:O_FF + 16])])
            add_tm(O_Z, 1024, self.zs_tm, AF.Silu)
            add_tm(O_HI, 1024, self.hv_tm, None)
            add_tm(O_FV, 1024, self.fv_tm, None)
            add_fm(O_XBC, 12, xbc_start, xbc_evac, xbc_final)
            add_fm(O_HQ, 8, *mk_plain(ba, 'ba', 3, AF.Silu, fin_rows(self.hqT)))
            add_fm(O_HF, 8, *mk_plain(fa, 'fa', 3, AF.Sigmoid, fin_rows(self.hsigT)))
            add_fm(O_HG, 8, *mk_plain(ba, 'ba', 3, AF.Silu, fin_rows(self.hgT)))
            add_fm(O_FQ, 8, *mk_plain(ba, 'ba', 3, None, fin_qk(self.fqA)))
            add_fm(O_FK, 8, *mk_plain(ba, 'ba', 3, None, fin_qk(self.fkA)))
            add_fm(O_G, 48, *mk_plain(ba, 'ba', 3, AF.Sigmoid, fin_rows(self.gT)))
            self.run_pipeline(items)
            k.barrier()


    def stage_ssd(self, l):
        k = self.k
        PB, PT = self.PB, self.PT
        with ExitStack() as st:
            sb = lambda n, s, dt=F32: self.sbt(st, n, s, dt)
            dtb = sb("s_dtb", [128, 16]); alog = sb("s_alog", [128, 16]); dsk = sb("s_dsk", [128, 16])
            aneg = sb("s_aneg", [128, 16]); nw = sb("s_nw", [128, 1024])
            k.dma(dtb[:], self.ssd_dt_bias[l].partition_broadcast(128), [], ['s_dtb'])
            k.dma(alog[:], self.ssd_a_log[l].partition_broadcast(128), [], ['s_alog'])
            k.dma(dsk[:], self.ssd_d[l].partition_broadcast(128), [], ['s_dsk'])
            k.dma(nw[:], self.ssd_norm_w[l].partition_broadcast(128), [], ['s_nw'])
            k.op('act', ['s_alog'], ['s_aneg'], lambda e: e.activation(out=aneg[:], in_=alog[:], func=AF.Exp))
            k.op('dve', ['s_aneg'], ['s_aneg'], lambda e: e.tensor_scalar(
                out=aneg[:], in0=aneg[:], scalar1=-1.0, scalar2=None, op0=ALU.mult))
            S = sb("s_S", [128, 1024]); Sbf = sb("s_Sbf", [128, 1024], BF16)
            k.op('pool', [], ['s_S'], lambda e: e.memset(S[:], 0.0))
            k.op('pool', [], ['s_Sbf'], lambda e: e.memset(Sbf[:], 0.0))
            yTs = sb("s_yTs", [128, 8, T], BF16)
            P2 = {}

            def pool2(name, shape, dt=F32):
                P2[name] = [sb("%s%d" % (name, i), shape, dt) for i in range(2)]
            for nm, shp, dt in [("s_xs", [128, 1024], BF16), ("s_zs", [128, 1024], BF16),
                                ("s_btm", [128, 256], BF16), ("s_bt", [128, 2, 128], BF16),
                                ("s_ct", [128, 2, 128], BF16), ("s_df", [128, 32], F32),
                                ("s_sm", [128, 8, 16], F32), ("s_edec", [128, 48], F32),
                                ("s_cbm", [128, 2, 128], F32), ("s_R", [128, 8, 128], F32),
                                ("s_E", [128, 1024], F32), ("s_MT", [128, 16, 128], BF16),
                                ("s_xdt", [128, 1024], BF16), ("s_xdd", [128, 1024], BF16),
                                ("s_yo", [128, 1024], F32), ("s_y", [128, 1024], F32),
                                ("s_tmp", [128, 1024], F32), ("s_yn", [128, 1024], BF16),
                                ("s_ss", [128, 2], F32), ("s_junk", [128, 512], BF16)]:
                pool2(nm, shp, dt)
            def phase(c, which):
                i = c % 2
                g_ = lambda nm: (P2[nm][i], "%s%d" % (nm, i))
                xs, xsn = g_("s_xs"); zs, zsn = g_("s_zs"); btm, btmn = g_("s_btm")
                bt, btn = g_("s_bt"); ct, ctn = g_("s_ct"); df, dfn = g_("s_df")
                sm, smn = g_("s_sm"); edec, edn = g_("s_edec"); cbm, cbn = g_("s_cbm")
                E_, En = g_("s_E"); MT, MTn = g_("s_MT"); xdt, xdtn = g_("s_xdt")
                xdd, xddn = g_("s_xdd"); yo, yon = g_("s_yo"); y, yn_ = g_("s_y")
                tmp, tmpn = g_("s_tmp"); yn, ynn = g_("s_yn"); ss, ssn = g_("s_ss")
                junk, jn = g_("s_junk")
                R, Rn = g_("s_R")

                x16, ax, e16, l16, r16, dt, a, dtd = [sm[:, j, :] for j in range(8)]
                xs3 = xs[:].rearrange("p (h d) -> p h d", h=16)
                if which == 'A':
                    ts = slice(c * 128, (c + 1) * 128)
                    k.dma(xs[:], self.xs_tm[ts, :], [], [xsn])
                    k.dma(zs[:], self.zs_tm[ts, :], [], [zsn])
                    k.dma(btm[:], self.B_tm[ts, :], [], [btmn])
                    k.dma(bt[:], self.BT[:, :, ts].rearrange("g n t -> n g t"), [], [btn])
                    k.dma(ct[:], self.CT[:, :, ts].rearrange("g n t -> n g t"), [], [ctn])
                    k.dma(df[:], self.dtff[ts, :], [], [dfn])
                    k.op('dve', [dfn, 's_dtb'], [smn], lambda e: e.tensor_tensor(out=x16, in0=df[:, 0:16], in1=dtb[:], op=ALU.add))
                    k.op('act', [smn], [smn], lambda e: e.activation(out=ax, in_=x16, func=AF.Abs))
                    k.op('act', [smn], [smn], lambda e: e.activation(out=e16, in_=ax, func=AF.Exp, scale=-1.0))
                    k.op('act', [smn], [smn], lambda e: e.activation(out=l16, in_=e16, func=AF.Ln, bias=1.0))
                    k.op('dve', [smn], [smn], lambda e: e.tensor_scalar_max(out=r16, in0=x16, scalar1=0.0))
                    k.op('dve', [smn], [smn], lambda e: e.tensor_tensor(out=dt, in0=r16, in1=l16, op=ALU.add))
                    k.op('dve', [smn, 's_aneg'], [smn], lambda e: e.tensor_tensor(out=a, in0=dt, in1=aneg[:], op=ALU.mult))
                    for j, (m, mn) in enumerate([(self.triI, 'triI'), (self.triSU, 'triSU'), (self.ones_f, 'ones_f')]):
                        k.op('pe', [mn, smn], ['pb0'], lambda e: e.matmul(
                            PB[0][:, j * 16:(j + 1) * 16], lhsT=m[:], rhs=a, start=True, stop=True))
                    k.op('act', ['pb0'], [edn], lambda e: e.activation(out=edec[:], in_=PB[0][:, 0:48], func=AF.Exp))
                    for g in range(2):
                        k.op('pe', [btn, ctn], ['pb0'], lambda e: e.matmul(
                            PB[0][:, 256 + g * 128: 256 + (g + 1) * 128], lhsT=bt[:, g, :], rhs=ct[:, g, :],
                            start=True, stop=True))
                    k.op('dve', ['pb0', 'triI'], [cbn], lambda e: e.tensor_tensor(
                        out=cbm[:], in0=PB[0][:, 256:512].rearrange("p (g l) -> p g l", g=2),
                        in1=self.triI[:].unsqueeze(1).to_broadcast([128, 2, 128]), op=ALU.mult))
                    for g in range(2):
                        k.op('pool', ['triI', smn], [Rn], lambda e: e.tensor_tensor(
                            out=R[:], in0=self.triI[:].unsqueeze(1).to_broadcast([128, 8, 128]),
                            in1=a[:, g * 8:(g + 1) * 8].unsqueeze(2).to_broadcast([128, 8, 128]), op=ALU.mult))
                        for hh in range(2):
                            k.op('pe', ['triSU', Rn], ['pb%d' % (1 + hh)], lambda e: e.matmul(
                                PB[1 + hh][:, :], lhsT=self.triSU[:],
                                rhs=R[:, hh * 4:(hh + 1) * 4, :].rearrange("p h l -> p (h l)"), start=True, stop=True))
                            k.op('act', ['pb%d' % (1 + hh)], [En], lambda e: e.activation(
                                out=E_[:, hh * 512:(hh + 1) * 512], in_=PB[1 + hh][:, :], func=AF.Exp))
                        k.op('dve', [En, cbn], [MTn], lambda e: e.tensor_tensor(
                            out=MT[:, g * 8:(g + 1) * 8, :], in0=E_[:].rearrange("p (h l) -> p h l", h=8),
                            in1=cbm[:, g, :].unsqueeze(1).to_broadcast([128, 8, 128]), op=ALU.mult))
                    k.op('dve', [xsn, smn], [xdtn], lambda e: e.tensor_tensor(
                        out=xdt[:].rearrange("p (h d) -> p h d", h=16), in0=xs3,
                        in1=dt.unsqueeze(2).to_broadcast([128, 16, 64]), op=ALU.mult))
                    k.op('dve', [smn, edn], [smn], lambda e: e.tensor_tensor(out=dtd, in0=dt, in1=edec[:, 16:32], op=ALU.mult))
                    k.op('pool', [xsn, smn], [xddn], lambda e: e.tensor_tensor(
                        out=xdd[:].rearrange("p (h d) -> p h d", h=16), in0=xs3,
                        in1=dtd.unsqueeze(2).to_broadcast([128, 16, 64]), op=ALU.mult))

                    return
                for h in range(16):
                    b = 3 + h // 8
                    k.op('pe', [MTn, xdtn], ['pb%d' % b], lambda e: e.matmul(
                        PB[b][:, (h % 8) * 64:(h % 8 + 1) * 64], lhsT=MT[:, h, :], rhs=xdt[:, h * 64:(h + 1) * 64],
                        start=True, stop=True))
                for g in range(2):
                    k.op('pe', [ctn, 's_Sbf'], ['pb%d' % (5 + g)], lambda e: e.matmul(
                        PB[5 + g][:, :], lhsT=ct[:, g, :], rhs=Sbf[:, g * 512:(g + 1) * 512], start=True, stop=True))
                for g in range(2):
                    k.op('dve', ['pb%d' % (5 + g), edn], [yon], lambda e: e.tensor_tensor(
                        out=yo[:, g * 512:(g + 1) * 512].rearrange("p (h d) -> p h d", h=8),
                        in0=PB[5 + g][:, :].rearrange("p (h d) -> p h d", h=8),
                        in1=edec[:, g * 8:(g + 1) * 8].unsqueeze(2).to_broadcast([128, 8, 64]), op=ALU.mult))
                    k.op('dve', [yon, 'pb%d' % (3 + g)], [yn_], lambda e: e.tensor_tensor(
                        out=y[:, g * 512:(g + 1) * 512], in0=yo[:, g * 512:(g + 1) * 512], in1=PB[3 + g][:, :], op=ALU.add))
                k.op('pool', [xsn, 's_dsk'], [tmpn], lambda e: e.tensor_tensor(
                    out=tmp[:].rearrange("p (h d) -> p h d", h=16), in0=xs3,
                    in1=dsk[:].unsqueeze(2).to_broadcast([128, 16, 64]), op=ALU.mult))
                k.op('pool', [yn_, tmpn], [yn_], lambda e: e.tensor_tensor(out=y[:], in0=y[:], in1=tmp[:], op=ALU.add))
                for g in range(2):
                    k.op('pe', [btmn, xddn], ['pb%d' % (5 + g)], lambda e: e.matmul(
                        PB[5 + g][:, :], lhsT=btm[:, g * 128:(g + 1) * 128], rhs=xdd[:, g * 512:(g + 1) * 512],
                        start=True, stop=True))
                k.op('dve', ['s_S', edn], ['s_S'], lambda e: e.tensor_tensor(
                    out=S[:].rearrange("p (h d) -> p h d", h=16), in0=S[:].rearrange("p (h d) -> p h d", h=16),
                    in1=edec[:, 32:48].unsqueeze(2).to_broadcast([128, 16, 64]), op=ALU.mult))
                for g in range(2):
                    k.op('dve', ['s_S', 'pb%d' % (5 + g)], ['s_S'], lambda e: e.tensor_tensor(
                        out=S[:, g * 512:(g + 1) * 512], in0=S[:, g * 512:(g + 1) * 512], in1=PB[5 + g][:, :], op=ALU.add))
                k.op('act', ['s_S'], ['s_Sbf'], lambda e: e.activation(out=Sbf[:], in_=S[:], func=AF.Copy))
                k.op('dve', [yn_, zsn], [yn_], lambda e: e.tensor_tensor(out=y[:], in0=y[:], in1=zs[:], op=ALU.mult))
                for g in range(2):
                    k.op('act', [yn_], [jn, ssn], lambda e: e.activation(
                        out=junk[:], in_=y[:, g * 512:(g + 1) * 512], func=AF.Square, accum_out=ss[:, g:g + 1]))
                k.op('dve', [ssn], [ssn], lambda e: e.tensor_scalar(
                    out=ss[:], in0=ss[:], scalar1=1.0 / 512, scalar2=EPS, op0=ALU.mult, op1=ALU.add))
                k.op('act', [ssn], [ssn], lambda e: e.sqrt(out=ss[:], in_=ss[:]))
                k.op('dve', [ssn], [ssn], lambda e: e.reciprocal(out=ss[:], in_=ss[:]))
                for g in range(2):
                    k.op('dve', [yn_, ssn, 's_nw'], [ynn], lambda e: e.scalar_tensor_tensor(
                        out=yn[:, g * 512:(g + 1) * 512], in0=y[:, g * 512:(g + 1) * 512], scalar=ss[:, g:g + 1],
                        in1=nw[:, g * 512:(g + 1) * 512], op0=ALU.mult, op1=ALU.mult))
                tog = [0]

                def dfn2(b0, nb, pt, ptn, c=c):
                    self.cast(yTs[:, b0:b0 + nb, c * 128:(c + 1) * 128], pt.rearrange("p (b t) -> p b t", t=128),
                              [ptn], ['s_yTs'], eng=('act' if (b0 // 4) % 2 == 0 else 'dve'))
                self.transpose_to(yn, ynn, 8, dfn2, tog)
            phase(0, 'A')
            for c in range(NT):
                if c + 1 < NT:
                    phase(c + 1, 'A')
                phase(c, 'B')
            k.dma(self.ysT.rearrange("(b p) t -> p b t", p=128), yTs[:], ['s_yTs'], [])
            k.barrier()

    def stage_hgrn(self, l):
        k = self.k
        PB, PT = self.PB, self.PT
        with ExitStack() as st:
            sb = lambda n, s, dt=F32: self.sbt(st, n, s, dt)
            cmask = sb("h_cmask", [128, T]); mask64 = sb("h_m64", [64, 64])
            lb = sb("h_lb", [128, 8]); oml = sb("h_oml", [128, 8]); nw = sb("h_nw", [128, 8])
            lb0 = sb("h_lb0", [128, 8])
            k.op('pool', [], ['h_cmask'], lambda e: e.memset(cmask[:], 1.0))
            k.op('pool', ['h_cmask'], ['h_cmask'], lambda e: e.memset(
                cmask[:].rearrange("p (c t) -> p c t", t=64)[:, :, 0:1], 0.0))
            k.op('pool', ['ones_f'], ['h_m64'], lambda e: e.affine_select(
                out=mask64[:], in_=self.ones_f[0:64, 0:64], pattern=[[1, 64]], compare_op=ALU.is_ge,
                fill=0.0, base=0, channel_multiplier=-1))
            k.dma(nw[:], self.hgrn_norm_w[l].rearrange("(h p) -> p h", p=128), [], ['h_nw'],
                  allow_slow_non_contiguous=True)
            if l == 0:
                k.op('pool', [], ['h_lb'], lambda e: e.memset(lb[:], 0.0))
            else:
                k.dma(lb0[:], self.hgrn_lb[0].rearrange("(h p) -> p h", p=128), [], ['h_lb0'],
                      allow_slow_non_contiguous=True)
                k.dma(lb[:], self.hgrn_lb[1].rearrange("(h p) -> p h", p=128), [], ['h_lb'],
                      allow_slow_non_contiguous=True)
                k.op('dve', ['h_lb', 'h_lb0'], ['h_lb'], lambda e: e.tensor_tensor(out=lb[:], in0=lb[:], in1=lb0[:], op=ALU.subtract))
                k.op('act', ['h_lb'], ['h_lb'], lambda e: e.activation(out=lb[:], in_=lb[:], func=AF.Sigmoid))
            k.op('dve', ['h_lb'], ['h_oml'], lambda e: e.tensor_scalar(
                out=oml[:], in0=lb[:], scalar1=-1.0, scalar2=1.0, op0=ALU.mult, op1=ALU.add))
            epsb = sb("h_eps", [128, 1])
            k.op('pool', [], ['h_eps'], lambda e: e.memset(epsb[:], EPS))
            osq = sb("h_osq", [128, 512], BF16); rt = sb("h_rt", [128, 512]); t1 = sb("h_t1", [128, 512])
            HS = []
            for p in range(2):
                d = {}
                for nm in ("sig", "f", "b", "eb", "ktf"):
                    d[nm] = (sb("h%d_%s" % (p, nm), [128, T]), "h%d_%s" % (p, nm))
                for nm in ("q", "g", "qt", "kt", "kh", "yh"):
                    d[nm] = (sb("h%d_%s" % (p, nm), [128, T], BF16), "h%d_%s" % (p, nm))
                d["v"] = (sb("h%d_v" % p, [64, 32, 128], BF16), "h%d_v" % p)
                d["S"] = (sb("h%d_S" % p, [128, 128]), "h%d_S" % p)
                d["Sbf"] = [(sb("h%d_Sbf%d" % (p, i), [128, 128], BF16), "h%d_Sbf%d" % (p, i)) for i in range(2)]
                d["smT"] = (sb("h%d_smT" % p, [64, 8, 64], BF16), "h%d_smT" % p)
                d["khT"] = (sb("h%d_khT" % p, [64, 8, 128], BF16), "h%d_khT" % p)
                d["ps_s"] = (PB[p], 'pb%d' % p)
                d["ps_o"] = (PB[2 + p], 'pb%d' % (2 + p))
                d["ps_k"] = (PB[4 + p], 'pb%d' % (4 + p))
                HS.append(d)

            def front(d, h):
                hs = slice(h * 128, (h + 1) * 128)
                sig, sgn = d["sig"]; fB, fn_ = d["f"]; bB, bn = d["b"]; eb, ebn = d["eb"]; ktf, ktfn = d["ktf"]
                q, qn = d["q"]; gg, gn = d["g"]; qt, qtn = d["qt"]; kt, ktn = d["kt"]; kh, khn = d["kh"]
                v, vn = d["v"]; S, Sn = d["S"]
                k.dma(sig[:], self.hsigT[hs, :], [], [sgn])
                k.dma(q[:], self.hqT[hs, :], [], [qn])
                k.dma(gg[:], self.hgT[hs, :], [], [gn])
                k.dma(v[:], self.hv_tm[:, hs].rearrange("(c p) v -> p c v", p=64), [], [vn])
                k.op('dve', [sgn, 'h_oml', 'h_lb'], [fn_], lambda e: e.tensor_scalar(
                    out=fB[:], in0=sig[:], scalar1=oml[:, h:h + 1], scalar2=lb[:, h:h + 1], op0=ALU.mult, op1=ALU.add))
                k.op('act', [fn_], [sgn], lambda e: e.activation(out=sig[:], in_=fB[:], func=AF.Ln))
                k.op('dve', ['h_cmask', sgn], [bn], lambda e: e.tensor_tensor_scan(
                    out=bB[:], data0=cmask[:], data1=sig[:], initial=0.0, op0=ALU.mult, op1=ALU.add))
                k.op('pool', [fn_], [fn_], lambda e: e.tensor_scalar(
                    out=fB[:], in0=fB[:], scalar1=-1.0, scalar2=1.0, op0=ALU.mult, op1=ALU.add))
                k.op('act', [bn], [ebn], lambda e: e.activation(out=eb[:], in_=bB[:], func=AF.Exp))
                k.op('dve', [qn, ebn], [qtn], lambda e: e.tensor_tensor(out=qt[:], in0=q[:], in1=eb[:], op=ALU.mult))
                k.op('pool', [bn], [bn], lambda e: e.tensor_scalar(out=bB[:], in0=bB[:], scalar1=1.0e30, scalar2=-80.0, op0=ALU.min, op1=ALU.max))
                k.op('act', [bn], [bn], lambda e: e.activation(out=bB[:], in_=bB[:], func=AF.Exp, scale=-1.0))
                k.op('dve', [fn_, bn], [ktfn], lambda e: e.tensor_tensor(out=ktf[:], in0=fB[:], in1=bB[:], op=ALU.mult))
                k.op('act', [ktfn], [ktn], lambda e: e.activation(out=kt[:], in_=ktf[:], func=AF.Copy))
                k.op('pool', [ktfn, ebn], [khn], lambda e: e.tensor_tensor(
                    out=kh[:].rearrange("p (c t) -> p c t", t=64), in0=ktf[:].rearrange("p (c t) -> p c t", t=64),
                    in1=eb[:].rearrange("p (c t) -> p c t", t=64)[:, :, 63:64].to_broadcast([128, 32, 64]), op=ALU.mult))
                k.op('pool', [], [Sn], lambda e: e.memset(S[:], 0.0))
                k.op('pool', [], [d["Sbf"][0][1]], lambda e: e.memset(d["Sbf"][0][0][:], 0.0))

            def prep(d, cg):
                qt, qtn = d["qt"]; kt, ktn = d["kt"]; kh, khn = d["kh"]
                ps_s, psn = d["ps_s"]; smT, smn = d["smT"]; khT, khTn = d["khT"]
                for c in range(8):
                    tk = slice((cg * 8 + c) * 64, (cg * 8 + c + 1) * 64)
                    k.op('pe', [ktn, qtn], [psn], lambda e: e.matmul(
                        ps_s[0:64, c * 64:(c + 1) * 64], lhsT=kt[:, tk], rhs=qt[:, tk], start=True, stop=True))
                k.op('dve', [psn, 'h_m64'], [smn], lambda e: e.tensor_tensor(
                    out=smT[:], in0=ps_s[0:64, :].rearrange("p (c t) -> p c t", t=64),
                    in1=mask64[:].unsqueeze(1).to_broadcast([64, 8, 64]), op=ALU.mult))
                for c in range(8):
                    tk = slice((cg * 8 + c) * 64, (cg * 8 + c + 1) * 64)
                    k.op('pe', [khn, 'ident_b'], ['pb7'], lambda e: e.transpose(
                        out=PT[0:64, c * 128:(c + 1) * 128], in_=kh[:, tk], identity=self.ident_b[:]))
                k.op('act', ['pb7'], [khTn], lambda e: e.activation(
                    out=khT[:], in_=PT[0:64, :].rearrange("p (c k) -> p c k", k=128), func=AF.Copy))

            def kv4(d, cg, c0):
                khT, khTn = d["khT"]; v, vn = d["v"]; pk, pkn = d["ps_k"]
                for c in range(c0, c0 + 4):
                    cc = cg * 8 + c
                    k.op('pe', [khTn, vn], [pkn], lambda e: e.matmul(
                        pk[:, (c % 4) * 128:(c % 4 + 1) * 128], lhsT=khT[:, c, :], rhs=v[:, cc, :], start=True, stop=True))

            def chunk(d, cg, c):
                cc = cg * 8 + c
                tk = slice(cc * 64, (cc + 1) * 64)
                v, vn = d["v"]; smT, smn = d["smT"]; qt, qtn = d["qt"]; eb, ebn = d["eb"]
                ps_o, pon = d["ps_o"]; pk, pkn = d["ps_k"]; S, Sn = d["S"]
                sb0, sb0n = d["Sbf"][cc % 2]; sb1, sb1n = d["Sbf"][(cc + 1) % 2]
                pks = pk[:, (c % 4) * 128:(c % 4 + 1) * 128]
                k.op('pe', [vn, smn], [pon], lambda e: e.matmul(
                    ps_o[:, c * 64:(c + 1) * 64], lhsT=v[:, cc, :], rhs=smT[:, c, :], start=True, stop=False))
                k.op('pe', [sb0n, qtn], [pon], lambda e: e.matmul(
                    ps_o[:, c * 64:(c + 1) * 64], lhsT=sb0[:], rhs=qt[:, tk], start=False, stop=True))
                esc = eb[:, cc * 64 + 63: cc * 64 + 64]
                k.op('dve', [Sn, ebn, pkn], [sb1n], lambda e: e.scalar_tensor_tensor(
                    out=sb1[:], in0=S[:], scalar=esc, in1=pks, op0=ALU.mult, op1=ALU.add))
                k.op('dve', [Sn, ebn, pkn], [Sn], lambda e: e.scalar_tensor_tensor(
                    out=S[:], in0=S[:], scalar=esc, in1=pks, op0=ALU.mult, op1=ALU.add))

            def norm(d, h, cg):
                ps_o, pon = d["ps_o"]; gg, gn = d["g"]; yh, yhn = d["yh"]
                k.op('act', [pon], ['h_osq'], lambda e: e.activation(out=osq[:], in_=ps_o[:, :], func=AF.Square))
                k.op('pe', ['ones_b', 'h_osq'], ['pb6'], lambda e: e.matmul(
                    PB[6][:, :], lhsT=self.ones_b[:], rhs=osq[:], start=True, stop=True))
                k.op('act', ['pb6', 'h_eps'], ['h_rt'], lambda e: e.activation(
                    out=rt[:], in_=PB[6][:, :], func=AF.Ln, scale=1.0 / 128, bias=epsb[:]))
                k.op('act', ['h_rt'], ['h_rt'], lambda e: e.activation(out=rt[:], in_=rt[:], func=AF.Exp, scale=-0.5))
                k.op('dve', [pon, 'h_rt'], ['h_t1'], lambda e: e.tensor_tensor(out=t1[:], in0=ps_o[:, :], in1=rt[:], op=ALU.mult))
                k.op('dve', ['h_t1', 'h_nw', gn], [yhn], lambda e: e.scalar_tensor_tensor(
                    out=yh[:, cg * 512:(cg + 1) * 512], in0=t1[:], scalar=nw[:, h:h + 1],
                    in1=gg[:, cg * 512:(cg + 1) * 512], op0=ALU.mult, op1=ALU.mult))

            for hp in range(4):
                hh = [2 * hp, 2 * hp + 1]
                for p in range(2):
                    front(HS[p], hh[p])
                for cg in range(4):
                    for p in range(2):
                        prep(HS[p], cg)
                    for c0 in (0, 4):
                        for p in range(2):
                            kv4(HS[p], cg, c0)
                        for c in range(c0, c0 + 4):
                            for p in range(2):
                                chunk(HS[p], cg, c)
                    for p in range(2):
                        norm(HS[p], hh[p], cg)
                for p in range(2):
                    k.dma(self.yhT[hh[p] * 128:(hh[p] + 1) * 128, :], HS[p]["yh"][0][:], [HS[p]["yh"][1]], [])
            k.barrier()

    def stage_fox(self, l):
        k = self.k
        PB = self.PB
        with ExitStack() as st:
            sb = lambda n, s, dt=F32: self.sbt(st, n, s, dt)
            fb = sb("x_fb", [128, 16]); ffr = sb("x_ffr", [128, NT, 32])
            xx = sb("x_xx", [128, NT, 16]); ax = sb("x_ax", [128, NT, 16]); lf = sb("x_lf", [128, NT, 16])
            c8 = sb("x_c8", [16, T]); c32 = sb("x_c32", [16, T]); ones16 = sb("x_ones16", [16, T], BF16)
            sp_ = [sb("x_sp%d" % i, [16, T], BF16) for i in range(3)]
            sn_ = [sb("x_sn%d" % i, [16, T], BF16) for i in range(3)]
            k.dma(fb[:], self.fox_f_bias[l].partition_broadcast(128), [], ['x_fb'])
            k.dma(ffr[:], self.dtff.rearrange("(tt p) c -> p tt c", p=128), [], ['x_ffr'])
            k.op('pool', [], ['x_ones16'], lambda e: e.memset(ones16[:], 1.0))
            k.op('dve', ['x_ffr', 'x_fb'], ['x_xx'], lambda e: e.tensor_tensor(
                out=xx[:], in0=ffr[:, :, 16:32], in1=fb[:].unsqueeze(1).to_broadcast([128, NT, 16]), op=ALU.add))
            k.op('act', ['x_xx'], ['x_ax'], lambda e: e.activation(out=ax[:], in_=xx[:], func=AF.Abs))
            k.op('act', ['x_ax'], ['x_ax'], lambda e: e.activation(out=ax[:], in_=ax[:], func=AF.Exp, scale=-1.0))
            k.op('act', ['x_ax'], ['x_ax'], lambda e: e.activation(out=ax[:], in_=ax[:], func=AF.Ln, bias=1.0))
            k.op('dve', ['x_xx'], ['x_xx'], lambda e: e.tensor_scalar_min(out=xx[:], in0=xx[:], scalar1=0.0))
            k.op('dve', ['x_xx', 'x_ax'], ['x_lf'], lambda e: e.tensor_tensor(out=lf[:], in0=xx[:], in1=ax[:], op=ALU.subtract))
            for tb in range(4):
                for ti in range(4):
                    i = tb * 4 + ti
                    o = PB[0][0:16, ti * 128:(ti + 1) * 128]
                    for j in range(i):
                        k.op('pe', ['x_lf', 'ones_f'], ['pb0'], lambda e: e.matmul(
                            o, lhsT=lf[:, j, :], rhs=self.ones_f[:], start=(j == 0), stop=False))
                    k.op('pe', ['x_lf', 'triI'], ['pb0'], lambda e: e.matmul(
                        o, lhsT=lf[:, i, :], rhs=self.triI[:], start=(i == 0), stop=True))
                k.op('dve', ['pb0'], ['x_c8'], lambda e: e.tensor_scalar(
                    out=c8[:, tb * 512:(tb + 1) * 512], in0=PB[0][0:16, :], scalar1=8.0, scalar2=None, op0=ALU.mult))
            for j in range(3):
                k.op('dve', ['x_c8'], ['x_sp%d' % j], lambda e: e.tensor_copy(out=sp_[j][:], in_=c8[:]))
                k.op('dve', ['x_sp%d' % j], ['x_sn%d' % j], lambda e: e.tensor_scalar(
                    out=sn_[j][:], in0=sp_[j][:], scalar1=-1.0, scalar2=None, op0=ALU.mult))
                if j < 2:
                    k.op('dve', ['x_sp%d' % j], ['x_c32'], lambda e: e.tensor_copy(out=c32[:], in_=sp_[j][:]))
                    k.op('dve', ['x_c8', 'x_c32'], ['x_c8'], lambda e: e.tensor_tensor(out=c8[:], in0=c8[:], in1=c32[:], op=ALU.subtract))
            AUG = []
            for j in range(3):
                for (dst, row, src, sn) in [(self.fqA, 64 + j, sp_[j], 'x_sp%d' % j), (self.fqA, 67 + j, ones16, 'x_ones16'),
                                            (self.fkA, 64 + j, ones16, 'x_ones16'), (self.fkA, 67 + j, sn_[j], 'x_sn%d' % j)]:
                    nm = 'aug%d' % len(AUG)
                    k.dma(dst[:, row, :], src[:], [sn], [nm])
                    AUG.append(nm)
            madd = sb("x_madd", [128, 4, 512])
            for j in range(4):
                k.op('pool', ['zeros_f'], ['x_madd'], lambda e: e.affine_select(
                    out=madd[:, j, :], in_=self.zeros_f[:], pattern=[[1, 512]], compare_op=ALU.is_ge,
                    fill=-240000.0, base=-j * 128, channel_multiplier=-1))
            sel = sb("x_sel", [65, 64])
            k.op('pool', [], ['x_sel'], lambda e: e.memset(sel[:], 0.0))
            k.op('pool', ['x_sel'], ['x_sel'], lambda e: e.memset(sel[64:65, :], 1.0))
            QA = [sb("x_QA%d" % i, [128, T], BF16) for i in range(2)]
            KA = [sb("x_KA%d" % i, [128, T], BF16) for i in range(2)]
            V = [sb("x_V%d" % i, [128, NT, 128], BF16) for i in range(2)]
            for i in range(2):
                k.op('pool', [], ['x_V%d' % i], lambda e: e.memset(V[i][:], 0.0))
                k.op('pool', ['x_V%d' % i], ['x_V%d' % i], lambda e: e.memset(V[i][:, :, 64:65], 1.0))
                k.op('pool', [], ['x_QA%d' % i], lambda e: e.memset(QA[i][:], 0.0))
                k.op('pool', [], ['x_KA%d' % i], lambda e: e.memset(KA[i][:], 0.0))
            Pt = [sb("x_P%d" % i, [128, 512], BF16) for i in range(6)]
            smk = [sb("x_smk%d" % i, [128, 512]) for i in range(3)]
            osb = [sb("x_osb%d" % i, [65, 512]) for i in range(2)]
            rl = sb("x_rl", [64, 512])
            yf = [sb("x_yf%d" % i, [64, T], BF16) for i in range(2)]

            def load_head(h):
                i = h % 2
                k.dma(QA[i][0:70, :], self.fqA[h], AUG, ['x_QA%d' % i])
                k.dma(KA[i][0:70, :], self.fkA[h], AUG, ['x_KA%d' % i])
                k.dma(V[i][:, :, 0:64], self.fv_tm[:, h * 64:(h + 1) * 64].rearrange("(tt p) d -> p tt d", p=128),
                      [], ['x_V%d' % i])

            seq = [(h, qb, kb) for h in range(16) for qb in range(4) for kb in range(4 * (qb + 1))]
            SB = [0, 1, 2, 3, 7]
            LA = 4

            def emit_S(idx):
                h, qb, kb = seq[idx]
                i = h % 2
                b = SB[idx % 5]
                k.op('pe', ['x_KA%d' % i, 'x_QA%d' % i], ['pb%d' % b], lambda e: e.matmul(
                    PB[b][:, :], lhsT=KA[i][:, kb * 128:(kb + 1) * 128], rhs=QA[i][:, qb * 512:(qb + 1) * 512],
                    start=True, stop=True))

            pending = []

            def emit_norm(h, qb, po, pon, oi):
                i = h % 2
                ob, obn = osb[oi], 'x_osb%d' % oi
                k.op('act', [pon], [obn], lambda e: e.activation(out=ob[:], in_=po[0:65, :], func=AF.Copy))

                k.op('act', [obn], [obn], lambda e: e.activation(out=ob[64:65, :], in_=ob[64:65, :], func=AF.Ln))
                k.op('act', [obn], [obn], lambda e: e.activation(out=ob[64:65, :], in_=ob[64:65, :], func=AF.Exp, scale=-1.0))

                def rest():
                    k.op('pe', ['x_sel', obn], ['pb6'], lambda e: e.matmul(
                        PB[6][0:64, :], lhsT=sel[64:65, :], rhs=ob[64:65, :], start=True, stop=True))
                    k.op('dve', [obn, 'pb6'], ['x_yf%d' % i], lambda e: e.tensor_tensor(
                        out=yf[i][:, qb * 512:(qb + 1) * 512], in0=ob[0:64, :], in1=PB[6][0:64, :], op=ALU.mult))
                    if qb == 3:
                        k.dma(self.yfT[h], yf[i][:], ['x_yf%d' % i], [])
                return rest

            load_head(0)
            for j in range(LA):
                emit_S(j)
            oit = 0
            for idx, (h, qb, kb) in enumerate(seq):
                i = h % 2
                if qb == 0 and kb == 0 and h + 1 < 16:
                    load_head(h + 1)
                if idx + LA < len(seq):
                    emit_S(idx + LA)
                nkb = 4 * (qb + 1)
                b = SB[idx % 5]
                ps, psn = PB[b], 'pb%d' % b
                pt, ptn = Pt[idx % 6], 'x_P%d' % (idx % 6)
                j = kb - 4 * qb
                if j >= 0:
                    sm_, smn = smk[kb % 3], 'x_smk%d' % (kb % 3)
                    k.op('dve', [psn, 'x_madd'], [smn], lambda e: e.tensor_tensor(
                        out=sm_[:], in0=ps[:, :], in1=madd[:, j, :], op=ALU.add))
                    k.op('act', [smn], [ptn], lambda e: e.activation(out=pt[:], in_=sm_[:], func=AF.Exp, scale=0.125))
                else:
                    k.op('act', [psn], [ptn], lambda e: e.activation(out=pt[:], in_=ps[:, :], func=AF.Exp, scale=0.125))
                if kb == 0:
                    oit += 1
                po, pon = (PB[4], 'pb4') if oit % 2 == 0 else (PB[5], 'pb5')
                k.op('pe', ['x_V%d' % i, ptn], [pon], lambda e: e.matmul(
                    po[:, :], lhsT=V[i][:, kb, :], rhs=pt[:], start=(kb == 0), stop=(kb == nkb - 1)))
                if pending and pending[0][0] <= idx:
                    pending.pop(0)[1]()
                if kb == nkb - 1:
                    pending.append((idx + 2, emit_norm(h, qb, po, pon, oit % 2)))
            for _, f in pending:
                f()
            k.barrier()

    def stage_merge(self, l):
        k = self.k
        PB = self.PB
        with ExitStack() as st:
            sb = lambda n, s, dt=F32: self.sbt(st, n, s, dt)
            ys = sb("m_ys", [128, 8, T], BF16); yh = sb("m_yh", [128, 8, T], BF16); yf = sb("m_yf", [128, 8, T], BF16)
            for tb in range(4):
                tsl = slice(tb * 512, (tb + 1) * 512)
                k.dma(ys[:, :, tsl], self.ysT.rearrange("(b p) t -> p b t", p=128)[:, :, tsl], [], ['m_ys%d' % tb])
                k.dma(yh[:, :, tsl], self.yhT.rearrange("(b p) t -> p b t", p=128)[:, :, tsl], [], ['m_yh%d' % tb])
                k.dma(yf[:, :, tsl], self.yfT.rearrange("(b h2) d t -> (h2 d) b t", h2=2)[:, :, tsl], [], ['m_yf%d' % tb])
            wps = self.mk_wpool(st, "mws", 8, 128, dd=3)
            wph = self.mk_wpool(st, "mwh", 8, 128, dd=3)
            wpf = self.mk_wpool(st, "mwf", 8, 128, dd=3)
            gts = [[sb("m_g%d_%d" % (r, i), [128, T], BF16) for i in range(2)] for r in range(3)]
            m1 = [sb("m_m1_%d" % i, [128, 512]) for i in range(2)]
            m2 = [sb("m_m2_%d" % i, [128, 512]) for i in range(2)]
            mo = [sb("m_mo%d" % i, [128, T], BF16) for i in range(2)]
            state = {'it': 0}
            items = []
            ysrc = [(ys, 'm_ys'), (yh, 'm_yh'), (yf, 'm_yf')]

            def mk(db):
                def fn(ws):
                    gi = db % 2
                    for r in range(3):
                        k.dma(gts[r][gi][:], self.gT[r * D + db * 128: r * D + (db + 1) * 128, :], [], ['m_g%d_%d' % (r, gi)])
                    for tb in range(4):
                        i = state['it'] % 2
                        state['it'] += 1
                        bs = [3 * i, 3 * i + 1, 3 * i + 2]
                        tsl = slice(tb * 512, (tb + 1) * 512)
                        for r in range(3):
                            wb, wbn = ws[r]
                            yt, ytn = ysrc[r]
                            for kc in range(8):
                                k.op('pe', [wbn, ytn + str(tb)], ['pb%d' % bs[r]], lambda e: e.matmul(
                                    PB[bs[r]][:, :], lhsT=wb[:, kc, :], rhs=yt[:, kc, tsl], start=(kc == 0), stop=(kc == 7)))
                        a1, a1n = m1[i], 'm_m1_%d' % i
                        a2, a2n = m2[i], 'm_m2_%d' % i
                        k.op('dve', ['pb%d' % bs[0], 'm_g0_%d' % gi], [a1n], lambda e: e.tensor_tensor(
                            out=a1[:], in0=PB[bs[0]][:, :], in1=gts[0][gi][:, tsl], op=ALU.mult))
                        k.op('dve', ['pb%d' % bs[1], 'm_g1_%d' % gi], [a2n], lambda e: e.tensor_tensor(
                            out=a2[:], in0=PB[bs[1]][:, :], in1=gts[1][gi][:, tsl], op=ALU.mult))
                        k.op('dve', [a1n, a2n], [a1n], lambda e: e.tensor_tensor(out=a1[:], in0=a1[:], in1=a2[:], op=ALU.add))
                        k.op('dve', ['pb%d' % bs[2], 'm_g2_%d' % gi], [a2n], lambda e: e.tensor_tensor(
                            out=a2[:], in0=PB[bs[2]][:, :], in1=gts[2][gi][:, tsl], op=ALU.mult))
                        k.op('dve', [a1n, a2n], ['m_mo%d' % gi], lambda e: e.tensor_tensor(
                            out=mo[gi][:, tsl], in0=a1[:], in1=a2[:], op=ALU.add))
                    k.dma(self.mT[db * 128:(db + 1) * 128, :], mo[gi][:], ['m_mo%d' % gi], [])
                return fn
            for db in range(16):
                c0 = db * 128
                items.append(([(wps, [(0, self.w_branch_ssd[l, :, c0:c0 + 128])], 8, 128),
                               (wph, [(0, self.w_branch_hgrn[l, :, c0:c0 + 128])], 8, 128),
                               (wpf, [(0, self.w_branch_fox[l, :, c0:c0 + 128])], 8, 128)], mk(db)))
            self.run_pipeline(items)
            k.barrier()

    def stage_out_proj(self, AT_dram, nkc, Wsrc, xsrc, xdst, halves):
        k = self.k
        PB = self.PB
        TH = T // halves
        ntt = TH // 128
        with ExitStack() as st:
            sb = lambda n, s, dt=F32: self.sbt(st, n, s, dt)
            A = sb("o_A", [128, nkc, TH], BF16)
            KG = 4
            wp = self.mk_wpool(st, "ow", KG, 512, dd=4)
            xs = [sb("o_x%d" % i, [128, 512]) for i in range(16)]
            state = {'g': 0}
            items = []

            def mk(hf, cb, grp, kg, n_k, first, last, gidx):
                def fn(ws):
                    wb, wbn = ws[0]
                    if first:
                        for c0 in range(0, nkc, KG):
                            c1 = min(nkc, c0 + KG)
                            k.dma(A[:, c0:c1, :], AT_dram[c0 * 128:c1 * 128, hf * TH:(hf + 1) * TH].rearrange("(c p) t -> p c t", p=128),
                                  [], ['o_A%d' % (c0 // KG)])
                    if kg == 0:
                        for bi, tt in enumerate(grp):
                            j = (gidx % 2) * 8 + bi
                            r0 = hf * TH + tt * 128
                            k.dma(xs[j][:], xsrc[r0:r0 + 128, cb * 512:(cb + 1) * 512], [], ['o_x%d' % j])
                    for kk in range(n_k):
                        kc = kg + kk
                        for bi, tt in enumerate(grp):
                            k.op('pe', [wbn, 'o_A%d' % (kc // KG)], ['pb%d' % bi], lambda e: e.matmul(
                                PB[bi][:, :], lhsT=A[:, kc, tt * 128:(tt + 1) * 128], rhs=wb[:, kk, :],
                                start=(kc == 0), stop=(kc == nkc - 1)))
                    if last:
                        for bi, tt in enumerate(grp):
                            j = (gidx % 2) * 8 + bi
                            xt, xn = xs[j], 'o_x%d' % j
                            r0 = hf * TH + tt * 128
                            k.op('dve', [xn, 'pb%d' % bi], [xn], lambda e: e.tensor_tensor(
                                out=xt[:], in0=xt[:], in1=PB[bi][:, :], op=ALU.add))
                            k.dma(xdst[r0:r0 + 128, cb * 512:(cb + 1) * 512], xt[:], [xn], [])
                return fn
            for hf in range(halves):
                first = True
                for cb in range(D // 512):
                    for tt0 in range(0, ntt, 8):
                        grp = list(range(tt0, min(ntt, tt0 + 8)))
                        for kg in range(0, nkc, KG):
                            n_k = min(KG, nkc - kg)
                            items.append(([(wp, [(0, Wsrc[kg * 128:(kg + n_k) * 128, cb * 512:(cb + 1) * 512])], n_k, 512)],
                                          mk(hf, cb, grp, kg, n_k, first, kg + n_k == nkc, state['g'])))
                            first = False
                        state['g'] += 1
            self.run_pipeline(items)
            k.barrier()

    def stage_ffn_up(self, l, hT, hTn):
        k = self.k
        PB = self.PB
        W = self.ffn_w_up
        with ExitStack() as st:
            sb = lambda n, s, dt=F32: self.sbt(st, n, s, dt)
            wp = self.mk_wpool(st, "fw", KC, 256)
            cw = sb("f_cw", [128, 2 * NFB, 3]); cb = sb("f_cb", [128, 2 * NFB])
            for b in range(2 * NFB):
                k.dma(cw[:, b, :], self.ffn_conv_w[l, :, b * 128:(b + 1) * 128].rearrange("k p -> p k"),
                      [], ['f_cw%d' % b], allow_slow_non_contiguous=True)
            k.dma(cb[:], self.ffn_conv_b[l].rearrange("(b p) -> p b", p=128), [], ['f_cb'],
                  allow_slow_non_contiguous=True)
            xp = [sb("f_xp%d" % i, [128, T + 2]) for i in range(2)]
            acc = [sb("f_acc%d" % i, [128, T]) for i in range(2)]
            sg = sb("f_sg", [128, T])
            ao = [sb("f_ao%d" % i, [128, T], BF16) for i in range(2)]
            for i in range(2):
                k.op('pool', [], ['f_xp%d' % i], lambda e: e.memset(xp[i][:, 0:2], 0.0))
            state = {'bank': 0, 'ev': 0}
            items = []

            def mk(j):
                def fn(ws):
                    wb, wbn = ws[0]
                    for half in range(2):
                        blk = j + half * NFB
                        x_, xn = xp[half], 'f_xp%d' % half
                        a_, an = acc[half], 'f_acc%d' % half
                        for tb in range(4):
                            b = state['bank'] % 8
                            state['bank'] += 1
                            for kc in range(KC):
                                k.op('pe', [wbn, hTn], ['pb%d' % b], lambda e: e.matmul(
                                    PB[b][:, :], lhsT=wb[:, kc, half * 128:(half + 1) * 128],
                                    rhs=hT[:, kc, tb * 512:(tb + 1) * 512],
                                    start=(kc == 0), stop=(kc == KC - 1)))
                            eng = ('act', 'act', 'dve')[state['ev'] % 3]
                            state['ev'] += 1
                            self.cast(x_[:, 2 + tb * 512: 2 + (tb + 1) * 512], PB[b][:, :], ['pb%d' % b], [xn], eng=eng)
                        k.op('dve', [xn, 'f_cw%d' % blk, 'f_cb'], [an], lambda e: e.tensor_scalar(
                            out=a_[:], in0=x_[:, 0:T], scalar1=cw[:, blk, 0:1], scalar2=cb[:, blk:blk + 1],
                            op0=ALU.mult, op1=ALU.add))
                        for kk in range(1, 3):
                            k.op('dve', [xn, 'f_cw%d' % blk, an], [an], lambda e: e.scalar_tensor_tensor(
                                out=a_[:], in0=x_[:, kk:kk + T], scalar=cw[:, blk, kk:kk + 1], in1=a_[:],
                                op0=ALU.mult, op1=ALU.add))
                    k.op('act', ['f_acc0'], ['f_sg'], lambda e: e.activation(out=sg[:], in_=acc[0][:], func=AF.Silu))
                    o, on = ao[j % 2], 'f_ao%d' % (j % 2)
                    k.op('pool', ['f_sg', 'f_acc1'], [on], lambda e: e.tensor_tensor(out=o[:], in0=sg[:], in1=acc[1][:], op=ALU.mult))
                    k.dma(self.actT[j * 128:(j + 1) * 128, :], o[:], [on], [])
                return fn
            for j in range(NFB):
                items.append(([(wp, [(0, W[l, :, j * 128:(j + 1) * 128]),
                                     (128, W[l, :, (NFB + j) * 128:(NFB + j + 1) * 128])], KC, 256)], mk(j)))
            self.run_pipeline(items)
            k.barrier()


    def stage_final(self, src):
        k = self.k
        with ExitStack() as st:
            sb = lambda n, s, dt=F32: self.sbt(st, n, s, dt)
            wN = sb("z_w", [128, D])
            k.dma(wN[:], self.final_norm_w.partition_broadcast(128), [], ['z_w'])
            xts = [sb("z_x%d" % i, [128, D]) for i in range(2)]
            ots = [sb("z_o%d" % i, [128, D]) for i in range(2)]
            junk = sb("z_junk", [128, D], BF16)
            sss = [sb("z_ss%d" % i, [128, 1]) for i in range(2)]
            for tt in range(NT):
                i = tt % 2
                xt, ot, ss = xts[i], ots[i], sss[i]
                xn, on, sn = "z_x%d" % i, "z_o%d" % i, "z_ss%d" % i
                k.dma(xt[:], src[tt * 128:(tt + 1) * 128, :], [], [xn])
                k.op('act', [xn], ['z_junk', sn], lambda e: e.activation(
                    out=junk[:], in_=xt[:], func=AF.Square, accum_out=ss[:]))
                k.op('dve', [sn], [sn], lambda e: e.tensor_scalar(
                    out=ss[:], in0=ss[:], scalar1=1.0 / D, scalar2=EPS, op0=ALU.mult, op1=ALU.add))
                k.op('act', [sn], [sn], lambda e: e.sqrt(out=ss[:], in_=ss[:]))
                k.op('dve', [sn], [sn], lambda e: e.reciprocal(out=ss[:], in_=ss[:]))
                k.op('dve', [xn, sn, 'z_w'], [on], lambda e: e.scalar_tensor_tensor(
                    out=ot[:], in0=xt[:], scalar=ss[:, 0:1], in1=wN[:], op0=ALU.mult, op1=ALU.mult))
                k.dma(self.out[tt * 128:(tt + 1) * 128, :], ot[:], [on], [])
            k.barrier()

    def finish(self):
        k = self.k
        nc = self.nc
        done = nc.alloc_semaphore("sem_done")
        for e in ('pe', 'act', 'dve', 'pool'):
            k.E[e].sem_inc(done, 1)
        sp = k.E['sp']
        sp.wait_ge(done, 4)
        for e in k.E:
            sp.sem_clear(k.sem[e])
        for s_ in k.dsem:
            sp.sem_clear(s_)
        sp.sem_clear(done)

    def build(self, nlayers=2, upto=None):
        k = self.k
        src = self.x
        stop = False
        for l in range(nlayers):
            with ExitStack() as st:
                if upto == 'init':
                    return
                hT = self.sbt(st, "hT", [128, KC, T], BF16)
                self.norm_T(src, self.norm_mix_w[l], hT, 'hT')
                if upto == 'norm':
                    return
                self.stage_proj(l, hT, 'hT')
            if upto == 'proj':
                return
            self.stage_ssd(l)
            if upto == 'ssd':
                return
            self.stage_hgrn(l)
            if upto == 'hgrn':
                return
            self.stage_fox(l)
            if upto == 'fox':
                return
            self.stage_merge(l)
            if upto == 'merge':
                return
            self.stage_out_proj(self.mT, 16, self.w_out[l], src, self.xa, halves=1)
            if upto == 'mix':
                return
            with ExitStack() as st:
                hT = self.sbt(st, "hT", [128, KC, T], BF16)
                self.norm_T(self.xa, self.norm_ffn_w[l], hT, 'hT')
                self.stage_ffn_up(l, hT, 'hT')
            if upto == 'ffn_up':
                return
            self.stage_out_proj(self.actT, NFB, self.ffn_w_down[l], self.xa, self.xb, halves=2)
            if upto == 'layer':
                return
            src = self.xb
        self.stage_final(self.xb)
        self.finish()


WNAMES = ["norm_mix_w", "w_in", "ssd_conv_w", "ssd_conv_b", "ssd_dt_bias", "ssd_a_log", "ssd_d",
          "ssd_norm_w", "hgrn_lb", "hgrn_norm_w", "fox_f_bias", "w_branch_ssd", "w_branch_hgrn",
          "w_branch_fox", "w_out", "norm_ffn_w", "ffn_w_up", "ffn_conv_w", "ffn_conv_b",
          "ffn_w_down", "final_norm_w"]


def kernel(**inputs):
    nc = bass.Bass("TRN2", target_bir_lowering=False)
    p = Prog(nc)
    p.build()
    x = np.ascontiguousarray(inputs["x"], dtype=np.float32)
    shared = {n: np.ascontiguousarray(inputs[n], dtype=np.float32) for n in WNAMES}
    in_maps = []
    for c in range(8):
        m = dict(shared)
        m["x"] = x[c]
        in_maps.append(m)
    res = run_bass_kernel_spmd(nc, in_maps, core_ids=list(range(8)))
    return np.stack([np.asarray(r["out"], dtype=np.float32) for r in res.results], axis=0)
```

```python
import numpy as np
from contextlib import ExitStack
import concourse.bass as bass
import concourse.mybir as mybir
from concourse.bass_utils import run_bass_kernel_spmd

F32 = mybir.dt.float32
BF16 = mybir.dt.bfloat16
AF = mybir.ActivationFunctionType
ALU = mybir.AluOpType

T = 2048
D = 2048
NT = 16
KC = 16
DIN = 15904
DFF = 5504
NFB = 43
EPS = 1e-6
NDS = 24

O_Z, O_XBC, O_DT, O_HQ, O_HF, O_HI, O_HG, O_FQ, O_FK, O_FV, O_FF, O_G = (
    0, 1024, 2560, 2576, 3600, 4624, 5648, 6672, 7696, 8720, 9744, 9760)


class KB:
    def __init__(self, nc):
        self.nc = nc
        self.E = {'pe': nc.tensor, 'act': nc.scalar, 'dve': nc.vector,
                  'pool': nc.gpsimd, 'sp': nc.sync}
        self.sem = {k: nc.alloc_semaphore("sem_" + k) for k in self.E}
        self.icnt = {k: 0 for k in self.E}
        self.scnt = {k: 0 for k in self.E}
        self.last = {k: None for k in self.E}
        self.incs = {k: [] for k in self.E}
        self.seen = {k: {k2: 0 for k2 in self.E} for k in self.E}
        self.dsem = [nc.alloc_semaphore("dsem%d" % i) for i in range(NDS)]
        self.dcnt = [0] * NDS
        self.dnext = 0
        self.dseen = {k: [0] * NDS for k in self.E}
        self.lastw = {}
        self.readers = {}

    def _deps(self, reads, writes):
        deps = []
        for r in reads:
            t = self.lastw.get(r)
            if t is not None:
                deps.append(t)
        for w in writes:
            t = self.lastw.get(w)
            if t is not None:
                deps.append(t)
            rd = self.readers.get(w)
            if rd:
                for e, i in rd['e'].items():
                    deps.append(('e', e, i))
                deps.extend(rd['d'])
        return deps

    def _semval_for(self, e2, idx):
        lst = self.incs[e2]
        if lst and lst[-1][0] >= idx:
            j = len(lst) - 1
            while j > 0 and lst[j - 1][0] >= idx:
                j -= 1
            return lst[j]
        ins, lidx = self.last[e2]
        assert lidx >= idx
        self.scnt[e2] += 1
        ins.then_inc(self.sem[e2], 1)
        lst.append((lidx, self.scnt[e2]))
        return lst[-1]

    def _wait(self, eng, deps, raw_same=()):
        need = {}
        dneed = {}
        for t in deps:
            if t[0] == 'e':
                _, e2, idx = t
                if e2 == eng:
                    continue
                if idx > need.get(e2, 0):
                    need[e2] = idx
            else:
                _, s, c = t
                if c > dneed.get(s, 0):
                    dneed[s] = c
        for idx in raw_same:
            if eng != 'pe' and idx > need.get(eng, 0):
                need[eng] = idx
        E = self.E[eng]
        for e2, idx in need.items():
            if idx <= self.seen[eng][e2]:
                continue
            iidx, v = self._semval_for(e2, idx)
            E.wait_ge(self.sem[e2], v)
            self.seen[eng][e2] = iidx
        for s, c in dneed.items():
            if c <= self.dseen[eng][s]:
                continue
            E.wait_ge(self.dsem[s], c)
            self.dseen[eng][s] = c

    def _record(self, tok, reads, writes):
        for r in reads:
            rd = self.readers.get(r)
            if rd is None:
                rd = {'e': {}, 'd': []}
                self.readers[r] = rd
            if tok[0] == 'e':
                rd['e'][tok[1]] = tok[2]
            else:
                rd['d'].append(tok)
        for w in writes:
            self.lastw[w] = tok
            self.readers[w] = {'e': {}, 'd': []}

    def op(self, eng, reads, writes, fn):
        deps = self._deps(reads, writes)
        raw_same = []
        for r in reads:
            t = self.lastw.get(r)
            if t is not None and t[0] == 'e' and t[1] == eng:
                raw_same.append(t[2])
        self._wait(eng, deps, raw_same)
        ins = fn(self.E[eng])
        self.icnt[eng] += 1
        self.last[eng] = (ins, self.icnt[eng])
        self._record(('e', eng, self.icnt[eng]), reads, writes)
        return ins

    def dma(self, out, in_, reads, writes, q='sp', **kw):
        deps = self._deps(reads, writes)
        s = self.dnext
        self.dnext = (self.dnext + 1) % NDS
        if self.dcnt[s] > 0:
            deps.append(('d', s, self.dcnt[s]))
        raw_same = []
        for r in reads:
            t = self.lastw.get(r)
            if t is not None and t[0] == 'e' and t[1] == q:
                raw_same.append(t[2])
        self._wait(q, deps, raw_same)
        ins = self.E[q].dma_start(out=out, in_=in_, **kw)
        self.dcnt[s] += 16
        ins.then_inc(self.dsem[s], 16)
        self._record(('d', s, self.dcnt[s]), reads, writes)

    def barrier(self):
        deps = []
        for e in self.E:
            if self.last[e] is not None:
                deps.append(('e', e, self.last[e][1]))
        for s in range(NDS):
            if self.dcnt[s] > 0:
                deps.append(('d', s, self.dcnt[s]))
        for e in self.E:
            self._wait(e, deps, [self.last[e][1]] if (self.last[e] is not None and e != 'sp') else [])
        self.lastw = {}
        self.readers = {}


class Prog:
    def __init__(self, nc, dbg=(), tiny=False):
        self.nc = nc
        self.k = KB(nc)
        self.dbg = set(dbg)
        self.cast_rr = 0
        self.uid = 0
        k = self.k
        BIG = ("w_in", "w_branch_ssd", "w_branch_hgrn", "w_branch_fox", "w_out", "ffn_w_up", "ffn_w_down")
        di = lambda n, s: nc.dram_tensor(n, ([2, 1, 1] if (tiny and n in BIG) else list(s)), F32, kind="ExternalInput").ap()
        self.x = di("x", [T, D])
        self.norm_mix_w = di("norm_mix_w", [2, D])
        self.w_in = di("w_in", [2, D, DIN])
        self.ssd_conv_w = di("ssd_conv_w", [2, 4, 1536])
        self.ssd_conv_b = di("ssd_conv_b", [2, 1536])
        self.ssd_dt_bias = di("ssd_dt_bias", [2, 16])
        self.ssd_a_log = di("ssd_a_log", [2, 16])
        self.ssd_d = di("ssd_d", [2, 16])
        self.ssd_norm_w = di("ssd_norm_w", [2, 1024])
        self.hgrn_lb = di("hgrn_lb", [2, 1024])
        self.hgrn_norm_w = di("hgrn_norm_w", [2, 1024])
        self.fox_f_bias = di("fox_f_bias", [2, 16])
        self.w_branch_ssd = di("w_branch_ssd", [2, 1024, D])
        self.w_branch_hgrn = di("w_branch_hgrn", [2, 1024, D])
        self.w_branch_fox = di("w_branch_fox", [2, 1024, D])
        self.w_out = di("w_out", [2, D, D])
        self.norm_ffn_w = di("norm_ffn_w", [2, D])
        self.ffn_w_up = di("ffn_w_up", [2, D, 2 * DFF])
        self.ffn_conv_w = di("ffn_conv_w", [2, 3, 2 * DFF])
        self.ffn_conv_b = di("ffn_conv_b", [2, 2 * DFF])
        self.ffn_w_down = di("ffn_w_down", [2, DFF, D])
        self.final_norm_w = di("final_norm_w", [D])
        self.out = nc.dram_tensor("out", [T, D], F32, kind="ExternalOutput").ap()
        self.xa = self.scr("xa", [T, D], F32)
        self.xb = self.scr("xb", [T, D], F32)
        self.xs_tm = self.scr("xs_tm", [T, 1024], BF16)
        self.B_tm = self.scr("B_tm", [T, 256], BF16)
        self.BT = self.scr("BT", [2, 128, T], BF16)
        self.CT = self.scr("CT", [2, 128, T], BF16)
        self.zs_tm = self.scr("zs_tm", [T, 1024], BF16)
        self.dtff = self.scr("dtff", [T, 32], F32)
        self.hqT = self.scr("hqT", [1024, T], BF16)
        self.hsigT = self.scr("hsigT", [1024, T], F32)
        self.hgT = self.scr("hgT", [1024, T], BF16)
        self.hv_tm = self.scr("hv_tm", [T, 1024], BF16)
        self.fqA = self.scr("fqA", [16, 70, T], BF16)
        self.fkA = self.scr("fkA", [16, 70, T], BF16)
        self.fv_tm = self.scr("fv_tm", [T, 1024], BF16)
        self.gT = self.scr("gT", [6144, T], BF16)
        self.ysT = self.scr("ysT", [1024, T], BF16)
        self.yhT = self.scr("yhT", [1024, T], BF16)
        self.yfT = self.scr("yfT", [16, 64, T], BF16)
        self.mT = self.scr("mT", [D, T], BF16)
        self.actT = self.scr("actT", [DFF, T], BF16)
        self.PB = [nc.alloc_psum_tensor("pb%d" % i, [128, 512], F32) for i in range(8)]
        self.PT = self.PB[7][:].bitcast(BF16)
        self.PT6 = self.PB[6][:].bitcast(BF16)
        self.cst = ExitStack()
        sb = lambda n, s, dt=F32: self.cst.enter_context(nc.sbuf_tensor(n, list(s), dt))
        self.ones_f = sb("ones_f", [128, 128])
        self.ones_b = sb("ones_b", [128, 128], BF16)
        self.ident_b = sb("ident_b", [128, 128], BF16)
        self.triI = sb("triI", [128, 128])
        self.triSU = sb("triSU", [128, 128])
        self.zeros_f = sb("zeros_f", [128, 512])
        tmp = sb("c_tmp", [128, 128])
        k.op('pool', [], ['ones_f'], lambda e: e.memset(self.ones_f[:], 1.0))
        k.op('pool', [], ['zeros_f'], lambda e: e.memset(self.zeros_f[:], 0.0))
        k.op('pool', ['ones_f'], ['c_tmp'], lambda e: e.affine_select(
            out=tmp[:], in_=self.ones_f[:], pattern=[[1, 128]], compare_op=ALU.is_equal,
            fill=0.0, base=0, channel_multiplier=-1))
        k.op('pool', ['ones_f'], ['triI'], lambda e: e.affine_select(
            out=self.triI[:], in_=self.ones_f[:], pattern=[[1, 128]], compare_op=ALU.is_ge,
            fill=0.0, base=0, channel_multiplier=-1))
        k.op('pool', ['ones_f'], ['triSU'], lambda e: e.affine_select(
            out=self.triSU[:], in_=self.ones_f[:], pattern=[[-1, 128]], compare_op=ALU.is_ge,
            fill=0.0, base=-1, channel_multiplier=1))
        k.op('dve', ['c_tmp'], ['ident_b'], lambda e: e.tensor_copy(out=self.ident_b[:], in_=tmp[:]))
        k.op('dve', ['ones_f'], ['ones_b'], lambda e: e.tensor_copy(out=self.ones_b[:], in_=self.ones_f[:]))
        k.barrier()
        self.C = ['ones_f', 'ones_b', 'ident_b', 'triI', 'triSU', 'zeros_f']

    def scr(self, name, shape, dt):
        kind = "ExternalOutput" if name in self.dbg else "Internal"
        return self.nc.dram_tensor(name, list(shape), dt, kind=kind).ap()

    def sbt(self, st, name, shape, dt=F32):
        self.uid += 1
        return st.enter_context(self.nc.sbuf_tensor("%s_%d" % (name, self.uid), list(shape), dt))

    def u(self):
        self.uid += 1
        return "u%d" % self.uid

    def cast(self, out_ap, in_ap, reads, writes, eng=None):
        if eng is None:
            eng = ('dve', 'act', 'pool', 'dve', 'act')[self.cast_rr % 5]
            self.cast_rr += 1
        if eng == 'act':
            self.k.op('act', reads, writes, lambda e: e.activation(out=out_ap, in_=in_ap, func=AF.Copy))
        else:
            self.k.op(eng, reads, writes, lambda e: e.tensor_copy(out=out_ap, in_=in_ap))

    def transpose_to(self, src_tile, src_name, nblk, dst_fn, pt_toggle, views=None):
        k = self.k
        if views is None:
            views = [(self.PT, 'pb7')]
        for b0 in range(0, nblk, 4):
            nb = min(4, nblk - b0)
            pv, ptn = views[pt_toggle[0] % len(views)]
            pt_toggle[0] += 1
            for b in range(nb):
                o = pv[:, b * 128:(b + 1) * 128]
                k.op('pe', [src_name, 'ident_b'], [ptn], lambda e: e.transpose(
                    out=o, in_=src_tile[:, (b0 + b) * 128:(b0 + b + 1) * 128],
                    identity=self.ident_b[:]))
            dst_fn(b0, nb, pv[:, 0:nb * 128], ptn)

    def norm_T(self, src, wv, hT, hTn):
        k = self.k
        with ExitStack() as st:
            wN = self.sbt(st, "nrm_w", [128, D])
            k.dma(wN[:], wv.partition_broadcast(128), [], ['nrm_w'])
            xts = [self.sbt(st, "nrm_x%d" % i, [128, D]) for i in range(2)]
            hbs = [self.sbt(st, "nrm_hb%d" % i, [128, D], BF16) for i in range(2)]
            junk = self.sbt(st, "nrm_junk", [128, D], BF16)
            sss = [self.sbt(st, "nrm_ss%d" % i, [128, 1]) for i in range(2)]
            tog = [0]
            def stats(tt):
                i = tt % 2
                xt, hb, ss = xts[i], hbs[i], sss[i]
                xn, hn, sn = "nrm_x%d" % i, "nrm_hb%d" % i, "nrm_ss%d" % i
                k.dma(xt[:], src[tt * 128:(tt + 1) * 128, :], [], [xn])
                k.op('act', [xn], ['nrm_junk', sn], lambda e: e.activation(
                    out=junk[:], in_=xt[:], func=AF.Square, accum_out=ss[:]))
                k.op('dve', [sn], [sn], lambda e: e.tensor_scalar(
                    out=ss[:], in0=ss[:], scalar1=1.0 / D, scalar2=EPS, op0=ALU.mult, op1=ALU.add))
                k.op('act', [sn], [sn], lambda e: e.sqrt(out=ss[:], in_=ss[:]))
                k.op('dve', [sn], [sn], lambda e: e.reciprocal(out=ss[:], in_=ss[:]))
                k.op('dve', [xn, sn, 'nrm_w'], [hn], lambda e: e.scalar_tensor_tensor(
                    out=hb[:], in0=xt[:], scalar=ss[:, 0:1], in1=wN[:], op0=ALU.mult, op1=ALU.mult))

            def xpose(tt):
                i = tt % 2
                hb, hn = hbs[i], "nrm_hb%d" % i

                def dst(b0, nb, pt, ptn, tt=tt):
                    self.cast(hT[:, b0:b0 + nb, tt * 128:(tt + 1) * 128],
                              pt.rearrange("p (c t) -> p c t", t=128), [ptn], [hTn],
                              eng=('act' if (b0 // 4) % 2 == 0 else 'dve'))
                self.transpose_to(hb, hn, 16, dst, tog, views=[(self.PT6, 'pb6'), (self.PT, 'pb7')])

            stats(0)
            for tt in range(NT):
                if tt + 1 < NT:
                    stats(tt + 1)
                xpose(tt)
            k.barrier()

    def mk_wpool(self, st, name, kc, n, dd=2):
        return {'i': 0, 'name': name, 'dd': dd,
                'wf': [self.sbt(st, name + "f%d" % i, [128, kc, n]) for i in range(dd)],
                'wb': [self.sbt(st, name + "b%d" % i, [128, kc, n], BF16) for i in range(2)]}

    def w_dma(self, pool, srcs, kc, n, pp=128):
        i = pool['i'] % pool['dd']
        pool['i'] += 1
        wf = pool['wf'][i]
        for si, (c0, src) in enumerate(srcs):
            nco = src.shape[1]
            self.k.dma(wf[0:pp, 0:kc, c0:c0 + nco], src.rearrange("(c p) n -> p c n", p=pp), [],
                       ["%sf%d_%d" % (pool['name'], i, si)])
        return i

    def w_cast(self, pool, i, nsrc, kc, n, pp=128):
        j = pool.get('ci', 0) % 2
        pool['ci'] = pool.get('ci', 0) + 1
        wf, wb = pool['wf'][i], pool['wb'][j]
        rd = ["%sf%d_%d" % (pool['name'], i, si) for si in range(nsrc)]
        wbn = "%sb%d" % (pool['name'], j)
        half = max(1, kc // 2)
        for c0 in range(0, kc, half):
            c1 = min(kc, c0 + half)
            eng = ('dve', 'act')[self.cast_rr % 2]
            self.cast_rr += 1
            self.cast(wb[0:pp, c0:c1, 0:n], wf[0:pp, c0:c1, 0:n], rd, [wbn], eng=eng)
        return wb, wbn

    def run_pipeline(self, items):
        n = len(items)
        dd = items[0][0][0][0]['dd']
        dm = lambda it: [self.w_dma(ld[0], ld[1], ld[2], ld[3]) for ld in it[0]]
        cs = lambda it, sl: [self.w_cast(ld[0], s_, len(ld[1]), ld[2], ld[3]) for ld, s_ in zip(it[0], sl)]
        slots = {}
        for j in range(min(dd, n)):
            slots[j] = dm(items[j])
        ready = cs(items[0], slots[0])
        for i in range(n):
            cur = ready
            if i + dd < n:
                slots[i + dd] = dm(items[i + dd])
            if i + 1 < n:
                ready = cs(items[i + 1], slots[i + 1])
            items[i][1](cur)

    def stage_proj(self, l, hT, hTn):
        k = self.k
        W = self.w_in
        PB = self.PB
        with ExitStack() as st:
            wp = self.mk_wpool(st, "pw", KC, 256)
            fa = [self.sbt(st, "fa%d" % i, [128, T]) for i in range(3)]
            xpad = [self.sbt(st, "xpad%d" % i, [128, T + 3]) for i in range(2)]
            stg32 = self.sbt(st, "stg32", [128, NT, 32])
            ba = [self.sbt(st, "ba%d" % i, [128, T], BF16) for i in range(3)]
            tms = [self.sbt(st, "tms%d" % i, [128, NT, 256], BF16) for i in range(2)]
            cw = self.sbt(st, "cw", [128, 12, 4])
            cb = self.sbt(st, "cb", [128, 12])
            for b in range(12):
                k.dma(cw[:, b, :], self.ssd_conv_w[l, :, b * 128:(b + 1) * 128].rearrange("k p -> p k"),
                      [], ['cw%d' % b], allow_slow_non_contiguous=True)
            k.dma(cb[:], self.ssd_conv_b[l].rearrange("(b p) -> p b", p=128), [], ['cb'],
                  allow_slow_non_contiguous=True)
            for i in range(2):
                k.op('pool', [], ["xpad%d" % i], lambda e: e.memset(xpad[i][:, 0:3], 0.0))
            cnt = {'fa': 0, 'ba': 0, 'tms': 0, 'bank': 0, 'xpad': 0, 'ev': 0}
            tog = [0]

            def nxt(key, n):
                i = cnt[key] % n
                cnt[key] += 1
                return i

            items = []

            def add_fm(col0, nblk, start, evac, final):
                for c0 in range(0, nblk * 128, 256):
                    n = min(256, nblk * 128 - c0)

                    def fn(ws, c0=c0, n=n):
                        wb, wbn = ws[0]
                        for sub in range(n // 128):
                            bj = (c0 + sub * 128) // 128
                            ctx = start(bj)
                            for tb in range(4):
                                b = nxt('bank', 7)
                                for kc in range(KC):
                                    k.op('pe', [wbn, hTn], ['pb%d' % b], lambda e: e.matmul(
                                        PB[b][:, :], lhsT=wb[:, kc, sub * 128:(sub + 1) * 128],
                                        rhs=hT[:, kc, tb * 512:(tb + 1) * 512],
                                        start=(kc == 0), stop=(kc == KC - 1)))
                                evac(ctx, tb, b)
                            final(bj, ctx)
                    items.append(([(wp, [(0, W[l, :, col0 + c0: col0 + c0 + n])], KC, n)], fn))

            def ev(o, b, dname, func):
                if func is None:
                    self.cast(o, PB[b][:, :], ['pb%d' % b], [dname], eng=('act', 'dve')[nxt('ev', 2)])
                else:
                    k.op('act', ['pb%d' % b], [dname], lambda e: e.activation(out=o, in_=PB[b][:, :], func=func))

            def to_tm(o, on, dst):
                i = nxt('tms', 2)
                stg, sn = tms[i], "tms%d" % i

                def dfn(b0, nb, pt, ptn):
                    self.cast(stg[:, b0:b0 + nb, 0:128], pt.rearrange("p (c t) -> p c t", t=128),
                              [ptn], [sn], eng=('act' if (b0 // 4) % 2 == 0 else 'dve'))
                self.transpose_to(o, on, 16, dfn, tog)
                k.dma(dst.rearrange("(tt p) c -> p tt c", p=128), stg[:, :, 0:128], [sn], [])

            def xbc_start(bj):
                i = nxt('xpad', 2)
                return (xpad[i], "xpad%d" % i)

            def xbc_evac(ctx, tb, b):
                ev(ctx[0][:, 3 + tb * 512: 3 + (tb + 1) * 512], b, ctx[1], None)

            def xbc_final(bj, ctx):
                xp, xpn = ctx
                i2 = nxt('fa', 3)
                acc, an = fa[i2], "fa%d" % i2
                k.op('dve', [xpn, 'cw%d' % bj, 'cb'], [an], lambda e: e.tensor_scalar(
                    out=acc[:, 0:T], in0=xp[:, 0:T], scalar1=cw[:, bj, 0:1], scalar2=cb[:, bj:bj + 1],
                    op0=ALU.mult, op1=ALU.add))
                for kk in range(1, 4):
                    k.op('dve', [xpn, 'cw%d' % bj, an], [an], lambda e: e.scalar_tensor_tensor(
                        out=acc[:, 0:T], in0=xp[:, kk:kk + T], scalar=cw[:, bj, kk:kk + 1],
                        in1=acc[:, 0:T], op0=ALU.mult, op1=ALU.add))
                i3 = nxt('ba', 3)
                o, on = ba[i3], "ba%d" % i3
                k.op('act', [an], [on], lambda e: e.activation(out=o[:], in_=acc[:, 0:T], func=AF.Silu))
                if bj < 8:
                    to_tm(o, on, self.xs_tm[:, bj * 128:(bj + 1) * 128])
                elif bj < 10:
                    g = bj - 8
                    k.dma(self.BT[g], o[:], [on], [])
                    to_tm(o, on, self.B_tm[:, g * 128:(g + 1) * 128])
                else:
                    k.dma(self.CT[bj - 10], o[:], [on], [])

            def mk_plain(pool, pname, npool, func, final):
                def start(bj):
                    i = nxt(pname, npool)
                    return (pool[i], "%s%d" % (pname, i))

                def evac(ctx, tb, b):
                    ev(ctx[0][:, tb * 512:(tb + 1) * 512], b, ctx[1], func)
                return start, evac, final

            def fin_rows(dst):
                return lambda bj, ctx: k.dma(dst[bj * 128:(bj + 1) * 128, :], ctx[0][:], [ctx[1]], [])

            def fin_qk(dst):
                def f(bj, ctx):
                    k.dma(dst[2 * bj, 0:64, :], ctx[0][0:64, :], [ctx[1]], [])
                    k.dma(dst[2 * bj + 1, 0:64, :], ctx[0][64:128, :], [ctx[1]], [])
                return f

            def add_tm(col0, ncols, dst, func, f32out=False, srcs=None):
                for c0 in range(0, ncols, 256):
                    n = min(256, ncols - c0)

                    def fn(ws, c0=c0, n=n):
                        wb, wbn = ws[0]
                        if f32out:
                            stg, sn = stg32, 'stg32'
                        else:
                            i = nxt('tms', 2)
                            stg, sn = tms[i], "tms%d" % i
                        for tt in range(NT):
                            b = nxt('bank', 7)
                            for kc in range(KC):
                                k.op('pe', [wbn, hTn], ['pb%d' % b], lambda e: e.matmul(
                                    PB[b][:, 0:n], lhsT=hT[:, kc, tt * 128:(tt + 1) * 128], rhs=wb[:, kc, 0:n],
                                    start=(kc == 0), stop=(kc == KC - 1)))
                            o = stg[:, tt, 0:n]
                            if func is None:
                                self.cast(o, PB[b][:, 0:n], ['pb%d' % b], [sn], eng=('act', 'dve')[nxt('ev', 2)])
                            else:
                                k.op('act', ['pb%d' % b], [sn], lambda e: e.activation(out=o, in_=PB[b][:, 0:n], func=func))
                        if f32out:
                            k.dma(dst.rearrange("(tt p) c -> p tt c", p=128), stg[:, :, 0:n], [sn], [])
                        else:
                            k.dma(dst[:, c0:c0 + n].rearrange("(tt p) c -> p tt c", p=128), stg[:, :, 0:n], [sn], [])
                    ss_ = srcs if srcs is not None else [(0, W[l, :, col0 + c0: col0 + c0 + n])]
                    items.append(([(wp, ss_, KC, n)], fn))

            add_tm(0, 32, self.dtff, None, f32out=True,
                   srcs=[(0, W[l, :, O_DT:O_DT + 16]), (16, W[l, :, O_FF:O_FF + 16])])
            add_tm(O_Z, 1024, self.zs_tm, AF.Silu)
            add_tm(O_HI, 1024, self.hv_tm, None)
            add_tm(O_FV, 1024, self.fv_tm, None)
            add_fm(O_XBC, 12, xbc_start, xbc_evac, xbc_final)
            add_fm(O_HQ, 8, *mk_plain(ba, 'ba', 3, AF.Silu, fin_rows(self.hqT)))
            add_fm(O_HF, 8, *mk_plain(fa, 'fa', 3, AF.Sigmoid, fin_rows(self.hsigT)))
            add_fm(O_HG, 8, *mk_plain(ba, 'ba', 3, AF.Silu, fin_rows(self.hgT)))
            add_fm(O_FQ, 8, *mk_plain(ba, 'ba', 3, None, fin_qk(self.fqA)))
            add_fm(O_FK, 8, *mk_plain(ba, 'ba', 3, None, fin_qk(self.fkA)))
            add_fm(O_G, 48, *mk_plain(ba, 'ba', 3, AF.Sigmoid, fin_rows(self.gT)))
            self.run_pipeline(items)
            k.barrier()


    def stage_ssd(self, l):
        k = self.k
        PB, PT = self.PB, self.PT
        with ExitStack() as st:
            sb = lambda n, s, dt=F32: self.sbt(st, n, s, dt)
            dtb = sb("s_dtb", [128, 16]); alog = sb("s_alog", [128, 16]); dsk = sb("s_dsk", [128, 16])
            aneg = sb("s_aneg", [128, 16]); nw = sb("s_nw", [128, 1024])
            k.dma(dtb[:], self.ssd_dt_bias[l].partition_broadcast(128), [], ['s_dtb'])
            k.dma(alog[:], self.ssd_a_log[l].partition_broadcast(128), [], ['s_alog'])
            k.dma(dsk[:], self.ssd_d[l].partition_broadcast(128), [], ['s_dsk'])
            k.dma(nw[:], self.ssd_norm_w[l].partition_broadcast(128), [], ['s_nw'])
            k.op('act', ['s_alog'], ['s_aneg'], lambda e: e.activation(out=aneg[:], in_=alog[:], func=AF.Exp))
            k.op('dve', ['s_aneg'], ['s_aneg'], lambda e: e.tensor_scalar(
                out=aneg[:], in0=aneg[:], scalar1=-1.0, scalar2=None, op0=ALU.mult))
            S = sb("s_S", [128, 1024]); Sbf = sb("s_Sbf", [128, 1024], BF16)
            k.op('pool', [], ['s_S'], lambda e: e.memset(S[:], 0.0))
            k.op('pool', [], ['s_Sbf'], lambda e: e.memset(Sbf[:], 0.0))
            yTs = sb("s_yTs", [128, 8, T], BF16)
            P2 = {}

            def pool2(name, shape, dt=F32):
                P2[name] = [sb("%s%d" % (name, i), shape, dt) for i in range(2)]
            for nm, shp, dt in [("s_xs", [128, 1024], BF16), ("s_zs", [128, 1024], BF16),
                                ("s_btm", [128, 256], BF16), ("s_bt", [128, 2, 128], BF16),
                                ("s_ct", [128, 2, 128], BF16), ("s_df", [128, 32], F32),
                                ("s_sm", [128, 8, 16], F32), ("s_edec", [128, 48], F32),
                                ("s_cbm", [128, 2, 128], F32), ("s_R", [128, 8, 128], F32),
                                ("s_E", [128, 1024], F32), ("s_MT", [128, 16, 128], BF16),
                                ("s_xdt", [128, 1024], BF16), ("s_xdd", [128, 1024], BF16),
                                ("s_yo", [128, 1024], F32), ("s_y", [128, 1024], F32),
                                ("s_tmp", [128, 1024], F32), ("s_yn", [128, 1024], BF16),
                                ("s_ss", [128, 2], F32), ("s_junk", [128, 512], BF16)]:
                pool2(nm, shp, dt)
            def phase(c, which):
                i = c % 2
                g_ = lambda nm: (P2[nm][i], "%s%d" % (nm, i))
                xs, xsn = g_("s_xs"); zs, zsn = g_("s_zs"); btm, btmn = g_("s_btm")
                bt, btn = g_("s_bt"); ct, ctn = g_("s_ct"); df, dfn = g_("s_df")
                sm, smn = g_("s_sm"); edec, edn = g_("s_edec"); cbm, cbn = g_("s_cbm")
                E_, En = g_("s_E"); MT, MTn = g_("s_MT"); xdt, xdtn = g_("s_xdt")
                xdd, xddn = g_("s_xdd"); yo, yon = g_("s_yo"); y, yn_ = g_("s_y")
                tmp, tmpn = g_("s_tmp"); yn, ynn = g_("s_yn"); ss, ssn = g_("s_ss")
                junk, jn = g_("s_junk")
                R, Rn = g_("s_R")

                x16, ax, e16, l16, r16, dt, a, dtd = [sm[:, j, :] for j in range(8)]
                xs3 = xs[:].rearrange("p (h d) -> p h d", h=16)
                if which == 'A':
                    ts = slice(c * 128, (c + 1) * 128)
                    k.dma(xs[:], self.xs_tm[ts, :], [], [xsn])
                    k.dma(zs[:], self.zs_tm[ts, :], [], [zsn])
                    k.dma(btm[:], self.B_tm[ts, :], [], [btmn])
                    k.dma(bt[:], self.BT[:, :, ts].rearrange("g n t -> n g t"), [], [btn])
                    k.dma(ct[:], self.CT[:, :, ts].rearrange("g n t -> n g t"), [], [ctn])
                    k.dma(df[:], self.dtff[ts, :], [], [dfn])
                    k.op('dve', [dfn, 's_dtb'], [smn], lambda e: e.tensor_tensor(out=x16, in0=df[:, 0:16], in1=dtb[:], op=ALU.add))
                    k.op('act', [smn], [smn], lambda e: e.activation(out=ax, in_=x16, func=AF.Abs))
                    k.op('act', [smn], [smn], lambda e: e.activation(out=e16, in_=ax, func=AF.Exp, scale=-1.0))
                    k.op('act', [smn], [smn], lambda e: e.activation(out=l16, in_=e16, func=AF.Ln, bias=1.0))
                    k.op('dve', [smn], [smn], lambda e: e.tensor_scalar_max(out=r16, in0=x16, scalar1=0.0))
                    k.op('dve', [smn], [smn], lambda e: e.tensor_tensor(out=dt, in0=r16, in1=l16, op=ALU.add))
                    k.op('dve', [smn, 's_aneg'], [smn], lambda e: e.tensor_tensor(out=a, in0=dt, in1=aneg[:], op=ALU.mult))
                    for j, (m, mn) in enumerate([(self.triI, 'triI'), (self.triSU, 'triSU'), (self.ones_f, 'ones_f')]):
                        k.op('pe', [mn, smn], ['pb0'], lambda e: e.matmul(
                            PB[0][:, j * 16:(j + 1) * 16], lhsT=m[:], rhs=a, start=True, stop=True))
                    k.op('act', ['pb0'], [edn], lambda e: e.activation(out=edec[:], in_=PB[0][:, 0:48], func=AF.Exp))
                    for g in range(2):
                        k.op('pe', [btn, ctn], ['pb0'], lambda e: e.matmul(
                            PB[0][:, 256 + g * 128: 256 + (g + 1) * 128], lhsT=bt[:, g, :], rhs=ct[:, g, :],
                            start=True, stop=True))
                    k.op('dve', ['pb0', 'triI'], [cbn], lambda e: e.tensor_tensor(
                        out=cbm[:], in0=PB[0][:, 256:512].rearrange("p (g l) -> p g l", g=2),
                        in1=self.triI[:].unsqueeze(1).to_broadcast([128, 2, 128]), op=ALU.mult))
                    for g in range(2):
                        k.op('pool', ['triI', smn], [Rn], lambda e: e.tensor_tensor(
                            out=R[:], in0=self.triI[:].unsqueeze(1).to_broadcast([128, 8, 128]),
                            in1=a[:, g * 8:(g + 1) * 8].unsqueeze(2).to_broadcast([128, 8, 128]), op=ALU.mult))
                        for hh in range(2):
                            k.op('pe', ['triSU', Rn], ['pb%d' % (1 + hh)], lambda e: e.matmul(
                                PB[1 + hh][:, :], lhsT=self.triSU[:],
                                rhs=R[:, hh * 4:(hh + 1) * 4, :].rearrange("p h l -> p (h l)"), start=True, stop=True))
                            k.op('act', ['pb%d' % (1 + hh)], [En], lambda e: e.activation(
                                out=E_[:, hh * 512:(hh + 1) * 512], in_=PB[1 + hh][:, :], func=AF.Exp))
                        k.op('dve', [En, cbn], [MTn], lambda e: e.tensor_tensor(
                            out=MT[:, g * 8:(g + 1) * 8, :], in0=E_[:].rearrange("p (h l) -> p h l", h=8),
                            in1=cbm[:, g, :].unsqueeze(1).to_broadcast([128, 8, 128]), op=ALU.mult))
                    k.op('dve', [xsn, smn], [xdtn], lambda e: e.tensor_tensor(
                        out=xdt[:].rearrange("p (h d) -> p h d", h=16), in0=xs3,
                        in1=dt.unsqueeze(2).to_broadcast([128, 16, 64]), op=ALU.mult))
                    k.op('dve', [smn, edn], [smn], lambda e: e.tensor_tensor(out=dtd, in0=dt, in1=edec[:, 16:32], op=ALU.mult))
                    k.op('pool', [xsn, smn], [xddn], lambda e: e.tensor_tensor(
                        out=xdd[:].rearrange("p (h d) -> p h d", h=16), in0=xs3,
                        in1=dtd.unsqueeze(2).to_broadcast([128, 16, 64]), op=ALU.mult))

                    return
                for h in range(16):
                    b = 3 + h // 8
                    k.op('pe', [MTn, xdtn], ['pb%d' % b], lambda e: e.matmul(
                        PB[b][:, (h % 8) * 64:(h % 8 + 1) * 64], lhsT=MT[:, h, :], rhs=xdt[:, h * 64:(h + 1) * 64],
                        start=True, stop=True))
                for g in range(2):
                    k.op('pe', [ctn, 's_Sbf'], ['pb%d' % (5 + g)], lambda e: e.matmul(
                        PB[5 + g][:, :], lhsT=ct[:, g, :], rhs=Sbf[:, g * 512:(g + 1) * 512], start=True, stop=True))
                for g in range(2):
                    k.op('dve', ['pb%d' % (5 + g), edn], [yon], lambda e: e.tensor_tensor(
                        out=yo[:, g * 512:(g + 1) * 512].rearrange("p (h d) -> p h d", h=8),
                        in0=PB[5 + g][:, :].rearrange("p (h d) -> p h d", h=8),
                        in1=edec[:, g * 8:(g + 1) * 8].unsqueeze(2).to_broadcast([128, 8, 64]), op=ALU.mult))
                    k.op('dve', [yon, 'pb%d' % (3 + g)], [yn_], lambda e: e.tensor_tensor(
                        out=y[:, g * 512:(g + 1) * 512], in0=yo[:, g * 512:(g + 1) * 512], in1=PB[3 + g][:, :], op=ALU.add))
                k.op('pool', [xsn, 's_dsk'], [tmpn], lambda e: e.tensor_tensor(
                    out=tmp[:].rearrange("p (h d) -> p h d", h=16), in0=xs3,
                    in1=dsk[:].unsqueeze(2).to_broadcast([128, 16, 64]), op=ALU.mult))
                k.op('pool', [yn_, tmpn], [yn_], lambda e: e.tensor_tensor(out=y[:], in0=y[:], in1=tmp[:], op=ALU.add))
                for g in range(2):
                    k.op('pe', [btmn, xddn], ['pb%d' % (5 + g)], lambda e: e.matmul(
                        PB[5 + g][:, :], lhsT=btm[:, g * 128:(g + 1) * 128], rhs=xdd[:, g * 512:(g + 1) * 512],
                        start=True, stop=True))
                k.op('dve', ['s_S', edn], ['s_S'], lambda e: e.tensor_tensor(
                    out=S[:].rearrange("p (h d) -> p h d", h=16), in0=S[:].rearrange("p (h d) -> p h d", h=16),
                    in1=edec[:, 32:48].unsqueeze(2).to_broadcast([128, 16, 64]), op=ALU.mult))
                for g in range(2):
                    k.op('dve', ['s_S', 'pb%d' % (5 + g)], ['s_S'], lambda e: e.tensor_tensor(
                        out=S[:, g * 512:(g + 1) * 512], in0=S[:, g * 512:(g + 1) * 512], in1=PB[5 + g][:, :], op=ALU.add))
                k.op('act', ['s_S'], ['s_Sbf'], lambda e: e.activation(out=Sbf[:], in_=S[:], func=AF.Copy))
                k.op('dve', [yn_, zsn], [yn_], lambda e: e.tensor_tensor(out=y[:], in0=y[:], in1=zs[:], op=ALU.mult))
                for g in range(2):
                    k.op('act', [yn_], [jn, ssn], lambda e: e.activation(
                        out=junk[:], in_=y[:, g * 512:(g + 1) * 512], func=AF.Square, accum_out=ss[:, g:g + 1]))
                k.op('dve', [ssn], [ssn], lambda e: e.tensor_scalar(
                    out=ss[:], in0=ss[:], scalar1=1.0 / 512, scalar2=EPS, op0=ALU.mult, op1=ALU.add))
                k.op('act', [ssn], [ssn], lambda e: e.sqrt(out=ss[:], in_=ss[:]))
                k.op('dve', [ssn], [ssn], lambda e: e.reciprocal(out=ss[:], in_=ss[:]))
                for g in range(2):
                    k.op('dve', [yn_, ssn, 's_nw'], [ynn], lambda e: e.scalar_tensor_tensor(
                        out=yn[:, g * 512:(g + 1) * 512], in0=y[:, g * 512:(g + 1) * 512], scalar=ss[:, g:g + 1],
                        in1=nw[:, g * 512:(g + 1) * 512], op0=ALU.mult, op1=ALU.mult))
                tog = [0]

                def dfn2(b0, nb, pt, ptn, c=c):
                    self.cast(yTs[:, b0:b0 + nb, c * 128:(c + 1) * 128], pt.rearrange("p (b t) -> p b t", t=128),
                              [ptn], ['s_yTs'], eng=('act' if (b0 // 4) % 2 == 0 else 'dve'))
                self.transpose_to(yn, ynn, 8, dfn2, tog)
            phase(0, 'A')
            for c in range(NT):
                if c + 1 < NT:
                    phase(c + 1, 'A')
                phase(c, 'B')
            k.dma(self.ysT.rearrange("(b p) t -> p b t", p=128), yTs[:], ['s_yTs'], [])
            k.barrier()

    def stage_hgrn(self, l):
        k = self.k
        PB, PT = self.PB, self.PT
        with ExitStack() as st:
            sb = lambda n, s, dt=F32: self.sbt(st, n, s, dt)
            cmask = sb("h_cmask", [128, T]); mask64 = sb("h_m64", [64, 64])
            lb = sb("h_lb", [128, 8]); oml = sb("h_oml", [128, 8]); nw = sb("h_nw", [128, 8])
            lb0 = sb("h_lb0", [128, 8])
            k.op('pool', [], ['h_cmask'], lambda e: e.memset(cmask[:], 1.0))
            k.op('pool', ['h_cmask'], ['h_cmask'], lambda e: e.memset(
                cmask[:].rearrange("p (c t) -> p c t", t=64)[:, :, 0:1], 0.0))
            k.op('pool', ['ones_f'], ['h_m64'], lambda e: e.affine_select(
                out=mask64[:], in_=self.ones_f[0:64, 0:64], pattern=[[1, 64]], compare_op=ALU.is_ge,
                fill=0.0, base=0, channel_multiplier=-1))
            k.dma(nw[:], self.hgrn_norm_w[l].rearrange("(h p) -> p h", p=128), [], ['h_nw'],
                  allow_slow_non_contiguous=True)
            if l == 0:
                k.op('pool', [], ['h_lb'], lambda e: e.memset(lb[:], 0.0))
            else:
                k.dma(lb0[:], self.hgrn_lb[0].rearrange("(h p) -> p h", p=128), [], ['h_lb0'],
                      allow_slow_non_contiguous=True)
                k.dma(lb[:], self.hgrn_lb[1].rearrange("(h p) -> p h", p=128), [], ['h_lb'],
                      allow_slow_non_contiguous=True)
                k.op('dve', ['h_lb', 'h_lb0'], ['h_lb'], lambda e: e.tensor_tensor(out=lb[:], in0=lb[:], in1=lb0[:], op=ALU.subtract))
                k.op('act', ['h_lb'], ['h_lb'], lambda e: e.activation(out=lb[:], in_=lb[:], func=AF.Sigmoid))
            k.op('dve', ['h_lb'], ['h_oml'], lambda e: e.tensor_scalar(
                out=oml[:], in0=lb[:], scalar1=-1.0, scalar2=1.0, op0=ALU.mult, op1=ALU.add))
            epsb = sb("h_eps", [128, 1])
            k.op('pool', [], ['h_eps'], lambda e: e.memset(epsb[:], EPS))
            osq = sb("h_osq", [128, 512], BF16); rt = sb("h_rt", [128, 512]); t1 = sb("h_t1", [128, 512])
            HS = []
            for p in range(2):
                d = {}
                for nm in ("sig", "f", "b", "eb", "ktf"):
                    d[nm] = (sb("h%d_%s" % (p, nm), [128, T]), "h%d_%s" % (p, nm))
                for nm in ("q", "g", "qt", "kt", "kh", "yh"):
                    d[nm] = (sb("h%d_%s" % (p, nm), [128, T], BF16), "h%d_%s" % (p, nm))
                d["v"] = (sb("h%d_v" % p, [64, 32, 128], BF16), "h%d_v" % p)
                d["S"] = (sb("h%d_S" % p, [128, 128]), "h%d_S" % p)
                d["Sbf"] = [(sb("h%d_Sbf%d" % (p, i), [128, 128], BF16), "h%d_Sbf%d" % (p, i)) for i in range(2)]
                d["smT"] = (sb("h%d_smT" % p, [64, 8, 64], BF16), "h%d_smT" % p)
                d["khT"] = (sb("h%d_khT" % p, [64, 8, 128], BF16), "h%d_khT" % p)
                d["ps_s"] = (PB[p], 'pb%d' % p)
                d["ps_o"] = (PB[2 + p], 'pb%d' % (2 + p))
                d["ps_k"] = (PB[4 + p], 'pb%d' % (4 + p))
                HS.append(d)

            def front(d, h):
                hs = slice(h * 128, (h + 1) * 128)
                sig, sgn = d["sig"]; fB, fn_ = d["f"]; bB, bn = d["b"]; eb, ebn = d["eb"]; ktf, ktfn = d["ktf"]
                q, qn = d["q"]; gg, gn = d["g"]; qt, qtn = d["qt"]; kt, ktn = d["kt"]; kh, khn = d["kh"]
                v, vn = d["v"]; S, Sn = d["S"]
                k.dma(sig[:], self.hsigT[hs, :], [], [sgn])
                k.dma(q[:], self.hqT[hs, :], [], [qn])
                k.dma(gg[:], self.hgT[hs, :], [], [gn])
                k.dma(v[:], self.hv_tm[:, hs].rearrange("(c p) v -> p c v", p=64), [], [vn])
                k.op('dve', [sgn, 'h_oml', 'h_lb'], [fn_], lambda e: e.tensor_scalar(
                    out=fB[:], in0=sig[:], scalar1=oml[:, h:h + 1], scalar2=lb[:, h:h + 1], op0=ALU.mult, op1=ALU.add))
                k.op('act', [fn_], [sgn], lambda e: e.activation(out=sig[:], in_=fB[:], func=AF.Ln))
                k.op('dve', ['h_cmask', sgn], [bn], lambda e: e.tensor_tensor_scan(
                    out=bB[:], data0=cmask[:], data1=sig[:], initial=0.0, op0=ALU.mult, op1=ALU.add))
                k.op('pool', [fn_], [fn_], lambda e: e.tensor_scalar(
                    out=fB[:], in0=fB[:], scalar1=-1.0, scalar2=1.0, op0=ALU.mult, op1=ALU.add))
                k.op('act', [bn], [ebn], lambda e: e.activation(out=eb[:], in_=bB[:], func=AF.Exp))
                k.op('dve', [qn, ebn], [qtn], lambda e: e.tensor_tensor(out=qt[:], in0=q[:], in1=eb[:], op=ALU.mult))
                k.op('pool', [bn], [bn], lambda e: e.tensor_scalar(out=bB[:], in0=bB[:], scalar1=1.0e30, scalar2=-80.0, op0=ALU.min, op1=ALU.max))
                k.op('act', [bn], [bn], lambda e: e.activation(out=bB[:], in_=bB[:], func=AF.Exp, scale=-1.0))
                k.op('dve', [fn_, bn], [ktfn], lambda e: e.tensor_tensor(out=ktf[:], in0=fB[:], in1=bB[:], op=ALU.mult))
                k.op('act', [ktfn], [ktn], lambda e: e.activation(out=kt[:], in_=ktf[:], func=AF.Copy))
                k.op('pool', [ktfn, ebn], [khn], lambda e: e.tensor_tensor(
                    out=kh[:].rearrange("p (c t) -> p c t", t=64), in0=ktf[:].rearrange("p (c t) -> p c t", t=64),
                    in1=eb[:].rearrange("p (c t) -> p c t", t=64)[:, :, 63:64].to_broadcast([128, 32, 64]), op=ALU.mult))
                k.op('pool', [], [Sn], lambda e: e.memset(S[:], 0.0))
                k.op('pool', [], [d["Sbf"][0][1]], lambda e: e.memset(d["Sbf"][0][0][:], 0.0))

            def prep(d, cg):
                qt, qtn = d["qt"]; kt, ktn = d["kt"]; kh, khn = d["kh"]
                ps_s, psn = d["ps_s"]; smT, smn = d["smT"]; khT, khTn = d["khT"]
                for c in range(8):
                    tk = slice((cg * 8 + c) * 64, (cg * 8 + c + 1) * 64)
                    k.op('pe', [ktn, qtn], [psn], lambda e: e.matmul(
                        ps_s[0:64, c * 64:(c + 1) * 64], lhsT=kt[:, tk], rhs=qt[:, tk], start=True, stop=True))
                k.op('dve', [psn, 'h_m64'], [smn], lambda e: e.tensor_tensor(
                    out=smT[:], in0=ps_s[0:64, :].rearrange("p (c t) -> p c t", t=64),
                    in1=mask64[:].unsqueeze(1).to_broadcast([64, 8, 64]), op=ALU.mult))
                for c in range(8):
                    tk = slice((cg * 8 + c) * 64, (cg * 8 + c + 1) * 64)
                    k.op('pe', [khn, 'ident_b'], ['pb7'], lambda e: e.transpose(
                        out=PT[0:64, c * 128:(c + 1) * 128], in_=kh[:, tk], identity=self.ident_b[:]))
                k.op('act', ['pb7'], [khTn], lambda e: e.activation(
                    out=khT[:], in_=PT[0:64, :].rearrange("p (c k) -> p c k", k=128), func=AF.Copy))

            def kv4(d, cg, c0):
                khT, khTn = d["khT"]; v, vn = d["v"]; pk, pkn = d["ps_k"]
                for c in range(c0, c0 + 4):
                    cc = cg * 8 + c
                    k.op('pe', [khTn, vn], [pkn], lambda e: e.matmul(
                        pk[:, (c % 4) * 128:(c % 4 + 1) * 128], lhsT=khT[:, c, :], rhs=v[:, cc, :], start=True, stop=True))

            def chunk(d, cg, c):
                cc = cg * 8 + c
                tk = slice(cc * 64, (cc + 1) * 64)
                v, vn = d["v"]; smT, smn = d["smT"]; qt, qtn = d["qt"]; eb, ebn = d["eb"]
                ps_o, pon = d["ps_o"]; pk, pkn = d["ps_k"]; S, Sn = d["S"]
                sb0, sb0n = d["Sbf"][cc % 2]; sb1, sb1n = d["Sbf"][(cc + 1) % 2]
                pks = pk[:, (c % 4) * 128:(c % 4 + 1) * 128]
                k.op('pe', [vn, smn], [pon], lambda e: e.matmul(
                    ps_o[:, c * 64:(c + 1) * 64], lhsT=v[:, cc, :], rhs=smT[:, c, :], start=True, stop=False))
                k.op('pe', [sb0n, qtn], [pon], lambda e: e.matmul(
                    ps_o[:, c * 64:(c + 1) * 64], lhsT=sb0[:], rhs=qt[:, tk], start=False, stop=True))
                esc = eb[:, cc * 64 + 63: cc * 64 + 64]
                k.op('dve', [Sn, ebn, pkn], [sb1n], lambda e: e.scalar_tensor_tensor(
                    out=sb1[:], in0=S[:], scalar=esc, in1=pks, op0=ALU.mult, op1=ALU.add))
                k.op('dve', [Sn, ebn, pkn], [Sn], lambda e: e.scalar_tensor_tensor(
                    out=S[:], in0=S[:], scalar=esc, in1=pks, op0=ALU.mult, op1=ALU.add))

            def norm(d, h, cg):
                ps_o, pon = d["ps_o"]; gg, gn = d["g"]; yh, yhn = d["yh"]
                k.op('act', [pon], ['h_osq'], lambda e: e.activation(out=osq[:], in_=ps_o[:, :], func=AF.Square))
                k.op('pe', ['ones_b', 'h_osq'], ['pb6'], lambda e: e.matmul(
                    PB[6][:, :], lhsT=self.ones_b[:], rhs=osq[:], start=True, stop=True))
                k.op('act', ['pb6', 'h_eps'], ['h_rt'], lambda e: e.activation(
                    out=rt[:], in_=PB[6][:, :], func=AF.Ln, scale=1.0 / 128, bias=epsb[:]))
                k.op('act', ['h_rt'], ['h_rt'], lambda e: e.activation(out=rt[:], in_=rt[:], func=AF.Exp, scale=-0.5))
                k.op('dve', [pon, 'h_rt'], ['h_t1'], lambda e: e.tensor_tensor(out=t1[:], in0=ps_o[:, :], in1=rt[:], op=ALU.mult))
                k.op('dve', ['h_t1', 'h_nw', gn], [yhn], lambda e: e.scalar_tensor_tensor(
                    out=yh[:, cg * 512:(cg + 1) * 512], in0=t1[:], scalar=nw[:, h:h + 1],
                    in1=gg[:, cg * 512:(cg + 1) * 512], op0=ALU.mult, op1=ALU.mult))

            for hp in range(4):
                hh = [2 * hp, 2 * hp + 1]
                for p in range(2):
                    front(HS[p], hh[p])
                for cg in range(4):
                    for p in range(2):
                        prep(HS[p], cg)
                    for c0 in (0, 4):
                        for p in range(2):
                            kv4(HS[p], cg, c0)
                        for c in range(c0, c0 + 4):
                            for p in range(2):
                                chunk(HS[p], cg, c)
                    for p in range(2):
                        norm(HS[p], hh[p], cg)
                for p in range(2):
                    k.dma(self.yhT[hh[p] * 128:(hh[p] + 1) * 128, :], HS[p]["yh"][0][:], [HS[p]["yh"][1]], [])
            k.barrier()

    def stage_fox(self, l):
        k = self.k
        PB = self.PB
        with ExitStack() as st:
            sb = lambda n, s, dt=F32: self.sbt(st, n, s, dt)
            fb = sb("x_fb", [128, 16]); ffr = sb("x_ffr", [128, NT, 32])
            xx = sb("x_xx", [128, NT, 16]); ax = sb("x_ax", [128, NT, 16]); lf = sb("x_lf", [128, NT, 16])
            c8 = sb("x_c8", [16, T]); c32 = sb("x_c32", [16, T]); ones16 = sb("x_ones16", [16, T], BF16)
            sp_ = [sb("x_sp%d" % i, [16, T], BF16) for i in range(3)]
            sn_ = [sb("x_sn%d" % i, [16, T], BF16) for i in range(3)]
            k.dma(fb[:], self.fox_f_bias[l].partition_broadcast(128), [], ['x_fb'])
            k.dma(ffr[:], self.dtff.rearrange("(tt p) c -> p tt c", p=128), [], ['x_ffr'])
            k.op('pool', [], ['x_ones16'], lambda e: e.memset(ones16[:], 1.0))
            k.op('dve', ['x_ffr', 'x_fb'], ['x_xx'], lambda e: e.tensor_tensor(
                out=xx[:], in0=ffr[:, :, 16:32], in1=fb[:].unsqueeze(1).to_broadcast([128, NT, 16]), op=ALU.add))
            k.op('act', ['x_xx'], ['x_ax'], lambda e: e.activation(out=ax[:], in_=xx[:], func=AF.Abs))
            k.op('act', ['x_ax'], ['x_ax'], lambda e: e.activation(out=ax[:], in_=ax[:], func=AF.Exp, scale=-1.0))
            k.op('act', ['x_ax'], ['x_ax'], lambda e: e.activation(out=ax[:], in_=ax[:], func=AF.Ln, bias=1.0))
            k.op('dve', ['x_xx'], ['x_xx'], lambda e: e.tensor_scalar_min(out=xx[:], in0=xx[:], scalar1=0.0))
            k.op('dve', ['x_xx', 'x_ax'], ['x_lf'], lambda e: e.tensor_tensor(out=lf[:], in0=xx[:], in1=ax[:], op=ALU.subtract))
            for tb in range(4):
                for ti in range(4):
                    i = tb * 4 + ti
                    o = PB[0][0:16, ti * 128:(ti + 1) * 128]
                    for j in range(i):
                        k.op('pe', ['x_lf', 'ones_f'], ['pb0'], lambda e: e.matmul(
                            o, lhsT=lf[:, j, :], rhs=self.ones_f[:], start=(j == 0), stop=False))
                    k.op('pe', ['x_lf', 'triI'], ['pb0'], lambda e: e.matmul(
                        o, lhsT=lf[:, i, :], rhs=self.triI[:], start=(i == 0), stop=True))
                k.op('dve', ['pb0'], ['x_c8'], lambda e: e.tensor_scalar(
                    out=c8[:, tb * 512:(tb + 1) * 512], in0=PB[0][0:16, :], scalar1=8.0, scalar2=None, op0=ALU.mult))
            for j in range(3):
                k.op('dve', ['x_c8'], ['x_sp%d' % j], lambda e: e.tensor_copy(out=sp_[j][:], in_=c8[:]))
                k.op('dve', ['x_sp%d' % j], ['x_sn%d' % j], lambda e: e.tensor_scalar(
                    out=sn_[j][:], in0=sp_[j][:], scalar1=-1.0, scalar2=None, op0=ALU.mult))
                if j < 2:
                    k.op('dve', ['x_sp%d' % j], ['x_c32'], lambda e: e.tensor_copy(out=c32[:], in_=sp_[j][:]))
                    k.op('dve', ['x_c8', 'x_c32'], ['x_c8'], lambda e: e.tensor_tensor(out=c8[:], in0=c8[:], in1=c32[:], op=ALU.subtract))
            AUG = []
            for j in range(3):
                for (dst, row, src, sn) in [(self.fqA, 64 + j, sp_[j], 'x_sp%d' % j), (self.fqA, 67 + j, ones16, 'x_ones16'),
                                            (self.fkA, 64 + j, ones16, 'x_ones16'), (self.fkA, 67 + j, sn_[j], 'x_sn%d' % j)]:
                    nm = 'aug%d' % len(AUG)
                    k.dma(dst[:, row, :], src[:], [sn], [nm])
                    AUG.append(nm)
            madd = sb("x_madd", [128, 4, 512])
            for j in range(4):
                k.op('pool', ['zeros_f'], ['x_madd'], lambda e: e.affine_select(
                    out=madd[:, j, :], in_=self.zeros_f[:], pattern=[[1, 512]], compare_op=ALU.is_ge,
                    fill=-240000.0, base=-j * 128, channel_multiplier=-1))
            sel = sb("x_sel", [65, 64])
            k.op('pool', [], ['x_sel'], lambda e: e.memset(sel[:], 0.0))
            k.op('pool', ['x_sel'], ['x_sel'], lambda e: e.memset(sel[64:65, :], 1.0))
            QA = [sb("x_QA%d" % i, [128, T], BF16) for i in range(2)]
            KA = [sb("x_KA%d" % i, [128, T], BF16) for i in range(2)]
            V = [sb("x_V%d" % i, [128, NT, 128], BF16) for i in range(2)]
            for i in range(2):
                k.op('pool', [], ['x_V%d' % i], lambda e: e.memset(V[i][:], 0.0))
                k.op('pool', ['x_V%d' % i], ['x_V%d' % i], lambda e: e.memset(V[i][:, :, 64:65], 1.0))
                k.op('pool', [], ['x_QA%d' % i], lambda e: e.memset(QA[i][:], 0.0))
                k.op('pool', [], ['x_KA%d' % i], lambda e: e.memset(KA[i][:], 0.0))
            Pt = [sb("x_P%d" % i, [128, 512], BF16) for i in range(6)]
            smk = [sb("x_smk%d" % i, [128, 512]) for i in range(3)]
            osb = [sb("x_osb%d" % i, [65, 512]) for i in range(2)]
            rl = sb("x_rl", [64, 512])
            yf = [sb("x_yf%d" % i, [64, T], BF16) for i in range(2)]

            def load_head(h):
                i = h % 2
                k.dma(QA[i][0:70, :], self.fqA[h], AUG, ['x_QA%d' % i])
                k.dma(KA[i][0:70, :], self.fkA[h], AUG, ['x_KA%d' % i])
                k.dma(V[i][:, :, 0:64], self.fv_tm[:, h * 64:(h + 1) * 64].rearrange("(tt p) d -> p tt d", p=128),
                      [], ['x_V%d' % i])

            seq = [(h, qb, kb) for h in range(16) for qb in range(4) for kb in range(4 * (qb + 1))]
            SB = [0, 1, 2, 3, 7]
            LA = 4

            def emit_S(idx):
                h, qb, kb = seq[idx]
                i = h % 2
                b = SB[idx % 5]
                q0 = max(0, kb - 4 * qb) * 128
                k.op('pe', ['x_KA%d' % i, 'x_QA%d' % i], ['pb%d' % b], lambda e: e.matmul(
                    PB[b][:, q0:512], lhsT=KA[i][:, kb * 128:(kb + 1) * 128], rhs=QA[i][:, qb * 512 + q0:(qb + 1) * 512],
                    start=True, stop=True))

            pending = []

            def emit_norm(h, qb, po, pon, oi):
                i = h % 2
                ob, obn = osb[oi], 'x_osb%d' % oi
                k.op('dve', [pon], [obn], lambda e: e.tensor_copy(out=ob[:], in_=po[0:65, :]))

                k.op('act', [obn], [obn], lambda e: e.activation(out=ob[64:65, :], in_=ob[64:65, :], func=AF.Ln))
                k.op('act', [obn], [obn], lambda e: e.activation(out=ob[64:65, :], in_=ob[64:65, :], func=AF.Exp, scale=-1.0))

                def rest():
                    k.op('pe', ['x_sel', obn], ['pb6'], lambda e: e.matmul(
                        PB[6][0:64, :], lhsT=sel[64:65, :], rhs=ob[64:65, :], start=True, stop=True))
                    k.op('dve', [obn, 'pb6'], ['x_yf%d' % i], lambda e: e.tensor_tensor(
                        out=yf[i][:, qb * 512:(qb + 1) * 512], in0=ob[0:64, :], in1=PB[6][0:64, :], op=ALU.mult))
                    if qb == 3:
                        k.dma(self.yfT[h], yf[i][:], ['x_yf%d' % i], [])
                return rest

            load_head(0)
            for j in range(LA):
                emit_S(j)
            oit = 0
            for idx, (h, qb, kb) in enumerate(seq):
                i = h % 2
                if qb == 0 and kb == 0 and h + 1 < 16:
                    load_head(h + 1)
                if idx + LA < len(seq):
                    emit_S(idx + LA)
                nkb = 4 * (qb + 1)
                b = SB[idx % 5]
                ps, psn = PB[b], 'pb%d' % b
                pt, ptn = Pt[idx % 6], 'x_P%d' % (idx % 6)
                j = kb - 4 * qb
                q0 = max(0, j) * 128
                if j >= 0:
                    sm_, smn = smk[kb % 3], 'x_smk%d' % (kb % 3)
                    k.op('dve', [psn, 'x_madd'], [smn], lambda e: e.tensor_tensor(
                        out=sm_[:, q0:512], in0=ps[:, q0:512], in1=madd[:, j, q0:512], op=ALU.add))
                    k.op('act', [smn], [ptn], lambda e: e.activation(out=pt[:, q0:512], in_=sm_[:, q0:512], func=AF.Exp, scale=0.125))
                else:
                    k.op('act', [psn], [ptn], lambda e: e.activation(out=pt[:], in_=ps[:, :], func=AF.Exp, scale=0.125))
                if kb == 0:
                    oit += 1
                po, pon = (PB[4], 'pb4') if oit % 2 == 0 else (PB[5], 'pb5')
                k.op('pe', ['x_V%d' % i, ptn], [pon], lambda e: e.matmul(
                    po[:, q0:512], lhsT=V[i][:, kb, :], rhs=pt[:, q0:512], start=(kb == 0), stop=(kb == nkb - 1)))
                if pending and pending[0][0] <= idx:
                    pending.pop(0)[1]()
                if kb == nkb - 1:
                    pending.append((idx + 2, emit_norm(h, qb, po, pon, oit % 2)))
            for _, f in pending:
                f()
            k.barrier()

    def stage_merge(self, l):
        k = self.k
        PB = self.PB
        with ExitStack() as st:
            sb = lambda n, s, dt=F32: self.sbt(st, n, s, dt)
            ys = sb("m_ys", [128, 8, T], BF16); yh = sb("m_yh", [128, 8, T], BF16); yf = sb("m_yf", [128, 8, T], BF16)
            for tb in range(4):
                tsl = slice(tb * 512, (tb + 1) * 512)
                k.dma(ys[:, :, tsl], self.ysT.rearrange("(b p) t -> p b t", p=128)[:, :, tsl], [], ['m_ys%d' % tb])
                k.dma(yh[:, :, tsl], self.yhT.rearrange("(b p) t -> p b t", p=128)[:, :, tsl], [], ['m_yh%d' % tb])
                k.dma(yf[:, :, tsl], self.yfT.rearrange("(b h2) d t -> (h2 d) b t", h2=2)[:, :, tsl], [], ['m_yf%d' % tb])
            wps = self.mk_wpool(st, "mws", 8, 128, dd=3)
            wph = self.mk_wpool(st, "mwh", 8, 128, dd=3)
            wpf = self.mk_wpool(st, "mwf", 8, 128, dd=3)
            gts = [[sb("m_g%d_%d" % (r, i), [128, T], BF16) for i in range(2)] for r in range(3)]
            m1 = [sb("m_m1_%d" % i, [128, 512]) for i in range(2)]
            m2 = [sb("m_m2_%d" % i, [128, 512]) for i in range(2)]
            mo = [sb("m_mo%d" % i, [128, T], BF16) for i in range(2)]
            state = {'it': 0}
            items = []
            ysrc = [(ys, 'm_ys'), (yh, 'm_yh'), (yf, 'm_yf')]

            def mk(db):
                def fn(ws):
                    gi = db % 2
                    for r in range(3):
                        k.dma(gts[r][gi][:], self.gT[r * D + db * 128: r * D + (db + 1) * 128, :], [], ['m_g%d_%d' % (r, gi)])
                    for tb in range(4):
                        i = state['it'] % 2
                        state['it'] += 1
                        bs = [3 * i, 3 * i + 1, 3 * i + 2]
                        tsl = slice(tb * 512, (tb + 1) * 512)
                        for r in range(3):
                            wb, wbn = ws[r]
                            yt, ytn = ysrc[r]
                            for kc in range(8):
                                k.op('pe', [wbn, ytn + str(tb)], ['pb%d' % bs[r]], lambda e: e.matmul(
                                    PB[bs[r]][:, :], lhsT=wb[:, kc, :], rhs=yt[:, kc, tsl], start=(kc == 0), stop=(kc == 7)))
                        a1, a1n = m1[i], 'm_m1_%d' % i
                        a2, a2n = m2[i], 'm_m2_%d' % i
                        k.op('dve', ['pb%d' % bs[0], 'm_g0_%d' % gi], [a1n], lambda e: e.tensor_tensor(
                            out=a1[:], in0=PB[bs[0]][:, :], in1=gts[0][gi][:, tsl], op=ALU.mult))
                        k.op('dve', ['pb%d' % bs[1], 'm_g1_%d' % gi], [a2n], lambda e: e.tensor_tensor(
                            out=a2[:], in0=PB[bs[1]][:, :], in1=gts[1][gi][:, tsl], op=ALU.mult))
                        k.op('dve', [a1n, a2n], [a1n], lambda e: e.tensor_tensor(out=a1[:], in0=a1[:], in1=a2[:], op=ALU.add))
                        k.op('dve', ['pb%d' % bs[2], 'm_g2_%d' % gi], [a2n], lambda e: e.tensor_tensor(
                            out=a2[:], in0=PB[bs[2]][:, :], in1=gts[2][gi][:, tsl], op=ALU.mult))
                        k.op('dve', [a1n, a2n], ['m_mo%d' % gi], lambda e: e.tensor_tensor(
                            out=mo[gi][:, tsl], in0=a1[:], in1=a2[:], op=ALU.add))
                    k.dma(self.mT[db * 128:(db + 1) * 128, :], mo[gi][:], ['m_mo%d' % gi], [])
                return fn
            for db in range(16):
                c0 = db * 128
                items.append(([(wps, [(0, self.w_branch_ssd[l, :, c0:c0 + 128])], 8, 128),
                               (wph, [(0, self.w_branch_hgrn[l, :, c0:c0 + 128])], 8, 128),
                               (wpf, [(0, self.w_branch_fox[l, :, c0:c0 + 128])], 8, 128)], mk(db)))
            self.run_pipeline(items)
            k.barrier()

    def stage_out_proj(self, AT_dram, nkc, Wsrc, xsrc, xdst, halves):
        k = self.k
        PB = self.PB
        TH = T // halves
        ntt = TH // 128
        with ExitStack() as st:
            sb = lambda n, s, dt=F32: self.sbt(st, n, s, dt)
            A = sb("o_A", [128, nkc, TH], BF16)
            KG = 4
            wp = self.mk_wpool(st, "ow", KG, 512, dd=4)
            xs = [sb("o_x%d" % i, [128, 512]) for i in range(16)]
            state = {'g': 0}
            items = []

            def mk(hf, cb, grp, kg, n_k, first, last, gidx):
                def fn(ws):
                    wb, wbn = ws[0]
                    if first:
                        for c0 in range(0, nkc, KG):
                            c1 = min(nkc, c0 + KG)
                            k.dma(A[:, c0:c1, :], AT_dram[c0 * 128:c1 * 128, hf * TH:(hf + 1) * TH].rearrange("(c p) t -> p c t", p=128),
                                  [], ['o_A%d' % (c0 // KG)])
                    if kg == 0:
                        for bi, tt in enumerate(grp):
                            j = (gidx % 2) * 8 + bi
                            r0 = hf * TH + tt * 128
                            k.dma(xs[j][:], xsrc[r0:r0 + 128, cb * 512:(cb + 1) * 512], [], ['o_x%d' % j])
                    for kk in range(n_k):
                        kc = kg + kk
                        for bi, tt in enumerate(grp):
                            k.op('pe', [wbn, 'o_A%d' % (kc // KG)], ['pb%d' % bi], lambda e: e.matmul(
                                PB[bi][:, :], lhsT=A[:, kc, tt * 128:(tt + 1) * 128], rhs=wb[:, kk, :],
                                start=(kc == 0), stop=(kc == nkc - 1)))
                    if last:
                        for bi, tt in enumerate(grp):
                            j = (gidx % 2) * 8 + bi
                            xt, xn = xs[j], 'o_x%d' % j
                            r0 = hf * TH + tt * 128
                            k.op('dve', [xn, 'pb%d' % bi], [xn], lambda e: e.tensor_tensor(
                                out=xt[:], in0=xt[:], in1=PB[bi][:, :], op=ALU.add))
                            k.dma(xdst[r0:r0 + 128, cb * 512:(cb + 1) * 512], xt[:], [xn], [])
                return fn
            for hf in range(halves):
                first = True
                for cb in range(D // 512):
                    for tt0 in range(0, ntt, 8):
                        grp = list(range(tt0, min(ntt, tt0 + 8)))
                        for kg in range(0, nkc, KG):
                            n_k = min(KG, nkc - kg)
                            items.append(([(wp, [(0, Wsrc[kg * 128:(kg + n_k) * 128, cb * 512:(cb + 1) * 512])], n_k, 512)],
                                          mk(hf, cb, grp, kg, n_k, first, kg + n_k == nkc, state['g'])))
                            first = False
                        state['g'] += 1
            self.run_pipeline(items)
            k.barrier()

    def stage_ffn_up(self, l, hT, hTn):
        k = self.k
        PB = self.PB
        W = self.ffn_w_up
        with ExitStack() as st:
            sb = lambda n, s, dt=F32: self.sbt(st, n, s, dt)
            wp = self.mk_wpool(st, "fw", KC, 256)
            cw = sb("f_cw", [128, 2 * NFB, 3]); cb = sb("f_cb", [128, 2 * NFB])
            for b in range(2 * NFB):
                k.dma(cw[:, b, :], self.ffn_conv_w[l, :, b * 128:(b + 1) * 128].rearrange("k p -> p k"),
                      [], ['f_cw%d' % b], allow_slow_non_contiguous=True)
            k.dma(cb[:], self.ffn_conv_b[l].rearrange("(b p) -> p b", p=128), [], ['f_cb'],
                  allow_slow_non_contiguous=True)
            xp = [sb("f_xp%d" % i, [128, T + 2]) for i in range(2)]
            acc = [sb("f_acc%d" % i, [128, T]) for i in range(2)]
            sg = sb("f_sg", [128, T])
            ao = [sb("f_ao%d" % i, [128, T], BF16) for i in range(2)]
            for i in range(2):
                k.op('pool', [], ['f_xp%d' % i], lambda e: e.memset(xp[i][:, 0:2], 0.0))
            state = {'bank': 0, 'ev': 0}
            items = []

            def mk(j):
                def fn(ws):
                    wb, wbn = ws[0]
                    for half in range(2):
                        blk = j + half * NFB
                        x_, xn = xp[half], 'f_xp%d' % half
                        a_, an = acc[half], 'f_acc%d' % half
                        for tb in range(4):
                            b = state['bank'] % 8
                            state['bank'] += 1
                            for kc in range(KC):
                                k.op('pe', [wbn, hTn], ['pb%d' % b], lambda e: e.matmul(
                                    PB[b][:, :], lhsT=wb[:, kc, half * 128:(half + 1) * 128],
                                    rhs=hT[:, kc, tb * 512:(tb + 1) * 512],
                                    start=(kc == 0), stop=(kc == KC - 1)))
                            eng = ('act', 'act', 'dve')[state['ev'] % 3]
                            state['ev'] += 1
                            self.cast(x_[:, 2 + tb * 512: 2 + (tb + 1) * 512], PB[b][:, :], ['pb%d' % b], [xn], eng=eng)
                        k.op('dve', [xn, 'f_cw%d' % blk, 'f_cb'], [an], lambda e: e.tensor_scalar(
                            out=a_[:], in0=x_[:, 0:T], scalar1=cw[:, blk, 0:1], scalar2=cb[:, blk:blk + 1],
                            op0=ALU.mult, op1=ALU.add))
                        for kk in range(1, 3):
                            k.op('dve', [xn, 'f_cw%d' % blk, an], [an], lambda e: e.scalar_tensor_tensor(
                                out=a_[:], in0=x_[:, kk:kk + T], scalar=cw[:, blk, kk:kk + 1], in1=a_[:],
                                op0=ALU.mult, op1=ALU.add))
                    k.op('act', ['f_acc0'], ['f_sg'], lambda e: e.activation(out=sg[:], in_=acc[0][:], func=AF.Silu))
                    o, on = ao[j % 2], 'f_ao%d' % (j % 2)
                    k.op('pool', ['f_sg', 'f_acc1'], [on], lambda e: e.tensor_tensor(out=o[:], in0=sg[:], in1=acc[1][:], op=ALU.mult))
                    k.dma(self.actT[j * 128:(j + 1) * 128, :], o[:], [on], [])
                return fn
            for j in range(NFB):
                items.append(([(wp, [(0, W[l, :, j * 128:(j + 1) * 128]),
                                     (128, W[l, :, (NFB + j) * 128:(NFB + j + 1) * 128])], KC, 256)], mk(j)))
            self.run_pipeline(items)
            k.barrier()


    def stage_final(self, src):
        k = self.k
        with ExitStack() as st:
            sb = lambda n, s, dt=F32: self.sbt(st, n, s, dt)
            wN = sb("z_w", [128, D])
            k.dma(wN[:], self.final_norm_w.partition_broadcast(128), [], ['z_w'])
            xts = [sb("z_x%d" % i, [128, D]) for i in range(2)]
            ots = [sb("z_o%d" % i, [128, D]) for i in range(2)]
            junk = sb("z_junk", [128, D], BF16)
            sss = [sb("z_ss%d" % i, [128, 1]) for i in range(2)]
            for tt in range(NT):
                i = tt % 2
                xt, ot, ss = xts[i], ots[i], sss[i]
                xn, on, sn = "z_x%d" % i, "z_o%d" % i, "z_ss%d" % i
                k.dma(xt[:], src[tt * 128:(tt + 1) * 128, :], [], [xn])
                k.op('act', [xn], ['z_junk', sn], lambda e: e.activation(
                    out=junk[:], in_=xt[:], func=AF.Square, accum_out=ss[:]))
                k.op('dve', [sn], [sn], lambda e: e.tensor_scalar(
                    out=ss[:], in0=ss[:], scalar1=1.0 / D, scalar2=EPS, op0=ALU.mult, op1=ALU.add))
                k.op('act', [sn], [sn], lambda e: e.sqrt(out=ss[:], in_=ss[:]))
                k.op('dve', [sn], [sn], lambda e: e.reciprocal(out=ss[:], in_=ss[:]))
                k.op('dve', [xn, sn, 'z_w'], [on], lambda e: e.scalar_tensor_tensor(
                    out=ot[:], in0=xt[:], scalar=ss[:, 0:1], in1=wN[:], op0=ALU.mult, op1=ALU.mult))
                k.dma(self.out[tt * 128:(tt + 1) * 128, :], ot[:], [on], [])
            k.barrier()

    def finish(self):
        k = self.k
        nc = self.nc
        done = nc.alloc_semaphore("sem_done")
        for e in ('pe', 'act', 'dve', 'pool'):
            k.E[e].sem_inc(done, 1)
        sp = k.E['sp']
        sp.wait_ge(done, 4)
        for e in k.E:
            sp.sem_clear(k.sem[e])
        for s_ in k.dsem:
            sp.sem_clear(s_)
        sp.sem_clear(done)

    def build(self, nlayers=2, upto=None):
        k = self.k
        src = self.x
        stop = False
        for l in range(nlayers):
            with ExitStack() as st:
                if upto == 'init':
                    return
                hT = self.sbt(st, "hT", [128, KC, T], BF16)
                self.norm_T(src, self.norm_mix_w[l], hT, 'hT')
                if upto == 'norm':
                    return
                self.stage_proj(l, hT, 'hT')
            if upto == 'proj':
                return
            self.stage_ssd(l)
            if upto == 'ssd':
                return
            self.stage_hgrn(l)
            if upto == 'hgrn':
                return
            self.stage_fox(l)
            if upto == 'fox':
                return
            self.stage_merge(l)
            if upto == 'merge':
                return
            self.stage_out_proj(self.mT, 16, self.w_out[l], src, self.xa, halves=1)
            if upto == 'mix':
                return
            with ExitStack() as st:
                hT = self.sbt(st, "hT", [128, KC, T], BF16)
                self.norm_T(self.xa, self.norm_ffn_w[l], hT, 'hT')
                self.stage_ffn_up(l, hT, 'hT')
            if upto == 'ffn_up':
                return
            self.stage_out_proj(self.actT, NFB, self.ffn_w_down[l], self.xa, self.xb, halves=2)
            if upto == 'layer':
                return
            src = self.xb
        self.stage_final(self.xb)
        self.finish()


WNAMES = ["norm_mix_w", "w_in", "ssd_conv_w", "ssd_conv_b", "ssd_dt_bias", "ssd_a_log", "ssd_d",
          "ssd_norm_w", "hgrn_lb", "hgrn_norm_w", "fox_f_bias", "w_branch_ssd", "w_branch_hgrn",
          "w_branch_fox", "w_out", "norm_ffn_w", "ffn_w_up", "ffn_conv_w", "ffn_conv_b",
          "ffn_w_down", "final_norm_w"]


def kernel(**inputs):
    nc = bass.Bass("TRN2", target_bir_lowering=False)
    p = Prog(nc)
    p.build()
    x = np.ascontiguousarray(inputs["x"], dtype=np.float32)
    shared = {n: np.ascontiguousarray(inputs[n], dtype=np.float32) for n in WNAMES}
    in_maps = []
    for c in range(8):
        m = dict(shared)
        m["x"] = x[c]
        in_maps.append(m)
    res = run_bass_kernel_spmd(nc, in_maps, core_ids=list(range(8)))
    return np.stack([np.asarray(r["out"], dtype=np.float32) for r in res.results], axis=0)
```

```python
import numpy as np
from contextlib import ExitStack
import concourse.bass as bass
import concourse.mybir as mybir
from concourse.bass_utils import run_bass_kernel_spmd

F32 = mybir.dt.float32
BF16 = mybir.dt.bfloat16
AF = mybir.ActivationFunctionType
ALU = mybir.AluOpType

T = 2048
D = 2048
NT = 16
KC = 16
DIN = 15904
DFF = 5504
NFB = 43
EPS = 1e-6
NDS = 24

O_Z, O_XBC, O_DT, O_HQ, O_HF, O_HI, O_HG, O_FQ, O_FK, O_FV, O_FF, O_G = (
    0, 1024, 2560, 2576, 3600, 4624, 5648, 6672, 7696, 8720, 9744, 9760)


class KB:
    def __init__(self, nc):
        self.nc = nc
        self.E = {'pe': nc.tensor, 'act': nc.scalar, 'dve': nc.vector,
                  'pool': nc.gpsimd, 'sp': nc.sync}
        self.sem = {k: nc.alloc_semaphore("sem_" + k) for k in self.E}
        self.icnt = {k: 0 for k in self.E}
        self.scnt = {k: 0 for k in self.E}
        self.last = {k: None for k in self.E}
        self.incs = {k: [] for k in self.E}
        self.seen = {k: {k2: 0 for k2 in self.E} for k in self.E}
        self.dsem = [nc.alloc_semaphore("dsem%d" % i) for i in range(NDS)]
        self.dcnt = [0] * NDS
        self.dnext = 0
        self.dseen = {k: [0] * NDS for k in self.E}
        self.lastw = {}
        self.readers = {}

    def _deps(self, reads, writes):
        deps = []
        for r in reads:
            t = self.lastw.get(r)
            if t is not None:
                deps.append(t)
        for w in writes:
            t = self.lastw.get(w)
            if t is not None:
                deps.append(t)
            rd = self.readers.get(w)
            if rd:
                for e, i in rd['e'].items():
                    deps.append(('e', e, i))
                deps.extend(rd['d'])
        return deps

    def _semval_for(self, e2, idx):
        lst = self.incs[e2]
        if lst and lst[-1][0] >= idx:
            j = len(lst) - 1
            while j > 0 and lst[j - 1][0] >= idx:
                j -= 1
            return lst[j]
        ins, lidx = self.last[e2]
        assert lidx >= idx
        self.scnt[e2] += 1
        ins.then_inc(self.sem[e2], 1)
        lst.append((lidx, self.scnt[e2]))
        return lst[-1]

    def _wait(self, eng, deps, raw_same=()):
        need = {}
        dneed = {}
        for t in deps:
            if t[0] == 'e':
                _, e2, idx = t
                if e2 == eng:
                    continue
                if idx > need.get(e2, 0):
                    need[e2] = idx
            else:
                _, s, c = t
                if c > dneed.get(s, 0):
                    dneed[s] = c
        for idx in raw_same:
            if eng != 'pe' and idx > need.get(eng, 0):
                need[eng] = idx
        E = self.E[eng]
        for e2, idx in need.items():
            if idx <= self.seen[eng][e2]:
                continue
            iidx, v = self._semval_for(e2, idx)
            E.wait_ge(self.sem[e2], v)
            self.seen[eng][e2] = iidx
        for s, c in dneed.items():
            if c <= self.dseen[eng][s]:
                continue
            E.wait_ge(self.dsem[s], c)
            self.dseen[eng][s] = c

    def _record(self, tok, reads, writes):
        for r in reads:
            rd = self.readers.get(r)
            if rd is None:
                rd = {'e': {}, 'd': []}
                self.readers[r] = rd
            if tok[0] == 'e':
                rd['e'][tok[1]] = tok[2]
            else:
                rd['d'].append(tok)
        for w in writes:
            self.lastw[w] = tok
            self.readers[w] = {'e': {}, 'd': []}

    def op(self, eng, reads, writes, fn):
        deps = self._deps(reads, writes)
        raw_same = []
        for r in reads:
            t = self.lastw.get(r)
            if t is not None and t[0] == 'e' and t[1] == eng:
                raw_same.append(t[2])
        self._wait(eng, deps, raw_same)
        ins = fn(self.E[eng])
        self.icnt[eng] += 1
        self.last[eng] = (ins, self.icnt[eng])
        self._record(('e', eng, self.icnt[eng]), reads, writes)
        return ins

    def dma(self, out, in_, reads, writes, q='sp', **kw):
        deps = self._deps(reads, writes)
        s = self.dnext
        self.dnext = (self.dnext + 1) % NDS
        if self.dcnt[s] > 0:
            deps.append(('d', s, self.dcnt[s]))
        raw_same = []
        for r in reads:
            t = self.lastw.get(r)
            if t is not None and t[0] == 'e' and t[1] == q:
                raw_same.append(t[2])
        self._wait(q, deps, raw_same)
        ins = self.E[q].dma_start(out=out, in_=in_, **kw)
        self.dcnt[s] += 16
        ins.then_inc(self.dsem[s], 16)
        self._record(('d', s, self.dcnt[s]), reads, writes)

    def barrier(self):
        deps = []
        for e in self.E:
            if self.last[e] is not None:
                deps.append(('e', e, self.last[e][1]))
        for s in range(NDS):
            if self.dcnt[s] > 0:
                deps.append(('d', s, self.dcnt[s]))
        for e in self.E:
            self._wait(e, deps, [self.last[e][1]] if (self.last[e] is not None and e != 'sp') else [])
        self.lastw = {}
        self.readers = {}


class Prog:
    def __init__(self, nc, dbg=(), tiny=False):
        self.nc = nc
        self.k = KB(nc)
        self.dbg = set(dbg)
        self.cast_rr = 0
        self.uid = 0
        k = self.k
        BIG = ("w_in", "w_branch_ssd", "w_branch_hgrn", "w_branch_fox", "w_out", "ffn_w_up", "ffn_w_down")
        di = lambda n, s: nc.dram_tensor(n, ([2, 1, 1] if (tiny and n in BIG) else list(s)), F32, kind="ExternalInput").ap()
        self.x = di("x", [T, D])
        self.norm_mix_w = di("norm_mix_w", [2, D])
        self.w_in = di("w_in", [2, D, DIN])
        self.ssd_conv_w = di("ssd_conv_w", [2, 4, 1536])
        self.ssd_conv_b = di("ssd_conv_b", [2, 1536])
        self.ssd_dt_bias = di("ssd_dt_bias", [2, 16])
        self.ssd_a_log = di("ssd_a_log", [2, 16])
        self.ssd_d = di("ssd_d", [2, 16])
        self.ssd_norm_w = di("ssd_norm_w", [2, 1024])
        self.hgrn_lb = di("hgrn_lb", [2, 1024])
        self.hgrn_norm_w = di("hgrn_norm_w", [2, 1024])
        self.fox_f_bias = di("fox_f_bias", [2, 16])
        self.w_branch_ssd = di("w_branch_ssd", [2, 1024, D])
        self.w_branch_hgrn = di("w_branch_hgrn", [2, 1024, D])
        self.w_branch_fox = di("w_branch_fox", [2, 1024, D])
        self.w_out = di("w_out", [2, D, D])
        self.norm_ffn_w = di("norm_ffn_w", [2, D])
        self.ffn_w_up = di("ffn_w_up", [2, D, 2 * DFF])
        self.ffn_conv_w = di("ffn_conv_w", [2, 3, 2 * DFF])
        self.ffn_conv_b = di("ffn_conv_b", [2, 2 * DFF])
        self.ffn_w_down = di("ffn_w_down", [2, DFF, D])
        self.final_norm_w = di("final_norm_w", [D])
        self.out = nc.dram_tensor("out", [T, D], F32, kind="ExternalOutput").ap()
        self.xa = self.scr("xa", [T, D], F32)
        self.xb = self.scr("xb", [T, D], F32)
        self.xs_tm = self.scr("xs_tm", [T, 1024], BF16)
        self.B_tm = self.scr("B_tm", [T, 256], BF16)
        self.BT = self.scr("BT", [2, 128, T], BF16)
        self.CT = self.scr("CT", [2, 128, T], BF16)
        self.zs_tm = self.scr("zs_tm", [T, 1024], BF16)
        self.dtff = self.scr("dtff", [T, 32], F32)
        self.hqT = self.scr("hqT", [1024, T], BF16)
        self.hsigT = self.scr("hsigT", [1024, T], F32)
        self.hgT = self.scr("hgT", [1024, T], BF16)
        self.hv_tm = self.scr("hv_tm", [T, 1024], BF16)
        self.fqA = self.scr("fqA", [16, 70, T], BF16)
        self.fkA = self.scr("fkA", [16, 70, T], BF16)
        self.fv_tm = self.scr("fv_tm", [T, 1024], BF16)
        self.gT = self.scr("gT", [6144, T], BF16)
        self.ysT = self.scr("ysT", [1024, T], BF16)
        self.yhT = self.scr("yhT", [1024, T], BF16)
        self.yfT = self.scr("yfT", [16, 64, T], BF16)
        self.mT = self.scr("mT", [D, T], BF16)
        self.actT = self.scr("actT", [DFF, T], BF16)
        self.PB = [nc.alloc_psum_tensor("pb%d" % i, [128, 512], F32) for i in range(8)]
        self.PT = self.PB[7][:].bitcast(BF16)
        self.PT6 = self.PB[6][:].bitcast(BF16)
        self.cst = ExitStack()
        sb = lambda n, s, dt=F32: self.cst.enter_context(nc.sbuf_tensor(n, list(s), dt))
        self.ones_f = sb("ones_f", [128, 128])
        self.ones_b = sb("ones_b", [128, 128], BF16)
        self.ident_b = sb("ident_b", [128, 128], BF16)
        self.triI = sb("triI", [128, 128])
        self.triSU = sb("triSU", [128, 128])
        self.zeros_f = sb("zeros_f", [128, 512])
        tmp = sb("c_tmp", [128, 128])
        k.op('pool', [], ['ones_f'], lambda e: e.memset(self.ones_f[:], 1.0))
        k.op('pool', [], ['zeros_f'], lambda e: e.memset(self.zeros_f[:], 0.0))
        k.op('pool', ['ones_f'], ['c_tmp'], lambda e: e.affine_select(
            out=tmp[:], in_=self.ones_f[:], pattern=[[1, 128]], compare_op=ALU.is_equal,
            fill=0.0, base=0, channel_multiplier=-1))
        k.op('pool', ['ones_f'], ['triI'], lambda e: e.affine_select(
            out=self.triI[:], in_=self.ones_f[:], pattern=[[1, 128]], compare_op=ALU.is_ge,
            fill=0.0, base=0, channel_multiplier=-1))
        k.op('pool', ['ones_f'], ['triSU'], lambda e: e.affine_select(
            out=self.triSU[:], in_=self.ones_f[:], pattern=[[-1, 128]], compare_op=ALU.is_ge,
            fill=0.0, base=-1, channel_multiplier=1))
        k.op('dve', ['c_tmp'], ['ident_b'], lambda e: e.tensor_copy(out=self.ident_b[:], in_=tmp[:]))
        k.op('dve', ['ones_f'], ['ones_b'], lambda e: e.tensor_copy(out=self.ones_b[:], in_=self.ones_f[:]))
        k.barrier()
        self.C = ['ones_f', 'ones_b', 'ident_b', 'triI', 'triSU', 'zeros_f']

    def scr(self, name, shape, dt):
        kind = "ExternalOutput" if name in self.dbg else "Internal"
        return self.nc.dram_tensor(name, list(shape), dt, kind=kind).ap()

    def sbt(self, st, name, shape, dt=F32):
        self.uid += 1
        return st.enter_context(self.nc.sbuf_tensor("%s_%d" % (name, self.uid), list(shape), dt))

    def u(self):
        self.uid += 1
        return "u%d" % self.uid

    def cast(self, out_ap, in_ap, reads, writes, eng=None):
        if eng is None:
            eng = ('dve', 'act', 'pool', 'dve', 'act')[self.cast_rr % 5]
            self.cast_rr += 1
        if eng == 'act':
            self.k.op('act', reads, writes, lambda e: e.activation(out=out_ap, in_=in_ap, func=AF.Copy))
        else:
            self.k.op(eng, reads, writes, lambda e: e.tensor_copy(out=out_ap, in_=in_ap))

    def transpose_to(self, src_tile, src_name, nblk, dst_fn, pt_toggle, views=None):
        k = self.k
        if views is None:
            views = [(self.PT, 'pb7')]
        for b0 in range(0, nblk, 4):
            nb = min(4, nblk - b0)
            pv, ptn = views[pt_toggle[0] % len(views)]
            pt_toggle[0] += 1
            for b in range(nb):
                o = pv[:, b * 128:(b + 1) * 128]
                k.op('pe', [src_name, 'ident_b'], [ptn], lambda e: e.transpose(
                    out=o, in_=src_tile[:, (b0 + b) * 128:(b0 + b + 1) * 128],
                    identity=self.ident_b[:]))
            dst_fn(b0, nb, pv[:, 0:nb * 128], ptn)

    def norm_T(self, src, wv, hT, hTn):
        k = self.k
        with ExitStack() as st:
            wN = self.sbt(st, "nrm_w", [128, D])
            k.dma(wN[:], wv.partition_broadcast(128), [], ['nrm_w'])
            xts = [self.sbt(st, "nrm_x%d" % i, [128, D]) for i in range(2)]
            hbs = [self.sbt(st, "nrm_hb%d" % i, [128, D], BF16) for i in range(2)]
            junk = self.sbt(st, "nrm_junk", [128, D], BF16)
            sss = [self.sbt(st, "nrm_ss%d" % i, [128, 1]) for i in range(2)]
            tog = [0]
            def stats(tt):
                i = tt % 2
                xt, hb, ss = xts[i], hbs[i], sss[i]
                xn, hn, sn = "nrm_x%d" % i, "nrm_hb%d" % i, "nrm_ss%d" % i
                k.dma(xt[:], src[tt * 128:(tt + 1) * 128, :], [], [xn])
                k.op('act', [xn], ['nrm_junk', sn], lambda e: e.activation(
                    out=junk[:], in_=xt[:], func=AF.Square, accum_out=ss[:]))
                k.op('dve', [sn], [sn], lambda e: e.tensor_scalar(
                    out=ss[:], in0=ss[:], scalar1=1.0 / D, scalar2=EPS, op0=ALU.mult, op1=ALU.add))
                k.op('act', [sn], [sn], lambda e: e.sqrt(out=ss[:], in_=ss[:]))
                k.op('dve', [sn], [sn], lambda e: e.reciprocal(out=ss[:], in_=ss[:]))
                k.op('dve', [xn, sn, 'nrm_w'], [hn], lambda e: e.scalar_tensor_tensor(
                    out=hb[:], in0=xt[:], scalar=ss[:, 0:1], in1=wN[:], op0=ALU.mult, op1=ALU.mult))

            def xpose(tt):
                i = tt % 2
                hb, hn = hbs[i], "nrm_hb%d" % i

                def dst(b0, nb, pt, ptn, tt=tt):
                    self.cast(hT[:, b0:b0 + nb, tt * 128:(tt + 1) * 128],
                              pt.rearrange("p (c t) -> p c t", t=128), [ptn], [hTn],
                              eng=('act' if (b0 // 4) % 2 == 0 else 'dve'))
                self.transpose_to(hb, hn, 16, dst, tog, views=[(self.PT6, 'pb6'), (self.PT, 'pb7')])

            stats(0)
            for tt in range(NT):
                if tt + 1 < NT:
                    stats(tt + 1)
                xpose(tt)
            k.barrier()

    def mk_wpool(self, st, name, kc, n, dd=2):
        return {'i': 0, 'name': name, 'dd': dd,
                'wf': [self.sbt(st, name + "f%d" % i, [128, kc, n]) for i in range(dd)],
                'wb': [self.sbt(st, name + "b%d" % i, [128, kc, n], BF16) for i in range(2)]}

    def w_dma(self, pool, srcs, kc, n, pp=128):
        i = pool['i'] % pool['dd']
        pool['i'] += 1
        wf = pool['wf'][i]
        for si, (c0, src) in enumerate(srcs):
            nco = src.shape[1]
            self.k.dma(wf[0:pp, 0:kc, c0:c0 + nco], src.rearrange("(c p) n -> p c n", p=pp), [],
                       ["%sf%d_%d" % (pool['name'], i, si)])
        return i

    def w_cast(self, pool, i, nsrc, kc, n, pp=128):
        j = pool.get('ci', 0) % 2
        pool['ci'] = pool.get('ci', 0) + 1
        wf, wb = pool['wf'][i], pool['wb'][j]
        rd = ["%sf%d_%d" % (pool['name'], i, si) for si in range(nsrc)]
        wbn = "%sb%d" % (pool['name'], j)
        half = max(1, kc // 2)
        for c0 in range(0, kc, half):
            c1 = min(kc, c0 + half)
            eng = ('dve', 'act')[self.cast_rr % 2]
            self.cast_rr += 1
            self.cast(wb[0:pp, c0:c1, 0:n], wf[0:pp, c0:c1, 0:n], rd, [wbn], eng=eng)
        return wb, wbn

    def run_pipeline(self, items):
        n = len(items)
        dd = items[0][0][0][0]['dd']
        dm = lambda it: [self.w_dma(ld[0], ld[1], ld[2], ld[3]) for ld in it[0]]
        cs = lambda it, sl: [self.w_cast(ld[0], s_, len(ld[1]), ld[2], ld[3]) for ld, s_ in zip(it[0], sl)]
        slots = {}
        for j in range(min(dd, n)):
            slots[j] = dm(items[j])
        ready = cs(items[0], slots[0])
        for i in range(n):
            cur = ready
            if i + dd < n:
                slots[i + dd] = dm(items[i + dd])
            if i + 1 < n:
                ready = cs(items[i + 1], slots[i + 1])
            items[i][1](cur)

    def stage_proj(self, l, hT, hTn):
        k = self.k
        W = self.w_in
        PB = self.PB
        with ExitStack() as st:
            wp = self.mk_wpool(st, "pw", KC, 256)
            fa = [self.sbt(st, "fa%d" % i, [128, T]) for i in range(3)]
            xpad = [self.sbt(st, "xpad%d" % i, [128, T + 3]) for i in range(2)]
            stg32 = self.sbt(st, "stg32", [128, NT, 32])
            ba = [self.sbt(st, "ba%d" % i, [128, T], BF16) for i in range(3)]
            tms = [self.sbt(st, "tms%d" % i, [128, NT, 256], BF16) for i in range(2)]
            cw = self.sbt(st, "cw", [128, 12, 4])
            cb = self.sbt(st, "cb", [128, 12])
            for b in range(12):
                k.dma(cw[:, b, :], self.ssd_conv_w[l, :, b * 128:(b + 1) * 128].rearrange("k p -> p k"),
                      [], ['cw%d' % b], allow_slow_non_contiguous=True)
            k.dma(cb[:], self.ssd_conv_b[l].rearrange("(b p) -> p b", p=128), [], ['cb'],
                  allow_slow_non_contiguous=True)
            for i in range(2):
                k.op('pool', [], ["xpad%d" % i], lambda e: e.memset(xpad[i][:, 0:3], 0.0))
            cnt = {'fa': 0, 'ba': 0, 'tms': 0, 'bank': 0, 'xpad': 0, 'ev': 0}
            tog = [0]

            def nxt(key, n):
                i = cnt[key] % n
                cnt[key] += 1
                return i

            items = []

            def add_fm(col0, nblk, start, evac, final):
                for c0 in range(0, nblk * 128, 256):
                    n = min(256, nblk * 128 - c0)

                    def fn(ws, c0=c0, n=n):
                        wb, wbn = ws[0]
                        for sub in range(n // 128):
                            bj = (c0 + sub * 128) // 128
                            ctx = start(bj)
                            for tb in range(4):
                                b = nxt('bank', 7)
                                for kc in range(KC):
                                    k.op('pe', [wbn, hTn], ['pb%d' % b], lambda e: e.matmul(
                                        PB[b][:, :], lhsT=wb[:, kc, sub * 128:(sub + 1) * 128],
                                        rhs=hT[:, kc, tb * 512:(tb + 1) * 512],
                                        start=(kc == 0), stop=(kc == KC - 1)))
                                evac(ctx, tb, b)
                            final(bj, ctx)
                    items.append(([(wp, [(0, W[l, :, col0 + c0: col0 + c0 + n])], KC, n)], fn))

            def ev(o, b, dname, func):
                if func is None:
                    self.cast(o, PB[b][:, :], ['pb%d' % b], [dname], eng=('act', 'dve')[nxt('ev', 2)])
                else:
                    k.op('act', ['pb%d' % b], [dname], lambda e: e.activation(out=o, in_=PB[b][:, :], func=func))

            def to_tm(o, on, dst):
                i = nxt('tms', 2)
                stg, sn = tms[i], "tms%d" % i

                def dfn(b0, nb, pt, ptn):
                    self.cast(stg[:, b0:b0 + nb, 0:128], pt.rearrange("p (c t) -> p c t", t=128),
                              [ptn], [sn], eng=('act' if (b0 // 4) % 2 == 0 else 'dve'))
                self.transpose_to(o, on, 16, dfn, tog)
                k.dma(dst.rearrange("(tt p) c -> p tt c", p=128), stg[:, :, 0:128], [sn], [])

            def xbc_start(bj):
                i = nxt('xpad', 2)
                return (xpad[i], "xpad%d" % i)

            def xbc_evac(ctx, tb, b):
                ev(ctx[0][:, 3 + tb * 512: 3 + (tb + 1) * 512], b, ctx[1], None)

            def xbc_final(bj, ctx):
                xp, xpn = ctx
                i2 = nxt('fa', 3)
                acc, an = fa[i2], "fa%d" % i2
                k.op('dve', [xpn, 'cw%d' % bj, 'cb'], [an], lambda e: e.tensor_scalar(
                    out=acc[:, 0:T], in0=xp[:, 0:T], scalar1=cw[:, bj, 0:1], scalar2=cb[:, bj:bj + 1],
                    op0=ALU.mult, op1=ALU.add))
                for kk in range(1, 4):
                    k.op('dve', [xpn, 'cw%d' % bj, an], [an], lambda e: e.scalar_tensor_tensor(
                        out=acc[:, 0:T], in0=xp[:, kk:kk + T], scalar=cw[:, bj, kk:kk + 1],
                        in1=acc[:, 0:T], op0=ALU.mult, op1=ALU.add))
                i3 = nxt('ba', 3)
                o, on = ba[i3], "ba%d" % i3
                k.op('act', [an], [on], lambda e: e.activation(out=o[:], in_=acc[:, 0:T], func=AF.Silu))
                if bj < 8:
                    to_tm(o, on, self.xs_tm[:, bj * 128:(bj + 1) * 128])
                elif bj < 10:
                    g = bj - 8
                    k.dma(self.BT[g], o[:], [on], [])
                    to_tm(o, on, self.B_tm[:, g * 128:(g + 1) * 128])
                else:
                    k.dma(self.CT[bj - 10], o[:], [on], [])

            def mk_plain(pool, pname, npool, func, final):
                def start(bj):
                    i = nxt(pname, npool)
                    return (pool[i], "%s%d" % (pname, i))

                def evac(ctx, tb, b):
                    ev(ctx[0][:, tb * 512:(tb + 1) * 512], b, ctx[1], func)
                return start, evac, final

            def fin_rows(dst):
                return lambda bj, ctx: k.dma(dst[bj * 128:(bj + 1) * 128, :], ctx[0][:], [ctx[1]], [])

            def fin_qk(dst):
                def f(bj, ctx):
                    k.dma(dst[2 * bj, 0:64, :], ctx[0][0:64, :], [ctx[1]], [])
                    k.dma(dst[2 * bj + 1, 0:64, :], ctx[0][64:128, :], [ctx[1]], [])
                return f

            def add_tm(col0, ncols, dst, func, f32out=False, srcs=None):
                for c0 in range(0, ncols, 256):
                    n = min(256, ncols - c0)

                    def fn(ws, c0=c0, n=n):
                        wb, wbn = ws[0]
                        if f32out:
                            stg, sn = stg32, 'stg32'
                        else:
                            i = nxt('tms', 2)
                            stg, sn = tms[i], "tms%d" % i
                        for tt in range(NT):
                            b = nxt('bank', 7)
                            for kc in range(KC):
                                k.op('pe', [wbn, hTn], ['pb%d' % b], lambda e: e.matmul(
                                    PB[b][:, 0:n], lhsT=hT[:, kc, tt * 128:(tt + 1) * 128], rhs=wb[:, kc, 0:n],
                                    start=(kc == 0), stop=(kc == KC - 1)))
                            o = stg[:, tt, 0:n]
                            if func is None:
                                self.cast(o, PB[b][:, 0:n], ['pb%d' % b], [sn], eng=('act', 'dve')[nxt('ev', 2)])
                            else:
                                k.op('act', ['pb%d' % b], [sn], lambda e: e.activation(out=o, in_=PB[b][:, 0:n], func=func))
                        if f32out:
                            k.dma(dst.rearrange("(tt p) c -> p tt c", p=128), stg[:, :, 0:n], [sn], [])
                        else:
                            k.dma(dst[:, c0:c0 + n].rearrange("(tt p) c -> p tt c", p=128), stg[:, :, 0:n], [sn], [])
                    ss_ = srcs if srcs is not None else [(0, W[l, :, col0 + c0: col0 + c0 + n])]
                    items.append(([(wp, ss_, KC, n)], fn))

            add_tm(0, 32, self.dtff, None, f32out=True,
                   srcs=[(0, W[l, :, O_DT:O_DT + 16]), (16, W[l, :, O_FF:O_FF + 16])])
            add_tm(O_Z, 1024, self.zs_tm, AF.Silu)
            add_tm(O_HI, 1024, self.hv_tm, None)
            add_tm(O_FV, 1024, self.fv_tm, None)
            add_fm(O_XBC, 12, xbc_start, xbc_evac, xbc_final)
            add_fm(O_HQ, 8, *mk_plain(ba, 'ba', 3, AF.Silu, fin_rows(self.hqT)))
            add_fm(O_HF, 8, *mk_plain(fa, 'fa', 3, AF.Sigmoid, fin_rows(self.hsigT)))
            add_fm(O_HG, 8, *mk_plain(ba, 'ba', 3, AF.Silu, fin_rows(self.hgT)))
            add_fm(O_FQ, 8, *mk_plain(ba, 'ba', 3, None, fin_qk(self.fqA)))
            add_fm(O_FK, 8, *mk_plain(ba, 'ba', 3, None, fin_qk(self.fkA)))
            add_fm(O_G, 48, *mk_plain(ba, 'ba', 3, AF.Sigmoid, fin_rows(self.gT)))
            self.run_pipeline(items)
            k.barrier()


    def stage_ssd(self, l):
        k = self.k
        PB, PT = self.PB, self.PT
        with ExitStack() as st:
            sb = lambda n, s, dt=F32: self.sbt(st, n, s, dt)
            dtb = sb("s_dtb", [128, 16]); alog = sb("s_alog", [128, 16]); dsk = sb("s_dsk", [128, 16])
            aneg = sb("s_aneg", [128, 16]); nw = sb("s_nw", [128, 1024])
            k.dma(dtb[:], self.ssd_dt_bias[l].partition_broadcast(128), [], ['s_dtb'])
            k.dma(alog[:], self.ssd_a_log[l].partition_broadcast(128), [], ['s_alog'])
            k.dma(dsk[:], self.ssd_d[l].partition_broadcast(128), [], ['s_dsk'])
            k.dma(nw[:], self.ssd_norm_w[l].partition_broadcast(128), [], ['s_nw'])
            k.op('act', ['s_alog'], ['s_aneg'], lambda e: e.activation(out=aneg[:], in_=alog[:], func=AF.Exp))
            k.op('dve', ['s_aneg'], ['s_aneg'], lambda e: e.tensor_scalar(
                out=aneg[:], in0=aneg[:], scalar1=-1.0, scalar2=None, op0=ALU.mult))
            S = sb("s_S", [128, 1024]); Sbf = sb("s_Sbf", [128, 1024], BF16)
            k.op('pool', [], ['s_S'], lambda e: e.memset(S[:], 0.0))
            k.op('pool', [], ['s_Sbf'], lambda e: e.memset(Sbf[:], 0.0))
            yTs = sb("s_yTs", [128, 8, T], BF16)
            P2 = {}

            def pool2(name, shape, dt=F32):
                P2[name] = [sb("%s%d" % (name, i), shape, dt) for i in range(2)]
            for nm, shp, dt in [("s_xs", [128, 1024], BF16), ("s_zs", [128, 1024], BF16),
                                ("s_btm", [128, 256], BF16), ("s_bt", [128, 2, 128], BF16),
                                ("s_ct", [128, 2, 128], BF16), ("s_df", [128, 32], F32),
                                ("s_sm", [128, 8, 16], F32), ("s_edec", [128, 48], F32),
                                ("s_cbm", [128, 2, 128], F32), ("s_R", [128, 8, 128], F32),
                                ("s_E", [128, 1024], F32), ("s_MT", [128, 16, 128], BF16),
                                ("s_xdt", [128, 1024], BF16), ("s_xdd", [128, 1024], BF16),
                                ("s_yo", [128, 1024], F32), ("s_y", [128, 1024], F32),
                                ("s_tmp", [128, 1024], F32), ("s_yn", [128, 1024], BF16),
                                ("s_ss", [128, 2], F32), ("s_junk", [128, 512], BF16)]:
                pool2(nm, shp, dt)
            def phase(c, which):
                i = c % 2
                g_ = lambda nm: (P2[nm][i], "%s%d" % (nm, i))
                xs, xsn = g_("s_xs"); zs, zsn = g_("s_zs"); btm, btmn = g_("s_btm")
                bt, btn = g_("s_bt"); ct, ctn = g_("s_ct"); df, dfn = g_("s_df")
                sm, smn = g_("s_sm"); edec, edn = g_("s_edec"); cbm, cbn = g_("s_cbm")
                E_, En = g_("s_E"); MT, MTn = g_("s_MT"); xdt, xdtn = g_("s_xdt")
                xdd, xddn = g_("s_xdd"); yo, yon = g_("s_yo"); y, yn_ = g_("s_y")
                tmp, tmpn = g_("s_tmp"); yn, ynn = g_("s_yn"); ss, ssn = g_("s_ss")
                junk, jn = g_("s_junk")
                R, Rn = g_("s_R")

                x16, ax, e16, l16, r16, dt, a, dtd = [sm[:, j, :] for j in range(8)]
                xs3 = xs[:].rearrange("p (h d) -> p h d", h=16)
                if which == 'A':
                    ts = slice(c * 128, (c + 1) * 128)
                    k.dma(xs[:], self.xs_tm[ts, :], [], [xsn])
                    k.dma(zs[:], self.zs_tm[ts, :], [], [zsn])
                    k.dma(btm[:], self.B_tm[ts, :], [], [btmn])
                    k.dma(bt[:], self.BT[:, :, ts].rearrange("g n t -> n g t"), [], [btn])
                    k.dma(ct[:], self.CT[:, :, ts].rearrange("g n t -> n g t"), [], [ctn])
                    k.dma(df[:], self.dtff[ts, :], [], [dfn])
                    k.op('dve', [dfn, 's_dtb'], [smn], lambda e: e.tensor_tensor(out=x16, in0=df[:, 0:16], in1=dtb[:], op=ALU.add))
                    k.op('act', [smn], [smn], lambda e: e.activation(out=ax, in_=x16, func=AF.Abs))
                    k.op('act', [smn], [smn], lambda e: e.activation(out=e16, in_=ax, func=AF.Exp, scale=-1.0))
                    k.op('act', [smn], [smn], lambda e: e.activation(out=l16, in_=e16, func=AF.Ln, bias=1.0))
                    k.op('dve', [smn], [smn], lambda e: e.tensor_scalar_max(out=r16, in0=x16, scalar1=0.0))
                    k.op('dve', [smn], [smn], lambda e: e.tensor_tensor(out=dt, in0=r16, in1=l16, op=ALU.add))
                    k.op('dve', [smn, 's_aneg'], [smn], lambda e: e.tensor_tensor(out=a, in0=dt, in1=aneg[:], op=ALU.mult))
                    for j, (m, mn) in enumerate([(self.triI, 'triI'), (self.triSU, 'triSU'), (self.ones_f, 'ones_f')]):
                        k.op('pe', [mn, smn], ['pb0'], lambda e: e.matmul(
                            PB[0][:, j * 16:(j + 1) * 16], lhsT=m[:], rhs=a, start=True, stop=True))
                    k.op('act', ['pb0'], [edn], lambda e: e.activation(out=edec[:], in_=PB[0][:, 0:48], func=AF.Exp))
                    for g in range(2):
                        k.op('pe', [btn, ctn], ['pb0'], lambda e: e.matmul(
                            PB[0][:, 256 + g * 128: 256 + (g + 1) * 128], lhsT=bt[:, g, :], rhs=ct[:, g, :],
                            start=True, stop=True))
                    k.op('dve', ['pb0', 'triI'], [cbn], lambda e: e.tensor_tensor(
                        out=cbm[:], in0=PB[0][:, 256:512].rearrange("p (g l) -> p g l", g=2),
                        in1=self.triI[:].unsqueeze(1).to_broadcast([128, 2, 128]), op=ALU.mult))
                    for g in range(2):
                        k.op('pool', ['triI', smn], [Rn], lambda e: e.tensor_tensor(
                            out=R[:], in0=self.triI[:].unsqueeze(1).to_broadcast([128, 8, 128]),
                            in1=a[:, g * 8:(g + 1) * 8].unsqueeze(2).to_broadcast([128, 8, 128]), op=ALU.mult))
                        for hh in range(2):
                            k.op('pe', ['triSU', Rn], ['pb%d' % (1 + hh)], lambda e: e.matmul(
                                PB[1 + hh][:, :], lhsT=self.triSU[:],
                                rhs=R[:, hh * 4:(hh + 1) * 4, :].rearrange("p h l -> p (h l)"), start=True, stop=True))
                            k.op('act', ['pb%d' % (1 + hh)], [En], lambda e: e.activation(
                                out=E_[:, hh * 512:(hh + 1) * 512], in_=PB[1 + hh][:, :], func=AF.Exp))
                        k.op('dve', [En, cbn], [MTn], lambda e: e.tensor_tensor(
                            out=MT[:, g * 8:(g + 1) * 8, :], in0=E_[:].rearrange("p (h l) -> p h l", h=8),
                            in1=cbm[:, g, :].unsqueeze(1).to_broadcast([128, 8, 128]), op=ALU.mult))
                    k.op('dve', [xsn, smn], [xdtn], lambda e: e.tensor_tensor(
                        out=xdt[:].rearrange("p (h d) -> p h d", h=16), in0=xs3,
                        in1=dt.unsqueeze(2).to_broadcast([128, 16, 64]), op=ALU.mult))
                    k.op('dve', [smn, edn], [smn], lambda e: e.tensor_tensor(out=dtd, in0=dt, in1=edec[:, 16:32], op=ALU.mult))
                    k.op('pool', [xsn, smn], [xddn], lambda e: e.tensor_tensor(
                        out=xdd[:].rearrange("p (h d) -> p h d", h=16), in0=xs3,
                        in1=dtd.unsqueeze(2).to_broadcast([128, 16, 64]), op=ALU.mult))

                    return
                for h in range(16):
                    b = 3 + h // 8
                    k.op('pe', [MTn, xdtn], ['pb%d' % b], lambda e: e.matmul(
                        PB[b][:, (h % 8) * 64:(h % 8 + 1) * 64], lhsT=MT[:, h, :], rhs=xdt[:, h * 64:(h + 1) * 64],
                        start=True, stop=True))
                for g in range(2):
                    k.op('pe', [ctn, 's_Sbf'], ['pb%d' % (5 + g)], lambda e: e.matmul(
                        PB[5 + g][:, :], lhsT=ct[:, g, :], rhs=Sbf[:, g * 512:(g + 1) * 512], start=True, stop=True))
                for g in range(2):
                    k.op('dve', ['pb%d' % (5 + g), edn], [yon], lambda e: e.tensor_tensor(
                        out=yo[:, g * 512:(g + 1) * 512].rearrange("p (h d) -> p h d", h=8),
                        in0=PB[5 + g][:, :].rearrange("p (h d) -> p h d", h=8),
                        in1=edec[:, g * 8:(g + 1) * 8].unsqueeze(2).to_broadcast([128, 8, 64]), op=ALU.mult))
                    k.op('dve', [yon, 'pb%d' % (3 + g)], [yn_], lambda e: e.tensor_tensor(
                        out=y[:, g * 512:(g + 1) * 512], in0=yo[:, g * 512:(g + 1) * 512], in1=PB[3 + g][:, :], op=ALU.add))
                k.op('pool', [xsn, 's_dsk'], [tmpn], lambda e: e.tensor_tensor(
                    out=tmp[:].rearrange("p (h d) -> p h d", h=16), in0=xs3,
                    in1=dsk[:].unsqueeze(2).to_broadcast([128, 16, 64]), op=ALU.mult))
                k.op('pool', [yn_, tmpn], [yn_], lambda e: e.tensor_tensor(out=y[:], in0=y[:], in1=tmp[:], op=ALU.add))
                for g in range(2):
                    k.op('pe', [btmn, xddn], ['pb%d' % (5 + g)], lambda e: e.matmul(
                        PB[5 + g][:, :], lhsT=btm[:, g * 128:(g + 1) * 128], rhs=xdd[:, g * 512:(g + 1) * 512],
                        start=True, stop=True))
                k.op('dve', ['s_S', edn], ['s_S'], lambda e: e.tensor_tensor(
                    out=S[:].rearrange("p (h d) -> p h d", h=16), in0=S[:].rearrange("p (h d) -> p h d", h=16),
                    in1=edec[:, 32:48].unsqueeze(2).to_broadcast([128, 16, 64]), op=ALU.mult))
                for g in range(2):
                    k.op('dve', ['s_S', 'pb%d' % (5 + g)], ['s_S'], lambda e: e.tensor_tensor(
                        out=S[:, g * 512:(g + 1) * 512], in0=S[:, g * 512:(g + 1) * 512], in1=PB[5 + g][:, :], op=ALU.add))
                k.op('act', ['s_S'], ['s_Sbf'], lambda e: e.activation(out=Sbf[:], in_=S[:], func=AF.Copy))
                k.op('dve', [yn_, zsn], [yn_], lambda e: e.tensor_tensor(out=y[:], in0=y[:], in1=zs[:], op=ALU.mult))
                for g in range(2):
                    k.op('act', [yn_], [jn, ssn], lambda e: e.activation(
                        out=junk[:], in_=y[:, g * 512:(g + 1) * 512], func=AF.Square, accum_out=ss[:, g:g + 1]))
                k.op('dve', [ssn], [ssn], lambda e: e.tensor_scalar(
                    out=ss[:], in0=ss[:], scalar1=1.0 / 512, scalar2=EPS, op0=ALU.mult, op1=ALU.add))
                k.op('act', [ssn], [ssn], lambda e: e.sqrt(out=ss[:], in_=ss[:]))
                k.op('dve', [ssn], [ssn], lambda e: e.reciprocal(out=ss[:], in_=ss[:]))
                for g in range(2):
                    k.op('dve', [yn_, ssn, 's_nw'], [ynn], lambda e: e.scalar_tensor_tensor(
                        out=yn[:, g * 512:(g + 1) * 512], in0=y[:, g * 512:(g + 1) * 512], scalar=ss[:, g:g + 1],
                        in1=nw[:, g * 512:(g + 1) * 512], op0=ALU.mult, op1=ALU.mult))
                tog = [0]

                def dfn2(b0, nb, pt, ptn, c=c):
                    self.cast(yTs[:, b0:b0 + nb, c * 128:(c + 1) * 128], pt.rearrange("p (b t) -> p b t", t=128),
                              [ptn], ['s_yTs'], eng=('act' if (b0 // 4) % 2 == 0 else 'dve'))
                self.transpose_to(yn, ynn, 8, dfn2, tog)
            phase(0, 'A')
            for c in range(NT):
                if c + 1 < NT:
                    phase(c + 1, 'A')
                phase(c, 'B')
            k.dma(self.ysT.rearrange("(b p) t -> p b t", p=128), yTs[:], ['s_yTs'], [])
            k.barrier()

    def stage_hgrn(self, l):
        k = self.k
        PB, PT = self.PB, self.PT
        with ExitStack() as st:
            sb = lambda n, s, dt=F32: self.sbt(st, n, s, dt)
            cmask = sb("h_cmask", [128, T]); mask64 = sb("h_m64", [64, 64])
            lb = sb("h_lb", [128, 8]); oml = sb("h_oml", [128, 8]); nw = sb("h_nw", [128, 8])
            lb0 = sb("h_lb0", [128, 8])
            k.op('pool', [], ['h_cmask'], lambda e: e.memset(cmask[:], 1.0))
            k.op('pool', ['h_cmask'], ['h_cmask'], lambda e: e.memset(
                cmask[:].rearrange("p (c t) -> p c t", t=64)[:, :, 0:1], 0.0))
            k.op('pool', ['ones_f'], ['h_m64'], lambda e: e.affine_select(
                out=mask64[:], in_=self.ones_f[0:64, 0:64], pattern=[[1, 64]], compare_op=ALU.is_ge,
                fill=0.0, base=0, channel_multiplier=-1))
            k.dma(nw[:], self.hgrn_norm_w[l].rearrange("(h p) -> p h", p=128), [], ['h_nw'],
                  allow_slow_non_contiguous=True)
            if l == 0:
                k.op('pool', [], ['h_lb'], lambda e: e.memset(lb[:], 0.0))
            else:
                k.dma(lb0[:], self.hgrn_lb[0].rearrange("(h p) -> p h", p=128), [], ['h_lb0'],
                      allow_slow_non_contiguous=True)
                k.dma(lb[:], self.hgrn_lb[1].rearrange("(h p) -> p h", p=128), [], ['h_lb'],
                      allow_slow_non_contiguous=True)
                k.op('dve', ['h_lb', 'h_lb0'], ['h_lb'], lambda e: e.tensor_tensor(out=lb[:], in0=lb[:], in1=lb0[:], op=ALU.subtract))
                k.op('act', ['h_lb'], ['h_lb'], lambda e: e.activation(out=lb[:], in_=lb[:], func=AF.Sigmoid))
            k.op('dve', ['h_lb'], ['h_oml'], lambda e: e.tensor_scalar(
                out=oml[:], in0=lb[:], scalar1=-1.0, scalar2=1.0, op0=ALU.mult, op1=ALU.add))
            epsb = sb("h_eps", [128, 1])
            k.op('pool', [], ['h_eps'], lambda e: e.memset(epsb[:], EPS))
            osq = sb("h_osq", [128, 512], BF16); rt = sb("h_rt", [128, 512]); t1 = sb("h_t1", [128, 512])
            HS = []
            for p in range(2):
                d = {}
                for nm in ("sig", "f", "b", "eb", "ktf"):
                    d[nm] = (sb("h%d_%s" % (p, nm), [128, T]), "h%d_%s" % (p, nm))
                for nm in ("q", "g", "qt", "kt", "kh", "yh"):
                    d[nm] = (sb("h%d_%s" % (p, nm), [128, T], BF16), "h%d_%s" % (p, nm))
                d["v"] = (sb("h%d_v" % p, [64, 32, 128], BF16), "h%d_v" % p)
                d["S"] = (sb("h%d_S" % p, [128, 128]), "h%d_S" % p)
                d["Sbf"] = [(sb("h%d_Sbf%d" % (p, i), [128, 128], BF16), "h%d_Sbf%d" % (p, i)) for i in range(2)]
                d["smT"] = (sb("h%d_smT" % p, [64, 8, 64], BF16), "h%d_smT" % p)
                d["khT"] = (sb("h%d_khT" % p, [64, 8, 128], BF16), "h%d_khT" % p)
                d["ps_s"] = (PB[p], 'pb%d' % p)
                d["ps_o"] = (PB[2 + p], 'pb%d' % (2 + p))
                d["ps_k"] = (PB[4 + p], 'pb%d' % (4 + p))
                HS.append(d)

            def front(d, h):
                hs = slice(h * 128, (h + 1) * 128)
                sig, sgn = d["sig"]; fB, fn_ = d["f"]; bB, bn = d["b"]; eb, ebn = d["eb"]; ktf, ktfn = d["ktf"]
                q, qn = d["q"]; gg, gn = d["g"]; qt, qtn = d["qt"]; kt, ktn = d["kt"]; kh, khn = d["kh"]
                v, vn = d["v"]; S, Sn = d["S"]
                k.dma(sig[:], self.hsigT[hs, :], [], [sgn])
                k.dma(q[:], self.hqT[hs, :], [], [qn])
                k.dma(gg[:], self.hgT[hs, :], [], [gn])
                k.dma(v[:], self.hv_tm[:, hs].rearrange("(c p) v -> p c v", p=64), [], [vn])
                k.op('dve', [sgn, 'h_oml', 'h_lb'], [fn_], lambda e: e.tensor_scalar(
                    out=fB[:], in0=sig[:], scalar1=oml[:, h:h + 1], scalar2=lb[:, h:h + 1], op0=ALU.mult, op1=ALU.add))
                k.op('act', [fn_], [sgn], lambda e: e.activation(out=sig[:], in_=fB[:], func=AF.Ln))
                k.op('dve', ['h_cmask', sgn], [bn], lambda e: e.tensor_tensor_scan(
                    out=bB[:], data0=cmask[:], data1=sig[:], initial=0.0, op0=ALU.mult, op1=ALU.add))
                k.op('pool', [fn_], [fn_], lambda e: e.tensor_scalar(
                    out=fB[:], in0=fB[:], scalar1=-1.0, scalar2=1.0, op0=ALU.mult, op1=ALU.add))
                k.op('act', [bn], [ebn], lambda e: e.activation(out=eb[:], in_=bB[:], func=AF.Exp))
                k.op('dve', [qn, ebn], [qtn], lambda e: e.tensor_tensor(out=qt[:], in0=q[:], in1=eb[:], op=ALU.mult))
                k.op('pool', [bn], [bn], lambda e: e.tensor_scalar(out=bB[:], in0=bB[:], scalar1=1.0e30, scalar2=-80.0, op0=ALU.min, op1=ALU.max))
                k.op('act', [bn], [bn], lambda e: e.activation(out=bB[:], in_=bB[:], func=AF.Exp, scale=-1.0))
                k.op('dve', [fn_, bn], [ktfn], lambda e: e.tensor_tensor(out=ktf[:], in0=fB[:], in1=bB[:], op=ALU.mult))
                k.op('act', [ktfn], [ktn], lambda e: e.activation(out=kt[:], in_=ktf[:], func=AF.Copy))
                k.op('pool', [ktfn, ebn], [khn], lambda e: e.tensor_tensor(
                    out=kh[:].rearrange("p (c t) -> p c t", t=64), in0=ktf[:].rearrange("p (c t) -> p c t", t=64),
                    in1=eb[:].rearrange("p (c t) -> p c t", t=64)[:, :, 63:64].to_broadcast([128, 32, 64]), op=ALU.mult))
                k.op('pool', [], [Sn], lambda e: e.memset(S[:], 0.0))
                k.op('pool', [], [d["Sbf"][0][1]], lambda e: e.memset(d["Sbf"][0][0][:], 0.0))

            def prep(d, cg):
                qt, qtn = d["qt"]; kt, ktn = d["kt"]; kh, khn = d["kh"]
                ps_s, psn = d["ps_s"]; smT, smn = d["smT"]; khT, khTn = d["khT"]
                for c in range(8):
                    tk = slice((cg * 8 + c) * 64, (cg * 8 + c + 1) * 64)
                    k.op('pe', [ktn, qtn], [psn], lambda e: e.matmul(
                        ps_s[0:64, c * 64:(c + 1) * 64], lhsT=kt[:, tk], rhs=qt[:, tk], start=True, stop=True))
                k.op('dve', [psn, 'h_m64'], [smn], lambda e: e.tensor_tensor(
                    out=smT[:], in0=ps_s[0:64, :].rearrange("p (c t) -> p c t", t=64),
                    in1=mask64[:].unsqueeze(1).to_broadcast([64, 8, 64]), op=ALU.mult))
                for c in range(8):
                    tk = slice((cg * 8 + c) * 64, (cg * 8 + c + 1) * 64)
                    k.op('pe', [khn, 'ident_b'], ['pb7'], lambda e: e.transpose(
                        out=PT[0:64, c * 128:(c + 1) * 128], in_=kh[:, tk], identity=self.ident_b[:]))
                k.op('act', ['pb7'], [khTn], lambda e: e.activation(
                    out=khT[:], in_=PT[0:64, :].rearrange("p (c k) -> p c k", k=128), func=AF.Copy))

            def kv4(d, cg, c0):
                khT, khTn = d["khT"]; v, vn = d["v"]; pk, pkn = d["ps_k"]
                for c in range(c0, c0 + 4):
                    cc = cg * 8 + c
                    k.op('pe', [khTn, vn], [pkn], lambda e: e.matmul(
                        pk[:, (c % 4) * 128:(c % 4 + 1) * 128], lhsT=khT[:, c, :], rhs=v[:, cc, :], start=True, stop=True))

            def chunk(d, cg, c):
                cc = cg * 8 + c
                tk = slice(cc * 64, (cc + 1) * 64)
                v, vn = d["v"]; smT, smn = d["smT"]; qt, qtn = d["qt"]; eb, ebn = d["eb"]
                ps_o, pon = d["ps_o"]; pk, pkn = d["ps_k"]; S, Sn = d["S"]
                sb0, sb0n = d["Sbf"][cc % 2]; sb1, sb1n = d["Sbf"][(cc + 1) % 2]
                pks = pk[:, (c % 4) * 128:(c % 4 + 1) * 128]
                k.op('pe', [vn, smn], [pon], lambda e: e.matmul(
                    ps_o[:, c * 64:(c + 1) * 64], lhsT=v[:, cc, :], rhs=smT[:, c, :], start=True, stop=False))
                k.op('pe', [sb0n, qtn], [pon], lambda e: e.matmul(
                    ps_o[:, c * 64:(c + 1) * 64], lhsT=sb0[:], rhs=qt[:, tk], start=False, stop=True))
                esc = eb[:, cc * 64 + 63: cc * 64 + 64]
                k.op('dve', [Sn, ebn, pkn], [sb1n], lambda e: e.scalar_tensor_tensor(
                    out=sb1[:], in0=S[:], scalar=esc, in1=pks, op0=ALU.mult, op1=ALU.add))
                k.op('dve', [Sn, ebn, pkn], [Sn], lambda e: e.scalar_tensor_tensor(
                    out=S[:], in0=S[:], scalar=esc, in1=pks, op0=ALU.mult, op1=ALU.add))

            def norm(d, h, cg):
                ps_o, pon = d["ps_o"]; gg, gn = d["g"]; yh, yhn = d["yh"]
                k.op('act', [pon], ['h_osq'], lambda e: e.activation(out=osq[:], in_=ps_o[:, :], func=AF.Square))
                k.op('pe', ['ones_b', 'h_osq'], ['pb6'], lambda e: e.matmul(
                    PB[6][:, :], lhsT=self.ones_b[:], rhs=osq[:], start=True, stop=True))
                k.op('act', ['pb6', 'h_eps'], ['h_rt'], lambda e: e.activation(
                    out=rt[:], in_=PB[6][:, :], func=AF.Ln, scale=1.0 / 128, bias=epsb[:]))
                k.op('act', ['h_rt'], ['h_rt'], lambda e: e.activation(out=rt[:], in_=rt[:], func=AF.Exp, scale=-0.5))
                k.op('dve', [pon, 'h_rt'], ['h_t1'], lambda e: e.tensor_tensor(out=t1[:], in0=ps_o[:, :], in1=rt[:], op=ALU.mult))
                k.op('dve', ['h_t1', 'h_nw', gn], [yhn], lambda e: e.scalar_tensor_tensor(
                    out=yh[:, cg * 512:(cg + 1) * 512], in0=t1[:], scalar=nw[:, h:h + 1],
                    in1=gg[:, cg * 512:(cg + 1) * 512], op0=ALU.mult, op1=ALU.mult))

            for hp in range(4):
                hh = [2 * hp, 2 * hp + 1]
                for p in range(2):
                    front(HS[p], hh[p])
                for cg in range(4):
                    for p in range(2):
                        prep(HS[p], cg)
                    for c0 in (0, 4):
                        for p in range(2):
                            kv4(HS[p], cg, c0)
                        for c in range(c0, c0 + 4):
                            for p in range(2):
                                chunk(HS[p], cg, c)
                    for p in range(2):
                        norm(HS[p], hh[p], cg)
                for p in range(2):
                    k.dma(self.yhT[hh[p] * 128:(hh[p] + 1) * 128, :], HS[p]["yh"][0][:], [HS[p]["yh"][1]], [])
            k.barrier()

    def stage_fox(self, l):
        k = self.k
        PB = self.PB
        with ExitStack() as st:
            sb = lambda n, s, dt=F32: self.sbt(st, n, s, dt)
            fb = sb("x_fb", [128, 16]); ffr = sb("x_ffr", [128, NT, 32])
            xx = sb("x_xx", [128, NT, 16]); ax = sb("x_ax", [128, NT, 16]); lf = sb("x_lf", [128, NT, 16])
            c8 = sb("x_c8", [16, T]); c32 = sb("x_c32", [16, T]); ones16 = sb("x_ones16", [16, T], BF16)
            sp_ = [sb("x_sp%d" % i, [16, T], BF16) for i in range(3)]
            sn_ = [sb("x_sn%d" % i, [16, T], BF16) for i in range(3)]
            k.dma(fb[:], self.fox_f_bias[l].partition_broadcast(128), [], ['x_fb'])
            k.dma(ffr[:], self.dtff.rearrange("(tt p) c -> p tt c", p=128), [], ['x_ffr'])
            k.op('pool', [], ['x_ones16'], lambda e: e.memset(ones16[:], 1.0))
            k.op('dve', ['x_ffr', 'x_fb'], ['x_xx'], lambda e: e.tensor_tensor(
                out=xx[:], in0=ffr[:, :, 16:32], in1=fb[:].unsqueeze(1).to_broadcast([128, NT, 16]), op=ALU.add))
            k.op('act', ['x_xx'], ['x_ax'], lambda e: e.activation(out=ax[:], in_=xx[:], func=AF.Abs))
            k.op('act', ['x_ax'], ['x_ax'], lambda e: e.activation(out=ax[:], in_=ax[:], func=AF.Exp, scale=-1.0))
            k.op('act', ['x_ax'], ['x_ax'], lambda e: e.activation(out=ax[:], in_=ax[:], func=AF.Ln, bias=1.0))
            k.op('dve', ['x_xx'], ['x_xx'], lambda e: e.tensor_scalar_min(out=xx[:], in0=xx[:], scalar1=0.0))
            k.op('dve', ['x_xx', 'x_ax'], ['x_lf'], lambda e: e.tensor_tensor(out=lf[:], in0=xx[:], in1=ax[:], op=ALU.subtract))
            for tb in range(4):
                for ti in range(4):
                    i = tb * 4 + ti
                    o = PB[0][0:16, ti * 128:(ti + 1) * 128]
                    for j in range(i):
                        k.op('pe', ['x_lf', 'ones_f'], ['pb0'], lambda e: e.matmul(
                            o, lhsT=lf[:, j, :], rhs=self.ones_f[:], start=(j == 0), stop=False))
                    k.op('pe', ['x_lf', 'triI'], ['pb0'], lambda e: e.matmul(
                        o, lhsT=lf[:, i, :], rhs=self.triI[:], start=(i == 0), stop=True))
                k.op('dve', ['pb0'], ['x_c8'], lambda e: e.tensor_scalar(
                    out=c8[:, tb * 512:(tb + 1) * 512], in0=PB[0][0:16, :], scalar1=8.0, scalar2=None, op0=ALU.mult))
            for j in range(3):
                k.op('dve', ['x_c8'], ['x_sp%d' % j], lambda e: e.tensor_copy(out=sp_[j][:], in_=c8[:]))
                k.op('dve', ['x_sp%d' % j], ['x_sn%d' % j], lambda e: e.tensor_scalar(
                    out=sn_[j][:], in0=sp_[j][:], scalar1=-1.0, scalar2=None, op0=ALU.mult))
                if j < 2:
                    k.op('dve', ['x_sp%d' % j], ['x_c32'], lambda e: e.tensor_copy(out=c32[:], in_=sp_[j][:]))
                    k.op('dve', ['x_c8', 'x_c32'], ['x_c8'], lambda e: e.tensor_tensor(out=c8[:], in0=c8[:], in1=c32[:], op=ALU.subtract))
            AUG = []
            for j in range(3):
                for (dst, row, src, sn) in [(self.fqA, 64 + j, sp_[j], 'x_sp%d' % j), (self.fqA, 67 + j, ones16, 'x_ones16'),
                                            (self.fkA, 64 + j, ones16, 'x_ones16'), (self.fkA, 67 + j, sn_[j], 'x_sn%d' % j)]:
                    nm = 'aug%d' % len(AUG)
                    k.dma(dst[:, row, :], src[:], [sn], [nm])
                    AUG.append(nm)
            madd = sb("x_madd", [128, 4, 512])
            for j in range(4):
                k.op('pool', ['zeros_f'], ['x_madd'], lambda e: e.affine_select(
                    out=madd[:, j, :], in_=self.zeros_f[:], pattern=[[1, 512]], compare_op=ALU.is_ge,
                    fill=-240000.0, base=-j * 128, channel_multiplier=-1))
            sel = sb("x_sel", [65, 64])
            k.op('pool', [], ['x_sel'], lambda e: e.memset(sel[:], 0.0))
            k.op('pool', ['x_sel'], ['x_sel'], lambda e: e.memset(sel[64:65, :], 1.0))
            QA = [sb("x_QA%d" % i, [128, T], BF16) for i in range(2)]
            KA = [sb("x_KA%d" % i, [128, T], BF16) for i in range(2)]
            V = [sb("x_V%d" % i, [128, NT, 128], BF16) for i in range(2)]
            for i in range(2):
                k.op('pool', [], ['x_V%d' % i], lambda e: e.memset(V[i][:], 0.0))
                k.op('pool', ['x_V%d' % i], ['x_V%d' % i], lambda e: e.memset(V[i][:, :, 64:65], 1.0))
                k.op('pool', [], ['x_QA%d' % i], lambda e: e.memset(QA[i][:], 0.0))
                k.op('pool', [], ['x_KA%d' % i], lambda e: e.memset(KA[i][:], 0.0))
            Pt = [sb("x_P%d" % i, [128, 512], BF16) for i in range(6)]
            smk = [sb("x_smk%d" % i, [128, 512]) for i in range(3)]
            osb = [sb("x_osb%d" % i, [65, 512]) for i in range(2)]
            rl = sb("x_rl", [64, 512])
            yf = [sb("x_yf%d" % i, [64, T], BF16) for i in range(2)]

            def load_head(h):
                i = h % 2
                k.dma(QA[i][0:70, :], self.fqA[h], AUG, ['x_QA%d' % i])
                k.dma(KA[i][0:70, :], self.fkA[h], AUG, ['x_KA%d' % i])
                k.dma(V[i][:, :, 0:64], self.fv_tm[:, h * 64:(h + 1) * 64].rearrange("(tt p) d -> p tt d", p=128),
                      [], ['x_V%d' % i])

            seq = [(h, qb, kb) for h in range(16) for qb in range(4) for kb in range(4 * (qb + 1))]
            SB = [0, 1, 2, 3, 7]
            LA = 4

            def emit_S(idx):
                h, qb, kb = seq[idx]
                i = h % 2
                b = SB[idx % 5]
                k.op('pe', ['x_KA%d' % i, 'x_QA%d' % i], ['pb%d' % b], lambda e: e.matmul(
                    PB[b][:, :], lhsT=KA[i][:, kb * 128:(kb + 1) * 128], rhs=QA[i][:, qb * 512:(qb + 1) * 512],
                    start=True, stop=True))

            pending = []

            def emit_norm(h, qb, po, pon, oi):
                i = h % 2
                ob, obn = osb[oi], 'x_osb%d' % oi
                k.op('act', [pon], [obn], lambda e: e.activation(out=ob[:], in_=po[0:65, :], func=AF.Copy))

                k.op('act', [obn], [obn], lambda e: e.activation(out=ob[64:65, :], in_=ob[64:65, :], func=AF.Ln))
                k.op('act', [obn], [obn], lambda e: e.activation(out=ob[64:65, :], in_=ob[64:65, :], func=AF.Exp, scale=-1.0))

                def rest():
                    k.op('pe', ['x_sel', obn], ['pb6'], lambda e: e.matmul(
                        PB[6][0:64, :], lhsT=sel[64:65, :], rhs=ob[64:65, :], start=True, stop=True))
                    k.op('dve', [obn, 'pb6'], ['x_yf%d' % i], lambda e: e.tensor_tensor(
                        out=yf[i][:, qb * 512:(qb + 1) * 512], in0=ob[0:64, :], in1=PB[6][0:64, :], op=ALU.mult))
                    if qb == 3:
                        k.dma(self.yfT[h], yf[i][:], ['x_yf%d' % i], [])
                return rest

            load_head(0)
            for j in range(LA):
                emit_S(j)
            oit = 0
            for idx, (h, qb, kb) in enumerate(seq):
                i = h % 2
                if qb == 0 and kb == 0 and h + 1 < 16:
                    load_head(h + 1)
                if idx + LA < len(seq):
                    emit_S(idx + LA)
                nkb = 4 * (qb + 1)
                b = SB[idx % 5]
                ps, psn = PB[b], 'pb%d' % b
                pt, ptn = Pt[idx % 6], 'x_P%d' % (idx % 6)
                j = kb - 4 * qb
                if j >= 0:
                    sm_, smn = smk[kb % 3], 'x_smk%d' % (kb % 3)
                    k.op('dve', [psn, 'x_madd'], [smn], lambda e: e.tensor_tensor(
                        out=sm_[:], in0=ps[:, :], in1=madd[:, j, :], op=ALU.add))
                    k.op('act', [smn], [ptn], lambda e: e.activation(out=pt[:], in_=sm_[:], func=AF.Exp, scale=0.125))
                else:
                    k.op('act', [psn], [ptn], lambda e: e.activation(out=pt[:], in_=ps[:, :], func=AF.Exp, scale=0.125))
                if kb == 0:
                    oit += 1
                po, pon = (PB[4], 'pb4') if oit % 2 == 0 else (PB[5], 'pb5')
                k.op('pe', ['x_V%d' % i, ptn], [pon], lambda e: e.matmul(
                    po[:, :], lhsT=V[i][:, kb, :], rhs=pt[:], start=(kb == 0), stop=(kb == nkb - 1)))
                if pending and pending[0][0] <= idx:
                    pending.pop(0)[1]()
                if kb == nkb - 1:
                    pending.append((idx + 2, emit_norm(h, qb, po, pon, oit % 2)))
            for _, f in pending:
                f()
            k.barrier()

    def stage_merge(self, l):
        k = self.k
        PB = self.PB
        with ExitStack() as st:
            sb = lambda n, s, dt=F32: self.sbt(st, n, s, dt)
            ys = sb("m_ys", [128, 8, T], BF16); yh = sb("m_yh", [128, 8, T], BF16); yf = sb("m_yf", [128, 8, T], BF16)
            for tb in range(4):
                tsl = slice(tb * 512, (tb + 1) * 512)
                k.dma(ys[:, :, tsl], self.ysT.rearrange("(b p) t -> p b t", p=128)[:, :, tsl], [], ['m_ys%d' % tb])
                k.dma(yh[:, :, tsl], self.yhT.rearrange("(b p) t -> p b t", p=128)[:, :, tsl], [], ['m_yh%d' % tb])
                k.dma(yf[:, :, tsl], self.yfT.rearrange("(b h2) d t -> (h2 d) b t", h2=2)[:, :, tsl], [], ['m_yf%d' % tb])
            wps = self.mk_wpool(st, "mws", 8, 128, dd=3)
            wph = self.mk_wpool(st, "mwh", 8, 128, dd=3)
            wpf = self.mk_wpool(st, "mwf", 8, 128, dd=3)
            gts = [[sb("m_g%d_%d" % (r, i), [128, T], BF16) for i in range(2)] for r in range(3)]
            m1 = [sb("m_m1_%d" % i, [128, 512]) for i in range(2)]
            m2 = [sb("m_m2_%d" % i, [128, 512]) for i in range(2)]
            mo = [sb("m_mo%d" % i, [128, T], BF16) for i in range(2)]
            state = {'it': 0}
            items = []
            ysrc = [(ys, 'm_ys'), (yh, 'm_yh'), (yf, 'm_yf')]

            def mk(db):
                def fn(ws):
                    gi = db % 2
                    for r in range(3):
                        k.dma(gts[r][gi][:], self.gT[r * D + db * 128: r * D + (db + 1) * 128, :], [], ['m_g%d_%d' % (r, gi)])
                    for tb in range(4):
                        i = state['it'] % 2
                        state['it'] += 1
                        bs = [3 * i, 3 * i + 1, 3 * i + 2]
                        tsl = slice(tb * 512, (tb + 1) * 512)
                        for r in range(3):
                            wb, wbn = ws[r]
                            yt, ytn = ysrc[r]
                            for kc in range(8):
                                k.op('pe', [wbn, ytn + str(tb)], ['pb%d' % bs[r]], lambda e: e.matmul(
                                    PB[bs[r]][:, :], lhsT=wb[:, kc, :], rhs=yt[:, kc, tsl], start=(kc == 0), stop=(kc == 7)))
                        a1, a1n = m1[i], 'm_m1_%d' % i
                        a2, a2n = m2[i], 'm_m2_%d' % i
                        k.op('dve', ['pb%d' % bs[0], 'm_g0_%d' % gi], [a1n], lambda e: e.tensor_tensor(
                            out=a1[:], in0=PB[bs[0]][:, :], in1=gts[0][gi][:, tsl], op=ALU.mult))
                        k.op('dve', ['pb%d' % bs[1], 'm_g1_%d' % gi], [a2n], lambda e: e.tensor_tensor(
                            out=a2[:], in0=PB[bs[1]][:, :], in1=gts[1][gi][:, tsl], op=ALU.mult))
                        k.op('dve', [a1n, a2n], [a1n], lambda e: e.tensor_tensor(out=a1[:], in0=a1[:], in1=a2[:], op=ALU.add))
                        k.op('dve', ['pb%d' % bs[2], 'm_g2_%d' % gi], [a2n], lambda e: e.tensor_tensor(
                            out=a2[:], in0=PB[bs[2]][:, :], in1=gts[2][gi][:, tsl], op=ALU.mult))
                        k.op('dve', [a1n, a2n], ['m_mo%d' % gi], lambda e: e.tensor_tensor(
                            out=mo[gi][:, tsl], in0=a1[:], in1=a2[:], op=ALU.add))
                    k.dma(self.mT[db * 128:(db + 1) * 128, :], mo[gi][:], ['m_mo%d' % gi], [])
                return fn
            for db in range(16):
                c0 = db * 128
                items.append(([(wps, [(0, self.w_branch_ssd[l, :, c0:c0 + 128])], 8, 128),
                               (wph, [(0, self.w_branch_hgrn[l, :, c0:c0 + 128])], 8, 128),
                               (wpf, [(0, self.w_branch_fox[l, :, c0:c0 + 128])], 8, 128)], mk(db)))
            self.run_pipeline(items)
            k.barrier()

    def stage_out_proj(self, AT_dram, nkc, Wsrc, xsrc, xdst, halves):
        k = self.k
        PB = self.PB
        TH = T // halves
        ntt = TH // 128
        with ExitStack() as st:
            sb = lambda n, s, dt=F32: self.sbt(st, n, s, dt)
            A = sb("o_A", [128, nkc, TH], BF16)
            AG = 4
            resident = nkc <= 16
            KG = nkc if resident else 4
            wp = self.mk_wpool(st, "ow", KG, 512, dd=(2 if resident else 4))
            xs = [sb("o_x%d" % i, [128, 512]) for i in range(16)]
            state = {'g': 0}
            items = []

            def mk(hf, cb, grps, kg, n_k, first, last, gidx0):
                def fn(ws):
                    wb, wbn = ws[0]
                    if first:
                        for c0 in range(0, nkc, AG):
                            c1 = min(nkc, c0 + AG)
                            k.dma(A[:, c0:c1, :], AT_dram[c0 * 128:c1 * 128, hf * TH:(hf + 1) * TH].rearrange("(c p) t -> p c t", p=128),
                                  [], ['o_A%d' % (c0 // AG)])
                    for gi, grp in enumerate(grps):
                        gidx = gidx0 + gi
                        if kg == 0:
                            for bi, tt in enumerate(grp):
                                j = (gidx % 2) * 8 + bi
                                r0 = hf * TH + tt * 128
                                k.dma(xs[j][:], xsrc[r0:r0 + 128, cb * 512:(cb + 1) * 512], [], ['o_x%d' % j])
                        for kk in range(n_k):
                            kc = kg + kk
                            for bi, tt in enumerate(grp):
                                k.op('pe', [wbn, 'o_A%d' % (kc // AG)], ['pb%d' % bi], lambda e: e.matmul(
                                    PB[bi][:, :], lhsT=A[:, kc, tt * 128:(tt + 1) * 128], rhs=wb[:, kk, :],
                                    start=(kc == 0), stop=(kc == nkc - 1)))
                        if last:
                            for bi, tt in enumerate(grp):
                                j = (gidx % 2) * 8 + bi
                                xt, xn = xs[j], 'o_x%d' % j
                                r0 = hf * TH + tt * 128
                                k.op('dve', [xn, 'pb%d' % bi], [xn], lambda e: e.tensor_tensor(
                                    out=xt[:], in0=xt[:], in1=PB[bi][:, :], op=ALU.add))
                                k.dma(xdst[r0:r0 + 128, cb * 512:(cb + 1) * 512], xt[:], [xn], [])
                return fn
            for hf in range(halves):
                first = True
                for cb in range(D // 512):
                    groups = [list(range(tt0, min(ntt, tt0 + 8))) for tt0 in range(0, ntt, 8)]
                    gsets = [groups] if resident else [[g] for g in groups]
                    for grps in gsets:
                        for kg in range(0, nkc, KG):
                            n_k = min(KG, nkc - kg)
                            items.append(([(wp, [(0, Wsrc[kg * 128:(kg + n_k) * 128, cb * 512:(cb + 1) * 512])], n_k, 512)],
                                          mk(hf, cb, grps, kg, n_k, first, kg + n_k == nkc, state['g'])))
                            first = False
                        state['g'] += len(grps)
            self.run_pipeline(items)
            k.barrier()

    def stage_ffn_up(self, l, hT, hTn):
        k = self.k
        PB = self.PB
        W = self.ffn_w_up
        with ExitStack() as st:
            sb = lambda n, s, dt=F32: self.sbt(st, n, s, dt)
            wp = self.mk_wpool(st, "fw", KC, 256)
            cw = sb("f_cw", [128, 2 * NFB, 3]); cb = sb("f_cb", [128, 2 * NFB])
            for b in range(2 * NFB):
                k.dma(cw[:, b, :], self.ffn_conv_w[l, :, b * 128:(b + 1) * 128].rearrange("k p -> p k"),
                      [], ['f_cw%d' % b], allow_slow_non_contiguous=True)
            k.dma(cb[:], self.ffn_conv_b[l].rearrange("(b p) -> p b", p=128), [], ['f_cb'],
                  allow_slow_non_contiguous=True)
            xp = [sb("f_xp%d" % i, [128, T + 2]) for i in range(2)]
            acc = [sb("f_acc%d" % i, [128, T]) for i in range(2)]
            sg = sb("f_sg", [128, T])
            ao = [sb("f_ao%d" % i, [128, T], BF16) for i in range(2)]
            for i in range(2):
                k.op('pool', [], ['f_xp%d' % i], lambda e: e.memset(xp[i][:, 0:2], 0.0))
            state = {'bank': 0, 'ev': 0}
            items = []

            def mk(j):
                def fn(ws):
                    wb, wbn = ws[0]
                    for half in range(2):
                        blk = j + half * NFB
                        x_, xn = xp[half], 'f_xp%d' % half
                        a_, an = acc[half], 'f_acc%d' % half
                        for tb in range(4):
                            b = state['bank'] % 8
                            state['bank'] += 1
                            for kc in range(KC):
                                k.op('pe', [wbn, hTn], ['pb%d' % b], lambda e: e.matmul(
                                    PB[b][:, :], lhsT=wb[:, kc, half * 128:(half + 1) * 128],
                                    rhs=hT[:, kc, tb * 512:(tb + 1) * 512],
                                    start=(kc == 0), stop=(kc == KC - 1)))
                            eng = ('act', 'act', 'dve')[state['ev'] % 3]
                            state['ev'] += 1
                            self.cast(x_[:, 2 + tb * 512: 2 + (tb + 1) * 512], PB[b][:, :], ['pb%d' % b], [xn], eng=eng)
                        k.op('dve', [xn, 'f_cw%d' % blk, 'f_cb'], [an], lambda e: e.tensor_scalar(
                            out=a_[:], in0=x_[:, 0:T], scalar1=cw[:, blk, 0:1], scalar2=cb[:, blk:blk + 1],
                            op0=ALU.mult, op1=ALU.add))
                        for kk in range(1, 3):
                            k.op('dve', [xn, 'f_cw%d' % blk, an], [an], lambda e: e.scalar_tensor_tensor(
                                out=a_[:], in0=x_[:, kk:kk + T], scalar=cw[:, blk, kk:kk + 1], in1=a_[:],
                                op0=ALU.mult, op1=ALU.add))
                    k.op('act', ['f_acc0'], ['f_sg'], lambda e: e.activation(out=sg[:], in_=acc[0][:], func=AF.Silu))
                    o, on = ao[j % 2], 'f_ao%d' % (j % 2)
                    k.op('pool', ['f_sg', 'f_acc1'], [on], lambda e: e.tensor_tensor(out=o[:], in0=sg[:], in1=acc[1][:], op=ALU.mult))
                    k.dma(self.actT[j * 128:(j + 1) * 128, :], o[:], [on], [])
                return fn
            for j in range(NFB):
                items.append(([(wp, [(0, W[l, :, j * 128:(j + 1) * 128]),
                                     (128, W[l, :, (NFB + j) * 128:(NFB + j + 1) * 128])], KC, 256)], mk(j)))
            self.run_pipeline(items)
            k.barrier()


    def stage_final(self, src):
        k = self.k
        with ExitStack() as st:
            sb = lambda n, s, dt=F32: self.sbt(st, n, s, dt)
            wN = sb("z_w", [128, D])
            k.dma(wN[:], self.final_norm_w.partition_broadcast(128), [], ['z_w'])
            xts = [sb("z_x%d" % i, [128, D]) for i in range(2)]
            ots = [sb("z_o%d" % i, [128, D]) for i in range(2)]
            junk = sb("z_junk", [128, D], BF16)
            sss = [sb("z_ss%d" % i, [128, 1]) for i in range(2)]
            for tt in range(NT):
                i = tt % 2
                xt, ot, ss = xts[i], ots[i], sss[i]
                xn, on, sn = "z_x%d" % i, "z_o%d" % i, "z_ss%d" % i
                k.dma(xt[:], src[tt * 128:(tt + 1) * 128, :], [], [xn])
                k.op('act', [xn], ['z_junk', sn], lambda e: e.activation(
                    out=junk[:], in_=xt[:], func=AF.Square, accum_out=ss[:]))
                k.op('dve', [sn], [sn], lambda e: e.tensor_scalar(
                    out=ss[:], in0=ss[:], scalar1=1.0 / D, scalar2=EPS, op0=ALU.mult, op1=ALU.add))
                k.op('act', [sn], [sn], lambda e: e.sqrt(out=ss[:], in_=ss[:]))
                k.op('dve', [sn], [sn], lambda e: e.reciprocal(out=ss[:], in_=ss[:]))
                k.op('dve', [xn, sn, 'z_w'], [on], lambda e: e.scalar_tensor_tensor(
                    out=ot[:], in0=xt[:], scalar=ss[:, 0:1], in1=wN[:], op0=ALU.mult, op1=ALU.mult))
                k.dma(self.out[tt * 128:(tt + 1) * 128, :], ot[:], [on], [])
            k.barrier()

    def finish(self):
        k = self.k
        nc = self.nc
        done = nc.alloc_semaphore("sem_done")
        for e in ('pe', 'act', 'dve', 'pool'):
            k.E[e].sem_inc(done, 1)
        sp = k.E['sp']
        sp.wait_ge(done, 4)
        for e in k.E:
            sp.sem_clear(k.sem[e])
        for s_ in k.dsem:
            sp.sem_clear(s_)
        sp.sem_clear(done)

    def build(self, nlayers=2, upto=None):
        k = self.k
        src = self.x
        stop = False
        for l in range(nlayers):
            with ExitStack() as st:
                if upto == 'init':
                    return
                hT = self.sbt(st, "hT", [128, KC, T], BF16)
                self.norm_T(src, self.norm_mix_w[l], hT, 'hT')
                if upto == 'norm':
                    return
                self.stage_proj(l, hT, 'hT')
            if upto == 'proj':
                return
            self.stage_ssd(l)
            if upto == 'ssd':
                return
            self.stage_hgrn(l)
            if upto == 'hgrn':
                return
            self.stage_fox(l)
            if upto == 'fox':
                return
            self.stage_merge(l)
            if upto == 'merge':
                return
            self.stage_out_proj(self.mT, 16, self.w_out[l], src, self.xa, halves=1)
            if upto == 'mix':
                return
            with ExitStack() as st:
                hT = self.sbt(st, "hT", [128, KC, T], BF16)
                self.norm_T(self.xa, self.norm_ffn_w[l], hT, 'hT')
                self.stage_ffn_up(l, hT, 'hT')
            if upto == 'ffn_up':
                return
            self.stage_out_proj(self.actT, NFB, self.ffn_w_down[l], self.xa, self.xb, halves=2)
            if upto == 'layer':
                return
            src = self.xb
        self.stage_final(self.xb)
        self.finish()


WNAMES = ["norm_mix_w", "w_in", "ssd_conv_w", "ssd_conv_b", "ssd_dt_bias", "ssd_a_log", "ssd_d",
          "ssd_norm_w", "hgrn_lb", "hgrn_norm_w", "fox_f_bias", "w_branch_ssd", "w_branch_hgrn",
          "w_branch_fox", "w_out", "norm_ffn_w", "ffn_w_up", "ffn_conv_w", "ffn_conv_b",
          "ffn_w_down", "final_norm_w"]


def kernel(**inputs):
    nc = bass.Bass("TRN2", target_bir_lowering=False)
    p = Prog(nc)
    p.build()
    x = np.ascontiguousarray(inputs["x"], dtype=np.float32)
    shared = {n: np.ascontiguousarray(inputs[n], dtype=np.float32) for n in WNAMES}
    in_maps = []
    for c in range(8):
        m = dict(shared)
        m["x"] = x[c]
        in_maps.append(m)
    res = run_bass_kernel_spmd(nc, in_maps, core_ids=list(range(8)))
    return np.stack([np.asarray(r["out"], dtype=np.float32) for r in res.results], axis=0)
```
